# Optimizing a Trainium2 kernel written in Bass

```python
import math
import jax
import jax.numpy as jnp
from jax import lax

D_MODEL = 1024
BATCH = 4
SEQ = 4096
DEPTH = 2

F32 = jnp.float32
MEM_LEN = 256
MAX_OFFSET = 1024
BLOCK = 128
EPS = 1e-6
NEG_INF = -1e30
ROPE_BASE = 10000.0
BRANCH_W = 512
N_BRANCH = 4

RET_HEADS = 4
RET_DK = 64
RET_DV = BRANCH_W // RET_HEADS
RET_QK = RET_HEADS * RET_DK

SSM_HEADDIM = 64
SSM_INNER = BRANCH_W
SSM_HEADS = SSM_INNER // SSM_HEADDIM
SSM_GROUPS = 2
SSM_STATE = 128
SSM_CONV = 4
SSM_CHUNK = 128
SSM_XBC = SSM_INNER + 2 * SSM_GROUPS * SSM_STATE
DT_MIN = 1e-3
DT_MAX = 1e-1

MLA_HEADS = 4
MLA_Q_RANK = 256
MLA_KV_RANK = 128
MLA_NOPE = 64
MLA_ROPE = 32
MLA_V = BRANCH_W // MLA_HEADS
MLA_QH = MLA_NOPE + MLA_ROPE

DIFF_HEADS = 4
DIFF_DH = BRANCH_W // (2 * DIFF_HEADS)
DIFF_W = BRANCH_W

REL_BUCKETS = 32
REL_MAX_DIST = 128

XA_HEADS = 4
XA_DH = D_MODEL // XA_HEADS

D_FF = ((8 * D_MODEL // 3 + 127) // 128) * 128

IN_SIZES = (RET_QK, RET_QK, BRANCH_W, BRANCH_W,
            SSM_INNER, SSM_XBC, SSM_HEADS,
            MLA_Q_RANK, MLA_KV_RANK, MLA_ROPE,
            DIFF_W, DIFF_W, DIFF_W,
            N_BRANCH * D_MODEL)
IN_TOTAL = sum(IN_SIZES)

kernel_name = 'hybrid_gated_retention_ssd_mla_diffattn'


def split_last(x, sizes):
    out, start = [], 0
    for s in sizes:
        out.append(x[..., start:start + s])
        start += s
    return out


def rms_norm(x, g):
    xf = x.astype(F32)
    y = xf * lax.rsqrt(jnp.mean(xf * xf, axis=-1, keepdims=True) + EPS)
    return (y * g.astype(F32)).astype(x.dtype)


def group_norm(x):
    xf = x.astype(F32)
    mu = jnp.mean(xf, axis=-1, keepdims=True)
    xc = xf - mu
    return xc * lax.rsqrt(jnp.mean(xc * xc, axis=-1, keepdims=True) + EPS)


def rope(x, pos):
    d = x.shape[-1]
    half = d // 2
    inv = jnp.exp(-math.log(ROPE_BASE) * jnp.arange(half, dtype=F32) / half)
    ang = pos.astype(F32)[:, :, None, None] * inv
    cos, sin = jnp.cos(ang), jnp.sin(ang)
    xf = x.astype(F32)
    x1, x2 = xf[..., :half], xf[..., half:]
    return jnp.concatenate([x1 * cos - x2 * sin, x1 * sin + x2 * cos], axis=-1).astype(x.dtype)


def swiglu(x, w_gate, w_up, w_down):
    return (jax.nn.silu(x @ w_gate) * (x @ w_up)) @ w_down


def t5_bucket(rel):
    n = jnp.maximum(rel, 0)
    exact = REL_BUCKETS // 2
    nf = jnp.maximum(n, 1).astype(F32)
    large = exact + (jnp.log(nf / exact) / math.log(REL_MAX_DIST / exact)
                     * (REL_BUCKETS - exact)).astype(jnp.int32)
    return jnp.where(n < exact, n, jnp.minimum(large, REL_BUCKETS - 1))


def retention(rq, rk, rv, rg, pos):
    B_, S, _ = rq.shape
    H, dk, dv, C = RET_HEADS, RET_DK, RET_DV, BLOCK
    N = S // C
    q = rope(rq.reshape(B_, S, H, dk), pos).astype(F32)
    k = rope(rk.reshape(B_, S, H, dk), pos).astype(F32) * (dk ** -0.5)
    v = rv.reshape(B_, S, H, dv).astype(F32)
    log_g = jnp.log1p(-jnp.exp2(-5.0 - jnp.arange(H, dtype=F32)))
    i = jnp.arange(C, dtype=F32)
    rel = i[:, None] - i[None, :]
    intra = jnp.where(rel >= 0, jnp.exp(log_g[:, None, None] * jnp.maximum(rel, 0.0)), 0.0)
    q_dec = jnp.exp(log_g[:, None] * (i + 1.0))
    k_dec = jnp.exp(log_g[:, None] * (C - 1.0 - i))
    c_dec = jnp.exp(log_g * C)
    qc = q.reshape(B_, N, C, H, dk)
    kc = k.reshape(B_, N, C, H, dk)
    vc = v.reshape(B_, N, C, H, dv)
    att = jnp.einsum('bnihd,bnjhd->bnhij', qc, kc) * intra
    inner = jnp.einsum('bnhij,bnjhe->bnihe', att, vc)
    kv = jnp.einsum('bnjhd,hj,bnjhe->nbhde', kc, k_dec, vc)

    def step(state, kv_n):
        return state * c_dec[None, :, None, None] + kv_n, state

    _, prev = lax.scan(step, jnp.zeros_like(kv[0]), kv)
    cross = jnp.einsum('bnihd,nbhde,hi->bnihe', qc, prev, q_dec)
    o = group_norm((inner + cross).reshape(B_, S, H, dv))
    return (o.reshape(B_, S, H * dv) * jax.nn.silu(rg.astype(F32))).astype(rq.dtype)


def mamba2_ssd(z, xbc, dt, conv_w, conv_b, dt_bias, a_log, d_skip, norm_g):
    B_, S, ch = xbc.shape
    H, P, G, NS, L = SSM_HEADS, SSM_HEADDIM, SSM_GROUPS, SSM_STATE, SSM_CHUNK
    HG = H // G
    Nc = S // L
    conv = lax.conv_general_dilated(xbc, conv_w[:, None, :], window_strides=(1,),
                                    padding=[(SSM_CONV - 1, 0)],
                                    dimension_numbers=('NWC', 'WIO', 'NWC'),
                                    feature_group_count=ch)
    xbc = jax.nn.silu((conv + conv_b).astype(F32))
    xs, Bm, Cm = split_last(xbc, (SSM_INNER, G * NS, G * NS))
    dt = jax.nn.softplus(dt.astype(F32) + dt_bias.astype(F32))
    A = -jnp.exp(a_log.astype(F32))
    x_h = xs.reshape(B_, S, H, P)
    xdt = (x_h * dt[..., None]).reshape(B_, Nc, L, G, HG, P)
    dA = (dt * A).reshape(B_, Nc, L, G, HG)
    Bc = Bm.reshape(B_, Nc, L, G, NS)
    Cc = Cm.reshape(B_, Nc, L, G, NS)
    cs = jnp.cumsum(dA, axis=2)
    causal = (jnp.arange(L)[:, None] >= jnp.arange(L)[None, :]).astype(F32)
    seg = cs[:, :, :, None] - cs[:, :, None, :]
    decay = jnp.exp(jnp.minimum(seg, 0.0)) * causal[None, None, :, :, None, None]
    CB = jnp.einsum('bclgn,bcsgn->bclsg', Cc, Bc)
    y_diag = jnp.einsum('bclsg,bclsgk,bcsgkp->bclgkp', CB, decay, xdt)
    decay_states = jnp.exp(cs[:, :, -1:] - cs)
    states = jnp.einsum('bclgn,bclgk,bclgkp->cbgkpn', Bc, decay_states, xdt)
    chunk_decay = jnp.moveaxis(jnp.exp(cs[:, :, -1]), 1, 0)

    def step(hs, inp):
        s_c, a_c = inp
        return hs * a_c[..., None, None] + s_c, hs

    _, h_prev = lax.scan(step, jnp.zeros_like(states[0]), (states, chunk_decay))
    y_off = jnp.einsum('bclgn,cbgkpn,bclgk->bclgkp', Cc, h_prev, jnp.exp(cs))
    y = (y_diag + y_off).reshape(B_, S, H, P) + x_h * d_skip.astype(F32)[:, None]
    y = y.reshape(B_, S, SSM_INNER) * jax.nn.silu(z.astype(F32))
    return rms_norm(y, norm_g).astype(z.dtype)


def causal_block_attention(q, k, v, scale):
    B_, S, H, dk = q.shape
    nb = S // BLOCK
    qb = jnp.moveaxis(q.reshape(B_, nb, BLOCK, H, dk), 1, 0)
    kidx = jnp.arange(S)

    def one(args):
        qi, bi = args
        s = jnp.einsum('bqhd,bkhd->bhqk', qi, k).astype(F32) * scale
        qidx = bi * BLOCK + jnp.arange(BLOCK)
        s = jnp.where(kidx[None, :] <= qidx[:, None], s, NEG_INF)
        p = jax.nn.softmax(s, axis=-1).astype(v.dtype)
        return jnp.einsum('bhqk,bkhe->bqhe', p, v)

    o = lax.map(one, (qb, jnp.arange(nb)))
    return jnp.moveaxis(o, 0, 1).reshape(B_, S, H, v.shape[-1])


def mla(c_q, c_kv, k_rope, pos, q_norm, kv_norm, w_uq, w_ukv):
    B_, S, _ = c_q.shape
    H = MLA_HEADS
    q = (rms_norm(c_q, q_norm) @ w_uq).reshape(B_, S, H, MLA_QH)
    q_nope, q_pe = q[..., :MLA_NOPE], q[..., MLA_NOPE:]
    q = jnp.concatenate([q_nope, rope(q_pe, pos)], axis=-1)
    kv = (rms_norm(c_kv, kv_norm) @ w_ukv).reshape(B_, S, H, MLA_NOPE + MLA_V)
    k_nope, v = kv[..., :MLA_NOPE], kv[..., MLA_NOPE:]
    k_pe = jnp.broadcast_to(rope(k_rope[:, :, None, :], pos), (B_, S, H, MLA_ROPE))
    k = jnp.concatenate([k_nope, k_pe], axis=-1)
    o = causal_block_attention(q, k, v, MLA_QH ** -0.5)
    return o.reshape(B_, S, H * MLA_V)


def diff_attention(dq, dk, dv, pos, rel_bias, lam_params, lambda_init, sub_g):
    B_, S, _ = dq.shape
    H, dh = DIFF_HEADS, DIFF_DH
    nb = S // BLOCK
    scale = dh ** -0.5
    lp = lam_params.astype(F32)
    lam = jnp.exp(jnp.sum(lp[0] * lp[1])) - jnp.exp(jnp.sum(lp[2] * lp[3])) + lambda_init
    k = dk.reshape(B_, S, 2 * H, dh)
    v = dv.reshape(B_, S, H, 2 * dh)
    qb = jnp.moveaxis(dq.reshape(B_, nb, BLOCK, 2 * H, dh), 1, 0)
    pb = jnp.moveaxis(pos.reshape(B_, nb, BLOCK), 1, 0)
    kidx = jnp.arange(S)
    table = rel_bias.astype(F32)

    def one(args):
        qi, pi, bi = args
        s = jnp.einsum('bqnd,bknd->bnqk', qi, k).astype(F32) * scale
        bias = jnp.moveaxis(table[t5_bucket(pi[:, :, None] - pos[:, None, :])], -1, 1)
        s = s.reshape(B_, H, 2, BLOCK, S) + bias[:, :, None]
        qidx = bi * BLOCK + jnp.arange(BLOCK)
        s = jnp.where(kidx[None, :] <= qidx[:, None], s, NEG_INF)
        p = jax.nn.softmax(s, axis=-1)
        a = (p[:, :, 0] - lam * p[:, :, 1]).astype(v.dtype)
        return jnp.einsum('bhqk,bkhe->bqhe', a, v)

    o = lax.map(one, (qb, pb, jnp.arange(nb)))
    o = jnp.moveaxis(o, 0, 1).reshape(B_, S, H, 2 * dh)
    o = rms_norm(o, sub_g) * (1.0 - lambda_init)
    return o.reshape(B_, S, H * 2 * dh)


def cross_attention(hn, mn, wq, wk, wv, wo):
    B_, S, D = hn.shape
    M = mn.shape[1]
    q = (hn @ wq).reshape(B_, S, XA_HEADS, XA_DH)
    k = (mn @ wk).reshape(B_, M, XA_HEADS, XA_DH)
    v = (mn @ wv).reshape(B_, M, XA_HEADS, XA_DH)
    s = jnp.einsum('bqhd,bkhd->bhqk', q, k).astype(F32) * (XA_DH ** -0.5)
    p = jax.nn.softmax(s, axis=-1).astype(v.dtype)
    o = jnp.einsum('bhqk,bkhd->bqhd', p, v).reshape(B_, S, D)
    return o @ wo


def setup_inputs(seed: int = 0) -> dict:
    key = jax.random.key(seed)
    keys = iter(list(jax.random.split(key, 48)))
    L, D = DEPTH, D_MODEL

    def nrm(shape, fan_in):
        return jax.random.normal(next(keys), shape, F32) * (fan_in ** -0.5)

    def gain(shape):
        return 1.0 + 0.02 * jax.random.normal(next(keys), shape, F32)

    x = jax.random.normal(next(keys), (BATCH, SEQ, D), F32)
    mem = jax.random.normal(next(keys), (BATCH, MEM_LEN, D), F32)
    offsets = jax.random.randint(next(keys), (BATCH, 1), 0, MAX_OFFSET, dtype=jnp.int32)
    positions = offsets + jnp.arange(SEQ, dtype=jnp.int32)[None, :]
    dt0 = jnp.exp(jax.random.uniform(next(keys), (L, SSM_HEADS), F32, math.log(DT_MIN), math.log(DT_MAX)))
    return {
        'x': x,
        'mem': mem,
        'positions': positions,
        'ffn1_norm': gain((L, D)),
        'ffn1_w_gate': nrm((L, D, D_FF), D),
        'ffn1_w_up': nrm((L, D, D_FF), D),
        'ffn1_w_down': nrm((L, D_FF, D), D_FF),
        'mix_norm': gain((L, D)),
        'w_in': nrm((L, D, IN_TOTAL), D),
        'ssm_conv_w': nrm((L, SSM_CONV, SSM_XBC), SSM_CONV),
        'ssm_conv_b': 0.02 * jax.random.normal(next(keys), (L, SSM_XBC), F32),
        'ssm_dt_bias': dt0 + jnp.log(-jnp.expm1(-dt0)),
        'ssm_a_log': jnp.log(jax.random.uniform(next(keys), (L, SSM_HEADS), F32, 1.0, 16.0)),
        'ssm_d': gain((L, SSM_HEADS)),
        'ssm_norm': gain((L, SSM_INNER)),
        'mla_q_norm': gain((L, MLA_Q_RANK)),
        'mla_kv_norm': gain((L, MLA_KV_RANK)),
        'mla_w_uq': nrm((L, MLA_Q_RANK, MLA_HEADS * MLA_QH), MLA_Q_RANK),
        'mla_w_ukv': nrm((L, MLA_KV_RANK, MLA_HEADS * (MLA_NOPE + MLA_V)), MLA_KV_RANK),
        'diff_lambda': 0.1 * jax.random.normal(next(keys), (L, 4, DIFF_DH), F32),
        'diff_norm': gain((L, 2 * DIFF_DH)),
        'rel_bias': 0.5 * jax.random.normal(next(keys), (REL_BUCKETS, DIFF_HEADS), F32),
        'w_branch': nrm((L, N_BRANCH, BRANCH_W, D), BRANCH_W),
        'w_out': nrm((L, D, D), D),
        'xa_norm': gain((L, D)),
        'mem_norm': gain((L, D)),
        'xa_wq': nrm((L, D, D), D),
        'xa_wk': nrm((L, D, D), D),
        'xa_wv': nrm((L, D, D), D),
        'xa_wo': nrm((L, D, D), D),
        'ffn2_norm': gain((L, D)),
        'ffn2_w_gate': nrm((L, D, D_FF), D),
        'ffn2_w_up': nrm((L, D, D_FF), D),
        'ffn2_w_down': nrm((L, D_FF, D), D_FF),
        'final_norm': gain((D,)),
    }


def reference(x, mem, positions, ffn1_norm, ffn1_w_gate, ffn1_w_up, ffn1_w_down,
              mix_norm, w_in, ssm_conv_w, ssm_conv_b, ssm_dt_bias, ssm_a_log, ssm_d, ssm_norm,
              mla_q_norm, mla_kv_norm, mla_w_uq, mla_w_ukv, diff_lambda, diff_norm, rel_bias,
              w_branch, w_out, xa_norm, mem_norm, xa_wq, xa_wk, xa_wv, xa_wo,
              ffn2_norm, ffn2_w_gate, ffn2_w_up, ffn2_w_down, final_norm):
    B_, S, D = x.shape
    h = x
    for l in range(DEPTH):
        h = h + 0.5 * swiglu(rms_norm(h, ffn1_norm[l]), ffn1_w_gate[l], ffn1_w_up[l], ffn1_w_down[l])
        u = rms_norm(h, mix_norm[l])
        (rq, rk, rv, rg, sz, sxbc, sdt, cq, ckv, kr, dq, dk, dv, gl) = split_last(u @ w_in[l], IN_SIZES)
        y_ret = retention(rq, rk, rv, rg, positions)
        y_ssm = mamba2_ssd(sz, sxbc, sdt, ssm_conv_w[l], ssm_conv_b[l], ssm_dt_bias[l],
                           ssm_a_log[l], ssm_d[l], ssm_norm[l])
        y_mla = mla(cq, ckv, kr, positions, mla_q_norm[l], mla_kv_norm[l], mla_w_uq[l], mla_w_ukv[l])
        lambda_init = 0.8 - 0.6 * math.exp(-0.3 * l)
        y_diff = diff_attention(dq, dk, dv, positions, rel_bias, diff_lambda[l], lambda_init, diff_norm[l])
        ys = jnp.stack([y_ret, y_ssm.astype(y_ret.dtype), y_mla.astype(y_ret.dtype),
                        y_diff.astype(y_ret.dtype)], axis=2)
        gates = jax.nn.sigmoid(gl.reshape(B_, S, N_BRANCH, D))
        merged = jnp.sum(gates * jnp.einsum('bsnw,nwd->bsnd', ys, w_branch[l]), axis=2)
        h = h + merged @ w_out[l]
        h = h + cross_attention(rms_norm(h, xa_norm[l]), rms_norm(mem, mem_norm[l]),
                                xa_wq[l], xa_wk[l], xa_wv[l], xa_wo[l])
        h = h + 0.5 * swiglu(rms_norm(h, ffn2_norm[l]), ffn2_w_gate[l], ffn2_w_up[l], ffn2_w_down[l])
    return rms_norm(h, final_norm)
```

```python
import math
import numpy as np
import ml_dtypes
from contextlib import ExitStack, contextmanager
import concourse.bass as bass
import concourse.mybir as mybir

F32 = mybir.dt.float32
BF16 = mybir.dt.bfloat16
I32 = mybir.dt.int32
ALU = mybir.AluOpType
AF = mybir.ActivationFunctionType
AX = mybir.AxisListType
CC_INC = 1


def _is_ap(x):
    return hasattr(x, "tensor") and hasattr(x, "ap") and hasattr(x, "offset")


def _region(ap):
    t = ap.tensor
    name = t.name
    dims = ap.ap
    off = ap.offset
    sp = str(ap.space)
    if "PSUM" in sp:
        return (name, 0, 128, 0, 1 << 40)
    if "SB" in sp:
        pstep = dims[0][0]
        pcnt = dims[0][1]
        if pstep == 0:
            p0 = 0
            lo = off
            pstep = 1 << 60
        else:
            p0 = off // pstep
            lo = off % pstep
        ext = 0
        for st, cn in dims[1:]:
            ext += abs(st) * (cn - 1)
        return (name, p0, p0 + pcnt, lo, lo + ext + 1)
    else:
        ext = 0
        for st, cn in dims:
            ext += abs(st) * (cn - 1)
        return (name, 0, 1, off, off + ext + 1)


class Tracker:
    def __init__(self, sem, name):
        self.sem = sem
        self.val = 0
        self.name = name


class Eng:
    def __init__(self, name, eng, tracker):
        self.name = name
        self.eng = eng
        self.tr = tracker
        self.known = {}
        self.pending_noinc = False


class Sched:
    def __init__(self, nc, es):
        self.nc = nc
        self.es = es
        self.es0 = es
        self.recs = {}
        self.engs = {}
        for nm, e in (("pe", nc.tensor), ("act", nc.scalar), ("dve", nc.vector), ("pool", nc.gpsimd), ("sp", nc.sync)):
            tr = Tracker(es.enter_context(nc.semaphore("s_" + nm)), nm)
            self.engs[nm] = Eng(nm, e, tr)
        self.n_ins = 0
        self.n_wait = 0
        self._dma_tr = []
        self._names = {}

    def _uniq(self, name):
        k = self._names.get(name, 0)
        self._names[name] = k + 1
        return name if k == 0 else f"{name}__{k}"

    def sb(self, name, shape, dt):
        return self.es.enter_context(self.nc.sbuf_tensor(self._uniq(name), list(shape), dt))

    def ps(self, name, shape, dt):
        return self.es.enter_context(self.nc.psum_tensor(name, list(shape), dt))

    def dma_tracker(self, name):
        tr = Tracker(self.es0.enter_context(self.nc.semaphore("d_" + name)), name)
        self._dma_tr.append(tr)
        return tr

    def _deps(self, reads, writes, same_eng=None):
        deps = {}

        def add(tr, v):
            if deps.get(tr, 0) < v:
                deps[tr] = v

        for ap in reads:
            name, p0, p1, lo, hi = _region(ap)
            for r in self.recs.get(name, ()):
                if r[4] and r[0] < p1 and p0 < r[1] and r[2] < hi and lo < r[3]:
                    add(r[5], r[6])
        for ap in writes:
            name, p0, p1, lo, hi = _region(ap)
            for r in self.recs.get(name, ()):
                if r[0] < p1 and p0 < r[1] and r[2] < hi and lo < r[3]:
                    if same_eng is not None and r[5] is same_eng:
                        continue
                    add(r[5], r[6])
        return deps

    def _record(self, reads, writes, tr, val):
        for ap in writes:
            name, p0, p1, lo, hi = _region(ap)
            lst = self.recs.setdefault(name, [])
            lst[:] = [r for r in lst if not (p0 <= r[0] and r[1] <= p1 and lo <= r[2] and r[3] <= hi)]
            lst.append([p0, p1, lo, hi, True, tr, val])
        for ap in reads:
            name, p0, p1, lo, hi = _region(ap)
            lst = self.recs.setdefault(name, [])
            for r in lst:
                if (not r[4]) and r[5] is tr and r[0] == p0 and r[1] == p1 and r[2] == lo and r[3] == hi:
                    r[6] = val
                    break
            else:
                lst.append([p0, p1, lo, hi, False, tr, val])

    def _emit_waits(self, E, deps):
        for tr, v in deps.items():
            if E.known.get(tr, 0) >= v:
                continue
            if tr is E.tr and E.name == "pe":
                continue
            E.eng.wait_ge(tr.sem, v)
            E.known[tr] = v
            self.n_wait += 1

    def op(self, engname, method, *args, inc=True, extra_reads=(), extra_writes=(), **kwargs):
        E = self.engs[engname]
        writes, reads = [], []
        if "out" in kwargs:
            writes.append(kwargs["out"])
            pos_reads = args
        else:
            writes.append(args[0])
            pos_reads = args[1:]
        for a in pos_reads:
            if _is_ap(a):
                reads.append(a)
        for k, a in kwargs.items():
            if k == "out":
                continue
            if k == "accum_out":
                if a is not None:
                    writes.append(a)
                continue
            if _is_ap(a):
                reads.append(a)
        reads.extend(extra_reads)
        writes.extend(extra_writes)
        deps = self._deps(reads, writes, same_eng=E.tr)
        self._emit_waits(E, deps)
        ins = getattr(E.eng, method)(*args, **kwargs)
        val = E.tr.val + 1
        if inc:
            ins.then_inc(E.tr.sem, 1)
            E.tr.val = val
        self._record(reads, writes, E.tr, val)
        self.n_ins += 1
        return ins

    def dma(self, queue, out, in_, tracker, **kw):
        E = self.engs[queue]
        deps = self._deps([in_], [out])
        self._emit_waits(E, deps)
        ins = E.eng.dma_start(out=out, in_=in_, **kw)
        ins.then_inc(tracker.sem, 16)
        tracker.val += 16
        self._record([in_], [out], tracker, tracker.val)
        self.n_ins += 1
        return ins

    def finish(self, queue="sp"):
        E = self.engs[queue]
        for tr in self._dma_tr:
            if tr.val > 0:
                E.eng.wait_ge(tr.sem, tr.val)
        for e in self.engs.values():
            if e.tr.val > 0 and e is not E:
                E.eng.wait_ge(e.tr.sem, e.tr.val)

    def pe(self, m, *a, **k):
        return self.op("pe", m, *a, **k)

    def act(self, m, *a, **k):
        return self.op("act", m, *a, **k)

    def dve(self, m, *a, **k):
        return self.op("dve", m, *a, **k)

    def pool(self, m, *a, **k):
        return self.op("pool", m, *a, **k)

    def dma_group(self, queue, pairs, tracker, **kw):
        E = self.engs[queue]
        for out, in_ in pairs:
            deps = self._deps([in_], [out])
            self._emit_waits(E, deps)
            E.eng.dma_start(out=out, in_=in_, **kw).then_inc(tracker.sem, 16)
            tracker.val += 16
            self.n_ins += 1
        for out, in_ in pairs:
            self._record([in_], [out], tracker, tracker.val)

    def barrier(self):
        trs = [e.tr for e in self.engs.values()] + list(self._dma_tr)
        for E in self.engs.values():
            for tr in trs:
                if tr.val > 0 and E.known.get(tr, 0) < tr.val:
                    E.eng.wait_ge(tr.sem, tr.val)
                    E.known[tr] = tr.val
                    self.n_wait += 1

    @contextmanager
    def scope(self):
        old = self.es
        with ExitStack() as es2:
            self.es = es2
            try:
                yield
            finally:
                self.barrier()
                self.es = old

    def collective(self, kind, in_ap, out_ap, groups, tracker):
        E = self.engs["pool"]
        deps = self._deps([in_ap], [out_ap])
        self._emit_waits(E, deps)
        ins = E.eng.collective_compute(kind, mybir.AluOpType.bypass, replica_groups=groups, ins=[in_ap.opt()], outs=[out_ap.opt()])
        ins.then_inc(tracker.sem, CC_INC)
        tracker.val += CC_INC
        self._record([in_ap], [out_ap], tracker, tracker.val)
        self.n_ins += 1
        return ins

D = 1024
DFF = 2816
SEQ = 4096
NB = 4
MEM = 256
EPS = 1e-6
IN_SIZES = (256, 256, 512, 512, 512, 1024, 8, 256, 128, 32, 512, 512, 512, 4096)
IN_OFF = [0]
for _s in IN_SIZES:
    IN_OFF.append(IN_OFF[-1] + _s)
NTA = 2048
TB = 1024
NG = 512
WSLOT = 5632


class Ctx:
    def __init__(self, S, nbanks=8, nwslots=4, ntmp=4):
        self.S = S
        self.banks = [S.ps(f"bank{i}", [128, 512], F32) for i in range(nbanks)]
        self._bi = 0
        self.wtr = [S.dma_tracker(f"w{i}") for i in range(4)]
        self._wgen = 0
        self.set_wslots(nwslots)
        self.tmpf = [S.sb(f"tmpf{i}", [128, 512], F32) for i in range(ntmp)]
        self._ti = 0
        self.tmpb = [S.sb(f"tmpb{i}", [128, 512], BF16) for i in range(ntmp)]
        self._tbi = 0
        self.rstd = S.sb("rstd", [128, 512], F32)
        self.ones_f = S.sb("ones_f", [128, 128], F32)
        self.ones_b = S.sb("ones_b", [128, 128], BF16)
        S.pool("memset", self.ones_f[:], 1.0)
        S.pool("memset", self.ones_b[:], 1.0)
        self.ctr = S.dma_tracker("const")

    def set_wslots(self, n):
        self._wgen += 1
        self.wslots = [self.S.sb(f"wslot{self._wgen}_{i}", [128, WSLOT], BF16) for i in range(n)]
        self._wi = 0

    def bank(self):
        b = self.banks[self._bi % len(self.banks)]
        self._bi += 1
        return b

    def tf(self):
        t = self.tmpf[self._ti % len(self.tmpf)]
        self._ti += 1
        return t

    def tb(self):
        t = self.tmpb[self._tbi % len(self.tmpb)]
        self._tbi += 1
        return t

    def load_w(self, w_dram, c0, cw, queue="pool"):
        S = self.S
        K = w_dram.shape[0]
        nk = K // 128
        assert nk * cw <= WSLOT, (nk, cw)
        i = self._wi % len(self.wslots)
        self._wi += 1
        view = self.wslots[i][:, 0:nk * cw].rearrange("p (k c) -> p k c", c=cw)
        src = w_dram.rearrange("(k p) n -> p k n", p=128)[:, :, c0:c0 + cw]
        S.dma(queue, view, src, self.wtr[i])
        return view


def mm_chain(S, out, pairs):
    n = len(pairs)
    for i, (l, r) in enumerate(pairs):
        S.pe("matmul", out, lhsT=l, rhs=r, start=(i == 0), stop=(i == n - 1), inc=(i == n - 1))


def rmsnorm_fm(S, C, src, nch, gain, dst, ntok, dtot):
    for sg in range(ntok // NG):
        sl = slice(sg * NG, (sg + 1) * NG)
        ps = C.bank()
        for dc in range(nch):
            sq = C.tb()
            S.act("activation", out=sq[:], in_=src(dc)[:, sl], func=AF.Square)
            S.pe("matmul", ps[:], lhsT=C.ones_b[:], rhs=sq[:], start=(dc == 0), stop=(dc == nch - 1))
        S.act("activation", out=C.rstd[:], in_=ps[:], func=AF.Sqrt, bias=EPS, scale=1.0 / dtot)
        S.dve("reciprocal", out=C.rstd[:], in_=C.rstd[:])
        for dc in range(nch):
            S.dve("scalar_tensor_tensor", out=dst(dc)[:, sl], in0=src(dc)[:, sl], scalar=gain[:, dc:dc + 1],
                  in1=C.rstd[:], op0=ALU.mult, op1=ALU.mult)


def linear_fm(S, C, w_dram, xin, ntok, consume, ctile=512):
    K, N = w_dram.shape
    nk = K // 128
    ctile = min(ctile, (WSLOT // nk) // 128 * 128)
    for c0 in range(0, N, ctile):
        cw = min(ctile, N - c0)
        wt = C.load_w(w_dram, c0, cw)
        for sg in range(ntok // NG):
            sl = slice(sg * NG, (sg + 1) * NG)
            for oc in range(cw // 128):
                ps = C.bank()
                mm_chain(S, ps[:], [(wt[:, k, oc * 128:(oc + 1) * 128], xin(k)[:, sl]) for k in range(nk)])
                consume(c0 // 128 + oc, ps, sl)


def ffn_fm(S, C, wg, wu, wd, xn, h, act, ntok):
    for c0 in range(0, DFF, 512):
        cw = min(512, DFF - c0)
        wgt = C.load_w(wg, c0, cw)
        wut = C.load_w(wu, c0, cw)
        for sg in range(ntok // NG):
            sl = slice(sg * NG, (sg + 1) * NG)
            for oc in range(cw // 128):
                fc = c0 // 128 + oc
                pg = C.bank()
                mm_chain(S, pg[:], [(wgt[:, k, oc * 128:(oc + 1) * 128], xn[:, k, sl]) for k in range(8)])
                pu = C.bank()
                mm_chain(S, pu[:], [(wut[:, k, oc * 128:(oc + 1) * 128], xn[:, k, sl]) for k in range(8)])
                t = C.tf()
                S.act("activation", out=t[:], in_=pg[:], func=AF.Silu)
                S.dve("tensor_tensor", out=act[:, fc, sl], in0=t[:], in1=pu[:], op=ALU.mult)

    def upd(dc, ps, sl):
        S.dve("scalar_tensor_tensor", out=h[:, dc, sl], in0=ps[:], scalar=0.5, in1=h[:, dc, sl], op0=ALU.mult, op1=ALU.add)

    linear_fm(S, C, wd, lambda k: act[:, k, :], ntok, upd, ctile=256)


PAIRS = [[0, 1], [2, 3], [4, 5], [6, 7]]


def build_A(has_post, has_pre, final, fused=None):
    nc = fused["nc"] if fused else bass.Bass("TRN2", target_bir_lowering=False)
    pfx = fused["pfx"] if fused else ""

    def din(name, shape, dt=F32):
        return nc.dram_tensor(pfx + name, list(shape), dt, kind="ExternalInput").ap()

    def dout(name, shape, dt=F32):
        return nc.dram_tensor(pfx + name, list(shape), dt, kind="ExternalOutput").ap()

    if fused and fused.get("h_src") is not None:
        hT = fused["h_src"]
    else:
        hT = din("hT", [D, NTA])
    gains_d = din("gains", [128, 64])
    if has_post:
        if not fused:
            yT = din("yT", [2048, NTA])
        memT = din("memT", [D, MEM])
        wgl = din("wgl", [D, 4096]); wbr = din("wbr", [2048, D]); wout = din("wout", [D, D])
        wq = din("wq", [D, D]); wk = din("wk", [D, D]); wv = din("wv", [D, D]); wo = din("wo", [D, D])
        f2g = din("f2g", [D, DFF]); f2u = din("f2u", [D, DFF]); f2d = din("f2d", [DFF, D])
    if has_pre:
        f1g = din("f1g", [D, DFF]); f1u = din("f1u", [D, DFF]); f1d = din("f1d", [DFF, D])
        if fused:
            h1T = fused["h_dst"]
        else:
            h1T = dout("h1T", [D, NTA])
            uT = dout("uT", [D, NTA], BF16)
    if final:
        outT = dout("outT", [D, NTA])

    with ExitStack() as es:
        if fused:
            S, C = fused["S"], fused["C"]
            es.enter_context(S.scope())
            C.set_wslots(3)
        else:
            S = Sched(nc, es)
            C = Ctx(S)
        gains = S.sb("gains_sb", [128, 64], F32)
        S.dma_group("sp", [(gains[:], gains_d)], C.ctr)
        G_F1, G_MIX, G_MIXP, G_XA, G_MEM, G_F2, G_FIN, G_SSM = 0, 8, 16, 24, 32, 40, 48, 56
        h = S.sb("h", [128, 8, TB], F32)
        xn = S.sb("xn", [128, 8, TB], BF16)
        scr = S.sb("scr", [128, 24, TB], BF16)
        if fused:
            htr, otr, otr2, ytr, ystr, mtr = fused["a_tr"]
        else:
            htr = S.dma_tracker("h")
            otr = S.dma_tracker("o")
            otr2 = S.dma_tracker("o2")
            ytr = S.dma_tracker("y")
            ystr = S.dma_tracker("ys")
            mtr = S.dma_tracker("mem")
        if has_post:
            if fused:
                ylo = S.sb("ylo", [128, 4, NG], BF16)
                yhi = S.sb("yhi", [128, 4, NG], BF16)
            ssm_f = S.sb("ssm_f", [128, 4, TB], F32)
            macc = S.sb("macc", [128, 4, 512], F32)
            pT = S.sb("pT", [128, 2, NG], BF16)
            rl = S.sb("rl", [128, NG], F32)
            mnT = S.sb("mnT", [128, 8, MEM], BF16)
            kT = S.sb("kT", [128, 8, MEM], BF16)
            vtm = S.sb("vtm", [128, 2, D], BF16)
            with S.scope():
                mem_f = S.sb("mem_f", [128, 8, MEM], F32)
                S.dma("sp", mem_f[:], memT.rearrange("(c p) t -> p c t", p=128), mtr)
                psm = C.bank()
                for dc in range(8):
                    sq = C.tf()
                    S.dve("tensor_tensor", out=sq[:, 0:MEM], in0=mem_f[:, dc, :], in1=mem_f[:, dc, :], op=ALU.mult)
                    S.pe("matmul", psm[:, 0:MEM], lhsT=C.ones_f[:], rhs=sq[:, 0:MEM], start=(dc == 0), stop=(dc == 7))
                S.act("activation", out=C.rstd[:, 0:MEM], in_=psm[:, 0:MEM], func=AF.Sqrt, bias=EPS, scale=1.0 / D)
                S.dve("reciprocal", out=C.rstd[:, 0:MEM], in_=C.rstd[:, 0:MEM])
                for dc in range(8):
                    S.dve("scalar_tensor_tensor", out=mnT[:, dc, :], in0=mem_f[:, dc, :], scalar=gains[:, G_MEM + dc:G_MEM + dc + 1],
                          in1=C.rstd[:, 0:MEM], op0=ALU.mult, op1=ALU.mult)
            for c0 in range(0, D, 512):
                wt = C.load_w(wk, c0, 512)
                for oc in range(4):
                    ps = C.bank()
                    mm_chain(S, ps[:, 0:MEM], [(wt[:, k, oc * 128:(oc + 1) * 128], mnT[:, k, :]) for k in range(8)])
                    S.act("copy", out=kT[:, c0 // 128 + oc, :], in_=ps[:, 0:MEM])
            for c0 in range(0, D, 512):
                wt = C.load_w(wv, c0, 512)
                for mc in range(2):
                    ps = C.bank()
                    mm_chain(S, ps[:], [(mnT[:, k, mc * 128:(mc + 1) * 128], wt[:, k, :]) for k in range(8)])
                    S.act("copy", out=vtm[:, mc, c0:c0 + 512], in_=ps[:])

        for tb in range(NTA // TB):
            t0 = tb * TB
            S.dma("sp", h[:], hT.rearrange("(c p) t -> p c t", p=128)[:, :, t0:t0 + TB], htr)
            if has_post:
                ybuf = scr[:, 0:16, :]
                merged = scr[:, 16:24, :]
                rmsnorm_fm(S, C, lambda dc: h[:, dc, :], 8, gains[:, G_MIXP:G_MIXP + 8], lambda dc: xn[:, dc, :], TB, D)
                if not fused:
                    yv = yT.rearrange("(c p) t -> p c t", p=128)
                    S.dma("pool", scr[:, 0:4, :], yv[:, 0:4, t0:t0 + TB], ytr)
                    S.dma("pool", scr[:, 8:16, :], yv[:, 8:16, t0:t0 + TB], ytr)
                    S.dma("sp", ssm_f[:], yv[:, 4:8, t0:t0 + TB], ystr)
                else:
                    for i in range(4):
                        yv = fused["y_all"][i].rearrange("(c p) t -> p c t", p=128)
                        for sg in range(TB // NG):
                            c_lo = t0 + sg * NG
                            S.dma("sp", ylo[:], yv[:, :, c_lo:c_lo + NG], ytr)
                            S.dma("sp", yhi[:], yv[:, :, NTA + c_lo:NTA + c_lo + NG], ystr)
                            dst = ssm_f[:, :, sg * NG:(sg + 1) * NG] if i == 1 else scr[:, 4 * i:4 * i + 4, sg * NG:(sg + 1) * NG]
                            S.dve("tensor_scalar", out=dst, in0=ylo[:], scalar1=gains[:, 60:61], scalar2=None, op0=ALU.mult)
                            S.dve("scalar_tensor_tensor", out=dst, in0=yhi[:], scalar=gains[:, 61:62], in1=dst, op0=ALU.mult, op1=ALU.add)
                rmsnorm_fm(S, C, lambda dc: ssm_f[:, dc, :], 4, gains[:, G_SSM:G_SSM + 4], lambda dc: scr[:, 4 + dc, :], TB, 512)
                for c0 in range(0, D, 256):
                    for i in range(4):
                        wg_t = C.load_w(wgl, i * 1024 + c0, 256)
                        wb_t = C.load_w(wbr[i * 512:(i + 1) * 512, :], c0, 256)
                        for sg in range(TB // NG):
                            sl = slice(sg * NG, (sg + 1) * NG)
                            for oc in range(2):
                                dc = c0 // 128 + oc
                                ma = macc[:, sg * 2 + oc, :]
                                pg = C.bank()
                                mm_chain(S, pg[:], [(wg_t[:, k, oc * 128:(oc + 1) * 128], xn[:, k, sl]) for k in range(8)])
                                pz = C.bank()
                                mm_chain(S, pz[:], [(wb_t[:, k, oc * 128:(oc + 1) * 128], scr[:, 4 * i + k, sl]) for k in range(4)])
                                sg_t = C.tf()
                                S.act("activation", out=sg_t[:], in_=pg[:], func=AF.Sigmoid)
                                if i == 0:
                                    S.dve("tensor_tensor", out=ma, in0=sg_t[:], in1=pz[:], op=ALU.mult)
                                else:
                                    S.dve("tensor_tensor", out=sg_t[:], in0=sg_t[:], in1=pz[:], op=ALU.mult)
                                    if i < 3:
                                        S.dve("tensor_tensor", out=ma, in0=ma, in1=sg_t[:], op=ALU.add)
                                    else:
                                        S.dve("tensor_tensor", out=merged[:, dc, sl], in0=ma, in1=sg_t[:], op=ALU.add)

                def add_h(dc, ps, sl):
                    S.dve("tensor_tensor", out=h[:, dc, sl], in0=ps[:], in1=h[:, dc, sl], op=ALU.add)

                linear_fm(S, C, wout, lambda k: merged[:, k, :], TB, add_h)
                rmsnorm_fm(S, C, lambda dc: h[:, dc, :], 8, gains[:, G_XA:G_XA + 8], lambda dc: xn[:, dc, :], TB, D)
                qT = scr[:, 0:8, :]
                oT = scr[:, 8:16, :]

                def put_q(oc, ps, sl):
                    S.act("copy", out=qT[:, oc, sl], in_=ps[:])

                linear_fm(S, C, wq, lambda k: xn[:, k, :], TB, put_q)
                for hd in range(4):
                    for sg in range(TB // NG):
                        sl = slice(sg * NG, (sg + 1) * NG)
                        for mc in range(2):
                            ps = C.bank()
                            mm_chain(S, ps[:], [(kT[:, 2 * hd + dk, mc * 128:(mc + 1) * 128], qT[:, 2 * hd + dk, sl]) for dk in range(2)])
                            S.act("activation", out=pT[:, mc, :], in_=ps[:], func=AF.Exp, scale=1.0 / 16.0)
                        pl = C.bank()
                        mm_chain(S, pl[:], [(C.ones_b[:], pT[:, mc, :]) for mc in range(2)])
                        S.dve("reciprocal", out=rl[:], in_=pl[:])
                        for dvc in range(2):
                            po = C.bank()
                            mm_chain(S, po[:], [(vtm[:, mc, hd * 256 + dvc * 128: hd * 256 + (dvc + 1) * 128], pT[:, mc, :]) for mc in range(2)])
                            S.dve("tensor_tensor", out=oT[:, 2 * hd + dvc, sl], in0=po[:], in1=rl[:], op=ALU.mult)
                linear_fm(S, C, wo, lambda k: oT[:, k, :], TB, add_h)
                rmsnorm_fm(S, C, lambda dc: h[:, dc, :], 8, gains[:, G_F2:G_F2 + 8], lambda dc: xn[:, dc, :], TB, D)
                ffn_fm(S, C, f2g, f2u, f2d, xn, h, scr[:, 0:22, :], TB)
            if has_pre:
                rmsnorm_fm(S, C, lambda dc: h[:, dc, :], 8, gains[:, G_F1:G_F1 + 8], lambda dc: xn[:, dc, :], TB, D)
                ffn_fm(S, C, f1g, f1u, f1d, xn, h, scr[:, 0:22, :], TB)
                S.dma("sp", h1T.rearrange("(c p) t -> p c t", p=128)[:, :, t0:t0 + TB], h[:], otr)
                rmsnorm_fm(S, C, lambda dc: h[:, dc, :], 8, gains[:, G_MIX:G_MIX + 8], lambda dc: xn[:, dc, :], TB, D)
                if fused:
                    S.dma("sp", fused["u_src"][tb].rearrange("(c p) t -> p c t", p=128), xn[:], otr2)
                    S.collective("AllGather", fused["u_src"][tb], fused["u_all"][tb], PAIRS, fused["cc_tr"].pop())
                else:
                    S.dma("sp", uT.rearrange("(c p) t -> p c t", p=128)[:, :, t0:t0 + TB], xn[:], otr2)
            if final:
                rmsnorm_fm(S, C, lambda dc: h[:, dc, :], 8, gains[:, G_FIN:G_FIN + 8], lambda dc: h[:, dc, :], TB, D)
                S.dma("sp", outT.rearrange("(c p) t -> p c t", p=128)[:, :, t0:t0 + TB], h[:], otr)
        if not fused:
            S.finish()
        print("A built: ins", S.n_ins, "waits", S.n_wait, {k: e.tr.val for k, e in S.engs.items()})
    return nc

NQG = SEQ // NG
NBLK = SEQ // 128
TWO_PI = 2.0 * math.pi
MAGIC = 12582912.0
SKEW = 3


def t5_thresholds():
    n = np.arange(0, 512, dtype=np.int64)
    nf = np.maximum(n, 1).astype(np.float32)
    large = 16 + (np.log(nf / np.float32(16)) / np.float32(math.log(128 / 16)) * np.float32(16)).astype(np.int32)
    bucket = np.where(n < 16, n, np.minimum(large, 31))
    return [int(np.argmax(bucket >= m)) for m in range(1, 32)]


def build_B(phases=("ret", "ssd", "mla", "diff"), lambda_init=0.2, fused=None):
    nc = fused["nc"] if fused else bass.Bass("TRN2", target_bir_lowering=False)
    pfx = fused["pfx"] if fused else ""

    def din(name, shape, dt=F32):
        return nc.dram_tensor(pfx + name, list(shape), dt, kind="ExternalInput").ap()

    def dout(name, shape, dt=F32):
        if fused:
            return None
        return nc.dram_tensor(pfx + name, list(shape), dt, kind="ExternalOutput").ap()

    if not fused:
        uT_d = din("uT", [D, SEQ], BF16)
    pos_d = din("pos", [1, SEQ], mybir.dt.int32)
    cst_d = din("cst", [128, 16])
    if "mla" in phases:
        wmla = din("wmla", [D, 448])
        wuq = din("wuq", [256, 256])
        wukv = din("wukv", [128, 384])
        gmla = din("gmla", [128, 4])
        y_mla = dout("y_mla", [SEQ, 256])
    if "diff" in phases:
        wdiff = din("wdiff", [D, 768])
        dlam = din("dlam", [1, 256])
        dng = din("dng", [1, 128])
        dtbl = din("dtbl", [1, 64])
        y_diff = dout("y_diff", [SEQ, 256])
    if "ret" in phases:
        wret = din("wret", [D, 1024])
        y_ret = dout("y_ret", [SEQ, 256])
    if "ssd" in phases:
        wssd = din("wssd", [D, 776])
        cssd = din("cssd", [128, 24])
        rssd = din("rssd", [1, 12])
        y_ssd = dout("y_ssd", [SEQ, 256])

    with ExitStack() as es:
        if fused:
            S, C = fused["S"], fused["C"]
            es.enter_context(S.scope())
            C.set_wslots(3)
            tpb = fused["tpb"]
            utr, ptr_, otr0, otr1 = fused["b_tr"]
            otr = [otr0, otr1]
        else:
            S = Sched(nc, es)
            C = Ctx(S, nbanks=7, nwslots=3)
            tpb = S.ps("tpb", [128, 1024], BF16)
            utr = S.dma_tracker("u")
            ptr_ = S.dma_tracker("pos")
            otr = [S.dma_tracker(f"o{i}") for i in range(2)]
        cst = S.sb("cst_sb", [128, 16], F32)
        S.dma_group("sp", [(cst[:], cst_d)], C.ctr)
        uT = S.sb("uT_sb", [128, 8, SEQ], BF16)
        if fused:
            def load_u(tb):
                for r in range(2):
                    S.dma("sp", uT[:, :, r * NTA + tb * TB:r * NTA + (tb + 1) * TB],
                          fused["u_all"][tb][r * D:(r + 1) * D, :].rearrange("(c p) t -> p c t", p=128), fused["u_tr"][r * 2 + tb])

            load_u(0)
            load_u(1)
            late_u = [False]

            def load_late_u():
                if late_u[0]:
                    late_u[0] = False
                    load_u(1)
        else:
            S.dma("sp", uT[:], uT_d.rearrange("(c p) t -> p c t", p=128), utr)

            def load_late_u():
                pass
        yts = [S.sb(f"yts{i}", [128, 2, NG], BF16) for i in range(2)]
        yti = [0]

        def emit_y(branch, y_dram, g, ys):
            if not fused:
                S.dma("sp", y_dram[g * NG:(g + 1) * NG, :].rearrange("(q p) c -> p q c", p=128), ys[:], otr[g % 2])
                return
            yt = yts[yti[0] % 2]
            tr_ = otr[yti[0] % 2]
            yti[0] += 1
            for fc in range(2):
                ps = C.bank()
                for qb in range(4):
                    S.pe("transpose", ps[:, qb * 128:(qb + 1) * 128], ys[:, qb, fc * 128:(fc + 1) * 128], ident_f[:])
                S.act("copy", out=yt[:, fc, :], in_=ps[:])
            S.dma("sp", fused["y_src"][branch].rearrange("(c p) t -> p c t", p=128)[:, :, g * NG:(g + 1) * NG], yt[:], tr_)
            if g == NQG - 1:
                S.collective("AllGather", fused["y_src"][branch], fused["y_all"][branch], PAIRS, fused["cc_tr"].pop())
        ident_b = S.sb("ident_b", [128, 128], BF16)
        S.pool("memset", ident_b[:], 0.0)
        S.pool("affine_select", out=ident_b[:], in_=ident_b[:], pattern=[[-1, 128]], compare_op=ALU.not_equal, fill=1.0, base=0, channel_multiplier=1)
        ident_f = S.sb("ident_f", [128, 128], F32)
        S.pool("memset", ident_f[:], 0.0)
        S.pool("affine_select", out=ident_f[:], in_=ident_f[:], pattern=[[-1, 128]], compare_op=ALU.not_equal, fill=1.0, base=0, channel_multiplier=1)
        rel_i = S.sb("rel_i", [128, 128], I32)
        rel0 = S.sb("rel0", [128, 128], F32)
        S.pool("iota", rel_i[:], pattern=[[1, 128]], base=0, channel_multiplier=-1)
        S.pool("tensor_copy", out=rel0[:], in_=rel_i[:])
        ptr = ptr_

        def alloc_rope():
            return dict(pos_i=S.sb("pos_i", [128, NG], I32), pos_f=S.sb("pos_f", [128, NG], F32), tang=S.sb("tang", [128, NG], F32),
                        tcos=S.sb("tcos", [128, NG], F32), tsin=S.sb("tsin", [128, NG], F32))

        def rope_tables(T, g, inv_col, sign_col):
            pos_i, pos_f, tang, tcos, tsin = T["pos_i"], T["pos_f"], T["tang"], T["tcos"], T["tsin"]
            S.dma("sp", pos_i[:], pos_d[:, g * NG:(g + 1) * NG].partition_broadcast(128), ptr)
            S.dve("tensor_copy", out=pos_f[:], in_=pos_i[:])
            for (dst, shift) in ((tsin, 0.0), (tcos, math.pi / 2)):
                S.dve("tensor_scalar", out=tang[:], in0=pos_f[:], scalar1=cst[:, inv_col:inv_col + 1], scalar2=shift, op0=ALU.mult, op1=ALU.add)
                S.dve("tensor_scalar", out=dst[:], in0=tang[:], scalar1=1.0 / TWO_PI, scalar2=MAGIC, op0=ALU.mult, op1=ALU.add)
                S.dve("tensor_scalar", out=dst[:], in0=dst[:], scalar1=MAGIC, scalar2=-TWO_PI, op0=ALU.subtract, op1=ALU.mult)
                S.dve("tensor_tensor", out=tang[:], in0=tang[:], in1=dst[:], op=ALU.add)
                S.dve("tensor_scalar", out=tang[:], in0=tang[:], scalar1=math.pi, scalar2=-math.pi, op0=ALU.min, op1=ALU.max)
                S.act("activation", out=dst[:], in_=tang[:], func=AF.Sin)
            S.dve("tensor_scalar", out=tsin[:], in0=tsin[:], scalar1=cst[:, sign_col:sign_col + 1], scalar2=None, op0=ALU.mult)

        def proj_fm(wt, col0, ncols, tsl, ps_ap):
            mm_chain(S, ps_ap, [(wt[:, k, col0:col0 + ncols], uT[:, k, tsl]) for k in range(8)])

        def proj_tm(wt, c0, c1, blk, ps_ap):
            mm_chain(S, ps_ap, [(uT[:, k, blk * 128:(blk + 1) * 128], wt[:, k, c0:c1]) for k in range(8)])

        maskT = S.sb("maskT", [128, 128], F32)
        S.dve("tensor_scalar", out=maskT[:], in0=rel0[:], scalar1=0.0, scalar2=-30000.0, op0=ALU.is_lt, op1=ALU.mult)

        pti = [0]
        att_id = [0]

        def attention(nmaps, KT_of, QT_of, V_of, scale, prebias, exp_bias, epilogue):
            att_id[0] += 1
            PT = [S.sb(f"PT{att_id[0]}_{i}", [128, NG], BF16) for i in range(4)]

            def next_PT():
                t = PT[pti[0] % 4]
                pti[0] += 1
                return t

            accb = [C.banks[3], C.banks[4], C.banks[5], C.banks[6]]
            stb = [C.banks[0], C.banks[1], C.banks[2]]
            acc = [accb[qb][:, 0:129] for qb in range(4)]
            units = [(g, m, kb) for g in range(NQG) for m in range(nmaps) for kb in range(4 * g + 4)]

            def score_part(i):
                g, m, kb = units[i]
                r = max(0, kb - 4 * g)
                qlo = r * 128
                st = stb[i % 3]
                S.pe("matmul", st[:, qlo:NG], lhsT=KT_of(m)[:, kb * 128:(kb + 1) * 128], rhs=QT_of(m)[:, g * NG + qlo:(g + 1) * NG],
                     start=True, stop=True)
                for qb in range(r, 4):
                    delta = 4 * g + qb - kb
                    if delta <= 1:
                        pb = prebias(m, delta)
                        if pb is not None:
                            S.dve("tensor_tensor", out=st[:, qb * 128:(qb + 1) * 128], in0=st[:, qb * 128:(qb + 1) * 128], in1=pb, op=ALU.add)
                pt = next_PT()
                eb = exp_bias(m)
                if eb is None:
                    S.act("activation", out=pt[:, qlo:NG], in_=st[:, qlo:NG], func=AF.Exp, scale=scale)
                else:
                    S.act("activation", out=pt[:, qlo:NG], in_=st[:, qlo:NG], func=AF.Exp, scale=scale, bias=eb)
                return pt

            def pv_part(i, pt):
                g, m, kb = units[i]
                r = max(0, kb - 4 * g)
                for qb in range(r, 4):
                    S.pe("matmul", acc[qb], lhsT=pt[:, qb * 128:(qb + 1) * 128], rhs=V_of(m, kb),
                         start=(kb == 0), stop=(kb == 4 * g + qb))
                if kb == 4 * g + 3:
                    for qb in range(4):
                        epilogue(m, g, qb, acc[qb])

            pend = []
            for i in range(len(units)):
                pend.append((i, score_part(i)))
                if len(pend) > SKEW:
                    pv_part(*pend.pop(0))
            while pend:
                pv_part(*pend.pop(0))

        if "ssd" in phases:
          with S.scope():
            ws = C.load_w(wssd, 0, 512)
            wz = C.load_w(wssd, 512, 264)
            cs_t = S.sb("cssd_sb", [128, 24], F32)
            rs_t = S.sb("rssd_sb", [128, 12], F32)
            S.dma_group("sp", [(cs_t[:], cssd), (rs_t[:], rssd.partition_broadcast(128))], C.ctr)
            load_late_u()
            Aneg = S.sb("Aneg", [128, 4], F32)
            S.act("activation", out=Aneg[:], in_=rs_t[:, 4:8], func=AF.Exp)
            S.dve("tensor_scalar", out=Aneg[:], in0=Aneg[:], scalar1=-1.0, scalar2=None, op0=ALU.mult)
            causT = S.sb("causT", [128, 128], F32)
            S.dve("tensor_scalar", out=causT[:], in0=rel0[:], scalar1=0.0, scalar2=None, op0=ALU.is_ge)
            BT = S.sb("sBT", [128, SEQ], BF16)
            CT = S.sb("sCT", [128, SEQ], BF16)
            hst = S.sb("hst", [128, 256], F32)
            hbf = S.sb("hbf", [128, 256], BF16)
            S.pool("memset", hst[:], 0.0)
            pre = S.sb("spre", [128, 4, 3 + NG], F32)
            S.pool("memset", pre[:, :, 0:3], 0.0)
            xf = S.sb("sxf", [128, 2, NG], F32)
            zs2 = [S.sb(f"szs{i}", [128, 4, 256], F32) for i in range(2)]
            dtt2 = [S.sb(f"sdtt{i}", [128, 4, 4], F32) for i in range(2)]
            dA2 = [S.sb(f"sdA{i}", [128, 4, 4], F32) for i in range(2)]
            xtm2 = [S.sb(f"sxtm{i}", [128, 4, 256], F32) for i in range(2)]
            Btm2 = [S.sb(f"sBtm{i}", [128, 4, 128], BF16) for i in range(2)]
            dec = S.sb("sdec", [128, 4, 128], F32)
            MT = S.sb("sMT", [128, 4, 128], BF16)
            xdt = S.sb("sxdt", [128, 256], BF16)
            xd2b = [S.sb(f"sxd2{i}", [128, 256], BF16) for i in range(2)]
            ysbb = [S.sb(f"sysb{i}", [128, 256], F32) for i in range(2)]
            hbfb = [hbf, S.sb("hbf1", [128, 256], BF16)]
            cssb = [S.sb(f"scs{i}", [128, 16], F32) for i in range(2)]
            scs = [S.sb(f"ssc{i}", [128, 16], F32) for i in range(2)]
            sys_ = [S.sb(f"sys{i}", [128, 4, 256], F32) for i in range(2)]

            def ssd_project(g):
                tsl = slice(g * NG, (g + 1) * NG)
                zs, dtt, dA, xtm, Btm = zs2[g % 2], dtt2[g % 2], dA2[g % 2], xtm2[g % 2], Btm2[g % 2]
                for ci in range(4):
                    ps = C.bank()
                    proj_fm(ws, ci * 128, 128, tsl, ps[:])
                    S.act("copy", out=pre[:, ci, 3:3 + NG], in_=ps[:])
                for ci in range(4):
                    acc = C.tf()
                    S.dve("tensor_scalar", out=acc[:], in0=pre[:, ci, 0:NG], scalar1=cs_t[:, ci * 4:ci * 4 + 1], scalar2=None, op0=ALU.mult)
                    for k in range(1, 4):
                        S.dve("scalar_tensor_tensor", out=acc[:], in0=pre[:, ci, k:k + NG], scalar=cs_t[:, ci * 4 + k:ci * 4 + k + 1], in1=acc[:],
                              op0=ALU.mult, op1=ALU.add)
                    dst = xf[:, ci, :] if ci < 2 else (BT[:, tsl] if ci == 2 else CT[:, tsl])
                    S.act("activation", out=dst, in_=acc[:], func=AF.Silu, bias=cs_t[:, 16 + ci:17 + ci])
                S.dve("tensor_copy", out=pre[:, :, 0:3], in_=pre[:, :, NG:NG + 3])
                for bi in range(4):
                    blk = g * 4 + bi
                    ps = C.bank()
                    proj_tm(wz, 0, 264, blk, ps[:, 0:264])
                    S.act("activation", out=zs[:, bi, :], in_=ps[:, 0:256], func=AF.Silu)
                    S.dve("tensor_tensor", out=dtt[:, bi, :], in0=ps[:, 256:260], in1=rs_t[:, 0:4], op=ALU.add)
                S.act("activation", out=dtt[:], in_=dtt[:], func=AF.Exp)
                S.act("activation", out=dtt[:], in_=dtt[:], func=AF.Ln, bias=1.0)
                for bi in range(4):
                    S.dve("tensor_tensor", out=dA[:, bi, :], in0=dtt[:, bi, :], in1=Aneg[:], op=ALU.mult)
                for bi in range(4):
                    blk = g * 4 + bi
                    ps = C.bank()
                    for ci in range(2):
                        S.pe("transpose", ps[:, ci * 128:(ci + 1) * 128], xf[:, ci, bi * 128:(bi + 1) * 128], ident_f[:])
                    S.act("copy", out=xtm[:, bi, :], in_=ps[:, 0:256])
                    S.pe("transpose", tpb[:, bi * 128:(bi + 1) * 128], BT[:, blk * 128:(blk + 1) * 128], ident_b[:])
                    S.act("copy", out=Btm[:, bi, :], in_=tpb[:, bi * 128:(bi + 1) * 128])

            def ssd_stage_a(n):
                g, bi = n // 4, n % 4
                dtt, dA, xtm = dtt2[g % 2], dA2[g % 2], xtm2[g % 2]
                csl = slice(n * 128, (n + 1) * 128)
                dAc = dA[:, bi, :]
                cb = cssb[n % 2]
                sc = scs[n % 2]
                pc = C.bank()
                S.pe("matmul", pc[:, 0:4], lhsT=causT[:], rhs=dAc, start=True, stop=True)
                S.pe("matmul", pc[:, 8:12], lhsT=C.ones_f[:], rhs=dAc, start=True, stop=True)
                S.act("copy", out=cb[:, 0:12], in_=pc[:, 0:12])
                S.act("activation", out=sc[:, 0:4], in_=cb[:, 0:4], func=AF.Exp)
                S.dve("tensor_tensor", out=sc[:, 4:8], in0=cb[:, 8:12], in1=cb[:, 0:4], op=ALU.subtract)
                S.act("activation", out=sc[:, 4:8], in_=sc[:, 4:8], func=AF.Exp)
                S.act("activation", out=sc[:, 8:12], in_=cb[:, 8:12], func=AF.Exp)
                S.dve("tensor_tensor", out=sc[:, 12:16], in0=sc[:, 4:8], in1=dtt[:, bi, :], op=ALU.mult)
                pz = C.bank()
                dab = C.tf()
                dab4 = dab[:].rearrange("p (k l) -> p k l", k=4)
                S.dve("tensor_copy", out=dab4, in_=dAc.unsqueeze(2).to_broadcast([128, 4, 128]))
                for k in range(4):
                    S.pe("matmul", pz[:, k * 128:(k + 1) * 128], lhsT=dab[:, k * 128:(k + 1) * 128], rhs=causT[:], start=True, stop=True)
                for k in range(4):
                    S.dve("tensor_scalar", out=dec[:, k, :], in0=pz[:, k * 128:(k + 1) * 128], scalar1=cb[:, k:k + 1], scalar2=0.0,
                          op0=ALU.subtract, op1=ALU.min)
                S.act("activation", out=dec[:], in_=dec[:], func=AF.Exp)
                pcb = C.bank()
                S.pe("matmul", pcb[:, 0:128], lhsT=BT[:, csl], rhs=CT[:, csl], start=True, stop=True)
                cbm = C.tf()
                S.dve("tensor_tensor", out=cbm[:, 0:128], in0=pcb[:, 0:128], in1=causT[:], op=ALU.mult)
                S.dve("tensor_tensor", out=MT[:], in0=dec[:], in1=cbm[:, 0:128].unsqueeze(1).to_broadcast([128, 4, 128]), op=ALU.mult)
                xd2 = xd2b[n % 2]
                x3 = xtm[:, bi, :].rearrange("p (k e) -> p k e", k=4)
                S.dve("tensor_tensor", out=xdt[:].rearrange("p (k e) -> p k e", k=4), in0=x3,
                      in1=dtt[:, bi, :].unsqueeze(2).to_broadcast([128, 4, 64]), op=ALU.mult)
                S.dve("tensor_tensor", out=xd2[:].rearrange("p (k e) -> p k e", k=4), in0=x3,
                      in1=sc[:, 12:16].unsqueeze(2).to_broadcast([128, 4, 64]), op=ALU.mult)
                py = C.bank()
                for k in range(4):
                    ks = slice(k * 64, (k + 1) * 64)
                    S.pe("matmul", py[:, ks], lhsT=MT[:, k, :], rhs=xdt[:, ks], start=True, stop=True)
                S.act("copy", out=ysbb[n % 2][:], in_=py[:, 0:256])

            def ssd_stage_b(n):
                g, bi = n // 4, n % 4
                zs, xtm, Btm = zs2[g % 2], xtm2[g % 2], Btm2[g % 2]
                csl = slice(n * 128, (n + 1) * 128)
                sc = scs[n % 2]
                ysb = ysbb[n % 2]
                hb = hbfb[n % 2]
                if n > 0:
                    po = C.bank()
                    S.pe("matmul", po[:, 0:256], lhsT=CT[:, csl], rhs=hb[:], start=True, stop=True)
                if n < NBLK - 1:
                    pn = C.bank()
                    S.pe("matmul", pn[:, 0:256], lhsT=Btm[:, bi, :], rhs=xd2b[n % 2][:], start=True, stop=True)
                    h3 = hst[:].rearrange("p (k e) -> p k e", k=4)
                    S.dve("tensor_tensor", out=h3, in0=h3, in1=sc[:, 8:12].unsqueeze(2).to_broadcast([128, 4, 64]), op=ALU.mult)
                    S.dve("tensor_tensor", out=hst[:], in0=hst[:], in1=pn[:, 0:256], op=ALU.add)
                    S.act("copy", out=hbfb[(n + 1) % 2][:], in_=hst[:])
                if n > 0:
                    yo = C.tf()
                    S.dve("tensor_tensor", out=yo[:, 0:256].rearrange("p (k e) -> p k e", k=4), in0=po[:, 0:256].rearrange("p (k e) -> p k e", k=4),
                          in1=sc[:, 0:4].unsqueeze(2).to_broadcast([128, 4, 64]), op=ALU.mult)
                    S.dve("tensor_tensor", out=ysb[:], in0=ysb[:], in1=yo[:, 0:256], op=ALU.add)
                xd = C.tf()
                S.dve("tensor_tensor", out=xd[:, 0:256].rearrange("p (k e) -> p k e", k=4), in0=xtm[:, bi, :].rearrange("p (k e) -> p k e", k=4),
                      in1=rs_t[:, 8:12].unsqueeze(2).to_broadcast([128, 4, 64]), op=ALU.mult)
                S.dve("tensor_tensor", out=ysb[:], in0=ysb[:], in1=xd[:, 0:256], op=ALU.add)
                S.dve("tensor_tensor", out=sys_[g % 2][:, bi, :], in0=ysb[:], in1=zs[:, bi, :], op=ALU.mult)
                if bi == 3:
                    emit_y(1, y_ssd, g, sys_[g % 2])

            for g in range(NQG):
                ssd_project(g)
                for bi in range(4):
                    n = g * 4 + bi
                    ssd_stage_a(n)
                    if n >= 1:
                        ssd_stage_b(n - 1)
            ssd_stage_b(NBLK - 1)

        if "ret" in phases:
          with S.scope():
            wr = C.load_w(wret, 0, 512)
            wrv = C.load_w(wret, 512, 512)
            load_late_u()
            rQT = S.sb("rQT", [128, SEQ], BF16)
            rKT = S.sb("rKT", [128, SEQ], BF16)
            rKd = S.sb("rKd", [128, NBLK, 128], BF16)
            rV = S.sb("rV", [128, NBLK, 256], BF16)
            rG = S.sb("rG", [128, NBLK, 256], F32)
            decT = [S.sb(f"decT{hh}", [128, 128], F32) for hh in range(2)]
            qdec = S.sb("qdec", [128, 128], F32)
            kdecT = S.sb("kdecT", [128, 128], F32)
            c128 = S.sb("c128", [128, 2], F32)
            with S.scope():
                relp = S.sb("relp", [128, 128], F32)
                caus = S.sb("caus", [128, 128], F32)
                S.dve("tensor_scalar", out=relp[:], in0=rel0[:], scalar1=0.0, scalar2=None, op0=ALU.max)
                S.dve("tensor_scalar", out=caus[:], in0=rel0[:], scalar1=0.0, scalar2=None, op0=ALU.is_ge)
                for hh in range(2):
                    S.dve("tensor_scalar", out=decT[hh][:], in0=relp[:], scalar1=cst[:, 5 + hh:6 + hh], scalar2=None, op0=ALU.mult)
                    S.act("activation", out=decT[hh][:], in_=decT[hh][:], func=AF.Exp)
                    S.dve("tensor_tensor", out=decT[hh][:], in0=decT[hh][:], in1=caus[:], op=ALU.mult)
                io_i = S.sb("io_i", [128, 128], I32)
                io_f = S.sb("io_f", [128, 128], F32)
                S.pool("iota", io_i[:], pattern=[[1, 128]], base=1, channel_multiplier=0)
                S.pool("tensor_copy", out=io_f[:], in_=io_i[:])
                S.dve("tensor_scalar", out=io_f[:], in0=io_f[:], scalar1=cst[:, 4:5], scalar2=None, op0=ALU.mult)
                S.act("activation", out=qdec[:], in_=io_f[:], func=AF.Exp)
                for hh in range(2):
                    S.dve("tensor_scalar", out=kdecT[:, hh:hh + 1], in0=rel0[:, 127:128], scalar1=cst[:, 5 + hh:6 + hh], scalar2=None, op0=ALU.mult)
                S.act("activation", out=kdecT[:, 0:2], in_=kdecT[:, 0:2], func=AF.Exp)
                S.pool("memset", c128[:, 0:1], 128.0)
                S.dve("tensor_scalar", out=c128[:, 0:1], in0=c128[:, 0:1], scalar1=cst[:, 4:5], scalar2=None, op0=ALU.mult)
                S.act("activation", out=c128[:, 1:2], in_=c128[:, 0:1], func=AF.Exp)
            with S.scope():
                T = alloc_rope()
                for g in range(NQG):
                    tsl = slice(g * NG, (g + 1) * NG)
                    rope_tables(T, g, 0, 1)
                    for which in range(2):
                        pa = C.bank()
                        proj_fm(wr, which * 256, 128, tsl, pa[:])
                        pb = C.bank()
                        proj_fm(wr, which * 256 + 128, 128, tsl, pb[:])
                        t1 = C.tf()
                        S.dve("tensor_tensor", out=t1[:], in0=pa[:], in1=T["tcos"][:], op=ALU.mult)
                        t2 = C.tf()
                        S.dve("tensor_tensor", out=t2[:], in0=pb[:], in1=T["tsin"][:], op=ALU.mult)
                        S.dve("tensor_tensor", out=t1[:], in0=t1[:], in1=t2[:], op=ALU.add)
                        if which == 0:
                            S.act("copy", out=rQT[:, tsl], in_=t1[:])
                        else:
                            S.act("mul", out=rKT[:, tsl], in_=t1[:], mul=0.125)
                            for bi in range(4):
                                blk = g * 4 + bi
                                S.pe("transpose", tpb[:, bi * 128:(bi + 1) * 128], rKT[:, blk * 128:(blk + 1) * 128], ident_b[:])
                                for hh in range(2):
                                    S.dve("tensor_scalar", out=rKd[:, blk, hh * 64:(hh + 1) * 64], in0=tpb[:, bi * 128 + hh * 64:bi * 128 + (hh + 1) * 64],
                                          scalar1=kdecT[:, hh:hh + 1], scalar2=None, op0=ALU.mult)
                    for bi in range(4):
                        blk = g * 4 + bi
                        ps = C.bank()
                        proj_tm(wrv, 0, 512, blk, ps[:])
                        S.act("copy", out=rV[:, blk, :], in_=ps[:, 0:256])
                        S.act("activation", out=rG[:, blk, :], in_=ps[:, 256:512], func=AF.Silu)
            Sst = S.sb("Sst", [128, 128], F32)
            Sbf = [S.sb(f"Sbf{i}", [128, 128], BF16) for i in range(2)]
            S.pool("memset", Sst[:], 0.0)
            rys0 = S.sb("rys0", [128, 4, 256], F32)
            rys = [rys0, rys0]
            rsc = S.sb("rsc", [128, 64], F32)
            amb = [[S.sb(f"ram{i}{hh}", [128, 128], BF16) for hh in range(2)] for i in range(2)]
            qdb = [S.sb(f"rqd{i}", [128, 128], BF16) for i in range(2)]
            junk = S.sb("rjunk", [128, 2, 128], F32)

            def ret_stage_a(n):
                csl = slice(n * 128, (n + 1) * 128)
                if n > 0:
                    S.dve("tensor_tensor", out=qdb[n % 2][:], in0=rQT[:, csl], in1=qdec[:], op=ALU.mult)
                for hh in range(2):
                    hs = slice(hh * 64, hh * 64 + 64)
                    pa = C.bank()
                    S.pe("matmul", pa[:, 0:128], lhsT=rKT[hs, csl], rhs=rQT[hs, csl], start=True, stop=True)
                    S.dve("tensor_tensor", out=amb[n % 2][hh][:], in0=pa[:, 0:128], in1=decT[hh][:], op=ALU.mult)

            def ret_stage_b(n):
                g = n // 4
                if n < NBLK - 1:
                    pk = C.bank()
                    for hh in range(2):
                        S.pe("matmul", pk[hh * 64:(hh + 1) * 64, 0:128], lhsT=rKd[:, n, hh * 64:(hh + 1) * 64], rhs=rV[:, n, hh * 128:(hh + 1) * 128],
                             start=True, stop=True)
                    S.dve("scalar_tensor_tensor", out=Sst[:], in0=Sst[:], scalar=c128[:, 1:2], in1=pk[:, 0:128], op0=ALU.mult, op1=ALU.add)
                    S.act("copy", out=Sbf[(n + 1) % 2][:], in_=Sst[:])
                for hh in range(2):
                    hs = slice(hh * 64, hh * 64 + 64)
                    po = C.bank()
                    S.pe("matmul", po[:, 0:128], lhsT=amb[n % 2][hh][:], rhs=rV[:, n, hh * 128:(hh + 1) * 128], start=True, stop=(n == 0))
                    if n > 0:
                        S.pe("matmul", po[:, 0:128], lhsT=qdb[n % 2][hs, :], rhs=Sbf[n % 2][hs, :], start=False, stop=True)
                    c0 = ((2 * n + hh) % 8) * 8
                    S.pool("memset", rsc[:, c0:c0 + 2], 0.0)
                    S.act("activation", out=junk[:, 0, :], in_=po[:, 0:128], func=AF.Identity, accum_out=rsc[:, c0:c0 + 1])
                    S.act("activation", out=junk[:, 1, :], in_=po[:, 0:128], func=AF.Square, accum_out=rsc[:, c0 + 1:c0 + 2])
                    S.dve("tensor_scalar", out=rsc[:, c0 + 2:c0 + 3], in0=rsc[:, c0:c0 + 1], scalar1=1.0 / 128.0, scalar2=None, op0=ALU.mult)
                    S.dve("tensor_tensor", out=rsc[:, c0 + 3:c0 + 4], in0=rsc[:, c0 + 2:c0 + 3], in1=rsc[:, c0 + 2:c0 + 3], op=ALU.mult)
                    S.dve("tensor_scalar", out=rsc[:, c0 + 4:c0 + 5], in0=rsc[:, c0 + 1:c0 + 2], scalar1=1.0 / 128.0, scalar2=rsc[:, c0 + 3:c0 + 4],
                          op0=ALU.mult, op1=ALU.subtract)
                    S.act("activation", out=rsc[:, c0 + 5:c0 + 6], in_=rsc[:, c0 + 4:c0 + 5], func=AF.Sqrt, bias=EPS)
                    S.dve("reciprocal", out=rsc[:, c0 + 6:c0 + 7], in_=rsc[:, c0 + 5:c0 + 6])
                    x = C.tf()[:, 0:128]
                    S.dve("tensor_scalar", out=x, in0=po[:, 0:128], scalar1=rsc[:, c0 + 2:c0 + 3], scalar2=rsc[:, c0 + 6:c0 + 7],
                          op0=ALU.subtract, op1=ALU.mult)
                    S.dve("tensor_tensor", out=rys[g % 2][:, n % 4, hh * 128:(hh + 1) * 128], in0=x, in1=rG[:, n, hh * 128:(hh + 1) * 128], op=ALU.mult)
                if n % 4 == 3:
                    emit_y(0, y_ret, g, rys[g % 2])

            for n in range(NBLK + 1):
                if n < NBLK:
                    ret_stage_a(n)
                if n >= 1:
                    ret_stage_b(n - 1)

        if "mla" in phases:
          with S.scope():
            T = alloc_rope()
            tcos, tsin = T["tcos"], T["tsin"]
            gm = S.sb("gm", [128, 4], F32)
            S.dma_group("sp", [(gm[:], gmla)], C.ctr)
            load_late_u()
            wm = C.load_w(wmla, 0, 448)
            wq_t = C.load_w(wuq, 0, 256)
            wkv_t = C.load_w(wukv, 0, 384)
            KT = [S.sb(f"mKT{h}", [96, SEQ], BF16) for h in range(2)]
            QT = [S.sb(f"mQT{h}", [96, SEQ], BF16) for h in range(2)]
            Vm = S.sb("mV", [128, NBLK, 2, 129], BF16)
            S.pool("memset", Vm[:, :, :, 128:129], 1.0)
            cq_f = S.sb("cq_f", [128, 2, NG], F32)
            cqn = S.sb("cqn", [128, 2, NG], BF16)
            ckv_f = S.sb("ckv_f", [128, NG], F32)
            ckvn = S.sb("ckvn", [128, NG], BF16)
            for g in range(NQG):
                tsl = slice(g * NG, (g + 1) * NG)
                rope_tables(T, g, 2, 3)
                for c in range(2):
                    ps = C.bank()
                    proj_fm(wm, c * 128, 128, tsl, ps[:])
                    S.act("copy", out=cq_f[:, c, :], in_=ps[:])
                rmsnorm_fm(S, C, lambda dc: cq_f[:, dc, :], 2, gm[:, 0:2], lambda dc: cqn[:, dc, :], NG, 256)
                ps = C.bank()
                proj_fm(wm, 256, 128, tsl, ps[:])
                S.act("copy", out=ckv_f[:], in_=ps[:])
                rmsnorm_fm(S, C, lambda dc: ckv_f[:], 1, gm[:, 2:3], lambda dc: ckvn[:], NG, 128)
                pa = C.bank()
                proj_fm(wm, 384, 32, tsl, pa[64:96, :])
                pb = C.bank()
                proj_fm(wm, 416, 32, tsl, pb[64:96, :])
                t1 = C.tf()
                S.dve("tensor_tensor", out=t1[64:96, :], in0=pa[64:96, :], in1=tcos[64:96, :], op=ALU.mult)
                t2 = C.tf()
                S.dve("tensor_tensor", out=t2[64:96, :], in0=pb[64:96, :], in1=tsin[64:96, :], op=ALU.mult)
                for h in range(2):
                    S.dve("tensor_tensor", out=KT[h][64:96, tsl], in0=t1[64:96, :], in1=t2[64:96, :], op=ALU.add)
                for h in range(2):
                    ps = C.bank()
                    S.pe("matmul", ps[0:64, :], lhsT=wkv_t[:, 0, h * 192:h * 192 + 64], rhs=ckvn[:], start=True, stop=True)
                    S.act("copy", out=KT[h][0:64, tsl], in_=ps[0:64, :])
                    for bi in range(4):
                        blk = g * 4 + bi
                        ps = C.bank()
                        S.pe("matmul", ps[:, 0:128], lhsT=ckvn[:, bi * 128:(bi + 1) * 128], rhs=wkv_t[:, 0, h * 192 + 64:h * 192 + 192], start=True, stop=True)
                        S.act("copy", out=Vm[:, blk, h, 0:128], in_=ps[:, 0:128])
                    ps = C.bank()
                    mm_chain(S, ps[0:96, :], [(wq_t[:, k, h * 96:(h + 1) * 96], cqn[:, k, :]) for k in range(2)])
                    ps2 = C.bank()
                    mm_chain(S, ps2[64:96, :], [(wq_t[:, k, 192 + h * 32:192 + (h + 1) * 32], cqn[:, k, :]) for k in range(2)])
                    S.act("copy", out=QT[h][0:64, tsl], in_=ps[0:64, :])
                    t1 = C.tf()
                    S.dve("tensor_tensor", out=t1[64:96, :], in0=ps[64:96, :], in1=tcos[64:96, :], op=ALU.mult)
                    t2 = C.tf()
                    S.dve("tensor_tensor", out=t2[64:96, :], in0=ps2[64:96, :], in1=tsin[64:96, :], op=ALU.mult)
                    S.dve("tensor_tensor", out=QT[h][64:96, tsl], in0=t1[64:96, :], in1=t2[64:96, :], op=ALU.add)
            ystage = [S.sb(f"mys{i}", [128, 4, 256], F32) for i in range(2)]
            rcp = S.sb("m_rcp", [128, 8], F32)

            def mla_epi(m, g, qb, acc):
                ys = ystage[g % 2]
                col = (m * 4 + qb)
                S.dve("reciprocal", out=rcp[:, col:col + 1], in_=acc[:, 128:129])
                S.dve("tensor_scalar", out=ys[:, qb, m * 128:(m + 1) * 128], in0=acc[:, 0:128], scalar1=rcp[:, col:col + 1], scalar2=None, op0=ALU.mult)
                if m == 1 and qb == 3:
                    emit_y(2, y_mla, g, ys)

            attention(2, lambda m: KT[m][:, :], lambda m: QT[m][:, :], lambda m, kb: Vm[:, kb, m, :], 96 ** -0.5,
                      lambda m, d: (maskT[:] if d == 0 else None), lambda m: None, mla_epi)

        if "diff" in phases:
          with S.scope():
            wd = C.load_w(wdiff, 0, 512)
            wdv = C.load_w(wdiff, 512, 256)
            load_late_u()
            dQp = [S.sb(f"dQp{m}", [128, SEQ], BF16) for m in range(4)]
            for m in range(4):
                S.pool("memset", dQp[m][:], 0.0)
            dKT = S.sb("dKT", [128, 2, SEQ], BF16)
            dV = S.sb("dV", [128, NBLK, 2, 129], BF16)
            S.pool("memset", dV[:, :, :, 128:129], 1.0)
            for g in range(NQG):
                tsl = slice(g * NG, (g + 1) * NG)
                for c in range(2):
                    ps = C.bank()
                    proj_fm(wd, c * 128, 128, tsl, ps[:])
                    for w in range(2):
                        S.act("copy", out=dQp[2 * c + w][w * 64:(w + 1) * 64, tsl], in_=ps[w * 64:(w + 1) * 64, :])
                for c in range(2):
                    ps = C.bank()
                    proj_fm(wd, 256 + c * 128, 128, tsl, ps[:])
                    S.act("copy", out=dKT[:, c, tsl], in_=ps[:])
                for bi in range(4):
                    blk = g * 4 + bi
                    ps = C.bank()
                    proj_tm(wdv, 0, 256, blk, ps[:, 0:256])
                    S.act("copy", out=dV[:, blk, :, 0:128], in_=ps[:, 0:256].rearrange("p (h e) -> p h e", h=2))
            tbl = S.sb("tbl", [128, 64], F32)
            lamt = S.sb("lamt", [128, 256], F32)
            dngs = S.sb("dngs", [128, 128], F32)
            S.dma_group("sp", [(tbl[:], dtbl.partition_broadcast(128)), (lamt[:], dlam.partition_broadcast(128)),
                               (dngs[:], dng.partition_broadcast(128))], C.ctr)
            dT = S.sb("dT", [128, 64], F32)
            S.dve("tensor_tensor", out=dT[:, 1:64], in0=tbl[:, 1:64], in1=tbl[:, 0:63], op=ALU.subtract)
            rel1 = S.sb("rel1", [128, 128], F32)
            S.dve("tensor_scalar", out=rel1[:], in0=rel0[:], scalar1=128.0, scalar2=None, op0=ALU.add)
            thr = t5_thresholds()
            Bt = [[S.sb(f"Bt{hh}{dl}", [128, 128], F32) for dl in range(2)] for hh in range(2)]
            for hh in range(2):
                for dl in range(2):
                    relt = rel0 if dl == 0 else rel1
                    bt = Bt[hh][dl]
                    S.dve("tensor_scalar", out=bt[:], in0=relt[:], scalar1=0.0, scalar2=tbl[:, hh * 32:hh * 32 + 1], op0=ALU.mult, op1=ALU.add)
                    for m in range(1, 32):
                        tmp = C.tf()
                        S.dve("tensor_scalar", out=tmp[:, 0:128], in0=relt[:], scalar1=float(thr[m - 1]), scalar2=dT[:, hh * 32 + m:hh * 32 + m + 1],
                              op0=ALU.is_ge, op1=ALU.mult)
                        S.dve("tensor_tensor", out=bt[:], in0=bt[:], in1=tmp[:, 0:128], op=ALU.add)
                    S.dve("tensor_scalar", out=bt[:], in0=bt[:], scalar1=tbl[:, hh * 32 + 31:hh * 32 + 32], scalar2=8.0, op0=ALU.subtract, op1=ALU.mult)
                    if dl == 0:
                        S.dve("tensor_tensor", out=bt[:], in0=bt[:], in1=maskT[:], op=ALU.add)
            lsc = S.sb("lsc", [128, 8], F32)
            for i in range(2):
                tmp = C.tf()
                S.dve("tensor_tensor", out=tmp[:, 0:64], in0=lamt[:, i * 128:i * 128 + 64], in1=lamt[:, i * 128 + 64:i * 128 + 128], op=ALU.mult)
                S.dve("reduce_sum", out=lsc[:, i:i + 1], in_=tmp[:, 0:64], axis=AX.X)
                S.act("activation", out=lsc[:, 2 + i:3 + i], in_=lsc[:, i:i + 1], func=AF.Exp)
            S.dve("tensor_scalar", out=lsc[:, 4:5], in0=lsc[:, 3:4], scalar1=lsc[:, 2:3], scalar2=-float(lambda_init), op0=ALU.subtract, op1=ALU.add)
            S.dve("tensor_scalar", out=dngs[:], in0=dngs[:], scalar1=1.0 - float(lambda_init), scalar2=None, op0=ALU.mult)
            o1buf = S.sb("o1buf", [128, 2, 4, 128], F32)
            dys = [S.sb(f"dys{i}", [128, 4, 256], F32) for i in range(2)]
            dsc = S.sb("dsc", [128, 64], F32)
            dci = [0]

            def diff_epi(m, g, qb, acc):
                hh, which = m // 2, m % 2
                c0 = (dci[0] % 8) * 8
                dci[0] += 1
                S.dve("reciprocal", out=dsc[:, c0:c0 + 1], in_=acc[:, 128:129])
                if which == 0:
                    S.dve("tensor_scalar", out=o1buf[:, hh, qb, :], in0=acc[:, 0:128], scalar1=dsc[:, c0:c0 + 1], scalar2=None, op0=ALU.mult)
                    return
                ys = dys[g % 2]
                o2 = C.tf()
                S.dve("tensor_scalar", out=o2[:, 0:128], in0=acc[:, 0:128], scalar1=dsc[:, c0:c0 + 1], scalar2=None, op0=ALU.mult)
                S.dve("scalar_tensor_tensor", out=o2[:, 128:256], in0=o2[:, 0:128], scalar=lsc[:, 4:5], in1=o1buf[:, hh, qb, :], op0=ALU.mult, op1=ALU.add)
                S.dve("tensor_tensor", out=o2[:, 256:384], in0=o2[:, 128:256], in1=o2[:, 128:256], op=ALU.mult)
                S.dve("reduce_sum", out=dsc[:, c0 + 1:c0 + 2], in_=o2[:, 256:384], axis=AX.X)
                S.act("activation", out=dsc[:, c0 + 2:c0 + 3], in_=dsc[:, c0 + 1:c0 + 2], func=AF.Ln, scale=1.0 / 128.0, bias=EPS)
                S.act("activation", out=dsc[:, c0 + 3:c0 + 4], in_=dsc[:, c0 + 2:c0 + 3], func=AF.Exp, scale=-0.5)
                S.dve("scalar_tensor_tensor", out=ys[:, qb, hh * 128:(hh + 1) * 128], in0=o2[:, 128:256], scalar=dsc[:, c0 + 3:c0 + 4], in1=dngs[:],
                      op0=ALU.mult, op1=ALU.mult)
                if m == 3 and qb == 3:
                    emit_y(3, y_diff, g, ys)

            attention(4, lambda m: dKT[:, m // 2, :], lambda m: dQp[m][:, :],
                      lambda m, kb: dV[:, kb, m // 2, :], 0.125,
                      lambda m, d: Bt[m // 2][d][:], lambda m: tbl[:, (m // 2) * 32 + 31:(m // 2) * 32 + 32], diff_epi)

        if not fused:
            S.finish()
        print("B built: ins", S.n_ins, "waits", S.n_wait, {k: e.tr.val for k, e in S.engs.items()})
    return nc

def swap_halves(w, nheads, dh):
    K = w.shape[0]
    w4 = w.reshape(K, nheads, 2, dh // 2)
    return np.ascontiguousarray(w4[:, :, ::-1, :]).reshape(K, nheads * dh)


def b_consts(j):
    c = np.zeros((128, 16), np.float32)
    p = np.arange(128)
    c[:, 0] = np.exp(-math.log(10000.0) * ((p % 64) % 32).astype(np.float32) / 32).astype(np.float32)
    c[:, 1] = np.where((p % 64) < 32, -1.0, 1.0)
    c[:, 2] = np.exp(-math.log(10000.0) * (((p - 64) % 32) % 16).astype(np.float32) / 16).astype(np.float32)
    c[:, 3] = np.where(((p - 64) % 32) < 16, -1.0, 1.0)
    lg = np.log1p(-np.exp2(-5.0 - np.arange(4, dtype=np.float32))).astype(np.float32)
    c[:, 4] = lg[2 * j + p // 64]
    c[:, 5] = lg[2 * j]
    c[:, 6] = lg[2 * j + 1]
    return c


def b_inputs(inp, l, b, j, uT_bf16, phases=("ret", "ssd", "mla", "diff")):
    W = inp["w_in"][l]
    o = IN_OFF
    d = {"pos": np.ascontiguousarray(inp["positions"][b:b + 1, :]).astype(np.int32), "cst": b_consts(j)}
    if uT_bf16 is not None:
        d["uT"] = uT_bf16
    if "mla" in phases:
        kr = W[:, o[9]:o[10]]
        d["wmla"] = np.ascontiguousarray(np.concatenate([W[:, o[7]:o[8]], W[:, o[8]:o[9]], kr, swap_halves(kr, 1, 32)], axis=1))
        uq = inp["mla_w_uq"][l].reshape(256, 4, 96)[:, 2 * j:2 * j + 2, :]
        uq_rope_sw = swap_halves(np.ascontiguousarray(uq[:, :, 64:96]).reshape(256, 64), 2, 32)
        d["wuq"] = np.ascontiguousarray(np.concatenate([uq.reshape(256, 192), uq_rope_sw], axis=1))
        d["wukv"] = np.ascontiguousarray(inp["mla_w_ukv"][l].reshape(128, 4, 192)[:, 2 * j:2 * j + 2, :].reshape(128, 384))
        g = np.zeros((128, 4), np.float32)
        g[:, 0:2] = inp["mla_q_norm"][l].reshape(2, 128).T
        g[:, 2] = inp["mla_kv_norm"][l]
        d["gmla"] = g
    if "diff" in phases:
        dq = W[:, o[10]:o[11]][:, j * 256:(j + 1) * 256]
        dk = W[:, o[11]:o[12]][:, j * 256:(j + 1) * 256]
        dv = W[:, o[12]:o[13]][:, j * 256:(j + 1) * 256]
        d["wdiff"] = np.ascontiguousarray(np.concatenate([dq, dk, dv], axis=1))
        d["dlam"] = np.ascontiguousarray(inp["diff_lambda"][l].reshape(1, 256))
        d["dng"] = np.ascontiguousarray(inp["diff_norm"][l].reshape(1, 128))
        d["dtbl"] = np.ascontiguousarray(inp["rel_bias"][:, 2 * j:2 * j + 2].T.reshape(1, 64))
    if "ret" in phases:
        rq = W[:, o[0]:o[1]][:, j * 128:(j + 1) * 128]
        rk = W[:, o[1]:o[2]][:, j * 128:(j + 1) * 128]
        rv = W[:, o[2]:o[3]][:, j * 256:(j + 1) * 256]
        rg = W[:, o[3]:o[4]][:, j * 256:(j + 1) * 256]
        d["wret"] = np.ascontiguousarray(np.concatenate([rq, swap_halves(rq, 2, 64), rk, swap_halves(rk, 2, 64), rv, rg], axis=1))
    if "ssd" in phases:
        sz = W[:, o[4]:o[5]][:, j * 256:(j + 1) * 256]
        xbc = W[:, o[5]:o[6]]
        sx = xbc[:, j * 256:(j + 1) * 256]
        sB = xbc[:, 512 + j * 128:512 + (j + 1) * 128]
        sC = xbc[:, 768 + j * 128:768 + (j + 1) * 128]
        sdt = W[:, o[6]:o[7]][:, j * 4:(j + 1) * 4]
        d["wssd"] = np.ascontiguousarray(np.concatenate([sx, sB, sC, sz, sdt, sdt], axis=1))
        cw = inp["ssm_conv_w"][l]
        cb = inp["ssm_conv_b"][l]
        chans = [slice(j * 256, j * 256 + 128), slice(j * 256 + 128, j * 256 + 256), slice(512 + j * 128, 512 + (j + 1) * 128), slice(768 + j * 128, 768 + (j + 1) * 128)]
        cs = np.zeros((128, 24), np.float32)
        for ci, sl in enumerate(chans):
            cs[:, ci * 4:(ci + 1) * 4] = cw[:, sl].T
            cs[:, 16 + ci] = cb[sl]
        d["cssd"] = cs
        d["rssd"] = np.ascontiguousarray(np.concatenate([inp["ssm_dt_bias"][l][4 * j:4 * j + 4], inp["ssm_a_log"][l][4 * j:4 * j + 4], inp["ssm_d"][l][4 * j:4 * j + 4]]).reshape(1, 12).astype(np.float32))
    return d

from concourse.bass_utils import run_bass_kernel_spmd

DEPTH = 2
_DBG = {}


def pack_gains(d):
    g = np.zeros((128, 64), np.float32)
    for k, off in (("f1", 0), ("mix", 8), ("mixp", 16), ("xa", 24), ("mem", 32), ("f2", 40), ("fin", 48), ("ssm", 56)):
        if k in d:
            v = np.asarray(d[k], np.float32)
            n = v.shape[0] // 128
            g[:, off:off + n] = v.reshape(n, 128).T
    return g


def _c(a):
    return np.ascontiguousarray(a)


def build_fused():
    nc = bass.Bass("TRN2", target_bir_lowering=False)
    es = ExitStack()
    S = Sched(nc, es)
    C = Ctx(S, nbanks=7, nwslots=0)
    tpb = S.ps("tpb", [128, 1024], BF16)
    a_tr = [S.dma_tracker(f"a{i}") for i in range(6)]
    b_tr = [S.dma_tracker(f"b{i}") for i in range(4)]
    cc_tr = [S.dma_tracker(f"cc{i}") for i in range(12)]
    u_tr = [S.dma_tracker(f"ul{i}") for i in range(4)]

    def internal(name, shape, dt):
        return nc.dram_tensor(name, list(shape), dt, kind="Internal").ap()

    h_scr = [internal(f"h_scr{l}", [D, NTA], F32) for l in range(DEPTH)]
    u_src = [[internal(f"u_src{l}_{t}", [D, TB], BF16) for t in range(NTA // TB)] for l in range(DEPTH)]
    u_all = [[internal(f"u_all{l}_{t}", [2 * D, TB], BF16) for t in range(NTA // TB)] for l in range(DEPTH)]
    y_src = [[internal(f"y_src{l}_{i}", [256, SEQ], BF16) for i in range(4)] for l in range(DEPTH)]
    y_all = [[internal(f"y_all{l}_{i}", [512, SEQ], BF16) for i in range(4)] for l in range(DEPTH)]
    base = dict(nc=nc, S=S, C=C, tpb=tpb, a_tr=a_tr, b_tr=b_tr, cc_tr=cc_tr, u_tr=u_tr)
    build_A(False, True, False, fused=dict(base, pfx="A0_", h_src=None, h_dst=h_scr[0], u_src=u_src[0], u_all=u_all[0]))
    for l in range(DEPTH):
        lam0 = 0.8 - 0.6 * math.exp(-0.3 * l)
        build_B(lambda_init=lam0, fused=dict(base, pfx=f"B{l}_", u_all=u_all[l], y_src=y_src[l], y_all=y_all[l]))
        last = (l == DEPTH - 1)
        build_A(True, not last, last, fused=dict(base, pfx=f"A{l + 1}_", h_src=h_scr[l], h_dst=(None if last else h_scr[l + 1]),
                                                 y_all=y_all[l], u_src=(None if last else u_src[l + 1]), u_all=(None if last else u_all[l + 1])))
    S.finish()
    print("FUSED built: ins", S.n_ins, "waits", S.n_wait, {k: e.tr.val for k, e in S.engs.items()})
    es.close()
    return nc


def kernel(**inputs):
    inp = {k: np.asarray(v) for k, v in inputs.items()}
    cores = list(range(8))
    x = inp["x"]
    nc = build_fused()
    in_maps = []
    for c in cores:
        b, j = c // 2, c % 2
        d = {}
        d["A0_hT"] = _c(x[b, j * NTA:(j + 1) * NTA, :].T)
        d["A0_gains"] = pack_gains({"f1": inp["ffn1_norm"][0], "mix": inp["mix_norm"][0]})
        d["A0_f1g"] = inp["ffn1_w_gate"][0]; d["A0_f1u"] = inp["ffn1_w_up"][0]; d["A0_f1d"] = inp["ffn1_w_down"][0]
        for l in range(DEPTH):
            bi = b_inputs(inp, l, b, j, None)
            for k, v in bi.items():
                if k != "uT":
                    d[f"B{l}_{k}"] = v
            last = (l == DEPTH - 1)
            p = f"A{l + 1}_"
            g = {"mixp": inp["mix_norm"][l], "ssm": inp["ssm_norm"][l], "xa": inp["xa_norm"][l], "mem": inp["mem_norm"][l], "f2": inp["ffn2_norm"][l]}
            d[p + "memT"] = _c(inp["mem"][b].T)
            d[p + "wgl"] = _c(inp["w_in"][l][:, IN_OFF[13]:]); d[p + "wbr"] = _c(inp["w_branch"][l].reshape(2048, D)); d[p + "wout"] = inp["w_out"][l]
            d[p + "wq"] = inp["xa_wq"][l]; d[p + "wk"] = inp["xa_wk"][l]; d[p + "wv"] = inp["xa_wv"][l]; d[p + "wo"] = inp["xa_wo"][l]
            d[p + "f2g"] = inp["ffn2_w_gate"][l]; d[p + "f2u"] = inp["ffn2_w_up"][l]; d[p + "f2d"] = inp["ffn2_w_down"][l]
            if not last:
                g["f1"] = inp["ffn1_norm"][l + 1]
                g["mix"] = inp["mix_norm"][l + 1]
                d[p + "f1g"] = inp["ffn1_w_gate"][l + 1]; d[p + "f1u"] = inp["ffn1_w_up"][l + 1]; d[p + "f1d"] = inp["ffn1_w_down"][l + 1]
            else:
                g["fin"] = inp["final_norm"]
            gp = pack_gains(g)
            gp[:, 60] = 1.0 - j
            gp[:, 61] = float(j)
            d[p + "gains"] = gp
        in_maps.append(d)
    res = run_bass_kernel_spmd(nc, in_maps, core_ids=cores).results
    out = np.empty((NB, SEQ, D), np.float32)
    for c in cores:
        b, j = c // 2, c % 2
        out[b, j * NTA:(j + 1) * NTA, :] = np.asarray(res[c][f"A{DEPTH}_outT"]).T
    return out
```

```python
import math
import numpy as np
import ml_dtypes
from contextlib import ExitStack, contextmanager
import concourse.bass as bass
import concourse.mybir as mybir

F32 = mybir.dt.float32
BF16 = mybir.dt.bfloat16
I32 = mybir.dt.int32
ALU = mybir.AluOpType
AF = mybir.ActivationFunctionType
AX = mybir.AxisListType
CC_INC = 1
SAME_ENG_SLACK = 6


def _is_ap(x):
    return hasattr(x, "tensor") and hasattr(x, "ap") and hasattr(x, "offset")


def _region(ap):
    t = ap.tensor
    name = t.name
    dims = ap.ap
    off = ap.offset
    sp = str(ap.space)
    if "PSUM" in sp:
        return (name, 0, 128, 0, 1 << 40)
    if "SB" in sp:
        pstep = dims[0][0]
        pcnt = dims[0][1]
        if pstep == 0:
            p0 = 0
            lo = off
            pstep = 1 << 60
        else:
            p0 = off // pstep
            lo = off % pstep
        ext = 0
        for st, cn in dims[1:]:
            ext += abs(st) * (cn - 1)
        return (name, p0, p0 + pcnt, lo, lo + ext + 1)
    else:
        ext = 0
        for st, cn in dims:
            ext += abs(st) * (cn - 1)
        return (name, 0, 1, off, off + ext + 1)


class Tracker:
    def __init__(self, sem, name):
        self.sem = sem
        self.val = 0
        self.name = name


class Eng:
    def __init__(self, name, eng, tracker):
        self.name = name
        self.eng = eng
        self.tr = tracker
        self.known = {}
        self.pending_noinc = False


class Sched:
    def __init__(self, nc, es):
        self.nc = nc
        self.es = es
        self.es0 = es
        self.recs = {}
        self.engs = {}
        for nm, e in (("pe", nc.tensor), ("act", nc.scalar), ("dve", nc.vector), ("pool", nc.gpsimd), ("sp", nc.sync)):
            tr = Tracker(es.enter_context(nc.semaphore("s_" + nm)), nm)
            self.engs[nm] = Eng(nm, e, tr)
        self.n_ins = 0
        self.n_wait = 0
        self._dma_tr = []
        self._names = {}

    def _uniq(self, name):
        k = self._names.get(name, 0)
        self._names[name] = k + 1
        return name if k == 0 else f"{name}__{k}"

    def sb(self, name, shape, dt):
        return self.es.enter_context(self.nc.sbuf_tensor(self._uniq(name), list(shape), dt))

    def ps(self, name, shape, dt):
        return self.es.enter_context(self.nc.psum_tensor(name, list(shape), dt))

    def dma_tracker(self, name):
        tr = Tracker(self.es0.enter_context(self.nc.semaphore("d_" + name)), name)
        self._dma_tr.append(tr)
        return tr

    def _deps(self, reads, writes, same_eng=None):
        deps = {}

        def add(tr, v):
            if deps.get(tr, 0) < v:
                deps[tr] = v

        for ap in reads:
            name, p0, p1, lo, hi = _region(ap)
            for r in self.recs.get(name, ()):
                if r[4] and r[0] < p1 and p0 < r[1] and r[2] < hi and lo < r[3]:
                    add(r[5], r[6])
        for ap in writes:
            name, p0, p1, lo, hi = _region(ap)
            for r in self.recs.get(name, ()):
                if r[0] < p1 and p0 < r[1] and r[2] < hi and lo < r[3]:
                    if same_eng is not None and r[5] is same_eng:
                        continue
                    add(r[5], r[6])
        return deps

    def _record(self, reads, writes, tr, val):
        for ap in writes:
            name, p0, p1, lo, hi = _region(ap)
            lst = self.recs.setdefault(name, [])
            lst[:] = [r for r in lst if not (p0 <= r[0] and r[1] <= p1 and lo <= r[2] and r[3] <= hi)]
            lst.append([p0, p1, lo, hi, True, tr, val])
        for ap in reads:
            name, p0, p1, lo, hi = _region(ap)
            lst = self.recs.setdefault(name, [])
            for r in lst:
                if (not r[4]) and r[5] is tr and r[0] == p0 and r[1] == p1 and r[2] == lo and r[3] == hi:
                    r[6] = val
                    break
            else:
                lst.append([p0, p1, lo, hi, False, tr, val])

    def _emit_waits(self, E, deps):
        for tr, v in deps.items():
            if E.known.get(tr, 0) >= v:
                continue
            if tr is E.tr and E.name == "pe":
                continue
            if tr is E.tr and E.tr.val - v >= SAME_ENG_SLACK:
                continue
            E.eng.wait_ge(tr.sem, v)
            E.known[tr] = v
            self.n_wait += 1

    def op(self, engname, method, *args, inc=True, extra_reads=(), extra_writes=(), **kwargs):
        E = self.engs[engname]
        writes, reads = [], []
        if "out" in kwargs:
            writes.append(kwargs["out"])
            pos_reads = args
        else:
            writes.append(args[0])
            pos_reads = args[1:]
        for a in pos_reads:
            if _is_ap(a):
                reads.append(a)
        for k, a in kwargs.items():
            if k == "out":
                continue
            if k == "accum_out":
                if a is not None:
                    writes.append(a)
                continue
            if _is_ap(a):
                reads.append(a)
        reads.extend(extra_reads)
        writes.extend(extra_writes)
        deps = self._deps(reads, writes, same_eng=E.tr)
        self._emit_waits(E, deps)
        ins = getattr(E.eng, method)(*args, **kwargs)
        val = E.tr.val + 1
        if inc:
            ins.then_inc(E.tr.sem, 1)
            E.tr.val = val
        self._record(reads, writes, E.tr, val)
        self.n_ins += 1
        return ins

    def dma(self, queue, out, in_, tracker, **kw):
        E = self.engs[queue]
        deps = self._deps([in_], [out])
        self._emit_waits(E, deps)
        ins = E.eng.dma_start(out=out, in_=in_, **kw)
        ins.then_inc(tracker.sem, 16)
        tracker.val += 16
        self._record([in_], [out], tracker, tracker.val)
        self.n_ins += 1
        return ins

    def finish(self, queue="sp"):
        E = self.engs[queue]
        for tr in self._dma_tr:
            if tr.val > 0:
                E.eng.wait_ge(tr.sem, tr.val)
        for e in self.engs.values():
            if e.tr.val > 0 and e is not E:
                E.eng.wait_ge(e.tr.sem, e.tr.val)

    def pe(self, m, *a, **k):
        return self.op("pe", m, *a, **k)

    def act(self, m, *a, **k):
        return self.op("act", m, *a, **k)

    def dve(self, m, *a, **k):
        return self.op("dve", m, *a, **k)

    def pool(self, m, *a, **k):
        return self.op("pool", m, *a, **k)

    def dma_group(self, queue, pairs, tracker, **kw):
        E = self.engs[queue]
        for out, in_ in pairs:
            deps = self._deps([in_], [out])
            self._emit_waits(E, deps)
            E.eng.dma_start(out=out, in_=in_, **kw).then_inc(tracker.sem, 16)
            tracker.val += 16
            self.n_ins += 1
        for out, in_ in pairs:
            self._record([in_], [out], tracker, tracker.val)

    def barrier(self):
        trs = [e.tr for e in self.engs.values()] + list(self._dma_tr)
        for E in self.engs.values():
            for tr in trs:
                if tr.val > 0 and E.known.get(tr, 0) < tr.val:
                    E.eng.wait_ge(tr.sem, tr.val)
                    E.known[tr] = tr.val
                    self.n_wait += 1

    @contextmanager
    def scope(self):
        old = self.es
        with ExitStack() as es2:
            self.es = es2
            try:
                yield
            finally:
                self.barrier()
                self.es = old

    def collective(self, kind, in_ap, out_ap, groups, tracker):
        E = self.engs["pool"]
        deps = self._deps([in_ap], [out_ap])
        self._emit_waits(E, deps)
        ins = E.eng.collective_compute(kind, mybir.AluOpType.bypass, replica_groups=groups, ins=[in_ap.opt()], outs=[out_ap.opt()])
        ins.then_inc(tracker.sem, CC_INC)
        tracker.val += CC_INC
        self._record([in_ap], [out_ap], tracker, tracker.val)
        self.n_ins += 1
        return ins

D = 1024
DFF = 2816
SEQ = 4096
NB = 4
MEM = 256
EPS = 1e-6
IN_SIZES = (256, 256, 512, 512, 512, 1024, 8, 256, 128, 32, 512, 512, 512, 4096)
IN_OFF = [0]
for _s in IN_SIZES:
    IN_OFF.append(IN_OFF[-1] + _s)
NTA = 2048
TB = 1024
NG = 512
WSLOT = 5632


class Ctx:
    def __init__(self, S, nbanks=8, nwslots=4, ntmp=4):
        self.S = S
        self.banks = [S.ps(f"bank{i}", [128, 512], F32) for i in range(nbanks)]
        self._bi = 0
        self.wtr = [S.dma_tracker(f"w{i}") for i in range(4)]
        self._wgen = 0
        self.set_wslots(nwslots)
        self.tmpf = [S.sb(f"tmpf{i}", [128, 512], F32) for i in range(ntmp)]
        self._ti = 0
        self.tmpb = [S.sb(f"tmpb{i}", [128, 512], BF16) for i in range(ntmp)]
        self._tbi = 0
        self.rstd = S.sb("rstd", [128, 512], F32)
        self.ones_f = S.sb("ones_f", [128, 128], F32)
        self.ones_b = S.sb("ones_b", [128, 128], BF16)
        S.pool("memset", self.ones_f[:], 1.0)
        S.pool("memset", self.ones_b[:], 1.0)
        self.ctr = S.dma_tracker("const")

    def set_wslots(self, n):
        self._wgen += 1
        self.wslots = [self.S.sb(f"wslot{self._wgen}_{i}", [128, WSLOT], BF16) for i in range(n)]
        self._wi = 0

    def bank(self):
        b = self.banks[self._bi % len(self.banks)]
        self._bi += 1
        return b

    def tf(self):
        t = self.tmpf[self._ti % len(self.tmpf)]
        self._ti += 1
        return t

    def tb(self):
        t = self.tmpb[self._tbi % len(self.tmpb)]
        self._tbi += 1
        return t

    def load_w(self, w_dram, c0, cw, queue="pool"):
        S = self.S
        K = w_dram.shape[0]
        nk = K // 128
        assert nk * cw <= WSLOT, (nk, cw)
        i = self._wi % len(self.wslots)
        self._wi += 1
        view = self.wslots[i][:, 0:nk * cw].rearrange("p (k c) -> p k c", c=cw)
        src = w_dram.rearrange("(k p) n -> p k n", p=128)[:, :, c0:c0 + cw]
        S.dma(queue, view, src, self.wtr[i])
        return view


def mm_chain(S, out, pairs):
    n = len(pairs)
    for i, (l, r) in enumerate(pairs):
        S.pe("matmul", out, lhsT=l, rhs=r, start=(i == 0), stop=(i == n - 1), inc=(i == n - 1))


def rmsnorm_fm(S, C, src, nch, gain, dst, ntok, dtot):
    for sg in range(ntok // NG):
        sl = slice(sg * NG, (sg + 1) * NG)
        ps = C.bank()
        for dc in range(nch):
            sq = C.tb()
            S.act("activation", out=sq[:], in_=src(dc)[:, sl], func=AF.Square)
            S.pe("matmul", ps[:], lhsT=C.ones_b[:], rhs=sq[:], start=(dc == 0), stop=(dc == nch - 1))
        S.act("activation", out=C.rstd[:], in_=ps[:], func=AF.Sqrt, bias=EPS, scale=1.0 / dtot)
        S.dve("reciprocal", out=C.rstd[:], in_=C.rstd[:])
        for dc in range(nch):
            S.dve("scalar_tensor_tensor", out=dst(dc)[:, sl], in0=src(dc)[:, sl], scalar=gain[:, dc:dc + 1],
                  in1=C.rstd[:], op0=ALU.mult, op1=ALU.mult)


def linear_fm(S, C, w_dram, xin, ntok, consume, ctile=512):
    K, N = w_dram.shape
    nk = K // 128
    ctile = min(ctile, (WSLOT // nk) // 128 * 128)
    for c0 in range(0, N, ctile):
        cw = min(ctile, N - c0)
        wt = C.load_w(w_dram, c0, cw)
        for sg in range(ntok // NG):
            sl = slice(sg * NG, (sg + 1) * NG)
            for oc in range(cw // 128):
                ps = C.bank()
                mm_chain(S, ps[:], [(wt[:, k, oc * 128:(oc + 1) * 128], xin(k)[:, sl]) for k in range(nk)])
                consume(c0 // 128 + oc, ps, sl)


def ffn_fm(S, C, wg, wu, wd, xn, h, act, ntok):
    for c0 in range(0, DFF, 512):
        cw = min(512, DFF - c0)
        wgt = C.load_w(wg, c0, cw)
        wut = C.load_w(wu, c0, cw)
        for sg in range(ntok // NG):
            sl = slice(sg * NG, (sg + 1) * NG)
            for oc in range(cw // 128):
                fc = c0 // 128 + oc
                pg = C.bank()
                mm_chain(S, pg[:], [(wgt[:, k, oc * 128:(oc + 1) * 128], xn[:, k, sl]) for k in range(8)])
                pu = C.bank()
                mm_chain(S, pu[:], [(wut[:, k, oc * 128:(oc + 1) * 128], xn[:, k, sl]) for k in range(8)])
                t = C.tf()
                S.act("activation", out=t[:], in_=pg[:], func=AF.Silu)
                S.dve("tensor_tensor", out=act[:, fc, sl], in0=t[:], in1=pu[:], op=ALU.mult)

    def upd(dc, ps, sl):
        S.dve("scalar_tensor_tensor", out=h[:, dc, sl], in0=ps[:], scalar=0.5, in1=h[:, dc, sl], op0=ALU.mult, op1=ALU.add)

    linear_fm(S, C, wd, lambda k: act[:, k, :], ntok, upd, ctile=256)


PAIRS = [[0, 1], [2, 3], [4, 5], [6, 7]]


def build_A(has_post, has_pre, final, fused=None):
    nc = fused["nc"] if fused else bass.Bass("TRN2", target_bir_lowering=False)
    pfx = fused["pfx"] if fused else ""

    def din(name, shape, dt=F32):
        return nc.dram_tensor(pfx + name, list(shape), dt, kind="ExternalInput").ap()

    def dout(name, shape, dt=F32):
        return nc.dram_tensor(pfx + name, list(shape), dt, kind="ExternalOutput").ap()

    if fused and fused.get("h_src") is not None:
        hT = fused["h_src"]
    else:
        hT = din("hT", [D, NTA])
    gains_d = din("gains", [128, 64])
    if has_post:
        if not fused:
            yT = din("yT", [2048, NTA])
        memT = din("memT", [D, MEM])
        wgl = din("wgl", [D, 4096]); wbr = din("wbr", [2048, D]); wout = din("wout", [D, D])
        wq = din("wq", [D, D]); wk = din("wk", [D, D]); wv = din("wv", [D, D]); wo = din("wo", [D, D])
        f2g = din("f2g", [D, DFF]); f2u = din("f2u", [D, DFF]); f2d = din("f2d", [DFF, D])
    if has_pre:
        f1g = din("f1g", [D, DFF]); f1u = din("f1u", [D, DFF]); f1d = din("f1d", [DFF, D])
        if fused:
            h1T = fused["h_dst"]
        else:
            h1T = dout("h1T", [D, NTA])
            uT = dout("uT", [D, NTA], BF16)
    if final:
        outT = dout("outT", [D, NTA])

    with ExitStack() as es:
        if fused:
            S, C = fused["S"], fused["C"]
            es.enter_context(S.scope())
            C.set_wslots(3)
        else:
            S = Sched(nc, es)
            C = Ctx(S)
        gains = S.sb("gains_sb", [128, 64], F32)
        S.dma_group("sp", [(gains[:], gains_d)], C.ctr)
        G_F1, G_MIX, G_MIXP, G_XA, G_MEM, G_F2, G_FIN, G_SSM = 0, 8, 16, 24, 32, 40, 48, 56
        h = S.sb("h", [128, 8, TB], F32)
        xn = S.sb("xn", [128, 8, TB], BF16)
        scr = S.sb("scr", [128, 24, TB], BF16)
        if fused:
            htr, otr, otr2, ytr, ystr, mtr = fused["a_tr"]
        else:
            htr = S.dma_tracker("h")
            otr = S.dma_tracker("o")
            otr2 = S.dma_tracker("o2")
            ytr = S.dma_tracker("y")
            ystr = S.dma_tracker("ys")
            mtr = S.dma_tracker("mem")
        if has_post:
            if fused:
                ylo = S.sb("ylo", [128, 4, NG], BF16)
                yhi = S.sb("yhi", [128, 4, NG], BF16)
            ssm_f = S.sb("ssm_f", [128, 4, TB], F32)
            macc = S.sb("macc", [128, 4, 512], F32)
            pT = S.sb("pT", [128, 2, NG], BF16)
            rl = S.sb("rl", [128, NG], F32)
            mnT = S.sb("mnT", [128, 8, MEM], BF16)
            kT = S.sb("kT", [128, 8, MEM], BF16)
            vtm = S.sb("vtm", [128, 2, D], BF16)
            with S.scope():
                mem_f = S.sb("mem_f", [128, 8, MEM], F32)
                S.dma("sp", mem_f[:], memT.rearrange("(c p) t -> p c t", p=128), mtr)
                psm = C.bank()
                for dc in range(8):
                    sq = C.tf()
                    S.dve("tensor_tensor", out=sq[:, 0:MEM], in0=mem_f[:, dc, :], in1=mem_f[:, dc, :], op=ALU.mult)
                    S.pe("matmul", psm[:, 0:MEM], lhsT=C.ones_f[:], rhs=sq[:, 0:MEM], start=(dc == 0), stop=(dc == 7))
                S.act("activation", out=C.rstd[:, 0:MEM], in_=psm[:, 0:MEM], func=AF.Sqrt, bias=EPS, scale=1.0 / D)
                S.dve("reciprocal", out=C.rstd[:, 0:MEM], in_=C.rstd[:, 0:MEM])
                for dc in range(8):
                    S.dve("scalar_tensor_tensor", out=mnT[:, dc, :], in0=mem_f[:, dc, :], scalar=gains[:, G_MEM + dc:G_MEM + dc + 1],
                          in1=C.rstd[:, 0:MEM], op0=ALU.mult, op1=ALU.mult)
            for c0 in range(0, D, 512):
                wt = C.load_w(wk, c0, 512)
                for oc in range(4):
                    ps = C.bank()
                    mm_chain(S, ps[:, 0:MEM], [(wt[:, k, oc * 128:(oc + 1) * 128], mnT[:, k, :]) for k in range(8)])
                    S.act("copy", out=kT[:, c0 // 128 + oc, :], in_=ps[:, 0:MEM])
            for c0 in range(0, D, 512):
                wt = C.load_w(wv, c0, 512)
                for mc in range(2):
                    ps = C.bank()
                    mm_chain(S, ps[:], [(mnT[:, k, mc * 128:(mc + 1) * 128], wt[:, k, :]) for k in range(8)])
                    S.act("copy", out=vtm[:, mc, c0:c0 + 512], in_=ps[:])

        for tb in range(NTA // TB):
            t0 = tb * TB
            S.dma("sp", h[:], hT.rearrange("(c p) t -> p c t", p=128)[:, :, t0:t0 + TB], htr)
            if has_post:
                ybuf = scr[:, 0:16, :]
                merged = scr[:, 16:24, :]
                if not fused:
                    yv = yT.rearrange("(c p) t -> p c t", p=128)
                    S.dma("pool", scr[:, 0:4, :], yv[:, 0:4, t0:t0 + TB], ytr)
                    S.dma("pool", scr[:, 8:16, :], yv[:, 8:16, t0:t0 + TB], ytr)
                    S.dma("sp", ssm_f[:], yv[:, 4:8, t0:t0 + TB], ystr)
                else:
                    for i in range(4):
                        yv = fused["y_all"][i].rearrange("(c p) t -> p c t", p=128)
                        for sg in range(TB // NG):
                            c_lo = t0 + sg * NG
                            S.dma("sp", ylo[:], yv[:, :, c_lo:c_lo + NG], ytr)
                            S.dma("sp", yhi[:], yv[:, :, NTA + c_lo:NTA + c_lo + NG], ystr)
                            dst = ssm_f[:, :, sg * NG:(sg + 1) * NG] if i == 1 else scr[:, 4 * i:4 * i + 4, sg * NG:(sg + 1) * NG]
                            S.dve("tensor_scalar", out=dst, in0=ylo[:], scalar1=gains[:, 60:61], scalar2=None, op0=ALU.mult)
                            S.dve("scalar_tensor_tensor", out=dst, in0=yhi[:], scalar=gains[:, 61:62], in1=dst, op0=ALU.mult, op1=ALU.add)
                rmsnorm_fm(S, C, lambda dc: h[:, dc, :], 8, gains[:, G_MIXP:G_MIXP + 8], lambda dc: xn[:, dc, :], TB, D)
                rmsnorm_fm(S, C, lambda dc: ssm_f[:, dc, :], 4, gains[:, G_SSM:G_SSM + 4], lambda dc: scr[:, 4 + dc, :], TB, 512)
                for c0 in range(0, D, 256):
                    for i in range(4):
                        wg_t = C.load_w(wgl, i * 1024 + c0, 256)
                        wb_t = C.load_w(wbr[i * 512:(i + 1) * 512, :], c0, 256)
                        for sg in range(TB // NG):
                            sl = slice(sg * NG, (sg + 1) * NG)
                            for oc in range(2):
                                dc = c0 // 128 + oc
                                ma = macc[:, sg * 2 + oc, :]
                                pg = C.bank()
                                mm_chain(S, pg[:], [(wg_t[:, k, oc * 128:(oc + 1) * 128], xn[:, k, sl]) for k in range(8)])
                                pz = C.bank()
                                mm_chain(S, pz[:], [(wb_t[:, k, oc * 128:(oc + 1) * 128], scr[:, 4 * i + k, sl]) for k in range(4)])
                                sg_t = C.tf()
                                S.act("activation", out=sg_t[:], in_=pg[:], func=AF.Sigmoid)
                                if i == 0:
                                    S.dve("tensor_tensor", out=ma, in0=sg_t[:], in1=pz[:], op=ALU.mult)
                                else:
                                    S.dve("tensor_tensor", out=sg_t[:], in0=sg_t[:], in1=pz[:], op=ALU.mult)
                                    if i < 3:
                                        S.dve("tensor_tensor", out=ma, in0=ma, in1=sg_t[:], op=ALU.add)
                                    else:
                                        S.dve("tensor_tensor", out=merged[:, dc, sl], in0=ma, in1=sg_t[:], op=ALU.add)

                def add_h(dc, ps, sl):
                    S.dve("tensor_tensor", out=h[:, dc, sl], in0=ps[:], in1=h[:, dc, sl], op=ALU.add)

                linear_fm(S, C, wout, lambda k: merged[:, k, :], TB, add_h)
                rmsnorm_fm(S, C, lambda dc: h[:, dc, :], 8, gains[:, G_XA:G_XA + 8], lambda dc: xn[:, dc, :], TB, D)
                qT = scr[:, 0:8, :]
                oT = scr[:, 8:16, :]

                def put_q(oc, ps, sl):
                    S.act("copy", out=qT[:, oc, sl], in_=ps[:])

                linear_fm(S, C, wq, lambda k: xn[:, k, :], TB, put_q)
                for hd in range(4):
                    for sg in range(TB // NG):
                        sl = slice(sg * NG, (sg + 1) * NG)
                        for mc in range(2):
                            ps = C.bank()
                            mm_chain(S, ps[:], [(kT[:, 2 * hd + dk, mc * 128:(mc + 1) * 128], qT[:, 2 * hd + dk, sl]) for dk in range(2)])
                            S.act("activation", out=pT[:, mc, :], in_=ps[:], func=AF.Exp, scale=1.0 / 16.0)
                        pl = C.bank()
                        mm_chain(S, pl[:], [(C.ones_b[:], pT[:, mc, :]) for mc in range(2)])
                        S.dve("reciprocal", out=rl[:], in_=pl[:])
                        for dvc in range(2):
                            po = C.bank()
                            mm_chain(S, po[:], [(vtm[:, mc, hd * 256 + dvc * 128: hd * 256 + (dvc + 1) * 128], pT[:, mc, :]) for mc in range(2)])
                            S.dve("tensor_tensor", out=oT[:, 2 * hd + dvc, sl], in0=po[:], in1=rl[:], op=ALU.mult)
                linear_fm(S, C, wo, lambda k: oT[:, k, :], TB, add_h)
                rmsnorm_fm(S, C, lambda dc: h[:, dc, :], 8, gains[:, G_F2:G_F2 + 8], lambda dc: xn[:, dc, :], TB, D)
                ffn_fm(S, C, f2g, f2u, f2d, xn, h, scr[:, 0:22, :], TB)
            if has_pre:
                rmsnorm_fm(S, C, lambda dc: h[:, dc, :], 8, gains[:, G_F1:G_F1 + 8], lambda dc: xn[:, dc, :], TB, D)
                ffn_fm(S, C, f1g, f1u, f1d, xn, h, scr[:, 0:22, :], TB)
                S.dma("sp", h1T.rearrange("(c p) t -> p c t", p=128)[:, :, t0:t0 + TB], h[:], otr)
                rmsnorm_fm(S, C, lambda dc: h[:, dc, :], 8, gains[:, G_MIX:G_MIX + 8], lambda dc: xn[:, dc, :], TB, D)
                if fused:
                    S.dma("sp", fused["u_src"][tb].rearrange("(c p) t -> p c t", p=128), xn[:], otr2)
                    S.collective("AllGather", fused["u_src"][tb], fused["u_all"][tb], PAIRS, fused["cc_tr"].pop())
                else:
                    S.dma("sp", uT.rearrange("(c p) t -> p c t", p=128)[:, :, t0:t0 + TB], xn[:], otr2)
            if final:
                rmsnorm_fm(S, C, lambda dc: h[:, dc, :], 8, gains[:, G_FIN:G_FIN + 8], lambda dc: h[:, dc, :], TB, D)
                S.dma("sp", outT.rearrange("(c p) t -> p c t", p=128)[:, :, t0:t0 + TB], h[:], otr)
        if not fused:
            S.finish()
        print("A built: ins", S.n_ins, "waits", S.n_wait, {k: e.tr.val for k, e in S.engs.items()})
    return nc

NQG = SEQ // NG
NBLK = SEQ // 128
TWO_PI = 2.0 * math.pi
MAGIC = 12582912.0
SKEW = 3


def t5_thresholds():
    n = np.arange(0, 512, dtype=np.int64)
    nf = np.maximum(n, 1).astype(np.float32)
    large = 16 + (np.log(nf / np.float32(16)) / np.float32(math.log(128 / 16)) * np.float32(16)).astype(np.int32)
    bucket = np.where(n < 16, n, np.minimum(large, 31))
    return [int(np.argmax(bucket >= m)) for m in range(1, 32)]


def build_B(phases=("ret", "ssd", "mla", "diff"), lambda_init=0.2, fused=None):
    nc = fused["nc"] if fused else bass.Bass("TRN2", target_bir_lowering=False)
    pfx = fused["pfx"] if fused else ""

    def din(name, shape, dt=F32):
        return nc.dram_tensor(pfx + name, list(shape), dt, kind="ExternalInput").ap()

    def dout(name, shape, dt=F32):
        if fused:
            return None
        return nc.dram_tensor(pfx + name, list(shape), dt, kind="ExternalOutput").ap()

    if not fused:
        uT_d = din("uT", [D, SEQ], BF16)
    pos_d = din("pos", [1, SEQ], mybir.dt.int32)
    cst_d = din("cst", [128, 16])
    if "mla" in phases:
        wmla = din("wmla", [D, 448])
        wuq = din("wuq", [256, 256])
        wukv = din("wukv", [128, 384])
        gmla = din("gmla", [128, 4])
        y_mla = dout("y_mla", [SEQ, 256])
    if "diff" in phases:
        wdiff = din("wdiff", [D, 768])
        dlam = din("dlam", [1, 256])
        dng = din("dng", [1, 128])
        dtbl = din("dtbl", [1, 64])
        y_diff = dout("y_diff", [SEQ, 256])
    if "ret" in phases:
        wret = din("wret", [D, 1024])
        y_ret = dout("y_ret", [SEQ, 256])
    if "ssd" in phases:
        wssd = din("wssd", [D, 776])
        cssd = din("cssd", [128, 24])
        rssd = din("rssd", [1, 12])
        y_ssd = dout("y_ssd", [SEQ, 256])

    with ExitStack() as es:
        if fused:
            S, C = fused["S"], fused["C"]
            es.enter_context(S.scope())
            C.set_wslots(3)
            tpb = fused["tpb"]
            utr, ptr_, otr0, otr1 = fused["b_tr"]
            otr = [otr0, otr1]
        else:
            S = Sched(nc, es)
            C = Ctx(S, nbanks=7, nwslots=3)
            tpb = S.ps("tpb", [128, 1024], BF16)
            utr = S.dma_tracker("u")
            ptr_ = S.dma_tracker("pos")
            otr = [S.dma_tracker(f"o{i}") for i in range(2)]
        cst = S.sb("cst_sb", [128, 16], F32)
        S.dma_group("sp", [(cst[:], cst_d)], C.ctr)
        uT = S.sb("uT_sb", [128, 8, SEQ], BF16)
        if fused:
            def load_u(tb):
                for r in range(2):
                    S.dma("sp", uT[:, :, r * NTA + tb * TB:r * NTA + (tb + 1) * TB],
                          fused["u_all"][tb][r * D:(r + 1) * D, :].rearrange("(c p) t -> p c t", p=128), fused["u_tr"][r * 2 + tb])

            load_u(0)
            load_u(1)
            late_u = [False]

            def load_late_u():
                if late_u[0]:
                    late_u[0] = False
                    load_u(1)
        else:
            S.dma("sp", uT[:], uT_d.rearrange("(c p) t -> p c t", p=128), utr)

            def load_late_u():
                pass
        yts = [S.sb(f"yts{i}", [128, 2, NG], BF16) for i in range(2)]
        yti = [0]

        def emit_y(branch, y_dram, g, ys):
            if not fused:
                S.dma("sp", y_dram[g * NG:(g + 1) * NG, :].rearrange("(q p) c -> p q c", p=128), ys[:], otr[g % 2])
                return
            yt = yts[yti[0] % 2]
            tr_ = otr[yti[0] % 2]
            yti[0] += 1
            for fc in range(2):
                ps = C.bank()
                for qb in range(4):
                    S.pe("transpose", ps[:, qb * 128:(qb + 1) * 128], ys[:, qb, fc * 128:(fc + 1) * 128], ident_f[:])
                S.act("copy", out=yt[:, fc, :], in_=ps[:])
            S.dma("sp", fused["y_src"][branch].rearrange("(c p) t -> p c t", p=128)[:, :, g * NG:(g + 1) * NG], yt[:], tr_)
            if g == NQG - 1:
                S.collective("AllGather", fused["y_src"][branch], fused["y_all"][branch], PAIRS, fused["cc_tr"].pop())
        ident_b = S.sb("ident_b", [128, 128], BF16)
        S.pool("memset", ident_b[:], 0.0)
        S.pool("affine_select", out=ident_b[:], in_=ident_b[:], pattern=[[-1, 128]], compare_op=ALU.not_equal, fill=1.0, base=0, channel_multiplier=1)
        ident_f = S.sb("ident_f", [128, 128], F32)
        S.pool("memset", ident_f[:], 0.0)
        S.pool("affine_select", out=ident_f[:], in_=ident_f[:], pattern=[[-1, 128]], compare_op=ALU.not_equal, fill=1.0, base=0, channel_multiplier=1)
        rel_i = S.sb("rel_i", [128, 128], I32)
        rel0 = S.sb("rel0", [128, 128], F32)
        S.pool("iota", rel_i[:], pattern=[[1, 128]], base=0, channel_multiplier=-1)
        S.pool("tensor_copy", out=rel0[:], in_=rel_i[:])
        ptr = ptr_

        def alloc_rope():
            return dict(pos_i=S.sb("pos_i", [128, NG], I32), pos_f=S.sb("pos_f", [128, NG], F32), tang=S.sb("tang", [128, NG], F32),
                        tcos=S.sb("tcos", [128, NG], F32), tsin=S.sb("tsin", [128, NG], F32))

        def rope_tables(T, g, inv_col, sign_col):
            pos_i, pos_f, tang, tcos, tsin = T["pos_i"], T["pos_f"], T["tang"], T["tcos"], T["tsin"]
            S.dma("sp", pos_i[:], pos_d[:, g * NG:(g + 1) * NG].partition_broadcast(128), ptr)
            S.dve("tensor_copy", out=pos_f[:], in_=pos_i[:])
            for (dst, shift) in ((tsin, 0.0), (tcos, math.pi / 2)):
                S.dve("tensor_scalar", out=tang[:], in0=pos_f[:], scalar1=cst[:, inv_col:inv_col + 1], scalar2=shift, op0=ALU.mult, op1=ALU.add)
                S.dve("tensor_scalar", out=dst[:], in0=tang[:], scalar1=1.0 / TWO_PI, scalar2=MAGIC, op0=ALU.mult, op1=ALU.add)
                S.dve("tensor_scalar", out=dst[:], in0=dst[:], scalar1=MAGIC, scalar2=-TWO_PI, op0=ALU.subtract, op1=ALU.mult)
                S.dve("tensor_tensor", out=tang[:], in0=tang[:], in1=dst[:], op=ALU.add)
                S.dve("tensor_scalar", out=tang[:], in0=tang[:], scalar1=math.pi, scalar2=-math.pi, op0=ALU.min, op1=ALU.max)
                S.act("activation", out=dst[:], in_=tang[:], func=AF.Sin)
            S.dve("tensor_scalar", out=tsin[:], in0=tsin[:], scalar1=cst[:, sign_col:sign_col + 1], scalar2=None, op0=ALU.mult)

        def proj_fm(wt, col0, ncols, tsl, ps_ap):
            mm_chain(S, ps_ap, [(wt[:, k, col0:col0 + ncols], uT[:, k, tsl]) for k in range(8)])

        def proj_tm(wt, c0, c1, blk, ps_ap):
            mm_chain(S, ps_ap, [(uT[:, k, blk * 128:(blk + 1) * 128], wt[:, k, c0:c1]) for k in range(8)])

        maskT = S.sb("maskT", [128, 128], F32)
        S.dve("tensor_scalar", out=maskT[:], in0=rel0[:], scalar1=0.0, scalar2=-30000.0, op0=ALU.is_lt, op1=ALU.mult)

        pti = [0]
        att_id = [0]

        def attention(nmaps, KT_of, QT_of, V_of, scale, prebias, exp_bias, epilogue):
            att_id[0] += 1
            PT = [S.sb(f"PT{att_id[0]}_{i}", [128, NG], BF16) for i in range(4)]

            def next_PT():
                t = PT[pti[0] % 4]
                pti[0] += 1
                return t

            accb = [C.banks[3], C.banks[4], C.banks[5], C.banks[6]]
            stb = [C.banks[0], C.banks[1], C.banks[2]]
            acc = [accb[qb][:, 0:129] for qb in range(4)]
            units = [(g, m, kb) for g in range(NQG) for m in range(nmaps) for kb in range(4 * g + 4)]

            def score_part(i):
                g, m, kb = units[i]
                r = max(0, kb - 4 * g)
                qlo = r * 128
                st = stb[i % 3]
                S.pe("matmul", st[:, qlo:NG], lhsT=KT_of(m)[:, kb * 128:(kb + 1) * 128], rhs=QT_of(m)[:, g * NG + qlo:(g + 1) * NG],
                     start=True, stop=True)
                for qb in range(r, 4):
                    delta = 4 * g + qb - kb
                    if delta <= 1:
                        pb = prebias(m, delta)
                        if pb is not None:
                            S.dve("tensor_tensor", out=st[:, qb * 128:(qb + 1) * 128], in0=st[:, qb * 128:(qb + 1) * 128], in1=pb, op=ALU.add)
                pt = next_PT()
                eb = exp_bias(m)
                if eb is None:
                    S.act("activation", out=pt[:, qlo:NG], in_=st[:, qlo:NG], func=AF.Exp, scale=scale)
                else:
                    S.act("activation", out=pt[:, qlo:NG], in_=st[:, qlo:NG], func=AF.Exp, scale=scale, bias=eb)
                return pt

            def pv_part(i, pt):
                g, m, kb = units[i]
                r = max(0, kb - 4 * g)
                for qb in range(r, 4):
                    S.pe("matmul", acc[qb], lhsT=pt[:, qb * 128:(qb + 1) * 128], rhs=V_of(m, kb),
                         start=(kb == 0), stop=(kb == 4 * g + qb))
                if kb == 4 * g + 3:
                    for qb in range(4):
                        epilogue(m, g, qb, acc[qb])

            pend = []
            for i in range(len(units)):
                pend.append((i, score_part(i)))
                if len(pend) > SKEW:
                    pv_part(*pend.pop(0))
            while pend:
                pv_part(*pend.pop(0))

        if "ssd" in phases:
          with S.scope():
            ws = C.load_w(wssd, 0, 512)
            wz = C.load_w(wssd, 512, 264)
            cs_t = S.sb("cssd_sb", [128, 24], F32)
            rs_t = S.sb("rssd_sb", [128, 12], F32)
            S.dma_group("sp", [(cs_t[:], cssd), (rs_t[:], rssd.partition_broadcast(128))], C.ctr)
            load_late_u()
            Aneg = S.sb("Aneg", [128, 4], F32)
            S.act("activation", out=Aneg[:], in_=rs_t[:, 4:8], func=AF.Exp)
            S.dve("tensor_scalar", out=Aneg[:], in0=Aneg[:], scalar1=-1.0, scalar2=None, op0=ALU.mult)
            causT = S.sb("causT", [128, 128], F32)
            S.dve("tensor_scalar", out=causT[:], in0=rel0[:], scalar1=0.0, scalar2=None, op0=ALU.is_ge)
            BT = S.sb("sBT", [128, SEQ], BF16)
            CT = S.sb("sCT", [128, SEQ], BF16)
            hst = S.sb("hst", [128, 256], F32)
            hbf = S.sb("hbf", [128, 256], BF16)
            S.pool("memset", hst[:], 0.0)
            pre = S.sb("spre", [128, 4, 3 + NG], F32)
            S.pool("memset", pre[:, :, 0:3], 0.0)
            xf = S.sb("sxf", [128, 2, NG], F32)
            zs2 = [S.sb(f"szs{i}", [128, 4, 256], F32) for i in range(2)]
            dtt2 = [S.sb(f"sdtt{i}", [128, 4, 4], F32) for i in range(2)]
            dA2 = [S.sb(f"sdA{i}", [128, 4, 4], F32) for i in range(2)]
            xtm2 = [S.sb(f"sxtm{i}", [128, 4, 256], F32) for i in range(2)]
            Btm2 = [S.sb(f"sBtm{i}", [128, 4, 128], BF16) for i in range(2)]
            dec = S.sb("sdec", [128, 4, 128], F32)
            MT = S.sb("sMT", [128, 4, 128], BF16)
            xdt = S.sb("sxdt", [128, 256], BF16)
            xd2b = [S.sb(f"sxd2{i}", [128, 256], BF16) for i in range(2)]
            ysbb = [S.sb(f"sysb{i}", [128, 256], F32) for i in range(2)]
            hbfb = [hbf, S.sb("hbf1", [128, 256], BF16)]
            cssb = [S.sb(f"scs{i}", [128, 16], F32) for i in range(2)]
            scs = [S.sb(f"ssc{i}", [128, 16], F32) for i in range(2)]
            sys_ = [S.sb(f"sys{i}", [128, 4, 256], F32) for i in range(2)]

            def ssd_project(g):
                tsl = slice(g * NG, (g + 1) * NG)
                zs, dtt, dA, xtm, Btm = zs2[g % 2], dtt2[g % 2], dA2[g % 2], xtm2[g % 2], Btm2[g % 2]
                for ci in range(4):
                    ps = C.bank()
                    proj_fm(ws, ci * 128, 128, tsl, ps[:])
                    S.act("copy", out=pre[:, ci, 3:3 + NG], in_=ps[:])
                for ci in range(4):
                    acc = C.tf()
                    S.dve("tensor_scalar", out=acc[:], in0=pre[:, ci, 0:NG], scalar1=cs_t[:, ci * 4:ci * 4 + 1], scalar2=None, op0=ALU.mult)
                    for k in range(1, 4):
                        S.dve("scalar_tensor_tensor", out=acc[:], in0=pre[:, ci, k:k + NG], scalar=cs_t[:, ci * 4 + k:ci * 4 + k + 1], in1=acc[:],
                              op0=ALU.mult, op1=ALU.add)
                    dst = xf[:, ci, :] if ci < 2 else (BT[:, tsl] if ci == 2 else CT[:, tsl])
                    S.act("activation", out=dst, in_=acc[:], func=AF.Silu, bias=cs_t[:, 16 + ci:17 + ci])
                S.dve("tensor_copy", out=pre[:, :, 0:3], in_=pre[:, :, NG:NG + 3])
                for bi in range(4):
                    blk = g * 4 + bi
                    ps = C.bank()
                    proj_tm(wz, 0, 264, blk, ps[:, 0:264])
                    S.act("activation", out=zs[:, bi, :], in_=ps[:, 0:256], func=AF.Silu)
                    S.dve("tensor_tensor", out=dtt[:, bi, :], in0=ps[:, 256:260], in1=rs_t[:, 0:4], op=ALU.add)
                S.act("activation", out=dtt[:], in_=dtt[:], func=AF.Exp)
                S.act("activation", out=dtt[:], in_=dtt[:], func=AF.Ln, bias=1.0)
                for bi in range(4):
                    S.dve("tensor_tensor", out=dA[:, bi, :], in0=dtt[:, bi, :], in1=Aneg[:], op=ALU.mult)
                for bi in range(4):
                    blk = g * 4 + bi
                    ps = C.bank()
                    for ci in range(2):
                        S.pe("transpose", ps[:, ci * 128:(ci + 1) * 128], xf[:, ci, bi * 128:(bi + 1) * 128], ident_f[:])
                    S.act("copy", out=xtm[:, bi, :], in_=ps[:, 0:256])
                    S.pe("transpose", tpb[:, bi * 128:(bi + 1) * 128], BT[:, blk * 128:(blk + 1) * 128], ident_b[:])
                    S.act("copy", out=Btm[:, bi, :], in_=tpb[:, bi * 128:(bi + 1) * 128])

            def ssd_stage_a(n):
                g, bi = n // 4, n % 4
                dtt, dA, xtm = dtt2[g % 2], dA2[g % 2], xtm2[g % 2]
                csl = slice(n * 128, (n + 1) * 128)
                dAc = dA[:, bi, :]
                cb = cssb[n % 2]
                sc = scs[n % 2]
                pc = C.bank()
                S.pe("matmul", pc[:, 0:4], lhsT=causT[:], rhs=dAc, start=True, stop=True)
                S.pe("matmul", pc[:, 8:12], lhsT=C.ones_f[:], rhs=dAc, start=True, stop=True)
                S.act("copy", out=cb[:, 0:12], in_=pc[:, 0:12])
                S.act("activation", out=sc[:, 0:4], in_=cb[:, 0:4], func=AF.Exp)
                S.dve("tensor_tensor", out=sc[:, 4:8], in0=cb[:, 8:12], in1=cb[:, 0:4], op=ALU.subtract)
                S.act("activation", out=sc[:, 4:8], in_=sc[:, 4:8], func=AF.Exp)
                S.act("activation", out=sc[:, 8:12], in_=cb[:, 8:12], func=AF.Exp)
                S.dve("tensor_tensor", out=sc[:, 12:16], in0=sc[:, 4:8], in1=dtt[:, bi, :], op=ALU.mult)
                pz = C.bank()
                dab = C.tf()
                dab4 = dab[:].rearrange("p (k l) -> p k l", k=4)
                S.dve("tensor_copy", out=dab4, in_=dAc.unsqueeze(2).to_broadcast([128, 4, 128]))
                for k in range(4):
                    S.pe("matmul", pz[:, k * 128:(k + 1) * 128], lhsT=dab[:, k * 128:(k + 1) * 128], rhs=causT[:], start=True, stop=True)
                for k in range(4):
                    S.dve("tensor_scalar", out=dec[:, k, :], in0=pz[:, k * 128:(k + 1) * 128], scalar1=cb[:, k:k + 1], scalar2=0.0,
                          op0=ALU.subtract, op1=ALU.min)
                S.act("activation", out=dec[:], in_=dec[:], func=AF.Exp)
                pcb = C.bank()
                S.pe("matmul", pcb[:, 0:128], lhsT=BT[:, csl], rhs=CT[:, csl], start=True, stop=True)
                cbm = C.tf()
                S.dve("tensor_tensor", out=cbm[:, 0:128], in0=pcb[:, 0:128], in1=causT[:], op=ALU.mult)
                S.dve("tensor_tensor", out=MT[:], in0=dec[:], in1=cbm[:, 0:128].unsqueeze(1).to_broadcast([128, 4, 128]), op=ALU.mult)
                xd2 = xd2b[n % 2]
                x3 = xtm[:, bi, :].rearrange("p (k e) -> p k e", k=4)
                S.dve("tensor_tensor", out=xdt[:].rearrange("p (k e) -> p k e", k=4), in0=x3,
                      in1=dtt[:, bi, :].unsqueeze(2).to_broadcast([128, 4, 64]), op=ALU.mult)
                S.dve("tensor_tensor", out=xd2[:].rearrange("p (k e) -> p k e", k=4), in0=x3,
                      in1=sc[:, 12:16].unsqueeze(2).to_broadcast([128, 4, 64]), op=ALU.mult)
                py = C.bank()
                for k in range(4):
                    ks = slice(k * 64, (k + 1) * 64)
                    S.pe("matmul", py[:, ks], lhsT=MT[:, k, :], rhs=xdt[:, ks], start=True, stop=True)
                S.act("copy", out=ysbb[n % 2][:], in_=py[:, 0:256])

            def ssd_stage_b(n):
                g, bi = n // 4, n % 4
                zs, xtm, Btm = zs2[g % 2], xtm2[g % 2], Btm2[g % 2]
                csl = slice(n * 128, (n + 1) * 128)
                sc = scs[n % 2]
                ysb = ysbb[n % 2]
                hb = hbfb[n % 2]
                if n > 0:
                    po = C.bank()
                    S.pe("matmul", po[:, 0:256], lhsT=CT[:, csl], rhs=hb[:], start=True, stop=True)
                if n < NBLK - 1:
                    pn = C.bank()
                    S.pe("matmul", pn[:, 0:256], lhsT=Btm[:, bi, :], rhs=xd2b[n % 2][:], start=True, stop=True)
                    h3 = hst[:].rearrange("p (k e) -> p k e", k=4)
                    S.dve("tensor_tensor", out=h3, in0=h3, in1=sc[:, 8:12].unsqueeze(2).to_broadcast([128, 4, 64]), op=ALU.mult)
                    S.dve("tensor_tensor", out=hst[:], in0=hst[:], in1=pn[:, 0:256], op=ALU.add)
                    S.act("copy", out=hbfb[(n + 1) % 2][:], in_=hst[:])
                if n > 0:
                    yo = C.tf()
                    S.dve("tensor_tensor", out=yo[:, 0:256].rearrange("p (k e) -> p k e", k=4), in0=po[:, 0:256].rearrange("p (k e) -> p k e", k=4),
                          in1=sc[:, 0:4].unsqueeze(2).to_broadcast([128, 4, 64]), op=ALU.mult)
                    S.dve("tensor_tensor", out=ysb[:], in0=ysb[:], in1=yo[:, 0:256], op=ALU.add)
                xd = C.tf()
                S.dve("tensor_tensor", out=xd[:, 0:256].rearrange("p (k e) -> p k e", k=4), in0=xtm[:, bi, :].rearrange("p (k e) -> p k e", k=4),
                      in1=rs_t[:, 8:12].unsqueeze(2).to_broadcast([128, 4, 64]), op=ALU.mult)
                S.dve("tensor_tensor", out=ysb[:], in0=ysb[:], in1=xd[:, 0:256], op=ALU.add)
                S.dve("tensor_tensor", out=sys_[g % 2][:, bi, :], in0=ysb[:], in1=zs[:, bi, :], op=ALU.mult)
                if bi == 3:
                    emit_y(1, y_ssd, g, sys_[g % 2])

            for g in range(NQG):
                ssd_project(g)
                for bi in range(4):
                    n = g * 4 + bi
                    ssd_stage_a(n)
                    if n >= 1:
                        ssd_stage_b(n - 1)
            ssd_stage_b(NBLK - 1)

        if "ret" in phases:
          with S.scope():
            wr = C.load_w(wret, 0, 512)
            wrv = C.load_w(wret, 512, 512)
            load_late_u()
            rQT = S.sb("rQT", [128, SEQ], BF16)
            rKT = S.sb("rKT", [128, SEQ], BF16)
            rKd = S.sb("rKd", [128, NBLK, 128], BF16)
            rV = S.sb("rV", [128, NBLK, 256], BF16)
            rG = S.sb("rG", [128, NBLK, 256], F32)
            decT = [S.sb(f"decT{hh}", [128, 128], F32) for hh in range(2)]
            qdec = S.sb("qdec", [128, 128], F32)
            kdecT = S.sb("kdecT", [128, 128], F32)
            c128 = S.sb("c128", [128, 2], F32)
            with S.scope():
                relp = S.sb("relp", [128, 128], F32)
                caus = S.sb("caus", [128, 128], F32)
                S.dve("tensor_scalar", out=relp[:], in0=rel0[:], scalar1=0.0, scalar2=None, op0=ALU.max)
                S.dve("tensor_scalar", out=caus[:], in0=rel0[:], scalar1=0.0, scalar2=None, op0=ALU.is_ge)
                for hh in range(2):
                    S.dve("tensor_scalar", out=decT[hh][:], in0=relp[:], scalar1=cst[:, 5 + hh:6 + hh], scalar2=None, op0=ALU.mult)
                    S.act("activation", out=decT[hh][:], in_=decT[hh][:], func=AF.Exp)
                    S.dve("tensor_tensor", out=decT[hh][:], in0=decT[hh][:], in1=caus[:], op=ALU.mult)
                io_i = S.sb("io_i", [128, 128], I32)
                io_f = S.sb("io_f", [128, 128], F32)
                S.pool("iota", io_i[:], pattern=[[1, 128]], base=1, channel_multiplier=0)
                S.pool("tensor_copy", out=io_f[:], in_=io_i[:])
                S.dve("tensor_scalar", out=io_f[:], in0=io_f[:], scalar1=cst[:, 4:5], scalar2=None, op0=ALU.mult)
                S.act("activation", out=qdec[:], in_=io_f[:], func=AF.Exp)
                for hh in range(2):
                    S.dve("tensor_scalar", out=kdecT[:, hh:hh + 1], in0=rel0[:, 127:128], scalar1=cst[:, 5 + hh:6 + hh], scalar2=None, op0=ALU.mult)
                S.act("activation", out=kdecT[:, 0:2], in_=kdecT[:, 0:2], func=AF.Exp)
                S.pool("memset", c128[:, 0:1], 128.0)
                S.dve("tensor_scalar", out=c128[:, 0:1], in0=c128[:, 0:1], scalar1=cst[:, 4:5], scalar2=None, op0=ALU.mult)
                S.act("activation", out=c128[:, 1:2], in_=c128[:, 0:1], func=AF.Exp)
            with S.scope():
                T = alloc_rope()
                for g in range(NQG):
                    tsl = slice(g * NG, (g + 1) * NG)
                    rope_tables(T, g, 0, 1)
                    for which in range(2):
                        pa = C.bank()
                        proj_fm(wr, which * 256, 128, tsl, pa[:])
                        pb = C.bank()
                        proj_fm(wr, which * 256 + 128, 128, tsl, pb[:])
                        t1 = C.tf()
                        S.dve("tensor_tensor", out=t1[:], in0=pa[:], in1=T["tcos"][:], op=ALU.mult)
                        t2 = C.tf()
                        S.dve("tensor_tensor", out=t2[:], in0=pb[:], in1=T["tsin"][:], op=ALU.mult)
                        S.dve("tensor_tensor", out=t1[:], in0=t1[:], in1=t2[:], op=ALU.add)
                        if which == 0:
                            S.act("copy", out=rQT[:, tsl], in_=t1[:])
                        else:
                            S.act("mul", out=rKT[:, tsl], in_=t1[:], mul=0.125)
                            for bi in range(4):
                                blk = g * 4 + bi
                                S.pe("transpose", tpb[:, bi * 128:(bi + 1) * 128], rKT[:, blk * 128:(blk + 1) * 128], ident_b[:])
                                for hh in range(2):
                                    S.dve("tensor_scalar", out=rKd[:, blk, hh * 64:(hh + 1) * 64], in0=tpb[:, bi * 128 + hh * 64:bi * 128 + (hh + 1) * 64],
                                          scalar1=kdecT[:, hh:hh + 1], scalar2=None, op0=ALU.mult)
                    for bi in range(4):
                        blk = g * 4 + bi
                        ps = C.bank()
                        proj_tm(wrv, 0, 512, blk, ps[:])
                        S.act("copy", out=rV[:, blk, :], in_=ps[:, 0:256])
                        S.act("activation", out=rG[:, blk, :], in_=ps[:, 256:512], func=AF.Silu)
            Sst = S.sb("Sst", [128, 128], F32)
            Sbf = [S.sb(f"Sbf{i}", [128, 128], BF16) for i in range(2)]
            S.pool("memset", Sst[:], 0.0)
            rys0 = S.sb("rys0", [128, 4, 256], F32)
            rys = [rys0, rys0]
            rsc = S.sb("rsc", [128, 64], F32)
            amb = [[S.sb(f"ram{i}{hh}", [128, 128], BF16) for hh in range(2)] for i in range(2)]
            qdb = [S.sb(f"rqd{i}", [128, 128], BF16) for i in range(2)]
            junk = S.sb("rjunk", [128, 2, 128], F32)

            def ret_stage_a(n):
                csl = slice(n * 128, (n + 1) * 128)
                if n > 0:
                    S.dve("tensor_tensor", out=qdb[n % 2][:], in0=rQT[:, csl], in1=qdec[:], op=ALU.mult)
                for hh in range(2):
                    hs = slice(hh * 64, hh * 64 + 64)
                    pa = C.bank()
                    S.pe("matmul", pa[:, 0:128], lhsT=rKT[hs, csl], rhs=rQT[hs, csl], start=True, stop=True)
                    S.dve("tensor_tensor", out=amb[n % 2][hh][:], in0=pa[:, 0:128], in1=decT[hh][:], op=ALU.mult)

            def ret_stage_b(n):
                g = n // 4
                if n < NBLK - 1:
                    pk = C.bank()
                    for hh in range(2):
                        S.pe("matmul", pk[hh * 64:(hh + 1) * 64, 0:128], lhsT=rKd[:, n, hh * 64:(hh + 1) * 64], rhs=rV[:, n, hh * 128:(hh + 1) * 128],
                             start=True, stop=True)
                    S.dve("scalar_tensor_tensor", out=Sst[:], in0=Sst[:], scalar=c128[:, 1:2], in1=pk[:, 0:128], op0=ALU.mult, op1=ALU.add)
                    S.act("copy", out=Sbf[(n + 1) % 2][:], in_=Sst[:])
                for hh in range(2):
                    hs = slice(hh * 64, hh * 64 + 64)
                    po = C.bank()
                    S.pe("matmul", po[:, 0:128], lhsT=amb[n % 2][hh][:], rhs=rV[:, n, hh * 128:(hh + 1) * 128], start=True, stop=(n == 0))
                    if n > 0:
                        S.pe("matmul", po[:, 0:128], lhsT=qdb[n % 2][hs, :], rhs=Sbf[n % 2][hs, :], start=False, stop=True)
                    c0 = ((2 * n + hh) % 8) * 8
                    S.pool("memset", rsc[:, c0:c0 + 2], 0.0)
                    S.act("activation", out=junk[:, 0, :], in_=po[:, 0:128], func=AF.Identity, accum_out=rsc[:, c0:c0 + 1])
                    S.act("activation", out=junk[:, 1, :], in_=po[:, 0:128], func=AF.Square, accum_out=rsc[:, c0 + 1:c0 + 2])
                    S.dve("tensor_scalar", out=rsc[:, c0 + 2:c0 + 3], in0=rsc[:, c0:c0 + 1], scalar1=1.0 / 128.0, scalar2=None, op0=ALU.mult)
                    S.dve("tensor_tensor", out=rsc[:, c0 + 3:c0 + 4], in0=rsc[:, c0 + 2:c0 + 3], in1=rsc[:, c0 + 2:c0 + 3], op=ALU.mult)
                    S.dve("tensor_scalar", out=rsc[:, c0 + 4:c0 + 5], in0=rsc[:, c0 + 1:c0 + 2], scalar1=1.0 / 128.0, scalar2=rsc[:, c0 + 3:c0 + 4],
                          op0=ALU.mult, op1=ALU.subtract)
                    S.act("activation", out=rsc[:, c0 + 5:c0 + 6], in_=rsc[:, c0 + 4:c0 + 5], func=AF.Sqrt, bias=EPS)
                    S.dve("reciprocal", out=rsc[:, c0 + 6:c0 + 7], in_=rsc[:, c0 + 5:c0 + 6])
                    x = C.tf()[:, 0:128]
                    S.dve("tensor_scalar", out=x, in0=po[:, 0:128], scalar1=rsc[:, c0 + 2:c0 + 3], scalar2=rsc[:, c0 + 6:c0 + 7],
                          op0=ALU.subtract, op1=ALU.mult)
                    S.dve("tensor_tensor", out=rys[g % 2][:, n % 4, hh * 128:(hh + 1) * 128], in0=x, in1=rG[:, n, hh * 128:(hh + 1) * 128], op=ALU.mult)
                if n % 4 == 3:
                    emit_y(0, y_ret, g, rys[g % 2])

            for n in range(NBLK + 1):
                if n < NBLK:
                    ret_stage_a(n)
                if n >= 1:
                    ret_stage_b(n - 1)

        if "mla" in phases:
          with S.scope():
            T = alloc_rope()
            tcos, tsin = T["tcos"], T["tsin"]
            gm = S.sb("gm", [128, 4], F32)
            S.dma_group("sp", [(gm[:], gmla)], C.ctr)
            load_late_u()
            wm = C.load_w(wmla, 0, 448)
            wq_t = C.load_w(wuq, 0, 256)
            wkv_t = C.load_w(wukv, 0, 384)
            KT = [S.sb(f"mKT{h}", [96, SEQ], BF16) for h in range(2)]
            QT = [S.sb(f"mQT{h}", [96, SEQ], BF16) for h in range(2)]
            Vm = S.sb("mV", [128, NBLK, 2, 129], BF16)
            S.pool("memset", Vm[:, :, :, 128:129], 1.0)
            cq_f = S.sb("cq_f", [128, 2, NG], F32)
            cqn = S.sb("cqn", [128, 2, NG], BF16)
            ckv_f = S.sb("ckv_f", [128, NG], F32)
            ckvn = S.sb("ckvn", [128, NG], BF16)
            for g in range(NQG):
                tsl = slice(g * NG, (g + 1) * NG)
                rope_tables(T, g, 2, 3)
                for c in range(2):
                    ps = C.bank()
                    proj_fm(wm, c * 128, 128, tsl, ps[:])
                    S.act("copy", out=cq_f[:, c, :], in_=ps[:])
                rmsnorm_fm(S, C, lambda dc: cq_f[:, dc, :], 2, gm[:, 0:2], lambda dc: cqn[:, dc, :], NG, 256)
                ps = C.bank()
                proj_fm(wm, 256, 128, tsl, ps[:])
                S.act("copy", out=ckv_f[:], in_=ps[:])
                rmsnorm_fm(S, C, lambda dc: ckv_f[:], 1, gm[:, 2:3], lambda dc: ckvn[:], NG, 128)
                pa = C.bank()
                proj_fm(wm, 384, 32, tsl, pa[64:96, :])
                pb = C.bank()
                proj_fm(wm, 416, 32, tsl, pb[64:96, :])
                t1 = C.tf()
                S.dve("tensor_tensor", out=t1[64:96, :], in0=pa[64:96, :], in1=tcos[64:96, :], op=ALU.mult)
                t2 = C.tf()
                S.dve("tensor_tensor", out=t2[64:96, :], in0=pb[64:96, :], in1=tsin[64:96, :], op=ALU.mult)
                for h in range(2):
                    S.dve("tensor_tensor", out=KT[h][64:96, tsl], in0=t1[64:96, :], in1=t2[64:96, :], op=ALU.add)
                for h in range(2):
                    ps = C.bank()
                    S.pe("matmul", ps[0:64, :], lhsT=wkv_t[:, 0, h * 192:h * 192 + 64], rhs=ckvn[:], start=True, stop=True)
                    S.act("copy", out=KT[h][0:64, tsl], in_=ps[0:64, :])
                    for bi in range(4):
                        blk = g * 4 + bi
                        ps = C.bank()
                        S.pe("matmul", ps[:, 0:128], lhsT=ckvn[:, bi * 128:(bi + 1) * 128], rhs=wkv_t[:, 0, h * 192 + 64:h * 192 + 192], start=True, stop=True)
                        S.act("copy", out=Vm[:, blk, h, 0:128], in_=ps[:, 0:128])
                    ps = C.bank()
                    mm_chain(S, ps[0:96, :], [(wq_t[:, k, h * 96:(h + 1) * 96], cqn[:, k, :]) for k in range(2)])
                    ps2 = C.bank()
                    mm_chain(S, ps2[64:96, :], [(wq_t[:, k, 192 + h * 32:192 + (h + 1) * 32], cqn[:, k, :]) for k in range(2)])
                    S.act("copy", out=QT[h][0:64, tsl], in_=ps[0:64, :])
                    t1 = C.tf()
                    S.dve("tensor_tensor", out=t1[64:96, :], in0=ps[64:96, :], in1=tcos[64:96, :], op=ALU.mult)
                    t2 = C.tf()
                    S.dve("tensor_tensor", out=t2[64:96, :], in0=ps2[64:96, :], in1=tsin[64:96, :], op=ALU.mult)
                    S.dve("tensor_tensor", out=QT[h][64:96, tsl], in0=t1[64:96, :], in1=t2[64:96, :], op=ALU.add)
            ystage = [S.sb(f"mys{i}", [128, 4, 256], F32) for i in range(2)]
            rcp = S.sb("m_rcp", [128, 8], F32)

            def mla_epi(m, g, qb, acc):
                ys = ystage[g % 2]
                col = (m * 4 + qb)
                S.dve("reciprocal", out=rcp[:, col:col + 1], in_=acc[:, 128:129])
                S.dve("tensor_scalar", out=ys[:, qb, m * 128:(m + 1) * 128], in0=acc[:, 0:128], scalar1=rcp[:, col:col + 1], scalar2=None, op0=ALU.mult)
                if m == 1 and qb == 3:
                    emit_y(2, y_mla, g, ys)

            attention(2, lambda m: KT[m][:, :], lambda m: QT[m][:, :], lambda m, kb: Vm[:, kb, m, :], 96 ** -0.5,
                      lambda m, d: (maskT[:] if d == 0 else None), lambda m: None, mla_epi)

        if "diff" in phases:
          with S.scope():
            wd = C.load_w(wdiff, 0, 512)
            wdv = C.load_w(wdiff, 512, 256)
            load_late_u()
            dQp = [S.sb(f"dQp{m}", [128, SEQ], BF16) for m in range(4)]
            for m in range(4):
                S.pool("memset", dQp[m][:], 0.0)
            dKT = S.sb("dKT", [128, 2, SEQ], BF16)
            dV = S.sb("dV", [128, NBLK, 2, 129], BF16)
            S.pool("memset", dV[:, :, :, 128:129], 1.0)
            for g in range(NQG):
                tsl = slice(g * NG, (g + 1) * NG)
                for c in range(2):
                    ps = C.bank()
                    proj_fm(wd, c * 128, 128, tsl, ps[:])
                    for w in range(2):
                        S.act("copy", out=dQp[2 * c + w][w * 64:(w + 1) * 64, tsl], in_=ps[w * 64:(w + 1) * 64, :])
                for c in range(2):
                    ps = C.bank()
                    proj_fm(wd, 256 + c * 128, 128, tsl, ps[:])
                    S.act("copy", out=dKT[:, c, tsl], in_=ps[:])
                for bi in range(4):
                    blk = g * 4 + bi
                    ps = C.bank()
                    proj_tm(wdv, 0, 256, blk, ps[:, 0:256])
                    S.act("copy", out=dV[:, blk, :, 0:128], in_=ps[:, 0:256].rearrange("p (h e) -> p h e", h=2))
            tbl = S.sb("tbl", [128, 64], F32)
            lamt = S.sb("lamt", [128, 256], F32)
            dngs = S.sb("dngs", [128, 128], F32)
            S.dma_group("sp", [(tbl[:], dtbl.partition_broadcast(128)), (lamt[:], dlam.partition_broadcast(128)),
                               (dngs[:], dng.partition_broadcast(128))], C.ctr)
            dT = S.sb("dT", [128, 64], F32)
            S.dve("tensor_tensor", out=dT[:, 1:64], in0=tbl[:, 1:64], in1=tbl[:, 0:63], op=ALU.subtract)
            rel1 = S.sb("rel1", [128, 128], F32)
            S.dve("tensor_scalar", out=rel1[:], in0=rel0[:], scalar1=128.0, scalar2=None, op0=ALU.add)
            thr = t5_thresholds()
            Bt = [[S.sb(f"Bt{hh}{dl}", [128, 128], F32) for dl in range(2)] for hh in range(2)]
            for hh in range(2):
                for dl in range(2):
                    relt = rel0 if dl == 0 else rel1
                    bt = Bt[hh][dl]
                    S.dve("tensor_scalar", out=bt[:], in0=relt[:], scalar1=0.0, scalar2=tbl[:, hh * 32:hh * 32 + 1], op0=ALU.mult, op1=ALU.add)
                    for m in range(1, 32):
                        tmp = C.tf()
                        S.dve("tensor_scalar", out=tmp[:, 0:128], in0=relt[:], scalar1=float(thr[m - 1]), scalar2=dT[:, hh * 32 + m:hh * 32 + m + 1],
                              op0=ALU.is_ge, op1=ALU.mult)
                        S.dve("tensor_tensor", out=bt[:], in0=bt[:], in1=tmp[:, 0:128], op=ALU.add)
                    S.dve("tensor_scalar", out=bt[:], in0=bt[:], scalar1=tbl[:, hh * 32 + 31:hh * 32 + 32], scalar2=8.0, op0=ALU.subtract, op1=ALU.mult)
                    if dl == 0:
                        S.dve("tensor_tensor", out=bt[:], in0=bt[:], in1=maskT[:], op=ALU.add)
            lsc = S.sb("lsc", [128, 8], F32)
            for i in range(2):
                tmp = C.tf()
                S.dve("tensor_tensor", out=tmp[:, 0:64], in0=lamt[:, i * 128:i * 128 + 64], in1=lamt[:, i * 128 + 64:i * 128 + 128], op=ALU.mult)
                S.dve("reduce_sum", out=lsc[:, i:i + 1], in_=tmp[:, 0:64], axis=AX.X)
                S.act("activation", out=lsc[:, 2 + i:3 + i], in_=lsc[:, i:i + 1], func=AF.Exp)
            S.dve("tensor_scalar", out=lsc[:, 4:5], in0=lsc[:, 3:4], scalar1=lsc[:, 2:3], scalar2=-float(lambda_init), op0=ALU.subtract, op1=ALU.add)
            S.dve("tensor_scalar", out=dngs[:], in0=dngs[:], scalar1=1.0 - float(lambda_init), scalar2=None, op0=ALU.mult)
            o1buf = S.sb("o1buf", [128, 2, 4, 128], F32)
            dys = [S.sb(f"dys{i}", [128, 4, 256], F32) for i in range(2)]
            dsc = S.sb("dsc", [128, 64], F32)
            dci = [0]

            def diff_epi(m, g, qb, acc):
                hh, which = m // 2, m % 2
                c0 = (dci[0] % 8) * 8
                dci[0] += 1
                S.dve("reciprocal", out=dsc[:, c0:c0 + 1], in_=acc[:, 128:129])
                if which == 0:
                    S.dve("tensor_scalar", out=o1buf[:, hh, qb, :], in0=acc[:, 0:128], scalar1=dsc[:, c0:c0 + 1], scalar2=None, op0=ALU.mult)
                    return
                ys = dys[g % 2]
                o2 = C.tf()
                S.dve("tensor_scalar", out=o2[:, 0:128], in0=acc[:, 0:128], scalar1=dsc[:, c0:c0 + 1], scalar2=None, op0=ALU.mult)
                S.dve("scalar_tensor_tensor", out=o2[:, 128:256], in0=o2[:, 0:128], scalar=lsc[:, 4:5], in1=o1buf[:, hh, qb, :], op0=ALU.mult, op1=ALU.add)
                S.dve("tensor_tensor", out=o2[:, 256:384], in0=o2[:, 128:256], in1=o2[:, 128:256], op=ALU.mult)
                S.dve("reduce_sum", out=dsc[:, c0 + 1:c0 + 2], in_=o2[:, 256:384], axis=AX.X)
                S.act("activation", out=dsc[:, c0 + 2:c0 + 3], in_=dsc[:, c0 + 1:c0 + 2], func=AF.Ln, scale=1.0 / 128.0, bias=EPS)
                S.act("activation", out=dsc[:, c0 + 3:c0 + 4], in_=dsc[:, c0 + 2:c0 + 3], func=AF.Exp, scale=-0.5)
                S.dve("scalar_tensor_tensor", out=ys[:, qb, hh * 128:(hh + 1) * 128], in0=o2[:, 128:256], scalar=dsc[:, c0 + 3:c0 + 4], in1=dngs[:],
                      op0=ALU.mult, op1=ALU.mult)
                if m == 3 and qb == 3:
                    emit_y(3, y_diff, g, ys)

            attention(4, lambda m: dKT[:, m // 2, :], lambda m: dQp[m][:, :],
                      lambda m, kb: dV[:, kb, m // 2, :], 0.125,
                      lambda m, d: Bt[m // 2][d][:], lambda m: tbl[:, (m // 2) * 32 + 31:(m // 2) * 32 + 32], diff_epi)

        if not fused:
            S.finish()
        print("B built: ins", S.n_ins, "waits", S.n_wait, {k: e.tr.val for k, e in S.engs.items()})
    return nc

def swap_halves(w, nheads, dh):
    K = w.shape[0]
    w4 = w.reshape(K, nheads, 2, dh // 2)
    return np.ascontiguousarray(w4[:, :, ::-1, :]).reshape(K, nheads * dh)


def b_consts(j):
    c = np.zeros((128, 16), np.float32)
    p = np.arange(128)
    c[:, 0] = np.exp(-math.log(10000.0) * ((p % 64) % 32).astype(np.float32) / 32).astype(np.float32)
    c[:, 1] = np.where((p % 64) < 32, -1.0, 1.0)
    c[:, 2] = np.exp(-math.log(10000.0) * (((p - 64) % 32) % 16).astype(np.float32) / 16).astype(np.float32)
    c[:, 3] = np.where(((p - 64) % 32) < 16, -1.0, 1.0)
    lg = np.log1p(-np.exp2(-5.0 - np.arange(4, dtype=np.float32))).astype(np.float32)
    c[:, 4] = lg[2 * j + p // 64]
    c[:, 5] = lg[2 * j]
    c[:, 6] = lg[2 * j + 1]
    return c


def b_inputs(inp, l, b, j, uT_bf16, phases=("ret", "ssd", "mla", "diff")):
    W = inp["w_in"][l]
    o = IN_OFF
    d = {"pos": np.ascontiguousarray(inp["positions"][b:b + 1, :]).astype(np.int32), "cst": b_consts(j)}
    if uT_bf16 is not None:
        d["uT"] = uT_bf16
    if "mla" in phases:
        kr = W[:, o[9]:o[10]]
        d["wmla"] = np.ascontiguousarray(np.concatenate([W[:, o[7]:o[8]], W[:, o[8]:o[9]], kr, swap_halves(kr, 1, 32)], axis=1))
        uq = inp["mla_w_uq"][l].reshape(256, 4, 96)[:, 2 * j:2 * j + 2, :]
        uq_rope_sw = swap_halves(np.ascontiguousarray(uq[:, :, 64:96]).reshape(256, 64), 2, 32)
        d["wuq"] = np.ascontiguousarray(np.concatenate([uq.reshape(256, 192), uq_rope_sw], axis=1))
        d["wukv"] = np.ascontiguousarray(inp["mla_w_ukv"][l].reshape(128, 4, 192)[:, 2 * j:2 * j + 2, :].reshape(128, 384))
        g = np.zeros((128, 4), np.float32)
        g[:, 0:2] = inp["mla_q_norm"][l].reshape(2, 128).T
        g[:, 2] = inp["mla_kv_norm"][l]
        d["gmla"] = g
    if "diff" in phases:
        dq = W[:, o[10]:o[11]][:, j * 256:(j + 1) * 256]
        dk = W[:, o[11]:o[12]][:, j * 256:(j + 1) * 256]
        dv = W[:, o[12]:o[13]][:, j * 256:(j + 1) * 256]
        d["wdiff"] = np.ascontiguousarray(np.concatenate([dq, dk, dv], axis=1))
        d["dlam"] = np.ascontiguousarray(inp["diff_lambda"][l].reshape(1, 256))
        d["dng"] = np.ascontiguousarray(inp["diff_norm"][l].reshape(1, 128))
        d["dtbl"] = np.ascontiguousarray(inp["rel_bias"][:, 2 * j:2 * j + 2].T.reshape(1, 64))
    if "ret" in phases:
        rq = W[:, o[0]:o[1]][:, j * 128:(j + 1) * 128]
        rk = W[:, o[1]:o[2]][:, j * 128:(j + 1) * 128]
        rv = W[:, o[2]:o[3]][:, j * 256:(j + 1) * 256]
        rg = W[:, o[3]:o[4]][:, j * 256:(j + 1) * 256]
        d["wret"] = np.ascontiguousarray(np.concatenate([rq, swap_halves(rq, 2, 64), rk, swap_halves(rk, 2, 64), rv, rg], axis=1))
    if "ssd" in phases:
        sz = W[:, o[4]:o[5]][:, j * 256:(j + 1) * 256]
        xbc = W[:, o[5]:o[6]]
        sx = xbc[:, j * 256:(j + 1) * 256]
        sB = xbc[:, 512 + j * 128:512 + (j + 1) * 128]
        sC = xbc[:, 768 + j * 128:768 + (j + 1) * 128]
        sdt = W[:, o[6]:o[7]][:, j * 4:(j + 1) * 4]
        d["wssd"] = np.ascontiguousarray(np.concatenate([sx, sB, sC, sz, sdt, sdt], axis=1))
        cw = inp["ssm_conv_w"][l]
        cb = inp["ssm_conv_b"][l]
        chans = [slice(j * 256, j * 256 + 128), slice(j * 256 + 128, j * 256 + 256), slice(512 + j * 128, 512 + (j + 1) * 128), slice(768 + j * 128, 768 + (j + 1) * 128)]
        cs = np.zeros((128, 24), np.float32)
        for ci, sl in enumerate(chans):
            cs[:, ci * 4:(ci + 1) * 4] = cw[:, sl].T
            cs[:, 16 + ci] = cb[sl]
        d["cssd"] = cs
        d["rssd"] = np.ascontiguousarray(np.concatenate([inp["ssm_dt_bias"][l][4 * j:4 * j + 4], inp["ssm_a_log"][l][4 * j:4 * j + 4], inp["ssm_d"][l][4 * j:4 * j + 4]]).reshape(1, 12).astype(np.float32))
    return d

from concourse.bass_utils import run_bass_kernel_spmd

DEPTH = 2
_DBG = {}


def pack_gains(d):
    g = np.zeros((128, 64), np.float32)
    for k, off in (("f1", 0), ("mix", 8), ("mixp", 16), ("xa", 24), ("mem", 32), ("f2", 40), ("fin", 48), ("ssm", 56)):
        if k in d:
            v = np.asarray(d[k], np.float32)
            n = v.shape[0] // 128
            g[:, off:off + n] = v.reshape(n, 128).T
    return g


def _c(a):
    return np.ascontiguousarray(a)


def build_fused():
    nc = bass.Bass("TRN2", target_bir_lowering=False)
    es = ExitStack()
    S = Sched(nc, es)
    C = Ctx(S, nbanks=7, nwslots=0)
    tpb = S.ps("tpb", [128, 1024], BF16)
    a_tr = [S.dma_tracker(f"a{i}") for i in range(6)]
    b_tr = [S.dma_tracker(f"b{i}") for i in range(4)]
    cc_tr = [S.dma_tracker(f"cc{i}") for i in range(12)]
    u_tr = [S.dma_tracker(f"ul{i}") for i in range(4)]

    def internal(name, shape, dt):
        return nc.dram_tensor(name, list(shape), dt, kind="Internal").ap()

    h_scr = [internal(f"h_scr{l}", [D, NTA], F32) for l in range(DEPTH)]
    u_src = [[internal(f"u_src{l}_{t}", [D, TB], BF16) for t in range(NTA // TB)] for l in range(DEPTH)]
    u_all = [[internal(f"u_all{l}_{t}", [2 * D, TB], BF16) for t in range(NTA // TB)] for l in range(DEPTH)]
    y_src = [[internal(f"y_src{l}_{i}", [256, SEQ], BF16) for i in range(4)] for l in range(DEPTH)]
    y_all = [[internal(f"y_all{l}_{i}", [512, SEQ], BF16) for i in range(4)] for l in range(DEPTH)]
    base = dict(nc=nc, S=S, C=C, tpb=tpb, a_tr=a_tr, b_tr=b_tr, cc_tr=cc_tr, u_tr=u_tr)
    build_A(False, True, False, fused=dict(base, pfx="A0_", h_src=None, h_dst=h_scr[0], u_src=u_src[0], u_all=u_all[0]))
    for l in range(DEPTH):
        lam0 = 0.8 - 0.6 * math.exp(-0.3 * l)
        build_B(lambda_init=lam0, fused=dict(base, pfx=f"B{l}_", u_all=u_all[l], y_src=y_src[l], y_all=y_all[l]))
        last = (l == DEPTH - 1)
        build_A(True, not last, last, fused=dict(base, pfx=f"A{l + 1}_", h_src=h_scr[l], h_dst=(None if last else h_scr[l + 1]),
                                                 y_all=y_all[l], u_src=(None if last else u_src[l + 1]), u_all=(None if last else u_all[l + 1])))
    S.finish()
    print("FUSED built: ins", S.n_ins, "waits", S.n_wait, {k: e.tr.val for k, e in S.engs.items()})
    es.close()
    return nc


def kernel(**inputs):
    inp = {k: np.asarray(v) for k, v in inputs.items()}
    cores = list(range(8))
    x = inp["x"]
    nc = build_fused()
    in_maps = []
    for c in cores:
        b, j = c // 2, c % 2
        d = {}
        d["A0_hT"] = _c(x[b, j * NTA:(j + 1) * NTA, :].T)
        d["A0_gains"] = pack_gains({"f1": inp["ffn1_norm"][0], "mix": inp["mix_norm"][0]})
        d["A0_f1g"] = inp["ffn1_w_gate"][0]; d["A0_f1u"] = inp["ffn1_w_up"][0]; d["A0_f1d"] = inp["ffn1_w_down"][0]
        for l in range(DEPTH):
            bi = b_inputs(inp, l, b, j, None)
            for k, v in bi.items():
                if k != "uT":
                    d[f"B{l}_{k}"] = v
            last = (l == DEPTH - 1)
            p = f"A{l + 1}_"
            g = {"mixp": inp["mix_norm"][l], "ssm": inp["ssm_norm"][l], "xa": inp["xa_norm"][l], "mem": inp["mem_norm"][l], "f2": inp["ffn2_norm"][l]}
            d[p + "memT"] = _c(inp["mem"][b].T)
            d[p + "wgl"] = _c(inp["w_in"][l][:, IN_OFF[13]:]); d[p + "wbr"] = _c(inp["w_branch"][l].reshape(2048, D)); d[p + "wout"] = inp["w_out"][l]
            d[p + "wq"] = inp["xa_wq"][l]; d[p + "wk"] = inp["xa_wk"][l]; d[p + "wv"] = inp["xa_wv"][l]; d[p + "wo"] = inp["xa_wo"][l]
            d[p + "f2g"] = inp["ffn2_w_gate"][l]; d[p + "f2u"] = inp["ffn2_w_up"][l]; d[p + "f2d"] = inp["ffn2_w_down"][l]
            if not last:
                g["f1"] = inp["ffn1_norm"][l + 1]
                g["mix"] = inp["mix_norm"][l + 1]
                d[p + "f1g"] = inp["ffn1_w_gate"][l + 1]; d[p + "f1u"] = inp["ffn1_w_up"][l + 1]; d[p + "f1d"] = inp["ffn1_w_down"][l + 1]
            else:
                g["fin"] = inp["final_norm"]
            gp = pack_gains(g)
            gp[:, 60] = 1.0 - j
            gp[:, 61] = float(j)
            d[p + "gains"] = gp
        in_maps.append(d)
    res = run_bass_kernel_spmd(nc, in_maps, core_ids=cores).results
    out = np.empty((NB, SEQ, D), np.float32)
    for c in cores:
        b, j = c // 2, c % 2
        out[b, j * NTA:(j + 1) * NTA, :] = np.asarray(res[c][f"A{DEPTH}_outT"]).T
    return out
```

```python
import math
import numpy as np
import ml_dtypes
from contextlib import ExitStack, contextmanager
import concourse.bass as bass
import concourse.mybir as mybir

F32 = mybir.dt.float32
BF16 = mybir.dt.bfloat16
I32 = mybir.dt.int32
ALU = mybir.AluOpType
AF = mybir.ActivationFunctionType
AX = mybir.AxisListType
CC_INC = 1


def _is_ap(x):
    return hasattr(x, "tensor") and hasattr(x, "ap") and hasattr(x, "offset")


def _region(ap):
    t = ap.tensor
    name = t.name
    dims = ap.ap
    off = ap.offset
    sp = str(ap.space)
    if "PSUM" in sp:
        return (name, 0, 128, 0, 1 << 40)
    if "SB" in sp:
        pstep = dims[0][0]
        pcnt = dims[0][1]
        if pstep == 0:
            p0 = 0
            lo = off
            pstep = 1 << 60
        else:
            p0 = off // pstep
            lo = off % pstep
        ext = 0
        for st, cn in dims[1:]:
            ext += abs(st) * (cn - 1)
        return (name, p0, p0 + pcnt, lo, lo + ext + 1)
    else:
        ext = 0
        for st, cn in dims:
            ext += abs(st) * (cn - 1)
        return (name, 0, 1, off, off + ext + 1)


class Tracker:
    def __init__(self, sem, name):
        self.sem = sem
        self.val = 0
        self.name = name


class Eng:
    def __init__(self, name, eng, tracker):
        self.name = name
        self.eng = eng
        self.tr = tracker
        self.known = {}
        self.pending_noinc = False


class Sched:
    def __init__(self, nc, es):
        self.nc = nc
        self.es = es
        self.es0 = es
        self.recs = {}
        self.engs = {}
        for nm, e in (("pe", nc.tensor), ("act", nc.scalar), ("dve", nc.vector), ("pool", nc.gpsimd), ("sp", nc.sync)):
            tr = Tracker(es.enter_context(nc.semaphore("s_" + nm)), nm)
            self.engs[nm] = Eng(nm, e, tr)
        self.n_ins = 0
        self.n_wait = 0
        self._dma_tr = []
        self._names = {}

    def _uniq(self, name):
        k = self._names.get(name, 0)
        self._names[name] = k + 1
        return name if k == 0 else f"{name}__{k}"

    def sb(self, name, shape, dt):
        return self.es.enter_context(self.nc.sbuf_tensor(self._uniq(name), list(shape), dt))

    def ps(self, name, shape, dt):
        return self.es.enter_context(self.nc.psum_tensor(name, list(shape), dt))

    def dma_tracker(self, name):
        tr = Tracker(self.es0.enter_context(self.nc.semaphore("d_" + name)), name)
        self._dma_tr.append(tr)
        return tr

    def _deps(self, reads, writes, same_eng=None):
        deps = {}

        def add(tr, v):
            if deps.get(tr, 0) < v:
                deps[tr] = v

        for ap in reads:
            name, p0, p1, lo, hi = _region(ap)
            for r in self.recs.get(name, ()):
                if r[4] and r[0] < p1 and p0 < r[1] and r[2] < hi and lo < r[3]:
                    add(r[5], r[6])
        for ap in writes:
            name, p0, p1, lo, hi = _region(ap)
            for r in self.recs.get(name, ()):
                if r[0] < p1 and p0 < r[1] and r[2] < hi and lo < r[3]:
                    if same_eng is not None and r[5] is same_eng:
                        continue
                    add(r[5], r[6])
        return deps

    def _record(self, reads, writes, tr, val):
        for ap in writes:
            name, p0, p1, lo, hi = _region(ap)
            lst = self.recs.setdefault(name, [])
            lst[:] = [r for r in lst if not (p0 <= r[0] and r[1] <= p1 and lo <= r[2] and r[3] <= hi)]
            lst.append([p0, p1, lo, hi, True, tr, val])
        for ap in reads:
            name, p0, p1, lo, hi = _region(ap)
            lst = self.recs.setdefault(name, [])
            for r in lst:
                if (not r[4]) and r[5] is tr and r[0] == p0 and r[1] == p1 and r[2] == lo and r[3] == hi:
                    r[6] = val
                    break
            else:
                lst.append([p0, p1, lo, hi, False, tr, val])

    def _emit_waits(self, E, deps):
        for tr, v in deps.items():
            if E.known.get(tr, 0) >= v:
                continue
            if tr is E.tr and E.name == "pe":
                continue
            E.eng.wait_ge(tr.sem, v)
            E.known[tr] = v
            self.n_wait += 1

    def op(self, engname, method, *args, inc=True, extra_reads=(), extra_writes=(), **kwargs):
        E = self.engs[engname]
        writes, reads = [], []
        if "out" in kwargs:
            writes.append(kwargs["out"])
            pos_reads = args
        else:
            writes.append(args[0])
            pos_reads = args[1:]
        for a in pos_reads:
            if _is_ap(a):
                reads.append(a)
        for k, a in kwargs.items():
            if k == "out":
                continue
            if k == "accum_out":
                if a is not None:
                    writes.append(a)
                continue
            if _is_ap(a):
                reads.append(a)
        reads.extend(extra_reads)
        writes.extend(extra_writes)
        deps = self._deps(reads, writes, same_eng=E.tr)
        self._emit_waits(E, deps)
        ins = getattr(E.eng, method)(*args, **kwargs)
        val = E.tr.val + 1
        if inc:
            ins.then_inc(E.tr.sem, 1)
            E.tr.val = val
        self._record(reads, writes, E.tr, val)
        self.n_ins += 1
        return ins

    def dma(self, queue, out, in_, tracker, **kw):
        E = self.engs[queue]
        deps = self._deps([in_], [out])
        self._emit_waits(E, deps)
        ins = E.eng.dma_start(out=out, in_=in_, **kw)
        ins.then_inc(tracker.sem, 16)
        tracker.val += 16
        self._record([in_], [out], tracker, tracker.val)
        self.n_ins += 1
        return ins

    def finish(self, queue="sp"):
        E = self.engs[queue]
        for tr in self._dma_tr:
            if tr.val > 0:
                E.eng.wait_ge(tr.sem, tr.val)
        for e in self.engs.values():
            if e.tr.val > 0 and e is not E:
                E.eng.wait_ge(e.tr.sem, e.tr.val)

    def pe(self, m, *a, **k):
        return self.op("pe", m, *a, **k)

    def act(self, m, *a, **k):
        return self.op("act", m, *a, **k)

    def dve(self, m, *a, **k):
        return self.op("dve", m, *a, **k)

    def pool(self, m, *a, **k):
        return self.op("pool", m, *a, **k)

    def dma_group(self, queue, pairs, tracker, **kw):
        E = self.engs[queue]
        for out, in_ in pairs:
            deps = self._deps([in_], [out])
            self._emit_waits(E, deps)
            E.eng.dma_start(out=out, in_=in_, **kw).then_inc(tracker.sem, 16)
            tracker.val += 16
            self.n_ins += 1
        for out, in_ in pairs:
            self._record([in_], [out], tracker, tracker.val)

    def barrier(self):
        trs = [e.tr for e in self.engs.values()] + list(self._dma_tr)
        for E in self.engs.values():
            for tr in trs:
                if tr.val > 0 and E.known.get(tr, 0) < tr.val:
                    E.eng.wait_ge(tr.sem, tr.val)
                    E.known[tr] = tr.val
                    self.n_wait += 1

    @contextmanager
    def scope(self):
        old = self.es
        with ExitStack() as es2:
            self.es = es2
            try:
                yield
            finally:
                self.barrier()
                self.es = old

    def collective(self, kind, in_ap, out_ap, groups, tracker):
        E = self.engs["pool"]
        deps = self._deps([in_ap], [out_ap])
        self._emit_waits(E, deps)
        ins = E.eng.collective_compute(kind, mybir.AluOpType.bypass, replica_groups=groups, ins=[in_ap.opt()], outs=[out_ap.opt()])
        ins.then_inc(tracker.sem, CC_INC)
        tracker.val += CC_INC
        self._record([in_ap], [out_ap], tracker, tracker.val)
        self.n_ins += 1
        return ins

D = 1024
DFF = 2816
SEQ = 4096
NB = 4
MEM = 256
EPS = 1e-6
IN_SIZES = (256, 256, 512, 512, 512, 1024, 8, 256, 128, 32, 512, 512, 512, 4096)
IN_OFF = [0]
for _s in IN_SIZES:
    IN_OFF.append(IN_OFF[-1] + _s)
NTA = 2048
TB = 1024
NG = 512
WSLOT = 5632


class Ctx:
    def __init__(self, S, nbanks=8, nwslots=4, ntmp=4):
        self.S = S
        self.banks = [S.ps(f"bank{i}", [128, 512], F32) for i in range(nbanks)]
        self._bi = 0
        self.wtr = [S.dma_tracker(f"w{i}") for i in range(4)]
        self._wgen = 0
        self.set_wslots(nwslots)
        self.tmpf = [S.sb(f"tmpf{i}", [128, 512], F32) for i in range(ntmp)]
        self._ti = 0
        self.tmpb = [S.sb(f"tmpb{i}", [128, 512], BF16) for i in range(ntmp)]
        self._tbi = 0
        self.rstd = S.sb("rstd", [128, 512], F32)
        self.ones_f = S.sb("ones_f", [128, 128], F32)
        self.ones_b = S.sb("ones_b", [128, 128], BF16)
        S.pool("memset", self.ones_f[:], 1.0)
        S.pool("memset", self.ones_b[:], 1.0)
        self.ctr = S.dma_tracker("const")

    def set_wslots(self, n):
        self._wgen += 1
        self.wslots = [self.S.sb(f"wslot{self._wgen}_{i}", [128, WSLOT], BF16) for i in range(n)]
        self._wi = 0

    def bank(self):
        b = self.banks[self._bi % len(self.banks)]
        self._bi += 1
        return b

    def tf(self):
        t = self.tmpf[self._ti % len(self.tmpf)]
        self._ti += 1
        return t

    def tb(self):
        t = self.tmpb[self._tbi % len(self.tmpb)]
        self._tbi += 1
        return t

    def load_w(self, w_dram, c0, cw, queue="pool"):
        S = self.S
        K = w_dram.shape[0]
        nk = K // 128
        assert nk * cw <= WSLOT, (nk, cw)
        i = self._wi % len(self.wslots)
        self._wi += 1
        view = self.wslots[i][:, 0:nk * cw].rearrange("p (k c) -> p k c", c=cw)
        src = w_dram.rearrange("(k p) n -> p k n", p=128)[:, :, c0:c0 + cw]
        S.dma(queue, view, src, self.wtr[i])
        return view


def mm_chain(S, out, pairs):
    n = len(pairs)
    for i, (l, r) in enumerate(pairs):
        S.pe("matmul", out, lhsT=l, rhs=r, start=(i == 0), stop=(i == n - 1), inc=(i == n - 1))


def rmsnorm_fm(S, C, src, nch, gain, dst, ntok, dtot):
    for sg in range(ntok // NG):
        sl = slice(sg * NG, (sg + 1) * NG)
        ps = C.bank()
        for dc in range(nch):
            sq = C.tb()
            S.act("activation", out=sq[:], in_=src(dc)[:, sl], func=AF.Square)
            S.pe("matmul", ps[:], lhsT=C.ones_b[:], rhs=sq[:], start=(dc == 0), stop=(dc == nch - 1))
        S.act("activation", out=C.rstd[:], in_=ps[:], func=AF.Sqrt, bias=EPS, scale=1.0 / dtot)
        S.dve("reciprocal", out=C.rstd[:], in_=C.rstd[:])
        for dc in range(nch):
            S.dve("scalar_tensor_tensor", out=dst(dc)[:, sl], in0=src(dc)[:, sl], scalar=gain[:, dc:dc + 1],
                  in1=C.rstd[:], op0=ALU.mult, op1=ALU.mult)


def linear_fm(S, C, w_dram, xin, ntok, consume, ctile=512):
    K, N = w_dram.shape
    nk = K // 128
    ctile = min(ctile, (WSLOT // nk) // 128 * 128)
    for c0 in range(0, N, ctile):
        cw = min(ctile, N - c0)
        wt = C.load_w(w_dram, c0, cw)
        for sg in range(ntok // NG):
            sl = slice(sg * NG, (sg + 1) * NG)
            for oc in range(cw // 128):
                ps = C.bank()
                mm_chain(S, ps[:], [(wt[:, k, oc * 128:(oc + 1) * 128], xin(k)[:, sl]) for k in range(nk)])
                consume(c0 // 128 + oc, ps, sl)


def ffn_fm(S, C, wg, wu, wd, xn, h, act, ntok):
    for c0 in range(0, DFF, 512):
        cw = min(512, DFF - c0)
        wgt = C.load_w(wg, c0, cw)
        wut = C.load_w(wu, c0, cw)
        for sg in range(ntok // NG):
            sl = slice(sg * NG, (sg + 1) * NG)
            for oc in range(cw // 128):
                fc = c0 // 128 + oc
                pg = C.bank()
                mm_chain(S, pg[:], [(wgt[:, k, oc * 128:(oc + 1) * 128], xn[:, k, sl]) for k in range(8)])
                pu = C.bank()
                mm_chain(S, pu[:], [(wut[:, k, oc * 128:(oc + 1) * 128], xn[:, k, sl]) for k in range(8)])
                t = C.tf()
                S.act("activation", out=t[:], in_=pg[:], func=AF.Silu)
                S.dve("tensor_tensor", out=act[:, fc, sl], in0=t[:], in1=pu[:], op=ALU.mult)

    def upd(dc, ps, sl):
        S.dve("scalar_tensor_tensor", out=h[:, dc, sl], in0=ps[:], scalar=0.5, in1=h[:, dc, sl], op0=ALU.mult, op1=ALU.add)

    linear_fm(S, C, wd, lambda k: act[:, k, :], ntok, upd, ctile=256)


PAIRS = [[0, 1], [2, 3], [4, 5], [6, 7]]


def build_A(has_post, has_pre, final, fused=None):
    nc = fused["nc"] if fused else bass.Bass("TRN2", target_bir_lowering=False)
    pfx = fused["pfx"] if fused else ""

    def din(name, shape, dt=F32):
        return nc.dram_tensor(pfx + name, list(shape), dt, kind="ExternalInput").ap()

    def dout(name, shape, dt=F32):
        return nc.dram_tensor(pfx + name, list(shape), dt, kind="ExternalOutput").ap()

    if fused and fused.get("h_src") is not None:
        hT = fused["h_src"]
    else:
        hT = din("hT", [D, NTA])
    gains_d = din("gains", [128, 64])
    if has_post:
        if not fused:
            yT = din("yT", [2048, NTA])
        memT = din("memT", [D, MEM])
        wgl = din("wgl", [D, 4096]); wbr = din("wbr", [2048, D]); wout = din("wout", [D, D])
        wq = din("wq", [D, D]); wk = din("wk", [D, D]); wv = din("wv", [D, D]); wo = din("wo", [D, D])
        f2g = din("f2g", [D, DFF]); f2u = din("f2u", [D, DFF]); f2d = din("f2d", [DFF, D])
    if has_pre:
        f1g = din("f1g", [D, DFF]); f1u = din("f1u", [D, DFF]); f1d = din("f1d", [DFF, D])
        if fused:
            h1T = fused["h_dst"]
        else:
            h1T = dout("h1T", [D, NTA])
            uT = dout("uT", [D, NTA], BF16)
    if final:
        outT = dout("outT", [D, NTA])

    with ExitStack() as es:
        if fused:
            S, C = fused["S"], fused["C"]
            es.enter_context(S.scope())
            C.set_wslots(3)
        else:
            S = Sched(nc, es)
            C = Ctx(S)
        gains = S.sb("gains_sb", [128, 64], F32)
        S.dma_group("sp", [(gains[:], gains_d)], C.ctr)
        G_F1, G_MIX, G_MIXP, G_XA, G_MEM, G_F2, G_FIN, G_SSM = 0, 8, 16, 24, 32, 40, 48, 56
        h = S.sb("h", [128, 8, TB], F32)
        xn = S.sb("xn", [128, 8, TB], BF16)
        scr = S.sb("scr", [128, 24, TB], BF16)
        if fused:
            htr, otr, otr2, ytr, ystr, mtr, ytr2, ystr2 = fused["a_tr"]
        else:
            htr = S.dma_tracker("h")
            otr = S.dma_tracker("o")
            otr2 = S.dma_tracker("o2")
            ytr = S.dma_tracker("y")
            ystr = S.dma_tracker("ys")
            mtr = S.dma_tracker("mem")
        if has_post:
            ssm_f = S.sb("ssm_f", [128, 4, TB], F32)
            macc = S.sb("macc", [128, 4, 512], F32)
            pT = S.sb("pT", [128, 2, NG], BF16)
            rl = S.sb("rl", [128, NG], F32)
            mnT = S.sb("mnT", [128, 8, MEM], BF16)
            kT = S.sb("kT", [128, 8, MEM], BF16)
            vtm = S.sb("vtm", [128, 2, D], BF16)
            with S.scope():
                mem_f = S.sb("mem_f", [128, 8, MEM], F32)
                S.dma("sp", mem_f[:], memT.rearrange("(c p) t -> p c t", p=128), mtr)
                psm = C.bank()
                for dc in range(8):
                    sq = C.tf()
                    S.dve("tensor_tensor", out=sq[:, 0:MEM], in0=mem_f[:, dc, :], in1=mem_f[:, dc, :], op=ALU.mult)
                    S.pe("matmul", psm[:, 0:MEM], lhsT=C.ones_f[:], rhs=sq[:, 0:MEM], start=(dc == 0), stop=(dc == 7))
                S.act("activation", out=C.rstd[:, 0:MEM], in_=psm[:, 0:MEM], func=AF.Sqrt, bias=EPS, scale=1.0 / D)
                S.dve("reciprocal", out=C.rstd[:, 0:MEM], in_=C.rstd[:, 0:MEM])
                for dc in range(8):
                    S.dve("scalar_tensor_tensor", out=mnT[:, dc, :], in0=mem_f[:, dc, :], scalar=gains[:, G_MEM + dc:G_MEM + dc + 1],
                          in1=C.rstd[:, 0:MEM], op0=ALU.mult, op1=ALU.mult)
            for c0 in range(0, D, 512):
                wt = C.load_w(wk, c0, 512)
                for oc in range(4):
                    ps = C.bank()
                    mm_chain(S, ps[:, 0:MEM], [(wt[:, k, oc * 128:(oc + 1) * 128], mnT[:, k, :]) for k in range(8)])
                    S.act("copy", out=kT[:, c0 // 128 + oc, :], in_=ps[:, 0:MEM])
            for c0 in range(0, D, 512):
                wt = C.load_w(wv, c0, 512)
                for mc in range(2):
                    ps = C.bank()
                    mm_chain(S, ps[:], [(mnT[:, k, mc * 128:(mc + 1) * 128], wt[:, k, :]) for k in range(8)])
                    S.act("copy", out=vtm[:, mc, c0:c0 + 512], in_=ps[:])

        for tb in range(NTA // TB):
            t0 = tb * TB
            S.dma("sp", h[:], hT.rearrange("(c p) t -> p c t", p=128)[:, :, t0:t0 + TB], htr)
            if has_post:
                ybuf = scr[:, 0:16, :]
                merged = scr[:, 16:24, :]
                rmsnorm_fm(S, C, lambda dc: h[:, dc, :], 8, gains[:, G_MIXP:G_MIXP + 8], lambda dc: xn[:, dc, :], TB, D)
                if not fused:
                    yv = yT.rearrange("(c p) t -> p c t", p=128)
                    S.dma("pool", scr[:, 0:4, :], yv[:, 0:4, t0:t0 + TB], ytr)
                    S.dma("pool", scr[:, 8:16, :], yv[:, 8:16, t0:t0 + TB], ytr)
                    S.dma("sp", ssm_f[:], yv[:, 4:8, t0:t0 + TB], ystr)
                else:
                    stg = [(scr[:, 16 + 4 * q:20 + 4 * q, 0:NG], scr[:, 16 + 4 * q:20 + 4 * q, NG:2 * NG]) for q in range(2)]
                    stg_tr = [(ytr, ystr), (ytr2, ystr2)]
                    rnd = 0
                    for i in range(4):
                        yv = fused["y_all"][i].rearrange("(c p) t -> p c t", p=128)
                        for sg in range(TB // NG):
                            c_lo = t0 + sg * NG
                            ylo, yhi = stg[rnd % 2]
                            tl, th = stg_tr[rnd % 2]
                            rnd += 1
                            S.dma("sp", ylo, yv[:, :, c_lo:c_lo + NG], tl)
                            S.dma("sp", yhi, yv[:, :, NTA + c_lo:NTA + c_lo + NG], th)
                            dst = ssm_f[:, :, sg * NG:(sg + 1) * NG] if i == 1 else scr[:, 4 * i:4 * i + 4, sg * NG:(sg + 1) * NG]
                            S.dve("tensor_scalar", out=dst, in0=ylo, scalar1=gains[:, 60:61], scalar2=None, op0=ALU.mult)
                            S.dve("scalar_tensor_tensor", out=dst, in0=yhi, scalar=gains[:, 61:62], in1=dst, op0=ALU.mult, op1=ALU.add)
                rmsnorm_fm(S, C, lambda dc: ssm_f[:, dc, :], 4, gains[:, G_SSM:G_SSM + 4], lambda dc: scr[:, 4 + dc, :], TB, 512)
                for c0 in range(0, D, 256):
                    for i in range(4):
                        wg_t = C.load_w(wgl, i * 1024 + c0, 256)
                        wb_t = C.load_w(wbr[i * 512:(i + 1) * 512, :], c0, 256)
                        for sg in range(TB // NG):
                            sl = slice(sg * NG, (sg + 1) * NG)
                            for oc in range(2):
                                dc = c0 // 128 + oc
                                ma = macc[:, sg * 2 + oc, :]
                                pg = C.bank()
                                mm_chain(S, pg[:], [(wg_t[:, k, oc * 128:(oc + 1) * 128], xn[:, k, sl]) for k in range(8)])
                                pz = C.bank()
                                mm_chain(S, pz[:], [(wb_t[:, k, oc * 128:(oc + 1) * 128], scr[:, 4 * i + k, sl]) for k in range(4)])
                                sg_t = C.tf()
                                S.act("activation", out=sg_t[:], in_=pg[:], func=AF.Sigmoid)
                                if i == 0:
                                    S.dve("tensor_tensor", out=ma, in0=sg_t[:], in1=pz[:], op=ALU.mult)
                                else:
                                    S.dve("tensor_tensor", out=sg_t[:], in0=sg_t[:], in1=pz[:], op=ALU.mult)
                                    if i < 3:
                                        S.dve("tensor_tensor", out=ma, in0=ma, in1=sg_t[:], op=ALU.add)
                                    else:
                                        S.dve("tensor_tensor", out=merged[:, dc, sl], in0=ma, in1=sg_t[:], op=ALU.add)

                def add_h(dc, ps, sl):
                    S.dve("tensor_tensor", out=h[:, dc, sl], in0=ps[:], in1=h[:, dc, sl], op=ALU.add)

                linear_fm(S, C, wout, lambda k: merged[:, k, :], TB, add_h)
                rmsnorm_fm(S, C, lambda dc: h[:, dc, :], 8, gains[:, G_XA:G_XA + 8], lambda dc: xn[:, dc, :], TB, D)
                qT = scr[:, 0:8, :]
                oT = scr[:, 8:16, :]

                def put_q(oc, ps, sl):
                    S.act("copy", out=qT[:, oc, sl], in_=ps[:])

                linear_fm(S, C, wq, lambda k: xn[:, k, :], TB, put_q)
                for hd in range(4):
                    for sg in range(TB // NG):
                        sl = slice(sg * NG, (sg + 1) * NG)
                        for mc in range(2):
                            ps = C.bank()
                            mm_chain(S, ps[:], [(kT[:, 2 * hd + dk, mc * 128:(mc + 1) * 128], qT[:, 2 * hd + dk, sl]) for dk in range(2)])
                            S.act("activation", out=pT[:, mc, :], in_=ps[:], func=AF.Exp, scale=1.0 / 16.0)
                        pl = C.bank()
                        mm_chain(S, pl[:], [(C.ones_b[:], pT[:, mc, :]) for mc in range(2)])
                        S.dve("reciprocal", out=rl[:], in_=pl[:])
                        for dvc in range(2):
                            po = C.bank()
                            mm_chain(S, po[:], [(vtm[:, mc, hd * 256 + dvc * 128: hd * 256 + (dvc + 1) * 128], pT[:, mc, :]) for mc in range(2)])
                            S.dve("tensor_tensor", out=oT[:, 2 * hd + dvc, sl], in0=po[:], in1=rl[:], op=ALU.mult)
                linear_fm(S, C, wo, lambda k: oT[:, k, :], TB, add_h)
                rmsnorm_fm(S, C, lambda dc: h[:, dc, :], 8, gains[:, G_F2:G_F2 + 8], lambda dc: xn[:, dc, :], TB, D)
                ffn_fm(S, C, f2g, f2u, f2d, xn, h, scr[:, 0:22, :], TB)
            if has_pre:
                rmsnorm_fm(S, C, lambda dc: h[:, dc, :], 8, gains[:, G_F1:G_F1 + 8], lambda dc: xn[:, dc, :], TB, D)
                ffn_fm(S, C, f1g, f1u, f1d, xn, h, scr[:, 0:22, :], TB)
                S.dma("sp", h1T.rearrange("(c p) t -> p c t", p=128)[:, :, t0:t0 + TB], h[:], otr)
                rmsnorm_fm(S, C, lambda dc: h[:, dc, :], 8, gains[:, G_MIX:G_MIX + 8], lambda dc: xn[:, dc, :], TB, D)
                if fused:
                    S.dma("sp", fused["u_src"][tb].rearrange("(c p) t -> p c t", p=128), xn[:], otr2)
                    S.collective("AllGather", fused["u_src"][tb], fused["u_all"][tb], PAIRS, fused["cc_tr"].pop())
                else:
                    S.dma("sp", uT.rearrange("(c p) t -> p c t", p=128)[:, :, t0:t0 + TB], xn[:], otr2)
            if final:
                rmsnorm_fm(S, C, lambda dc: h[:, dc, :], 8, gains[:, G_FIN:G_FIN + 8], lambda dc: h[:, dc, :], TB, D)
                S.dma("sp", outT.rearrange("(c p) t -> p c t", p=128)[:, :, t0:t0 + TB], h[:], otr)
        if not fused:
            S.finish()
        print("A built: ins", S.n_ins, "waits", S.n_wait, {k: e.tr.val for k, e in S.engs.items()})
    return nc

NQG = SEQ // NG
NBLK = SEQ // 128
TWO_PI = 2.0 * math.pi
MAGIC = 12582912.0
SKEW = 3


def t5_thresholds():
    n = np.arange(0, 512, dtype=np.int64)
    nf = np.maximum(n, 1).astype(np.float32)
    large = 16 + (np.log(nf / np.float32(16)) / np.float32(math.log(128 / 16)) * np.float32(16)).astype(np.int32)
    bucket = np.where(n < 16, n, np.minimum(large, 31))
    return [int(np.argmax(bucket >= m)) for m in range(1, 32)]


def build_B(phases=("ret", "ssd", "mla", "diff"), lambda_init=0.2, fused=None):
    nc = fused["nc"] if fused else bass.Bass("TRN2", target_bir_lowering=False)
    pfx = fused["pfx"] if fused else ""

    def din(name, shape, dt=F32):
        return nc.dram_tensor(pfx + name, list(shape), dt, kind="ExternalInput").ap()

    def dout(name, shape, dt=F32):
        if fused:
            return None
        return nc.dram_tensor(pfx + name, list(shape), dt, kind="ExternalOutput").ap()

    if not fused:
        uT_d = din("uT", [D, SEQ], BF16)
    pos_d = din("pos", [1, SEQ], mybir.dt.int32)
    cst_d = din("cst", [128, 16])
    if "mla" in phases:
        wmla = din("wmla", [D, 448])
        wuq = din("wuq", [256, 256])
        wukv = din("wukv", [128, 384])
        gmla = din("gmla", [128, 4])
        y_mla = dout("y_mla", [SEQ, 256])
    if "diff" in phases:
        wdiff = din("wdiff", [D, 768])
        dlam = din("dlam", [1, 256])
        dng = din("dng", [1, 128])
        dtbl = din("dtbl", [1, 64])
        y_diff = dout("y_diff", [SEQ, 256])
    if "ret" in phases:
        wret = din("wret", [D, 1024])
        y_ret = dout("y_ret", [SEQ, 256])
    if "ssd" in phases:
        wssd = din("wssd", [D, 776])
        cssd = din("cssd", [128, 24])
        rssd = din("rssd", [1, 12])
        y_ssd = dout("y_ssd", [SEQ, 256])

    with ExitStack() as es:
        if fused:
            S, C = fused["S"], fused["C"]
            es.enter_context(S.scope())
            C.set_wslots(3)
            tpb = fused["tpb"]
            utr, ptr_, otr0, otr1 = fused["b_tr"]
            otr = [otr0, otr1]
        else:
            S = Sched(nc, es)
            C = Ctx(S, nbanks=7, nwslots=3)
            tpb = S.ps("tpb", [128, 1024], BF16)
            utr = S.dma_tracker("u")
            ptr_ = S.dma_tracker("pos")
            otr = [S.dma_tracker(f"o{i}") for i in range(2)]
        cst = S.sb("cst_sb", [128, 16], F32)
        S.dma_group("sp", [(cst[:], cst_d)], C.ctr)
        uT = S.sb("uT_sb", [128, 8, SEQ], BF16)
        if fused:
            def load_u(tb):
                for r in range(2):
                    S.dma("sp", uT[:, :, r * NTA + tb * TB:r * NTA + (tb + 1) * TB],
                          fused["u_all"][tb][r * D:(r + 1) * D, :].rearrange("(c p) t -> p c t", p=128), fused["u_tr"][r * 2 + tb])

            load_u(0)
            load_u(1)
            late_u = [False]

            def load_late_u():
                if late_u[0]:
                    late_u[0] = False
                    load_u(1)
        else:
            S.dma("sp", uT[:], uT_d.rearrange("(c p) t -> p c t", p=128), utr)

            def load_late_u():
                pass
        yts = [S.sb(f"yts{i}", [128, 2, NG], BF16) for i in range(2)]
        yti = [0]

        def emit_y(branch, y_dram, g, ys):
            if not fused:
                S.dma("sp", y_dram[g * NG:(g + 1) * NG, :].rearrange("(q p) c -> p q c", p=128), ys[:], otr[g % 2])
                return
            yt = yts[yti[0] % 2]
            tr_ = otr[yti[0] % 2]
            yti[0] += 1
            for fc in range(2):
                ps = C.bank()
                for qb in range(4):
                    S.pe("transpose", ps[:, qb * 128:(qb + 1) * 128], ys[:, qb, fc * 128:(fc + 1) * 128], ident_f[:])
                S.act("copy", out=yt[:, fc, :], in_=ps[:])
            S.dma("sp", fused["y_src"][branch].rearrange("(c p) t -> p c t", p=128)[:, :, g * NG:(g + 1) * NG], yt[:], tr_)
            if g == NQG - 1:
                S.collective("AllGather", fused["y_src"][branch], fused["y_all"][branch], PAIRS, fused["cc_tr"].pop())
        ident_b = S.sb("ident_b", [128, 128], BF16)
        S.pool("memset", ident_b[:], 0.0)
        S.pool("affine_select", out=ident_b[:], in_=ident_b[:], pattern=[[-1, 128]], compare_op=ALU.not_equal, fill=1.0, base=0, channel_multiplier=1)
        ident_f = S.sb("ident_f", [128, 128], F32)
        S.pool("memset", ident_f[:], 0.0)
        S.pool("affine_select", out=ident_f[:], in_=ident_f[:], pattern=[[-1, 128]], compare_op=ALU.not_equal, fill=1.0, base=0, channel_multiplier=1)
        rel_i = S.sb("rel_i", [128, 128], I32)
        rel0 = S.sb("rel0", [128, 128], F32)
        S.pool("iota", rel_i[:], pattern=[[1, 128]], base=0, channel_multiplier=-1)
        S.pool("tensor_copy", out=rel0[:], in_=rel_i[:])
        ptr = ptr_

        def alloc_rope():
            return dict(pos_i=S.sb("pos_i", [128, NG], I32), pos_f=S.sb("pos_f", [128, NG], F32), tang=S.sb("tang", [128, NG], F32),
                        tcos=S.sb("tcos", [128, NG], F32), tsin=S.sb("tsin", [128, NG], F32))

        def rope_tables(T, g, inv_col, sign_col):
            pos_i, pos_f, tang, tcos, tsin = T["pos_i"], T["pos_f"], T["tang"], T["tcos"], T["tsin"]
            S.dma("sp", pos_i[:], pos_d[:, g * NG:(g + 1) * NG].partition_broadcast(128), ptr)
            S.dve("tensor_copy", out=pos_f[:], in_=pos_i[:])
            for (dst, shift) in ((tsin, 0.0), (tcos, math.pi / 2)):
                S.dve("tensor_scalar", out=tang[:], in0=pos_f[:], scalar1=cst[:, inv_col:inv_col + 1], scalar2=shift, op0=ALU.mult, op1=ALU.add)
                S.dve("tensor_scalar", out=dst[:], in0=tang[:], scalar1=1.0 / TWO_PI, scalar2=MAGIC, op0=ALU.mult, op1=ALU.add)
                S.dve("tensor_scalar", out=dst[:], in0=dst[:], scalar1=MAGIC, scalar2=-TWO_PI, op0=ALU.subtract, op1=ALU.mult)
                S.dve("tensor_tensor", out=tang[:], in0=tang[:], in1=dst[:], op=ALU.add)
                S.dve("tensor_scalar", out=tang[:], in0=tang[:], scalar1=math.pi, scalar2=-math.pi, op0=ALU.min, op1=ALU.max)
                S.act("activation", out=dst[:], in_=tang[:], func=AF.Sin)
            S.dve("tensor_scalar", out=tsin[:], in0=tsin[:], scalar1=cst[:, sign_col:sign_col + 1], scalar2=None, op0=ALU.mult)

        def proj_fm(wt, col0, ncols, tsl, ps_ap):
            mm_chain(S, ps_ap, [(wt[:, k, col0:col0 + ncols], uT[:, k, tsl]) for k in range(8)])

        def proj_tm(wt, c0, c1, blk, ps_ap):
            mm_chain(S, ps_ap, [(uT[:, k, blk * 128:(blk + 1) * 128], wt[:, k, c0:c1]) for k in range(8)])

        maskT = S.sb("maskT", [128, 128], F32)
        S.dve("tensor_scalar", out=maskT[:], in0=rel0[:], scalar1=0.0, scalar2=-30000.0, op0=ALU.is_lt, op1=ALU.mult)

        pti = [0]
        att_id = [0]

        def attention(nmaps, KT_of, QT_of, V_of, scale, prebias, exp_bias, epilogue):
            att_id[0] += 1
            PT = [S.sb(f"PT{att_id[0]}_{i}", [128, NG], BF16) for i in range(4)]

            def next_PT():
                t = PT[pti[0] % 4]
                pti[0] += 1
                return t

            accb = [C.banks[3], C.banks[4], C.banks[5], C.banks[6]]
            stb = [C.banks[0], C.banks[1], C.banks[2]]
            acc = [accb[qb][:, 0:129] for qb in range(4)]
            units = [(g, m, kb) for g in range(NQG) for m in range(nmaps) for kb in range(4 * g + 4)]

            def score_part(i):
                g, m, kb = units[i]
                r = max(0, kb - 4 * g)
                qlo = r * 128
                st = stb[i % 3]
                S.pe("matmul", st[:, qlo:NG], lhsT=KT_of(m)[:, kb * 128:(kb + 1) * 128], rhs=QT_of(m)[:, g * NG + qlo:(g + 1) * NG],
                     start=True, stop=True)
                for qb in range(r, 4):
                    delta = 4 * g + qb - kb
                    if delta <= 1:
                        pb = prebias(m, delta)
                        if pb is not None:
                            S.dve("tensor_tensor", out=st[:, qb * 128:(qb + 1) * 128], in0=st[:, qb * 128:(qb + 1) * 128], in1=pb, op=ALU.add)
                pt = next_PT()
                eb = exp_bias(m)
                if eb is None:
                    S.act("activation", out=pt[:, qlo:NG], in_=st[:, qlo:NG], func=AF.Exp, scale=scale)
                else:
                    S.act("activation", out=pt[:, qlo:NG], in_=st[:, qlo:NG], func=AF.Exp, scale=scale, bias=eb)
                return pt

            def pv_part(i, pt):
                g, m, kb = units[i]
                r = max(0, kb - 4 * g)
                for qb in range(r, 4):
                    S.pe("matmul", acc[qb], lhsT=pt[:, qb * 128:(qb + 1) * 128], rhs=V_of(m, kb),
                         start=(kb == 0), stop=(kb == 4 * g + qb))
                if kb == 4 * g + 3:
                    for qb in range(4):
                        epilogue(m, g, qb, acc[qb])

            pend = []
            for i in range(len(units)):
                pend.append((i, score_part(i)))
                if len(pend) > SKEW:
                    pv_part(*pend.pop(0))
            while pend:
                pv_part(*pend.pop(0))

        if "ssd" in phases:
          with S.scope():
            ws = C.load_w(wssd, 0, 512)
            wz = C.load_w(wssd, 512, 264)
            cs_t = S.sb("cssd_sb", [128, 24], F32)
            rs_t = S.sb("rssd_sb", [128, 12], F32)
            S.dma_group("sp", [(cs_t[:], cssd), (rs_t[:], rssd.partition_broadcast(128))], C.ctr)
            load_late_u()
            Aneg = S.sb("Aneg", [128, 4], F32)
            S.act("activation", out=Aneg[:], in_=rs_t[:, 4:8], func=AF.Exp)
            S.dve("tensor_scalar", out=Aneg[:], in0=Aneg[:], scalar1=-1.0, scalar2=None, op0=ALU.mult)
            causT = S.sb("causT", [128, 128], F32)
            S.dve("tensor_scalar", out=causT[:], in0=rel0[:], scalar1=0.0, scalar2=None, op0=ALU.is_ge)
            BT = S.sb("sBT", [128, SEQ], BF16)
            CT = S.sb("sCT", [128, SEQ], BF16)
            hst = S.sb("hst", [128, 256], F32)
            hbf = S.sb("hbf", [128, 256], BF16)
            S.pool("memset", hst[:], 0.0)
            pre = S.sb("spre", [128, 4, 3 + NG], F32)
            S.pool("memset", pre[:, :, 0:3], 0.0)
            xf = S.sb("sxf", [128, 2, NG], F32)
            zs2 = [S.sb(f"szs{i}", [128, 4, 256], F32) for i in range(2)]
            dtt2 = [S.sb(f"sdtt{i}", [128, 4, 4], F32) for i in range(2)]
            dA2 = [S.sb(f"sdA{i}", [128, 4, 4], F32) for i in range(2)]
            xtm2 = [S.sb(f"sxtm{i}", [128, 4, 256], F32) for i in range(2)]
            Btm2 = [S.sb(f"sBtm{i}", [128, 4, 128], BF16) for i in range(2)]
            dec = S.sb("sdec", [128, 4, 128], F32)
            MT = S.sb("sMT", [128, 4, 128], BF16)
            xdt = S.sb("sxdt", [128, 256], BF16)
            xd2b = [S.sb(f"sxd2{i}", [128, 256], BF16) for i in range(2)]
            ysbb = [S.sb(f"sysb{i}", [128, 256], F32) for i in range(2)]
            hbfb = [hbf, S.sb("hbf1", [128, 256], BF16)]
            cssb = [S.sb(f"scs{i}", [128, 16], F32) for i in range(2)]
            scs = [S.sb(f"ssc{i}", [128, 16], F32) for i in range(2)]
            sys_ = [S.sb(f"sys{i}", [128, 4, 256], F32) for i in range(2)]

            def ssd_project(g):
                tsl = slice(g * NG, (g + 1) * NG)
                zs, dtt, dA, xtm, Btm = zs2[g % 2], dtt2[g % 2], dA2[g % 2], xtm2[g % 2], Btm2[g % 2]
                for ci in range(4):
                    ps = C.bank()
                    proj_fm(ws, ci * 128, 128, tsl, ps[:])
                    S.act("copy", out=pre[:, ci, 3:3 + NG], in_=ps[:])
                for ci in range(4):
                    acc = C.tf()
                    S.dve("tensor_scalar", out=acc[:], in0=pre[:, ci, 0:NG], scalar1=cs_t[:, ci * 4:ci * 4 + 1], scalar2=None, op0=ALU.mult)
                    for k in range(1, 4):
                        S.dve("scalar_tensor_tensor", out=acc[:], in0=pre[:, ci, k:k + NG], scalar=cs_t[:, ci * 4 + k:ci * 4 + k + 1], in1=acc[:],
                              op0=ALU.mult, op1=ALU.add)
                    dst = xf[:, ci, :] if ci < 2 else (BT[:, tsl] if ci == 2 else CT[:, tsl])
                    S.act("activation", out=dst, in_=acc[:], func=AF.Silu, bias=cs_t[:, 16 + ci:17 + ci])
                S.dve("tensor_copy", out=pre[:, :, 0:3], in_=pre[:, :, NG:NG + 3])
                for bi in range(4):
                    blk = g * 4 + bi
                    ps = C.bank()
                    proj_tm(wz, 0, 264, blk, ps[:, 0:264])
                    S.act("activation", out=zs[:, bi, :], in_=ps[:, 0:256], func=AF.Silu)
                    S.dve("tensor_tensor", out=dtt[:, bi, :], in0=ps[:, 256:260], in1=rs_t[:, 0:4], op=ALU.add)
                S.act("activation", out=dtt[:], in_=dtt[:], func=AF.Exp)
                S.act("activation", out=dtt[:], in_=dtt[:], func=AF.Ln, bias=1.0)
                for bi in range(4):
                    S.dve("tensor_tensor", out=dA[:, bi, :], in0=dtt[:, bi, :], in1=Aneg[:], op=ALU.mult)
                for bi in range(4):
                    blk = g * 4 + bi
                    ps = C.bank()
                    for ci in range(2):
                        S.pe("transpose", ps[:, ci * 128:(ci + 1) * 128], xf[:, ci, bi * 128:(bi + 1) * 128], ident_f[:])
                    S.act("copy", out=xtm[:, bi, :], in_=ps[:, 0:256])
                    S.pe("transpose", tpb[:, bi * 128:(bi + 1) * 128], BT[:, blk * 128:(blk + 1) * 128], ident_b[:])
                    S.act("copy", out=Btm[:, bi, :], in_=tpb[:, bi * 128:(bi + 1) * 128])

            def ssd_stage_a(n):
                g, bi = n // 4, n % 4
                dtt, dA, xtm = dtt2[g % 2], dA2[g % 2], xtm2[g % 2]
                csl = slice(n * 128, (n + 1) * 128)
                dAc = dA[:, bi, :]
                cb = cssb[n % 2]
                sc = scs[n % 2]
                pc = C.bank()
                S.pe("matmul", pc[:, 0:4], lhsT=causT[:], rhs=dAc, start=True, stop=True)
                S.pe("matmul", pc[:, 8:12], lhsT=C.ones_f[:], rhs=dAc, start=True, stop=True)
                S.act("copy", out=cb[:, 0:12], in_=pc[:, 0:12])
                S.act("activation", out=sc[:, 0:4], in_=cb[:, 0:4], func=AF.Exp)
                S.dve("tensor_tensor", out=sc[:, 4:8], in0=cb[:, 8:12], in1=cb[:, 0:4], op=ALU.subtract)
                S.act("activation", out=sc[:, 4:8], in_=sc[:, 4:8], func=AF.Exp)
                S.act("activation", out=sc[:, 8:12], in_=cb[:, 8:12], func=AF.Exp)
                S.dve("tensor_tensor", out=sc[:, 12:16], in0=sc[:, 4:8], in1=dtt[:, bi, :], op=ALU.mult)
                pz = C.bank()
                dab = C.tf()
                dab4 = dab[:].rearrange("p (k l) -> p k l", k=4)
                S.dve("tensor_copy", out=dab4, in_=dAc.unsqueeze(2).to_broadcast([128, 4, 128]))
                for k in range(4):
                    S.pe("matmul", pz[:, k * 128:(k + 1) * 128], lhsT=dab[:, k * 128:(k + 1) * 128], rhs=causT[:], start=True, stop=True)
                for k in range(4):
                    S.dve("tensor_scalar", out=dec[:, k, :], in0=pz[:, k * 128:(k + 1) * 128], scalar1=cb[:, k:k + 1], scalar2=0.0,
                          op0=ALU.subtract, op1=ALU.min)
                S.act("activation", out=dec[:], in_=dec[:], func=AF.Exp)
                pcb = C.bank()
                S.pe("matmul", pcb[:, 0:128], lhsT=BT[:, csl], rhs=CT[:, csl], start=True, stop=True)
                cbm = C.tf()
                S.dve("tensor_tensor", out=cbm[:, 0:128], in0=pcb[:, 0:128], in1=causT[:], op=ALU.mult)
                S.dve("tensor_tensor", out=MT[:], in0=dec[:], in1=cbm[:, 0:128].unsqueeze(1).to_broadcast([128, 4, 128]), op=ALU.mult)
                xd2 = xd2b[n % 2]
                x3 = xtm[:, bi, :].rearrange("p (k e) -> p k e", k=4)
                S.dve("tensor_tensor", out=xdt[:].rearrange("p (k e) -> p k e", k=4), in0=x3,
                      in1=dtt[:, bi, :].unsqueeze(2).to_broadcast([128, 4, 64]), op=ALU.mult)
                S.dve("tensor_tensor", out=xd2[:].rearrange("p (k e) -> p k e", k=4), in0=x3,
                      in1=sc[:, 12:16].unsqueeze(2).to_broadcast([128, 4, 64]), op=ALU.mult)
                py = C.bank()
                for k in range(4):
                    ks = slice(k * 64, (k + 1) * 64)
                    S.pe("matmul", py[:, ks], lhsT=MT[:, k, :], rhs=xdt[:, ks], start=True, stop=True)
                S.act("copy", out=ysbb[n % 2][:], in_=py[:, 0:256])

            def ssd_stage_b(n):
                g, bi = n // 4, n % 4
                zs, xtm, Btm = zs2[g % 2], xtm2[g % 2], Btm2[g % 2]
                csl = slice(n * 128, (n + 1) * 128)
                sc = scs[n % 2]
                ysb = ysbb[n % 2]
                hb = hbfb[n % 2]
                if n > 0:
                    po = C.bank()
                    S.pe("matmul", po[:, 0:256], lhsT=CT[:, csl], rhs=hb[:], start=True, stop=True)
                if n < NBLK - 1:
                    pn = C.bank()
                    S.pe("matmul", pn[:, 0:256], lhsT=Btm[:, bi, :], rhs=xd2b[n % 2][:], start=True, stop=True)
                    h3 = hst[:].rearrange("p (k e) -> p k e", k=4)
                    S.dve("tensor_tensor", out=h3, in0=h3, in1=sc[:, 8:12].unsqueeze(2).to_broadcast([128, 4, 64]), op=ALU.mult)
                    S.dve("tensor_tensor", out=hst[:], in0=hst[:], in1=pn[:, 0:256], op=ALU.add)
                    S.act("copy", out=hbfb[(n + 1) % 2][:], in_=hst[:])
                if n > 0:
                    yo = C.tf()
                    S.dve("tensor_tensor", out=yo[:, 0:256].rearrange("p (k e) -> p k e", k=4), in0=po[:, 0:256].rearrange("p (k e) -> p k e", k=4),
                          in1=sc[:, 0:4].unsqueeze(2).to_broadcast([128, 4, 64]), op=ALU.mult)
                    S.dve("tensor_tensor", out=ysb[:], in0=ysb[:], in1=yo[:, 0:256], op=ALU.add)
                xd = C.tf()
                S.dve("tensor_tensor", out=xd[:, 0:256].rearrange("p (k e) -> p k e", k=4), in0=xtm[:, bi, :].rearrange("p (k e) -> p k e", k=4),
                      in1=rs_t[:, 8:12].unsqueeze(2).to_broadcast([128, 4, 64]), op=ALU.mult)
                S.dve("tensor_tensor", out=ysb[:], in0=ysb[:], in1=xd[:, 0:256], op=ALU.add)
                S.dve("tensor_tensor", out=sys_[g % 2][:, bi, :], in0=ysb[:], in1=zs[:, bi, :], op=ALU.mult)
                if bi == 3:
                    emit_y(1, y_ssd, g, sys_[g % 2])

            for g in range(NQG):
                ssd_project(g)
                for bi in range(4):
                    n = g * 4 + bi
                    ssd_stage_a(n)
                    if n >= 1:
                        ssd_stage_b(n - 1)
            ssd_stage_b(NBLK - 1)

        if "ret" in phases:
          with S.scope():
            wr = C.load_w(wret, 0, 512)
            wrv = C.load_w(wret, 512, 512)
            load_late_u()
            rQT = S.sb("rQT", [128, SEQ], BF16)
            rKT = S.sb("rKT", [128, SEQ], BF16)
            rKd = S.sb("rKd", [128, NBLK, 128], BF16)
            rV = S.sb("rV", [128, NBLK, 256], BF16)
            rG = S.sb("rG", [128, NBLK, 256], F32)
            decT = [S.sb(f"decT{hh}", [128, 128], F32) for hh in range(2)]
            qdec = S.sb("qdec", [128, 128], F32)
            kdecT = S.sb("kdecT", [128, 128], F32)
            c128 = S.sb("c128", [128, 2], F32)
            with S.scope():
                relp = S.sb("relp", [128, 128], F32)
                caus = S.sb("caus", [128, 128], F32)
                S.dve("tensor_scalar", out=relp[:], in0=rel0[:], scalar1=0.0, scalar2=None, op0=ALU.max)
                S.dve("tensor_scalar", out=caus[:], in0=rel0[:], scalar1=0.0, scalar2=None, op0=ALU.is_ge)
                for hh in range(2):
                    S.dve("tensor_scalar", out=decT[hh][:], in0=relp[:], scalar1=cst[:, 5 + hh:6 + hh], scalar2=None, op0=ALU.mult)
                    S.act("activation", out=decT[hh][:], in_=decT[hh][:], func=AF.Exp)
                    S.dve("tensor_tensor", out=decT[hh][:], in0=decT[hh][:], in1=caus[:], op=ALU.mult)
                io_i = S.sb("io_i", [128, 128], I32)
                io_f = S.sb("io_f", [128, 128], F32)
                S.pool("iota", io_i[:], pattern=[[1, 128]], base=1, channel_multiplier=0)
                S.pool("tensor_copy", out=io_f[:], in_=io_i[:])
                S.dve("tensor_scalar", out=io_f[:], in0=io_f[:], scalar1=cst[:, 4:5], scalar2=None, op0=ALU.mult)
                S.act("activation", out=qdec[:], in_=io_f[:], func=AF.Exp)
                for hh in range(2):
                    S.dve("tensor_scalar", out=kdecT[:, hh:hh + 1], in0=rel0[:, 127:128], scalar1=cst[:, 5 + hh:6 + hh], scalar2=None, op0=ALU.mult)
                S.act("activation", out=kdecT[:, 0:2], in_=kdecT[:, 0:2], func=AF.Exp)
                S.pool("memset", c128[:, 0:1], 128.0)
                S.dve("tensor_scalar", out=c128[:, 0:1], in0=c128[:, 0:1], scalar1=cst[:, 4:5], scalar2=None, op0=ALU.mult)
                S.act("activation", out=c128[:, 1:2], in_=c128[:, 0:1], func=AF.Exp)
            with S.scope():
                T = alloc_rope()
                for g in range(NQG):
                    tsl = slice(g * NG, (g + 1) * NG)
                    rope_tables(T, g, 0, 1)
                    for which in range(2):
                        pa = C.bank()
                        proj_fm(wr, which * 256, 128, tsl, pa[:])
                        pb = C.bank()
                        proj_fm(wr, which * 256 + 128, 128, tsl, pb[:])
                        t1 = C.tf()
                        S.dve("tensor_tensor", out=t1[:], in0=pa[:], in1=T["tcos"][:], op=ALU.mult)
                        t2 = C.tf()
                        S.dve("tensor_tensor", out=t2[:], in0=pb[:], in1=T["tsin"][:], op=ALU.mult)
                        S.dve("tensor_tensor", out=t1[:], in0=t1[:], in1=t2[:], op=ALU.add)
                        if which == 0:
                            S.act("copy", out=rQT[:, tsl], in_=t1[:])
                        else:
                            S.act("mul", out=rKT[:, tsl], in_=t1[:], mul=0.125)
                            for bi in range(4):
                                blk = g * 4 + bi
                                S.pe("transpose", tpb[:, bi * 128:(bi + 1) * 128], rKT[:, blk * 128:(blk + 1) * 128], ident_b[:])
                                for hh in range(2):
                                    S.dve("tensor_scalar", out=rKd[:, blk, hh * 64:(hh + 1) * 64], in0=tpb[:, bi * 128 + hh * 64:bi * 128 + (hh + 1) * 64],
                                          scalar1=kdecT[:, hh:hh + 1], scalar2=None, op0=ALU.mult)
                    for bi in range(4):
                        blk = g * 4 + bi
                        ps = C.bank()
                        proj_tm(wrv, 0, 512, blk, ps[:])
                        S.act("copy", out=rV[:, blk, :], in_=ps[:, 0:256])
                        S.act("activation", out=rG[:, blk, :], in_=ps[:, 256:512], func=AF.Silu)
            Sst = S.sb("Sst", [128, 128], F32)
            Sbf = [S.sb(f"Sbf{i}", [128, 128], BF16) for i in range(2)]
            S.pool("memset", Sst[:], 0.0)
            rys0 = S.sb("rys0", [128, 4, 256], F32)
            rys = [rys0, rys0]
            rsc = S.sb("rsc", [128, 64], F32)
            amb = [[S.sb(f"ram{i}{hh}", [128, 128], BF16) for hh in range(2)] for i in range(2)]
            qdb = [S.sb(f"rqd{i}", [128, 128], BF16) for i in range(2)]
            junk = S.sb("rjunk", [128, 2, 128], F32)

            def ret_stage_a(n):
                csl = slice(n * 128, (n + 1) * 128)
                if n > 0:
                    S.dve("tensor_tensor", out=qdb[n % 2][:], in0=rQT[:, csl], in1=qdec[:], op=ALU.mult)
                for hh in range(2):
                    hs = slice(hh * 64, hh * 64 + 64)
                    pa = C.bank()
                    S.pe("matmul", pa[:, 0:128], lhsT=rKT[hs, csl], rhs=rQT[hs, csl], start=True, stop=True)
                    S.dve("tensor_tensor", out=amb[n % 2][hh][:], in0=pa[:, 0:128], in1=decT[hh][:], op=ALU.mult)

            def ret_stage_b(n):
                g = n // 4
                if n < NBLK - 1:
                    pk = C.bank()
                    for hh in range(2):
                        S.pe("matmul", pk[hh * 64:(hh + 1) * 64, 0:128], lhsT=rKd[:, n, hh * 64:(hh + 1) * 64], rhs=rV[:, n, hh * 128:(hh + 1) * 128],
                             start=True, stop=True)
                    S.dve("scalar_tensor_tensor", out=Sst[:], in0=Sst[:], scalar=c128[:, 1:2], in1=pk[:, 0:128], op0=ALU.mult, op1=ALU.add)
                    S.act("copy", out=Sbf[(n + 1) % 2][:], in_=Sst[:])
                for hh in range(2):
                    hs = slice(hh * 64, hh * 64 + 64)
                    po = C.bank()
                    S.pe("matmul", po[:, 0:128], lhsT=amb[n % 2][hh][:], rhs=rV[:, n, hh * 128:(hh + 1) * 128], start=True, stop=(n == 0))
                    if n > 0:
                        S.pe("matmul", po[:, 0:128], lhsT=qdb[n % 2][hs, :], rhs=Sbf[n % 2][hs, :], start=False, stop=True)
                    c0 = ((2 * n + hh) % 8) * 8
                    S.pool("memset", rsc[:, c0:c0 + 2], 0.0)
                    S.act("activation", out=junk[:, 0, :], in_=po[:, 0:128], func=AF.Identity, accum_out=rsc[:, c0:c0 + 1])
                    S.act("activation", out=junk[:, 1, :], in_=po[:, 0:128], func=AF.Square, accum_out=rsc[:, c0 + 1:c0 + 2])
                    S.dve("tensor_scalar", out=rsc[:, c0 + 2:c0 + 3], in0=rsc[:, c0:c0 + 1], scalar1=1.0 / 128.0, scalar2=None, op0=ALU.mult)
                    S.dve("tensor_tensor", out=rsc[:, c0 + 3:c0 + 4], in0=rsc[:, c0 + 2:c0 + 3], in1=rsc[:, c0 + 2:c0 + 3], op=ALU.mult)
                    S.dve("tensor_scalar", out=rsc[:, c0 + 4:c0 + 5], in0=rsc[:, c0 + 1:c0 + 2], scalar1=1.0 / 128.0, scalar2=rsc[:, c0 + 3:c0 + 4],
                          op0=ALU.mult, op1=ALU.subtract)
                    S.act("activation", out=rsc[:, c0 + 5:c0 + 6], in_=rsc[:, c0 + 4:c0 + 5], func=AF.Sqrt, bias=EPS)
                    S.dve("reciprocal", out=rsc[:, c0 + 6:c0 + 7], in_=rsc[:, c0 + 5:c0 + 6])
                    x = C.tf()[:, 0:128]
                    S.dve("tensor_scalar", out=x, in0=po[:, 0:128], scalar1=rsc[:, c0 + 2:c0 + 3], scalar2=rsc[:, c0 + 6:c0 + 7],
                          op0=ALU.subtract, op1=ALU.mult)
                    S.dve("tensor_tensor", out=rys[g % 2][:, n % 4, hh * 128:(hh + 1) * 128], in0=x, in1=rG[:, n, hh * 128:(hh + 1) * 128], op=ALU.mult)
                if n % 4 == 3:
                    emit_y(0, y_ret, g, rys[g % 2])

            for n in range(NBLK + 1):
                if n < NBLK:
                    ret_stage_a(n)
                if n >= 1:
                    ret_stage_b(n - 1)

        if "mla" in phases:
          with S.scope():
            T = alloc_rope()
            tcos, tsin = T["tcos"], T["tsin"]
            gm = S.sb("gm", [128, 4], F32)
            S.dma_group("sp", [(gm[:], gmla)], C.ctr)
            load_late_u()
            wm = C.load_w(wmla, 0, 448)
            wq_t = C.load_w(wuq, 0, 256)
            wkv_t = C.load_w(wukv, 0, 384)
            KT = [S.sb(f"mKT{h}", [96, SEQ], BF16) for h in range(2)]
            QT = [S.sb(f"mQT{h}", [96, SEQ], BF16) for h in range(2)]
            Vm = S.sb("mV", [128, NBLK, 2, 129], BF16)
            S.pool("memset", Vm[:, :, :, 128:129], 1.0)
            cq_f = S.sb("cq_f", [128, 2, NG], F32)
            cqn = S.sb("cqn", [128, 2, NG], BF16)
            ckv_f = S.sb("ckv_f", [128, NG], F32)
            ckvn = S.sb("ckvn", [128, NG], BF16)
            for g in range(NQG):
                tsl = slice(g * NG, (g + 1) * NG)
                rope_tables(T, g, 2, 3)
                for c in range(2):
                    ps = C.bank()
                    proj_fm(wm, c * 128, 128, tsl, ps[:])
                    S.act("copy", out=cq_f[:, c, :], in_=ps[:])
                rmsnorm_fm(S, C, lambda dc: cq_f[:, dc, :], 2, gm[:, 0:2], lambda dc: cqn[:, dc, :], NG, 256)
                ps = C.bank()
                proj_fm(wm, 256, 128, tsl, ps[:])
                S.act("copy", out=ckv_f[:], in_=ps[:])
                rmsnorm_fm(S, C, lambda dc: ckv_f[:], 1, gm[:, 2:3], lambda dc: ckvn[:], NG, 128)
                pa = C.bank()
                proj_fm(wm, 384, 32, tsl, pa[64:96, :])
                pb = C.bank()
                proj_fm(wm, 416, 32, tsl, pb[64:96, :])
                t1 = C.tf()
                S.dve("tensor_tensor", out=t1[64:96, :], in0=pa[64:96, :], in1=tcos[64:96, :], op=ALU.mult)
                t2 = C.tf()
                S.dve("tensor_tensor", out=t2[64:96, :], in0=pb[64:96, :], in1=tsin[64:96, :], op=ALU.mult)
                for h in range(2):
                    S.dve("tensor_tensor", out=KT[h][64:96, tsl], in0=t1[64:96, :], in1=t2[64:96, :], op=ALU.add)
                for h in range(2):
                    ps = C.bank()
                    S.pe("matmul", ps[0:64, :], lhsT=wkv_t[:, 0, h * 192:h * 192 + 64], rhs=ckvn[:], start=True, stop=True)
                    S.act("copy", out=KT[h][0:64, tsl], in_=ps[0:64, :])
                    for bi in range(4):
                        blk = g * 4 + bi
                        ps = C.bank()
                        S.pe("matmul", ps[:, 0:128], lhsT=ckvn[:, bi * 128:(bi + 1) * 128], rhs=wkv_t[:, 0, h * 192 + 64:h * 192 + 192], start=True, stop=True)
                        S.act("copy", out=Vm[:, blk, h, 0:128], in_=ps[:, 0:128])
                    ps = C.bank()
                    mm_chain(S, ps[0:96, :], [(wq_t[:, k, h * 96:(h + 1) * 96], cqn[:, k, :]) for k in range(2)])
                    ps2 = C.bank()
                    mm_chain(S, ps2[64:96, :], [(wq_t[:, k, 192 + h * 32:192 + (h + 1) * 32], cqn[:, k, :]) for k in range(2)])
                    S.act("copy", out=QT[h][0:64, tsl], in_=ps[0:64, :])
                    t1 = C.tf()
                    S.dve("tensor_tensor", out=t1[64:96, :], in0=ps[64:96, :], in1=tcos[64:96, :], op=ALU.mult)
                    t2 = C.tf()
                    S.dve("tensor_tensor", out=t2[64:96, :], in0=ps2[64:96, :], in1=tsin[64:96, :], op=ALU.mult)
                    S.dve("tensor_tensor", out=QT[h][64:96, tsl], in0=t1[64:96, :], in1=t2[64:96, :], op=ALU.add)
            ystage = [S.sb(f"mys{i}", [128, 4, 256], F32) for i in range(2)]
            rcp = S.sb("m_rcp", [128, 8], F32)

            def mla_epi(m, g, qb, acc):
                ys = ystage[g % 2]
                col = (m * 4 + qb)
                S.dve("reciprocal", out=rcp[:, col:col + 1], in_=acc[:, 128:129])
                S.dve("tensor_scalar", out=ys[:, qb, m * 128:(m + 1) * 128], in0=acc[:, 0:128], scalar1=rcp[:, col:col + 1], scalar2=None, op0=ALU.mult)
                if m == 1 and qb == 3:
                    emit_y(2, y_mla, g, ys)

            attention(2, lambda m: KT[m][:, :], lambda m: QT[m][:, :], lambda m, kb: Vm[:, kb, m, :], 96 ** -0.5,
                      lambda m, d: (maskT[:] if d == 0 else None), lambda m: None, mla_epi)

        if "diff" in phases:
          with S.scope():
            wd = C.load_w(wdiff, 0, 512)
            wdv = C.load_w(wdiff, 512, 256)
            load_late_u()
            dQp = [S.sb(f"dQp{m}", [128, SEQ], BF16) for m in range(4)]
            for m in range(4):
                S.pool("memset", dQp[m][:], 0.0)
            dKT = S.sb("dKT", [128, 2, SEQ], BF16)
            dV = S.sb("dV", [128, NBLK, 2, 129], BF16)
            S.pool("memset", dV[:, :, :, 128:129], 1.0)
            for g in range(NQG):
                tsl = slice(g * NG, (g + 1) * NG)
                for c in range(2):
                    ps = C.bank()
                    proj_fm(wd, c * 128, 128, tsl, ps[:])
                    for w in range(2):
                        S.act("copy", out=dQp[2 * c + w][w * 64:(w + 1) * 64, tsl], in_=ps[w * 64:(w + 1) * 64, :])
                for c in range(2):
                    ps = C.bank()
                    proj_fm(wd, 256 + c * 128, 128, tsl, ps[:])
                    S.act("copy", out=dKT[:, c, tsl], in_=ps[:])
                for bi in range(4):
                    blk = g * 4 + bi
                    ps = C.bank()
                    proj_tm(wdv, 0, 256, blk, ps[:, 0:256])
                    S.act("copy", out=dV[:, blk, :, 0:128], in_=ps[:, 0:256].rearrange("p (h e) -> p h e", h=2))
            tbl = S.sb("tbl", [128, 64], F32)
            lamt = S.sb("lamt", [128, 256], F32)
            dngs = S.sb("dngs", [128, 128], F32)
            S.dma_group("sp", [(tbl[:], dtbl.partition_broadcast(128)), (lamt[:], dlam.partition_broadcast(128)),
                               (dngs[:], dng.partition_broadcast(128))], C.ctr)
            dT = S.sb("dT", [128, 64], F32)
            S.dve("tensor_tensor", out=dT[:, 1:64], in0=tbl[:, 1:64], in1=tbl[:, 0:63], op=ALU.subtract)
            rel1 = S.sb("rel1", [128, 128], F32)
            S.dve("tensor_scalar", out=rel1[:], in0=rel0[:], scalar1=128.0, scalar2=None, op0=ALU.add)
            thr = t5_thresholds()
            Bt = [[S.sb(f"Bt{hh}{dl}", [128, 128], F32) for dl in range(2)] for hh in range(2)]
            for hh in range(2):
                for dl in range(2):
                    relt = rel0 if dl == 0 else rel1
                    bt = Bt[hh][dl]
                    S.dve("tensor_scalar", out=bt[:], in0=relt[:], scalar1=0.0, scalar2=tbl[:, hh * 32:hh * 32 + 1], op0=ALU.mult, op1=ALU.add)
                    for m in range(1, 32):
                        tmp = C.tf()
                        S.dve("tensor_scalar", out=tmp[:, 0:128], in0=relt[:], scalar1=float(thr[m - 1]), scalar2=dT[:, hh * 32 + m:hh * 32 + m + 1],
                              op0=ALU.is_ge, op1=ALU.mult)
                        S.dve("tensor_tensor", out=bt[:], in0=bt[:], in1=tmp[:, 0:128], op=ALU.add)
                    S.dve("tensor_scalar", out=bt[:], in0=bt[:], scalar1=tbl[:, hh * 32 + 31:hh * 32 + 32], scalar2=8.0, op0=ALU.subtract, op1=ALU.mult)
                    if dl == 0:
                        S.dve("tensor_tensor", out=bt[:], in0=bt[:], in1=maskT[:], op=ALU.add)
            lsc = S.sb("lsc", [128, 8], F32)
            for i in range(2):
                tmp = C.tf()
                S.dve("tensor_tensor", out=tmp[:, 0:64], in0=lamt[:, i * 128:i * 128 + 64], in1=lamt[:, i * 128 + 64:i * 128 + 128], op=ALU.mult)
                S.dve("reduce_sum", out=lsc[:, i:i + 1], in_=tmp[:, 0:64], axis=AX.X)
                S.act("activation", out=lsc[:, 2 + i:3 + i], in_=lsc[:, i:i + 1], func=AF.Exp)
            S.dve("tensor_scalar", out=lsc[:, 4:5], in0=lsc[:, 3:4], scalar1=lsc[:, 2:3], scalar2=-float(lambda_init), op0=ALU.subtract, op1=ALU.add)
            S.dve("tensor_scalar", out=dngs[:], in0=dngs[:], scalar1=1.0 - float(lambda_init), scalar2=None, op0=ALU.mult)
            o1buf = S.sb("o1buf", [128, 2, 4, 128], F32)
            dys = [S.sb(f"dys{i}", [128, 4, 256], F32) for i in range(2)]
            dsc = S.sb("dsc", [128, 64], F32)
            dci = [0]

            def diff_epi(m, g, qb, acc):
                hh, which = m // 2, m % 2
                c0 = (dci[0] % 8) * 8
                dci[0] += 1
                S.dve("reciprocal", out=dsc[:, c0:c0 + 1], in_=acc[:, 128:129])
                if which == 0:
                    S.dve("tensor_scalar", out=o1buf[:, hh, qb, :], in0=acc[:, 0:128], scalar1=dsc[:, c0:c0 + 1], scalar2=None, op0=ALU.mult)
                    return
                ys = dys[g % 2]
                o2 = C.tf()
                S.dve("tensor_scalar", out=o2[:, 0:128], in0=acc[:, 0:128], scalar1=dsc[:, c0:c0 + 1], scalar2=None, op0=ALU.mult)
                S.dve("scalar_tensor_tensor", out=o2[:, 128:256], in0=o2[:, 0:128], scalar=lsc[:, 4:5], in1=o1buf[:, hh, qb, :], op0=ALU.mult, op1=ALU.add)
                S.dve("tensor_tensor", out=o2[:, 256:384], in0=o2[:, 128:256], in1=o2[:, 128:256], op=ALU.mult)
                S.dve("reduce_sum", out=dsc[:, c0 + 1:c0 + 2], in_=o2[:, 256:384], axis=AX.X)
                S.act("activation", out=dsc[:, c0 + 2:c0 + 3], in_=dsc[:, c0 + 1:c0 + 2], func=AF.Ln, scale=1.0 / 128.0, bias=EPS)
                S.act("activation", out=dsc[:, c0 + 3:c0 + 4], in_=dsc[:, c0 + 2:c0 + 3], func=AF.Exp, scale=-0.5)
                S.dve("scalar_tensor_tensor", out=ys[:, qb, hh * 128:(hh + 1) * 128], in0=o2[:, 128:256], scalar=dsc[:, c0 + 3:c0 + 4], in1=dngs[:],
                      op0=ALU.mult, op1=ALU.mult)
                if m == 3 and qb == 3:
                    emit_y(3, y_diff, g, ys)

            attention(4, lambda m: dKT[:, m // 2, :], lambda m: dQp[m][:, :],
                      lambda m, kb: dV[:, kb, m // 2, :], 0.125,
                      lambda m, d: Bt[m // 2][d][:], lambda m: tbl[:, (m // 2) * 32 + 31:(m // 2) * 32 + 32], diff_epi)

        if not fused:
            S.finish()
        print("B built: ins", S.n_ins, "waits", S.n_wait, {k: e.tr.val for k, e in S.engs.items()})
    return nc

def swap_halves(w, nheads, dh):
    K = w.shape[0]
    w4 = w.reshape(K, nheads, 2, dh // 2)
    return np.ascontiguousarray(w4[:, :, ::-1, :]).reshape(K, nheads * dh)


def b_consts(j):
    c = np.zeros((128, 16), np.float32)
    p = np.arange(128)
    c[:, 0] = np.exp(-math.log(10000.0) * ((p % 64) % 32).astype(np.float32) / 32).astype(np.float32)
    c[:, 1] = np.where((p % 64) < 32, -1.0, 1.0)
    c[:, 2] = np.exp(-math.log(10000.0) * (((p - 64) % 32) % 16).astype(np.float32) / 16).astype(np.float32)
    c[:, 3] = np.where(((p - 64) % 32) < 16, -1.0, 1.0)
    lg = np.log1p(-np.exp2(-5.0 - np.arange(4, dtype=np.float32))).astype(np.float32)
    c[:, 4] = lg[2 * j + p // 64]
    c[:, 5] = lg[2 * j]
    c[:, 6] = lg[2 * j + 1]
    return c


def b_inputs(inp, l, b, j, uT_bf16, phases=("ret", "ssd", "mla", "diff")):
    W = inp["w_in"][l]
    o = IN_OFF
    d = {"pos": np.ascontiguousarray(inp["positions"][b:b + 1, :]).astype(np.int32), "cst": b_consts(j)}
    if uT_bf16 is not None:
        d["uT"] = uT_bf16
    if "mla" in phases:
        kr = W[:, o[9]:o[10]]
        d["wmla"] = np.ascontiguousarray(np.concatenate([W[:, o[7]:o[8]], W[:, o[8]:o[9]], kr, swap_halves(kr, 1, 32)], axis=1))
        uq = inp["mla_w_uq"][l].reshape(256, 4, 96)[:, 2 * j:2 * j + 2, :]
        uq_rope_sw = swap_halves(np.ascontiguousarray(uq[:, :, 64:96]).reshape(256, 64), 2, 32)
        d["wuq"] = np.ascontiguousarray(np.concatenate([uq.reshape(256, 192), uq_rope_sw], axis=1))
        d["wukv"] = np.ascontiguousarray(inp["mla_w_ukv"][l].reshape(128, 4, 192)[:, 2 * j:2 * j + 2, :].reshape(128, 384))
        g = np.zeros((128, 4), np.float32)
        g[:, 0:2] = inp["mla_q_norm"][l].reshape(2, 128).T
        g[:, 2] = inp["mla_kv_norm"][l]
        d["gmla"] = g
    if "diff" in phases:
        dq = W[:, o[10]:o[11]][:, j * 256:(j + 1) * 256]
        dk = W[:, o[11]:o[12]][:, j * 256:(j + 1) * 256]
        dv = W[:, o[12]:o[13]][:, j * 256:(j + 1) * 256]
        d["wdiff"] = np.ascontiguousarray(np.concatenate([dq, dk, dv], axis=1))
        d["dlam"] = np.ascontiguousarray(inp["diff_lambda"][l].reshape(1, 256))
        d["dng"] = np.ascontiguousarray(inp["diff_norm"][l].reshape(1, 128))
        d["dtbl"] = np.ascontiguousarray(inp["rel_bias"][:, 2 * j:2 * j + 2].T.reshape(1, 64))
    if "ret" in phases:
        rq = W[:, o[0]:o[1]][:, j * 128:(j + 1) * 128]
        rk = W[:, o[1]:o[2]][:, j * 128:(j + 1) * 128]
        rv = W[:, o[2]:o[3]][:, j * 256:(j + 1) * 256]
        rg = W[:, o[3]:o[4]][:, j * 256:(j + 1) * 256]
        d["wret"] = np.ascontiguousarray(np.concatenate([rq, swap_halves(rq, 2, 64), rk, swap_halves(rk, 2, 64), rv, rg], axis=1))
    if "ssd" in phases:
        sz = W[:, o[4]:o[5]][:, j * 256:(j + 1) * 256]
        xbc = W[:, o[5]:o[6]]
        sx = xbc[:, j * 256:(j + 1) * 256]
        sB = xbc[:, 512 + j * 128:512 + (j + 1) * 128]
        sC = xbc[:, 768 + j * 128:768 + (j + 1) * 128]
        sdt = W[:, o[6]:o[7]][:, j * 4:(j + 1) * 4]
        d["wssd"] = np.ascontiguousarray(np.concatenate([sx, sB, sC, sz, sdt, sdt], axis=1))
        cw = inp["ssm_conv_w"][l]
        cb = inp["ssm_conv_b"][l]
        chans = [slice(j * 256, j * 256 + 128), slice(j * 256 + 128, j * 256 + 256), slice(512 + j * 128, 512 + (j + 1) * 128), slice(768 + j * 128, 768 + (j + 1) * 128)]
        cs = np.zeros((128, 24), np.float32)
        for ci, sl in enumerate(chans):
            cs[:, ci * 4:(ci + 1) * 4] = cw[:, sl].T
            cs[:, 16 + ci] = cb[sl]
        d["cssd"] = cs
        d["rssd"] = np.ascontiguousarray(np.concatenate([inp["ssm_dt_bias"][l][4 * j:4 * j + 4], inp["ssm_a_log"][l][4 * j:4 * j + 4], inp["ssm_d"][l][4 * j:4 * j + 4]]).reshape(1, 12).astype(np.float32))
    return d

from concourse.bass_utils import run_bass_kernel_spmd

DEPTH = 2
_DBG = {}


def pack_gains(d):
    g = np.zeros((128, 64), np.float32)
    for k, off in (("f1", 0), ("mix", 8), ("mixp", 16), ("xa", 24), ("mem", 32), ("f2", 40), ("fin", 48), ("ssm", 56)):
        if k in d:
            v = np.asarray(d[k], np.float32)
            n = v.shape[0] // 128
            g[:, off:off + n] = v.reshape(n, 128).T
    return g


def _c(a):
    return np.ascontiguousarray(a)


def build_fused():
    nc = bass.Bass("TRN2", target_bir_lowering=False)
    es = ExitStack()
    S = Sched(nc, es)
    C = Ctx(S, nbanks=7, nwslots=0)
    tpb = S.ps("tpb", [128, 1024], BF16)
    a_tr = [S.dma_tracker(f"a{i}") for i in range(8)]
    b_tr = [S.dma_tracker(f"b{i}") for i in range(4)]
    cc_tr = [S.dma_tracker(f"cc{i}") for i in range(12)]
    u_tr = [S.dma_tracker(f"ul{i}") for i in range(4)]

    def internal(name, shape, dt):
        return nc.dram_tensor(name, list(shape), dt, kind="Internal").ap()

    h_scr = [internal(f"h_scr{l}", [D, NTA], F32) for l in range(DEPTH)]
    u_src = [[internal(f"u_src{l}_{t}", [D, TB], BF16) for t in range(NTA // TB)] for l in range(DEPTH)]
    u_all = [[internal(f"u_all{l}_{t}", [2 * D, TB], BF16) for t in range(NTA // TB)] for l in range(DEPTH)]
    y_src = [[internal(f"y_src{l}_{i}", [256, SEQ], BF16) for i in range(4)] for l in range(DEPTH)]
    y_all = [[internal(f"y_all{l}_{i}", [512, SEQ], BF16) for i in range(4)] for l in range(DEPTH)]
    base = dict(nc=nc, S=S, C=C, tpb=tpb, a_tr=a_tr, b_tr=b_tr, cc_tr=cc_tr, u_tr=u_tr)
    build_A(False, True, False, fused=dict(base, pfx="A0_", h_src=None, h_dst=h_scr[0], u_src=u_src[0], u_all=u_all[0]))
    for l in range(DEPTH):
        lam0 = 0.8 - 0.6 * math.exp(-0.3 * l)
        build_B(lambda_init=lam0, fused=dict(base, pfx=f"B{l}_", u_all=u_all[l], y_src=y_src[l], y_all=y_all[l]))
        last = (l == DEPTH - 1)
        build_A(True, not last, last, fused=dict(base, pfx=f"A{l + 1}_", h_src=h_scr[l], h_dst=(None if last else h_scr[l + 1]),
                                                 y_all=y_all[l], u_src=(None if last else u_src[l + 1]), u_all=(None if last else u_all[l + 1])))
    S.finish()
    print("FUSED built: ins", S.n_ins, "waits", S.n_wait, {k: e.tr.val for k, e in S.engs.items()})
    es.close()
    return nc


def kernel(**inputs):
    inp = {k: np.asarray(v) for k, v in inputs.items()}
    cores = list(range(8))
    x = inp["x"]
    nc = build_fused()
    in_maps = []
    for c in cores:
        b, j = c // 2, c % 2
        d = {}
        d["A0_hT"] = _c(x[b, j * NTA:(j + 1) * NTA, :].T)
        d["A0_gains"] = pack_gains({"f1": inp["ffn1_norm"][0], "mix": inp["mix_norm"][0]})
        d["A0_f1g"] = inp["ffn1_w_gate"][0]; d["A0_f1u"] = inp["ffn1_w_up"][0]; d["A0_f1d"] = inp["ffn1_w_down"][0]
        for l in range(DEPTH):
            bi = b_inputs(inp, l, b, j, None)
            for k, v in bi.items():
                if k != "uT":
                    d[f"B{l}_{k}"] = v
            last = (l == DEPTH - 1)
            p = f"A{l + 1}_"
            g = {"mixp": inp["mix_norm"][l], "ssm": inp["ssm_norm"][l], "xa": inp["xa_norm"][l], "mem": inp["mem_norm"][l], "f2": inp["ffn2_norm"][l]}
            d[p + "memT"] = _c(inp["mem"][b].T)
            d[p + "wgl"] = _c(inp["w_in"][l][:, IN_OFF[13]:]); d[p + "wbr"] = _c(inp["w_branch"][l].reshape(2048, D)); d[p + "wout"] = inp["w_out"][l]
            d[p + "wq"] = inp["xa_wq"][l]; d[p + "wk"] = inp["xa_wk"][l]; d[p + "wv"] = inp["xa_wv"][l]; d[p + "wo"] = inp["xa_wo"][l]
            d[p + "f2g"] = inp["ffn2_w_gate"][l]; d[p + "f2u"] = inp["ffn2_w_up"][l]; d[p + "f2d"] = inp["ffn2_w_down"][l]
            if not last:
                g["f1"] = inp["ffn1_norm"][l + 1]
                g["mix"] = inp["mix_norm"][l + 1]
                d[p + "f1g"] = inp["ffn1_w_gate"][l + 1]; d[p + "f1u"] = inp["ffn1_w_up"][l + 1]; d[p + "f1d"] = inp["ffn1_w_down"][l + 1]
            else:
                g["fin"] = inp["final_norm"]
            gp = pack_gains(g)
            gp[:, 60] = 1.0 - j
            gp[:, 61] = float(j)
            d[p + "gains"] = gp
        in_maps.append(d)
    res = run_bass_kernel_spmd(nc, in_maps, core_ids=cores).results
    out = np.empty((NB, SEQ, D), np.float32)
    for c in cores:
        b, j = c // 2, c % 2
        out[b, j * NTA:(j + 1) * NTA, :] = np.asarray(res[c][f"A{DEPTH}_outT"]).T
    return out
```

```python
import math
import numpy as np
import ml_dtypes
from contextlib import ExitStack, contextmanager
import concourse.bass as bass
import concourse.mybir as mybir

F32 = mybir.dt.float32
BF16 = mybir.dt.bfloat16
I32 = mybir.dt.int32
ALU = mybir.AluOpType
AF = mybir.ActivationFunctionType
AX = mybir.AxisListType
CC_INC = 1


def _is_ap(x):
    return hasattr(x, "tensor") and hasattr(x, "ap") and hasattr(x, "offset")


def _region(ap):
    t = ap.tensor
    name = t.name
    dims = ap.ap
    off = ap.offset
    sp = str(ap.space)
    if "PSUM" in sp:
        return (name, 0, 128, 0, 1 << 40)
    if "SB" in sp:
        pstep = dims[0][0]
        pcnt = dims[0][1]
        if pstep == 0:
            p0 = 0
            lo = off
            pstep = 1 << 60
        else:
            p0 = off // pstep
            lo = off % pstep
        ext = 0
        for st, cn in dims[1:]:
            ext += abs(st) * (cn - 1)
        return (name, p0, p0 + pcnt, lo, lo + ext + 1)
    else:
        ext = 0
        for st, cn in dims:
            ext += abs(st) * (cn - 1)
        return (name, 0, 1, off, off + ext + 1)


class Tracker:
    def __init__(self, sem, name):
        self.sem = sem
        self.val = 0
        self.name = name


class Eng:
    def __init__(self, name, eng, tracker):
        self.name = name
        self.eng = eng
        self.tr = tracker
        self.known = {}
        self.pending_noinc = False


class Sched:
    def __init__(self, nc, es):
        self.nc = nc
        self.es = es
        self.es0 = es
        self.recs = {}
        self.engs = {}
        for nm, e in (("pe", nc.tensor), ("act", nc.scalar), ("dve", nc.vector), ("pool", nc.gpsimd), ("sp", nc.sync)):
            tr = Tracker(es.enter_context(nc.semaphore("s_" + nm)), nm)
            self.engs[nm] = Eng(nm, e, tr)
        self.n_ins = 0
        self.n_wait = 0
        self._dma_tr = []
        self._names = {}

    def _uniq(self, name):
        k = self._names.get(name, 0)
        self._names[name] = k + 1
        return name if k == 0 else f"{name}__{k}"

    def sb(self, name, shape, dt):
        return self.es.enter_context(self.nc.sbuf_tensor(self._uniq(name), list(shape), dt))

    def ps(self, name, shape, dt):
        return self.es.enter_context(self.nc.psum_tensor(name, list(shape), dt))

    def dma_tracker(self, name):
        tr = Tracker(self.es0.enter_context(self.nc.semaphore("d_" + name)), name)
        self._dma_tr.append(tr)
        return tr

    def _deps(self, reads, writes, same_eng=None):
        deps = {}

        def add(tr, v):
            if deps.get(tr, 0) < v:
                deps[tr] = v

        for ap in reads:
            name, p0, p1, lo, hi = _region(ap)
            for r in self.recs.get(name, ()):
                if r[4] and r[0] < p1 and p0 < r[1] and r[2] < hi and lo < r[3]:
                    add(r[5], r[6])
        for ap in writes:
            name, p0, p1, lo, hi = _region(ap)
            for r in self.recs.get(name, ()):
                if r[0] < p1 and p0 < r[1] and r[2] < hi and lo < r[3]:
                    if same_eng is not None and r[5] is same_eng:
                        continue
                    add(r[5], r[6])
        return deps

    def _record(self, reads, writes, tr, val):
        for ap in writes:
            name, p0, p1, lo, hi = _region(ap)
            lst = self.recs.setdefault(name, [])
            lst[:] = [r for r in lst if not (p0 <= r[0] and r[1] <= p1 and lo <= r[2] and r[3] <= hi)]
            lst.append([p0, p1, lo, hi, True, tr, val])
        for ap in reads:
            name, p0, p1, lo, hi = _region(ap)
            lst = self.recs.setdefault(name, [])
            for r in lst:
                if (not r[4]) and r[5] is tr and r[0] == p0 and r[1] == p1 and r[2] == lo and r[3] == hi:
                    r[6] = val
                    break
            else:
                lst.append([p0, p1, lo, hi, False, tr, val])

    def _emit_waits(self, E, deps):
        for tr, v in deps.items():
            if E.known.get(tr, 0) >= v:
                continue
            if tr is E.tr and E.name == "pe":
                continue
            E.eng.wait_ge(tr.sem, v)
            E.known[tr] = v
            self.n_wait += 1

    def op(self, engname, method, *args, inc=True, extra_reads=(), extra_writes=(), **kwargs):
        E = self.engs[engname]
        writes, reads = [], []
        if "out" in kwargs:
            writes.append(kwargs["out"])
            pos_reads = args
        else:
            writes.append(args[0])
            pos_reads = args[1:]
        for a in pos_reads:
            if _is_ap(a):
                reads.append(a)
        for k, a in kwargs.items():
            if k == "out":
                continue
            if k == "accum_out":
                if a is not None:
                    writes.append(a)
                continue
            if _is_ap(a):
                reads.append(a)
        reads.extend(extra_reads)
        writes.extend(extra_writes)
        deps = self._deps(reads, writes, same_eng=E.tr)
        self._emit_waits(E, deps)
        ins = getattr(E.eng, method)(*args, **kwargs)
        val = E.tr.val + 1
        if inc:
            ins.then_inc(E.tr.sem, 1)
            E.tr.val = val
        self._record(reads, writes, E.tr, val)
        self.n_ins += 1
        return ins

    def dma(self, queue, out, in_, tracker, **kw):
        E = self.engs[queue]
        deps = self._deps([in_], [out])
        self._emit_waits(E, deps)
        ins = E.eng.dma_start(out=out, in_=in_, **kw)
        ins.then_inc(tracker.sem, 16)
        tracker.val += 16
        self._record([in_], [out], tracker, tracker.val)
        self.n_ins += 1
        return ins

    def finish(self, queue="sp"):
        E = self.engs[queue]
        for tr in self._dma_tr:
            if tr.val > 0:
                E.eng.wait_ge(tr.sem, tr.val)
        for e in self.engs.values():
            if e.tr.val > 0 and e is not E:
                E.eng.wait_ge(e.tr.sem, e.tr.val)

    def pe(self, m, *a, **k):
        return self.op("pe", m, *a, **k)

    def act(self, m, *a, **k):
        return self.op("act", m, *a, **k)

    def dve(self, m, *a, **k):
        return self.op("dve", m, *a, **k)

    def pool(self, m, *a, **k):
        return self.op("pool", m, *a, **k)

    def dma_group(self, queue, pairs, tracker, **kw):
        E = self.engs[queue]
        for out, in_ in pairs:
            deps = self._deps([in_], [out])
            self._emit_waits(E, deps)
            E.eng.dma_start(out=out, in_=in_, **kw).then_inc(tracker.sem, 16)
            tracker.val += 16
            self.n_ins += 1
        for out, in_ in pairs:
            self._record([in_], [out], tracker, tracker.val)

    def barrier(self):
        trs = [e.tr for e in self.engs.values()] + list(self._dma_tr)
        for E in self.engs.values():
            for tr in trs:
                if tr.val > 0 and E.known.get(tr, 0) < tr.val:
                    E.eng.wait_ge(tr.sem, tr.val)
                    E.known[tr] = tr.val
                    self.n_wait += 1

    @contextmanager
    def scope(self):
        old = self.es
        with ExitStack() as es2:
            self.es = es2
            try:
                yield
            finally:
                self.barrier()
                self.es = old

    def collective(self, kind, in_ap, out_ap, groups, tracker):
        E = self.engs["pool"]
        deps = self._deps([in_ap], [out_ap])
        self._emit_waits(E, deps)
        ins = E.eng.collective_compute(kind, mybir.AluOpType.bypass, replica_groups=groups, ins=[in_ap.opt()], outs=[out_ap.opt()])
        ins.then_inc(tracker.sem, CC_INC)
        tracker.val += CC_INC
        self._record([in_ap], [out_ap], tracker, tracker.val)
        self.n_ins += 1
        return ins

D = 1024
DFF = 2816
SEQ = 4096
NB = 4
MEM = 256
EPS = 1e-6
IN_SIZES = (256, 256, 512, 512, 512, 1024, 8, 256, 128, 32, 512, 512, 512, 4096)
IN_OFF = [0]
for _s in IN_SIZES:
    IN_OFF.append(IN_OFF[-1] + _s)
NTA = 2048
TB = 1024
NG = 512
WSLOT = 5632


class Ctx:
    def __init__(self, S, nbanks=8, nwslots=4, ntmp=4):
        self.S = S
        self.banks = [S.ps(f"bank{i}", [128, 512], F32) for i in range(nbanks)]
        self._bi = 0
        self.wtr = [S.dma_tracker(f"w{i}") for i in range(4)]
        self._wgen = 0
        self.set_wslots(nwslots)
        self.tmpf = [S.sb(f"tmpf{i}", [128, 512], F32) for i in range(ntmp)]
        self._ti = 0
        self.tmpb = [S.sb(f"tmpb{i}", [128, 512], BF16) for i in range(ntmp)]
        self._tbi = 0
        self.rstd = S.sb("rstd", [128, 512], F32)
        self.ones_f = S.sb("ones_f", [128, 128], F32)
        self.ones_b = S.sb("ones_b", [128, 128], BF16)
        S.pool("memset", self.ones_f[:], 1.0)
        S.pool("memset", self.ones_b[:], 1.0)
        self.ctr = S.dma_tracker("const")

    def set_wslots(self, n):
        self._wgen += 1
        self.wslots = [self.S.sb(f"wslot{self._wgen}_{i}", [128, WSLOT], BF16) for i in range(n)]
        self._wi = 0

    def bank(self):
        b = self.banks[self._bi % len(self.banks)]
        self._bi += 1
        return b

    def tf(self):
        t = self.tmpf[self._ti % len(self.tmpf)]
        self._ti += 1
        return t

    def tb(self):
        t = self.tmpb[self._tbi % len(self.tmpb)]
        self._tbi += 1
        return t

    def load_w(self, w_dram, c0, cw, queue="pool"):
        S = self.S
        K = w_dram.shape[0]
        nk = K // 128
        assert nk * cw <= WSLOT, (nk, cw)
        i = self._wi % len(self.wslots)
        self._wi += 1
        view = self.wslots[i][:, 0:nk * cw].rearrange("p (k c) -> p k c", c=cw)
        src = w_dram.rearrange("(k p) n -> p k n", p=128)[:, :, c0:c0 + cw]
        S.dma(queue, view, src, self.wtr[i])
        return view


def mm_chain(S, out, pairs):
    n = len(pairs)
    for i, (l, r) in enumerate(pairs):
        S.pe("matmul", out, lhsT=l, rhs=r, start=(i == 0), stop=(i == n - 1), inc=(i == n - 1))


def rmsnorm_fm(S, C, src, nch, gain, dst, ntok, dtot):
    for sg in range(ntok // NG):
        sl = slice(sg * NG, (sg + 1) * NG)
        ps = C.bank()
        for dc in range(nch):
            sq = C.tb()
            S.act("activation", out=sq[:], in_=src(dc)[:, sl], func=AF.Square)
            S.pe("matmul", ps[:], lhsT=C.ones_b[:], rhs=sq[:], start=(dc == 0), stop=(dc == nch - 1))
        S.act("activation", out=C.rstd[:], in_=ps[:], func=AF.Sqrt, bias=EPS, scale=1.0 / dtot)
        S.dve("reciprocal", out=C.rstd[:], in_=C.rstd[:])
        for dc in range(nch):
            S.dve("scalar_tensor_tensor", out=dst(dc)[:, sl], in0=src(dc)[:, sl], scalar=gain[:, dc:dc + 1],
                  in1=C.rstd[:], op0=ALU.mult, op1=ALU.mult)


def linear_fm(S, C, w_dram, xin, ntok, consume, ctile=512):
    K, N = w_dram.shape
    nk = K // 128
    ctile = min(ctile, (WSLOT // nk) // 128 * 128)
    for c0 in range(0, N, ctile):
        cw = min(ctile, N - c0)
        wt = C.load_w(w_dram, c0, cw)
        for sg in range(ntok // NG):
            sl = slice(sg * NG, (sg + 1) * NG)
            for oc in range(cw // 128):
                ps = C.bank()
                mm_chain(S, ps[:], [(wt[:, k, oc * 128:(oc + 1) * 128], xin(k)[:, sl]) for k in range(nk)])
                consume(c0 // 128 + oc, ps, sl)


def ffn_fm(S, C, wg, wu, wd, xn, h, act, ntok):
    for c0 in range(0, DFF, 512):
        cw = min(512, DFF - c0)
        wgt = C.load_w(wg, c0, cw)
        wut = C.load_w(wu, c0, cw)
        for sg in range(ntok // NG):
            sl = slice(sg * NG, (sg + 1) * NG)
            for oc in range(cw // 128):
                fc = c0 // 128 + oc
                pg = C.bank()
                mm_chain(S, pg[:], [(wgt[:, k, oc * 128:(oc + 1) * 128], xn[:, k, sl]) for k in range(8)])
                pu = C.bank()
                mm_chain(S, pu[:], [(wut[:, k, oc * 128:(oc + 1) * 128], xn[:, k, sl]) for k in range(8)])
                t = C.tf()
                S.act("activation", out=t[:], in_=pg[:], func=AF.Silu)
                S.dve("tensor_tensor", out=act[:, fc, sl], in0=t[:], in1=pu[:], op=ALU.mult)

    def upd(dc, ps, sl):
        S.dve("scalar_tensor_tensor", out=h[:, dc, sl], in0=ps[:], scalar=0.5, in1=h[:, dc, sl], op0=ALU.mult, op1=ALU.add)

    linear_fm(S, C, wd, lambda k: act[:, k, :], ntok, upd, ctile=256)


PAIRS = [[0, 1], [2, 3], [4, 5], [6, 7]]


def build_A(has_post, has_pre, final, fused=None):
    nc = fused["nc"] if fused else bass.Bass("TRN2", target_bir_lowering=False)
    pfx = fused["pfx"] if fused else ""

    def din(name, shape, dt=F32):
        return nc.dram_tensor(pfx + name, list(shape), dt, kind="ExternalInput").ap()

    def dout(name, shape, dt=F32):
        return nc.dram_tensor(pfx + name, list(shape), dt, kind="ExternalOutput").ap()

    if fused and fused.get("h_src") is not None:
        hT = fused["h_src"]
    else:
        hT = din("hT", [D, NTA])
    gains_d = din("gains", [128, 64])
    if has_post:
        if not fused:
            yT = din("yT", [2048, NTA])
        memT = din("memT", [D, MEM])
        wgl = din("wgl", [D, 4096]); wbr = din("wbr", [2048, D]); wout = din("wout", [D, D])
        wq = din("wq", [D, D]); wk = din("wk", [D, D]); wv = din("wv", [D, D]); wo = din("wo", [D, D])
        f2g = din("f2g", [D, DFF]); f2u = din("f2u", [D, DFF]); f2d = din("f2d", [DFF, D])
    if has_pre:
        f1g = din("f1g", [D, DFF]); f1u = din("f1u", [D, DFF]); f1d = din("f1d", [DFF, D])
        if fused:
            h1T = fused["h_dst"]
        else:
            h1T = dout("h1T", [D, NTA])
            uT = dout("uT", [D, NTA], BF16)
    if final:
        outT = dout("outT", [D, NTA])

    with ExitStack() as es:
        if fused:
            S, C = fused["S"], fused["C"]
            es.enter_context(S.scope())
            C.set_wslots(4)
        else:
            S = Sched(nc, es)
            C = Ctx(S)
        gains = S.sb("gains_sb", [128, 64], F32)
        S.dma_group("sp", [(gains[:], gains_d)], C.ctr)
        G_F1, G_MIX, G_MIXP, G_XA, G_MEM, G_F2, G_FIN, G_SSM = 0, 8, 16, 24, 32, 40, 48, 56
        h = S.sb("h", [128, 8, TB], F32)
        xn = S.sb("xn", [128, 8, TB], BF16)
        scr = S.sb("scr", [128, 24, TB], BF16)
        if fused:
            htr, otr, otr2, ytr, ystr, mtr, ytr2, ystr2 = fused["a_tr"]
        else:
            htr = S.dma_tracker("h")
            otr = S.dma_tracker("o")
            otr2 = S.dma_tracker("o2")
            ytr = S.dma_tracker("y")
            ystr = S.dma_tracker("ys")
            mtr = S.dma_tracker("mem")
        if has_post:
            ssm_f = S.sb("ssm_f", [128, 4, TB], F32)
            macc = S.sb("macc", [128, 4, 512], F32)
            pT = S.sb("pT", [128, 2, NG], BF16)
            rl = S.sb("rl", [128, NG], F32)
            mnT = S.sb("mnT", [128, 8, MEM], BF16)
            kT = S.sb("kT", [128, 8, MEM], BF16)
            vtm = S.sb("vtm", [128, 2, D], BF16)
            with S.scope():
                mem_f = S.sb("mem_f", [128, 8, MEM], F32)
                S.dma("sp", mem_f[:], memT.rearrange("(c p) t -> p c t", p=128), mtr)
                psm = C.bank()
                for dc in range(8):
                    sq = C.tf()
                    S.dve("tensor_tensor", out=sq[:, 0:MEM], in0=mem_f[:, dc, :], in1=mem_f[:, dc, :], op=ALU.mult)
                    S.pe("matmul", psm[:, 0:MEM], lhsT=C.ones_f[:], rhs=sq[:, 0:MEM], start=(dc == 0), stop=(dc == 7))
                S.act("activation", out=C.rstd[:, 0:MEM], in_=psm[:, 0:MEM], func=AF.Sqrt, bias=EPS, scale=1.0 / D)
                S.dve("reciprocal", out=C.rstd[:, 0:MEM], in_=C.rstd[:, 0:MEM])
                for dc in range(8):
                    S.dve("scalar_tensor_tensor", out=mnT[:, dc, :], in0=mem_f[:, dc, :], scalar=gains[:, G_MEM + dc:G_MEM + dc + 1],
                          in1=C.rstd[:, 0:MEM], op0=ALU.mult, op1=ALU.mult)
            for c0 in range(0, D, 512):
                wt = C.load_w(wk, c0, 512)
                for oc in range(4):
                    ps = C.bank()
                    mm_chain(S, ps[:, 0:MEM], [(wt[:, k, oc * 128:(oc + 1) * 128], mnT[:, k, :]) for k in range(8)])
                    S.act("copy", out=kT[:, c0 // 128 + oc, :], in_=ps[:, 0:MEM])
            for c0 in range(0, D, 512):
                wt = C.load_w(wv, c0, 512)
                for mc in range(2):
                    ps = C.bank()
                    mm_chain(S, ps[:], [(mnT[:, k, mc * 128:(mc + 1) * 128], wt[:, k, :]) for k in range(8)])
                    S.act("copy", out=vtm[:, mc, c0:c0 + 512], in_=ps[:])

        for tb in range(NTA // TB):
            t0 = tb * TB
            S.dma("sp", h[:], hT.rearrange("(c p) t -> p c t", p=128)[:, :, t0:t0 + TB], htr)
            if has_post:
                ybuf = scr[:, 0:16, :]
                merged = scr[:, 16:24, :]
                rmsnorm_fm(S, C, lambda dc: h[:, dc, :], 8, gains[:, G_MIXP:G_MIXP + 8], lambda dc: xn[:, dc, :], TB, D)
                if not fused:
                    yv = yT.rearrange("(c p) t -> p c t", p=128)
                    S.dma("pool", scr[:, 0:4, :], yv[:, 0:4, t0:t0 + TB], ytr)
                    S.dma("pool", scr[:, 8:16, :], yv[:, 8:16, t0:t0 + TB], ytr)
                    S.dma("sp", ssm_f[:], yv[:, 4:8, t0:t0 + TB], ystr)
                else:
                    stg = [(scr[:, 16 + 4 * q:20 + 4 * q, 0:NG], scr[:, 16 + 4 * q:20 + 4 * q, NG:2 * NG]) for q in range(2)]
                    stg_tr = [(ytr, ystr), (ytr2, ystr2)]
                    rnd = 0
                    for i in range(4):
                        yv = fused["y_all"][i].rearrange("(c p) t -> p c t", p=128)
                        for sg in range(TB // NG):
                            c_lo = t0 + sg * NG
                            ylo, yhi = stg[rnd % 2]
                            tl, th = stg_tr[rnd % 2]
                            rnd += 1
                            S.dma("sp", ylo, yv[:, :, c_lo:c_lo + NG], tl)
                            S.dma("sp", yhi, yv[:, :, NTA + c_lo:NTA + c_lo + NG], th)
                            dst = ssm_f[:, :, sg * NG:(sg + 1) * NG] if i == 1 else scr[:, 4 * i:4 * i + 4, sg * NG:(sg + 1) * NG]
                            S.dve("tensor_scalar", out=dst, in0=ylo, scalar1=gains[:, 60:61], scalar2=None, op0=ALU.mult)
                            S.dve("scalar_tensor_tensor", out=dst, in0=yhi, scalar=gains[:, 61:62], in1=dst, op0=ALU.mult, op1=ALU.add)
                rmsnorm_fm(S, C, lambda dc: ssm_f[:, dc, :], 4, gains[:, G_SSM:G_SSM + 4], lambda dc: scr[:, 4 + dc, :], TB, 512)
                for c0 in range(0, D, 256):
                    for i in range(4):
                        wg_t = C.load_w(wgl, i * 1024 + c0, 256)
                        wb_t = C.load_w(wbr[i * 512:(i + 1) * 512, :], c0, 256)
                        for sg in range(TB // NG):
                            sl = slice(sg * NG, (sg + 1) * NG)
                            for oc in range(2):
                                dc = c0 // 128 + oc
                                ma = macc[:, sg * 2 + oc, :]
                                pg = C.bank()
                                mm_chain(S, pg[:], [(wg_t[:, k, oc * 128:(oc + 1) * 128], xn[:, k, sl]) for k in range(8)])
                                pz = C.bank()
                                mm_chain(S, pz[:], [(wb_t[:, k, oc * 128:(oc + 1) * 128], scr[:, 4 * i + k, sl]) for k in range(4)])
                                sg_t = C.tf()
                                S.act("activation", out=sg_t[:], in_=pg[:], func=AF.Sigmoid)
                                if i == 0:
                                    S.dve("tensor_tensor", out=ma, in0=sg_t[:], in1=pz[:], op=ALU.mult)
                                else:
                                    S.dve("tensor_tensor", out=sg_t[:], in0=sg_t[:], in1=pz[:], op=ALU.mult)
                                    if i < 3:
                                        S.dve("tensor_tensor", out=ma, in0=ma, in1=sg_t[:], op=ALU.add)
                                    else:
                                        S.dve("tensor_tensor", out=merged[:, dc, sl], in0=ma, in1=sg_t[:], op=ALU.add)

                def add_h(dc, ps, sl):
                    S.dve("tensor_tensor", out=h[:, dc, sl], in0=ps[:], in1=h[:, dc, sl], op=ALU.add)

                linear_fm(S, C, wout, lambda k: merged[:, k, :], TB, add_h)
                rmsnorm_fm(S, C, lambda dc: h[:, dc, :], 8, gains[:, G_XA:G_XA + 8], lambda dc: xn[:, dc, :], TB, D)
                qT = scr[:, 0:8, :]
                oT = scr[:, 8:16, :]

                def put_q(oc, ps, sl):
                    S.act("copy", out=qT[:, oc, sl], in_=ps[:])

                linear_fm(S, C, wq, lambda k: xn[:, k, :], TB, put_q)
                for hd in range(4):
                    for sg in range(TB // NG):
                        sl = slice(sg * NG, (sg + 1) * NG)
                        for mc in range(2):
                            ps = C.bank()
                            mm_chain(S, ps[:], [(kT[:, 2 * hd + dk, mc * 128:(mc + 1) * 128], qT[:, 2 * hd + dk, sl]) for dk in range(2)])
                            S.act("activation", out=pT[:, mc, :], in_=ps[:], func=AF.Exp, scale=1.0 / 16.0)
                        pl = C.bank()
                        mm_chain(S, pl[:], [(C.ones_b[:], pT[:, mc, :]) for mc in range(2)])
                        S.dve("reciprocal", out=rl[:], in_=pl[:])
                        for dvc in range(2):
                            po = C.bank()
                            mm_chain(S, po[:], [(vtm[:, mc, hd * 256 + dvc * 128: hd * 256 + (dvc + 1) * 128], pT[:, mc, :]) for mc in range(2)])
                            S.dve("tensor_tensor", out=oT[:, 2 * hd + dvc, sl], in0=po[:], in1=rl[:], op=ALU.mult)
                linear_fm(S, C, wo, lambda k: oT[:, k, :], TB, add_h)
                rmsnorm_fm(S, C, lambda dc: h[:, dc, :], 8, gains[:, G_F2:G_F2 + 8], lambda dc: xn[:, dc, :], TB, D)
                ffn_fm(S, C, f2g, f2u, f2d, xn, h, scr[:, 0:22, :], TB)
            if has_pre:
                rmsnorm_fm(S, C, lambda dc: h[:, dc, :], 8, gains[:, G_F1:G_F1 + 8], lambda dc: xn[:, dc, :], TB, D)
                ffn_fm(S, C, f1g, f1u, f1d, xn, h, scr[:, 0:22, :], TB)
                S.dma("sp", h1T.rearrange("(c p) t -> p c t", p=128)[:, :, t0:t0 + TB], h[:], otr)
                rmsnorm_fm(S, C, lambda dc: h[:, dc, :], 8, gains[:, G_MIX:G_MIX + 8], lambda dc: xn[:, dc, :], TB, D)
                if fused:
                    S.dma("sp", fused["u_src"][tb].rearrange("(c p) t -> p c t", p=128), xn[:], otr2)
                    S.collective("AllGather", fused["u_src"][tb], fused["u_all"][tb], PAIRS, fused["cc_tr"].pop())
                else:
                    S.dma("sp", uT.rearrange("(c p) t -> p c t", p=128)[:, :, t0:t0 + TB], xn[:], otr2)
            if final:
                rmsnorm_fm(S, C, lambda dc: h[:, dc, :], 8, gains[:, G_FIN:G_FIN + 8], lambda dc: h[:, dc, :], TB, D)
                S.dma("sp", outT.rearrange("(c p) t -> p c t", p=128)[:, :, t0:t0 + TB], h[:], otr)
        if not fused:
            S.finish()
        print("A built: ins", S.n_ins, "waits", S.n_wait, {k: e.tr.val for k, e in S.engs.items()})
    return nc

NQG = SEQ // NG
NBLK = SEQ // 128
TWO_PI = 2.0 * math.pi
MAGIC = 12582912.0
SKEW = 3


def t5_thresholds():
    n = np.arange(0, 512, dtype=np.int64)
    nf = np.maximum(n, 1).astype(np.float32)
    large = 16 + (np.log(nf / np.float32(16)) / np.float32(math.log(128 / 16)) * np.float32(16)).astype(np.int32)
    bucket = np.where(n < 16, n, np.minimum(large, 31))
    return [int(np.argmax(bucket >= m)) for m in range(1, 32)]


def build_B(phases=("ret", "ssd", "mla", "diff"), lambda_init=0.2, fused=None):
    nc = fused["nc"] if fused else bass.Bass("TRN2", target_bir_lowering=False)
    pfx = fused["pfx"] if fused else ""

    def din(name, shape, dt=F32):
        return nc.dram_tensor(pfx + name, list(shape), dt, kind="ExternalInput").ap()

    def dout(name, shape, dt=F32):
        if fused:
            return None
        return nc.dram_tensor(pfx + name, list(shape), dt, kind="ExternalOutput").ap()

    if not fused:
        uT_d = din("uT", [D, SEQ], BF16)
    pos_d = din("pos", [1, SEQ], mybir.dt.int32)
    cst_d = din("cst", [128, 16])
    if "mla" in phases:
        wmla = din("wmla", [D, 448])
        wuq = din("wuq", [256, 256])
        wukv = din("wukv", [128, 384])
        gmla = din("gmla", [128, 4])
        y_mla = dout("y_mla", [SEQ, 256])
    if "diff" in phases:
        wdiff = din("wdiff", [D, 768])
        dlam = din("dlam", [1, 256])
        dng = din("dng", [1, 128])
        dtbl = din("dtbl", [1, 64])
        y_diff = dout("y_diff", [SEQ, 256])
    if "ret" in phases:
        wret = din("wret", [D, 1024])
        y_ret = dout("y_ret", [SEQ, 256])
    if "ssd" in phases:
        wssd = din("wssd", [D, 776])
        cssd = din("cssd", [128, 24])
        rssd = din("rssd", [1, 12])
        y_ssd = dout("y_ssd", [SEQ, 256])

    with ExitStack() as es:
        if fused:
            S, C = fused["S"], fused["C"]
            es.enter_context(S.scope())
            C.set_wslots(3)
            tpb = fused["tpb"]
            utr, ptr_, otr0, otr1 = fused["b_tr"]
            otr = [otr0, otr1]
        else:
            S = Sched(nc, es)
            C = Ctx(S, nbanks=7, nwslots=3)
            tpb = S.ps("tpb", [128, 1024], BF16)
            utr = S.dma_tracker("u")
            ptr_ = S.dma_tracker("pos")
            otr = [S.dma_tracker(f"o{i}") for i in range(2)]
        cst = S.sb("cst_sb", [128, 16], F32)
        S.dma_group("sp", [(cst[:], cst_d)], C.ctr)
        uT = S.sb("uT_sb", [128, 8, SEQ], BF16)
        if fused:
            def load_u(tb):
                for r in range(2):
                    S.dma("sp", uT[:, :, r * NTA + tb * TB:r * NTA + (tb + 1) * TB],
                          fused["u_all"][tb][r * D:(r + 1) * D, :].rearrange("(c p) t -> p c t", p=128), fused["u_tr"][r * 2 + tb])

            load_u(0)
            load_u(1)
            late_u = [False]

            def load_late_u():
                if late_u[0]:
                    late_u[0] = False
                    load_u(1)
        else:
            S.dma("sp", uT[:], uT_d.rearrange("(c p) t -> p c t", p=128), utr)

            def load_late_u():
                pass
        yts = [S.sb(f"yts{i}", [128, 2, NG], BF16) for i in range(2)]
        yti = [0]

        def emit_y(branch, y_dram, g, ys):
            if not fused:
                S.dma("sp", y_dram[g * NG:(g + 1) * NG, :].rearrange("(q p) c -> p q c", p=128), ys[:], otr[g % 2])
                return
            yt = yts[yti[0] % 2]
            tr_ = otr[yti[0] % 2]
            yti[0] += 1
            for fc in range(2):
                ps = C.bank()
                for qb in range(4):
                    S.pe("transpose", ps[:, qb * 128:(qb + 1) * 128], ys[:, qb, fc * 128:(fc + 1) * 128], ident_f[:])
                S.act("copy", out=yt[:, fc, :], in_=ps[:])
            S.dma("sp", fused["y_src"][branch].rearrange("(c p) t -> p c t", p=128)[:, :, g * NG:(g + 1) * NG], yt[:], tr_)
            if g == NQG - 1:
                S.collective("AllGather", fused["y_src"][branch], fused["y_all"][branch], PAIRS, fused["cc_tr"].pop())
        ident_b = S.sb("ident_b", [128, 128], BF16)
        S.pool("memset", ident_b[:], 0.0)
        S.pool("affine_select", out=ident_b[:], in_=ident_b[:], pattern=[[-1, 128]], compare_op=ALU.not_equal, fill=1.0, base=0, channel_multiplier=1)
        ident_f = S.sb("ident_f", [128, 128], F32)
        S.pool("memset", ident_f[:], 0.0)
        S.pool("affine_select", out=ident_f[:], in_=ident_f[:], pattern=[[-1, 128]], compare_op=ALU.not_equal, fill=1.0, base=0, channel_multiplier=1)
        rel_i = S.sb("rel_i", [128, 128], I32)
        rel0 = S.sb("rel0", [128, 128], F32)
        S.pool("iota", rel_i[:], pattern=[[1, 128]], base=0, channel_multiplier=-1)
        S.pool("tensor_copy", out=rel0[:], in_=rel_i[:])
        ptr = ptr_

        def alloc_rope():
            return dict(pos_i=S.sb("pos_i", [128, NG], I32), pos_f=S.sb("pos_f", [128, NG], F32), tang=S.sb("tang", [128, NG], F32),
                        tcos=S.sb("tcos", [128, NG], F32), tsin=S.sb("tsin", [128, NG], F32))

        def rope_tables(T, g, inv_col, sign_col):
            pos_i, pos_f, tang, tcos, tsin = T["pos_i"], T["pos_f"], T["tang"], T["tcos"], T["tsin"]
            S.dma("sp", pos_i[:], pos_d[:, g * NG:(g + 1) * NG].partition_broadcast(128), ptr)
            S.dve("tensor_copy", out=pos_f[:], in_=pos_i[:])
            for (dst, shift) in ((tsin, 0.0), (tcos, math.pi / 2)):
                S.dve("tensor_scalar", out=tang[:], in0=pos_f[:], scalar1=cst[:, inv_col:inv_col + 1], scalar2=shift, op0=ALU.mult, op1=ALU.add)
                S.dve("tensor_scalar", out=dst[:], in0=tang[:], scalar1=1.0 / TWO_PI, scalar2=MAGIC, op0=ALU.mult, op1=ALU.add)
                S.dve("tensor_scalar", out=dst[:], in0=dst[:], scalar1=MAGIC, scalar2=-TWO_PI, op0=ALU.subtract, op1=ALU.mult)
                S.dve("tensor_tensor", out=tang[:], in0=tang[:], in1=dst[:], op=ALU.add)
                S.dve("tensor_scalar", out=tang[:], in0=tang[:], scalar1=math.pi, scalar2=-math.pi, op0=ALU.min, op1=ALU.max)
                S.act("activation", out=dst[:], in_=tang[:], func=AF.Sin)
            S.dve("tensor_scalar", out=tsin[:], in0=tsin[:], scalar1=cst[:, sign_col:sign_col + 1], scalar2=None, op0=ALU.mult)

        def proj_fm(wt, col0, ncols, tsl, ps_ap):
            mm_chain(S, ps_ap, [(wt[:, k, col0:col0 + ncols], uT[:, k, tsl]) for k in range(8)])

        def proj_tm(wt, c0, c1, blk, ps_ap):
            mm_chain(S, ps_ap, [(uT[:, k, blk * 128:(blk + 1) * 128], wt[:, k, c0:c1]) for k in range(8)])

        maskT = S.sb("maskT", [128, 128], F32)
        S.dve("tensor_scalar", out=maskT[:], in0=rel0[:], scalar1=0.0, scalar2=-30000.0, op0=ALU.is_lt, op1=ALU.mult)

        pti = [0]
        att_id = [0]

        def attention(nmaps, KT_of, QT_of, V_of, scale, prebias, exp_bias, epilogue):
            att_id[0] += 1
            PT = [S.sb(f"PT{att_id[0]}_{i}", [128, NG], BF16) for i in range(4)]

            def next_PT():
                t = PT[pti[0] % 4]
                pti[0] += 1
                return t

            accb = [C.banks[3], C.banks[4], C.banks[5], C.banks[6]]
            stb = [C.banks[0], C.banks[1], C.banks[2]]
            acc = [accb[qb][:, 0:129] for qb in range(4)]
            units = [(g, m, kb) for g in range(NQG) for m in range(nmaps) for kb in range(4 * g + 4)]

            def score_part(i):
                g, m, kb = units[i]
                r = max(0, kb - 4 * g)
                qlo = r * 128
                st = stb[i % 3]
                S.pe("matmul", st[:, qlo:NG], lhsT=KT_of(m)[:, kb * 128:(kb + 1) * 128], rhs=QT_of(m)[:, g * NG + qlo:(g + 1) * NG],
                     start=True, stop=True)
                for qb in range(r, 4):
                    delta = 4 * g + qb - kb
                    if delta <= 1:
                        pb = prebias(m, delta)
                        if pb is not None:
                            S.dve("tensor_tensor", out=st[:, qb * 128:(qb + 1) * 128], in0=st[:, qb * 128:(qb + 1) * 128], in1=pb, op=ALU.add)
                pt = next_PT()
                eb = exp_bias(m)
                if eb is None:
                    S.act("activation", out=pt[:, qlo:NG], in_=st[:, qlo:NG], func=AF.Exp, scale=scale)
                else:
                    S.act("activation", out=pt[:, qlo:NG], in_=st[:, qlo:NG], func=AF.Exp, scale=scale, bias=eb)
                return pt

            def pv_part(i, pt):
                g, m, kb = units[i]
                r = max(0, kb - 4 * g)
                for qb in range(r, 4):
                    S.pe("matmul", acc[qb], lhsT=pt[:, qb * 128:(qb + 1) * 128], rhs=V_of(m, kb),
                         start=(kb == 0), stop=(kb == 4 * g + qb))
                if kb == 4 * g + 3:
                    for qb in range(4):
                        epilogue(m, g, qb, acc[qb])

            pend = []
            for i in range(len(units)):
                pend.append((i, score_part(i)))
                if len(pend) > SKEW:
                    pv_part(*pend.pop(0))
            while pend:
                pv_part(*pend.pop(0))

        if "ssd" in phases:
          with S.scope():
            ws = C.load_w(wssd, 0, 512)
            wz = C.load_w(wssd, 512, 264)
            cs_t = S.sb("cssd_sb", [128, 24], F32)
            rs_t = S.sb("rssd_sb", [128, 12], F32)
            S.dma_group("sp", [(cs_t[:], cssd), (rs_t[:], rssd.partition_broadcast(128))], C.ctr)
            load_late_u()
            Aneg = S.sb("Aneg", [128, 4], F32)
            S.act("activation", out=Aneg[:], in_=rs_t[:, 4:8], func=AF.Exp)
            S.dve("tensor_scalar", out=Aneg[:], in0=Aneg[:], scalar1=-1.0, scalar2=None, op0=ALU.mult)
            causT = S.sb("causT", [128, 128], F32)
            S.dve("tensor_scalar", out=causT[:], in0=rel0[:], scalar1=0.0, scalar2=None, op0=ALU.is_ge)
            BT = S.sb("sBT", [128, SEQ], BF16)
            CT = S.sb("sCT", [128, SEQ], BF16)
            hst = S.sb("hst", [128, 256], F32)
            hbf = S.sb("hbf", [128, 256], BF16)
            S.pool("memset", hst[:], 0.0)
            pre = S.sb("spre", [128, 4, 3 + NG], F32)
            S.pool("memset", pre[:, :, 0:3], 0.0)
            xf = S.sb("sxf", [128, 2, NG], F32)
            zs2 = [S.sb(f"szs{i}", [128, 4, 256], F32) for i in range(2)]
            dtt2 = [S.sb(f"sdtt{i}", [128, 4, 4], F32) for i in range(2)]
            dA2 = [S.sb(f"sdA{i}", [128, 4, 4], F32) for i in range(2)]
            xtm2 = [S.sb(f"sxtm{i}", [128, 4, 256], F32) for i in range(2)]
            Btm2 = [S.sb(f"sBtm{i}", [128, 4, 128], BF16) for i in range(2)]
            dec = S.sb("sdec", [128, 4, 128], F32)
            MT = S.sb("sMT", [128, 4, 128], BF16)
            xdt = S.sb("sxdt", [128, 256], BF16)
            xd2b = [S.sb(f"sxd2{i}", [128, 256], BF16) for i in range(2)]
            ysbb = [S.sb(f"sysb{i}", [128, 256], F32) for i in range(2)]
            hbfb = [hbf, S.sb("hbf1", [128, 256], BF16)]
            cssb = [S.sb(f"scs{i}", [128, 16], F32) for i in range(2)]
            scs = [S.sb(f"ssc{i}", [128, 16], F32) for i in range(2)]
            sys_ = [S.sb(f"sys{i}", [128, 4, 256], F32) for i in range(2)]

            def ssd_project(g):
                tsl = slice(g * NG, (g + 1) * NG)
                zs, dtt, dA, xtm, Btm = zs2[g % 2], dtt2[g % 2], dA2[g % 2], xtm2[g % 2], Btm2[g % 2]
                for ci in range(4):
                    ps = C.bank()
                    proj_fm(ws, ci * 128, 128, tsl, ps[:])
                    S.act("copy", out=pre[:, ci, 3:3 + NG], in_=ps[:])
                for ci in range(4):
                    acc = C.tf()
                    S.dve("tensor_scalar", out=acc[:], in0=pre[:, ci, 0:NG], scalar1=cs_t[:, ci * 4:ci * 4 + 1], scalar2=None, op0=ALU.mult)
                    for k in range(1, 4):
                        S.dve("scalar_tensor_tensor", out=acc[:], in0=pre[:, ci, k:k + NG], scalar=cs_t[:, ci * 4 + k:ci * 4 + k + 1], in1=acc[:],
                              op0=ALU.mult, op1=ALU.add)
                    dst = xf[:, ci, :] if ci < 2 else (BT[:, tsl] if ci == 2 else CT[:, tsl])
                    S.act("activation", out=dst, in_=acc[:], func=AF.Silu, bias=cs_t[:, 16 + ci:17 + ci])
                S.dve("tensor_copy", out=pre[:, :, 0:3], in_=pre[:, :, NG:NG + 3])
                for bi in range(4):
                    blk = g * 4 + bi
                    ps = C.bank()
                    proj_tm(wz, 0, 264, blk, ps[:, 0:264])
                    S.act("activation", out=zs[:, bi, :], in_=ps[:, 0:256], func=AF.Silu)
                    S.dve("tensor_tensor", out=dtt[:, bi, :], in0=ps[:, 256:260], in1=rs_t[:, 0:4], op=ALU.add)
                S.act("activation", out=dtt[:], in_=dtt[:], func=AF.Exp)
                S.act("activation", out=dtt[:], in_=dtt[:], func=AF.Ln, bias=1.0)
                for bi in range(4):
                    S.dve("tensor_tensor", out=dA[:, bi, :], in0=dtt[:, bi, :], in1=Aneg[:], op=ALU.mult)
                for bi in range(4):
                    blk = g * 4 + bi
                    ps = C.bank()
                    for ci in range(2):
                        S.pe("transpose", ps[:, ci * 128:(ci + 1) * 128], xf[:, ci, bi * 128:(bi + 1) * 128], ident_f[:])
                    S.act("copy", out=xtm[:, bi, :], in_=ps[:, 0:256])
                    S.pe("transpose", tpb[:, bi * 128:(bi + 1) * 128], BT[:, blk * 128:(blk + 1) * 128], ident_b[:])
                    S.act("copy", out=Btm[:, bi, :], in_=tpb[:, bi * 128:(bi + 1) * 128])

            def ssd_stage_a(n):
                g, bi = n // 4, n % 4
                dtt, dA, xtm = dtt2[g % 2], dA2[g % 2], xtm2[g % 2]
                csl = slice(n * 128, (n + 1) * 128)
                dAc = dA[:, bi, :]
                cb = cssb[n % 2]
                sc = scs[n % 2]
                pc = C.bank()
                S.pe("matmul", pc[:, 0:4], lhsT=causT[:], rhs=dAc, start=True, stop=True)
                S.pe("matmul", pc[:, 8:12], lhsT=C.ones_f[:], rhs=dAc, start=True, stop=True)
                S.act("copy", out=cb[:, 0:12], in_=pc[:, 0:12])
                S.act("activation", out=sc[:, 0:4], in_=cb[:, 0:4], func=AF.Exp)
                S.dve("tensor_tensor", out=sc[:, 4:8], in0=cb[:, 8:12], in1=cb[:, 0:4], op=ALU.subtract)
                S.act("activation", out=sc[:, 4:8], in_=sc[:, 4:8], func=AF.Exp)
                S.act("activation", out=sc[:, 8:12], in_=cb[:, 8:12], func=AF.Exp)
                S.dve("tensor_tensor", out=sc[:, 12:16], in0=sc[:, 4:8], in1=dtt[:, bi, :], op=ALU.mult)
                pz = C.bank()
                dab = C.tf()
                dab4 = dab[:].rearrange("p (k l) -> p k l", k=4)
                S.dve("tensor_copy", out=dab4, in_=dAc.unsqueeze(2).to_broadcast([128, 4, 128]))
                for k in range(4):
                    S.pe("matmul", pz[:, k * 128:(k + 1) * 128], lhsT=dab[:, k * 128:(k + 1) * 128], rhs=causT[:], start=True, stop=True)
                for k in range(4):
                    S.dve("tensor_scalar", out=dec[:, k, :], in0=pz[:, k * 128:(k + 1) * 128], scalar1=cb[:, k:k + 1], scalar2=0.0,
                          op0=ALU.subtract, op1=ALU.min)
                S.act("activation", out=dec[:], in_=dec[:], func=AF.Exp)
                pcb = C.bank()
                S.pe("matmul", pcb[:, 0:128], lhsT=BT[:, csl], rhs=CT[:, csl], start=True, stop=True)
                cbm = C.tf()
                S.dve("tensor_tensor", out=cbm[:, 0:128], in0=pcb[:, 0:128], in1=causT[:], op=ALU.mult)
                S.dve("tensor_tensor", out=MT[:], in0=dec[:], in1=cbm[:, 0:128].unsqueeze(1).to_broadcast([128, 4, 128]), op=ALU.mult)
                xd2 = xd2b[n % 2]
                x3 = xtm[:, bi, :].rearrange("p (k e) -> p k e", k=4)
                S.dve("tensor_tensor", out=xdt[:].rearrange("p (k e) -> p k e", k=4), in0=x3,
                      in1=dtt[:, bi, :].unsqueeze(2).to_broadcast([128, 4, 64]), op=ALU.mult)
                S.dve("tensor_tensor", out=xd2[:].rearrange("p (k e) -> p k e", k=4), in0=x3,
                      in1=sc[:, 12:16].unsqueeze(2).to_broadcast([128, 4, 64]), op=ALU.mult)
                py = C.bank()
                for k in range(4):
                    ks = slice(k * 64, (k + 1) * 64)
                    S.pe("matmul", py[:, ks], lhsT=MT[:, k, :], rhs=xdt[:, ks], start=True, stop=True)
                S.act("copy", out=ysbb[n % 2][:], in_=py[:, 0:256])

            def ssd_stage_b(n):
                g, bi = n // 4, n % 4
                zs, xtm, Btm = zs2[g % 2], xtm2[g % 2], Btm2[g % 2]
                csl = slice(n * 128, (n + 1) * 128)
                sc = scs[n % 2]
                ysb = ysbb[n % 2]
                hb = hbfb[n % 2]
                if n > 0:
                    po = C.bank()
                    S.pe("matmul", po[:, 0:256], lhsT=CT[:, csl], rhs=hb[:], start=True, stop=True)
                if n < NBLK - 1:
                    pn = C.bank()
                    S.pe("matmul", pn[:, 0:256], lhsT=Btm[:, bi, :], rhs=xd2b[n % 2][:], start=True, stop=True)
                    h3 = hst[:].rearrange("p (k e) -> p k e", k=4)
                    S.dve("tensor_tensor", out=h3, in0=h3, in1=sc[:, 8:12].unsqueeze(2).to_broadcast([128, 4, 64]), op=ALU.mult)
                    S.dve("tensor_tensor", out=hst[:], in0=hst[:], in1=pn[:, 0:256], op=ALU.add)
                    S.act("copy", out=hbfb[(n + 1) % 2][:], in_=hst[:])
                if n > 0:
                    yo = C.tf()
                    S.dve("tensor_tensor", out=yo[:, 0:256].rearrange("p (k e) -> p k e", k=4), in0=po[:, 0:256].rearrange("p (k e) -> p k e", k=4),
                          in1=sc[:, 0:4].unsqueeze(2).to_broadcast([128, 4, 64]), op=ALU.mult)
                    S.dve("tensor_tensor", out=ysb[:], in0=ysb[:], in1=yo[:, 0:256], op=ALU.add)
                xd = C.tf()
                S.dve("tensor_tensor", out=xd[:, 0:256].rearrange("p (k e) -> p k e", k=4), in0=xtm[:, bi, :].rearrange("p (k e) -> p k e", k=4),
                      in1=rs_t[:, 8:12].unsqueeze(2).to_broadcast([128, 4, 64]), op=ALU.mult)
                S.dve("tensor_tensor", out=ysb[:], in0=ysb[:], in1=xd[:, 0:256], op=ALU.add)
                S.dve("tensor_tensor", out=sys_[g % 2][:, bi, :], in0=ysb[:], in1=zs[:, bi, :], op=ALU.mult)
                if bi == 3:
                    emit_y(1, y_ssd, g, sys_[g % 2])

            for g in range(NQG):
                ssd_project(g)
                for bi in range(4):
                    n = g * 4 + bi
                    ssd_stage_a(n)
                    if n >= 1:
                        ssd_stage_b(n - 1)
            ssd_stage_b(NBLK - 1)

        if "ret" in phases:
          with S.scope():
            wr = C.load_w(wret, 0, 512)
            wrv = C.load_w(wret, 512, 512)
            load_late_u()
            rQT = S.sb("rQT", [128, SEQ], BF16)
            rKT = S.sb("rKT", [128, SEQ], BF16)
            rKd = S.sb("rKd", [128, NBLK, 128], BF16)
            rV = S.sb("rV", [128, NBLK, 256], BF16)
            rG = S.sb("rG", [128, NBLK, 256], F32)
            decT = [S.sb(f"decT{hh}", [128, 128], F32) for hh in range(2)]
            qdec = S.sb("qdec", [128, 128], F32)
            kdecT = S.sb("kdecT", [128, 128], F32)
            c128 = S.sb("c128", [128, 2], F32)
            with S.scope():
                relp = S.sb("relp", [128, 128], F32)
                caus = S.sb("caus", [128, 128], F32)
                S.dve("tensor_scalar", out=relp[:], in0=rel0[:], scalar1=0.0, scalar2=None, op0=ALU.max)
                S.dve("tensor_scalar", out=caus[:], in0=rel0[:], scalar1=0.0, scalar2=None, op0=ALU.is_ge)
                for hh in range(2):
                    S.dve("tensor_scalar", out=decT[hh][:], in0=relp[:], scalar1=cst[:, 5 + hh:6 + hh], scalar2=None, op0=ALU.mult)
                    S.act("activation", out=decT[hh][:], in_=decT[hh][:], func=AF.Exp)
                    S.dve("tensor_tensor", out=decT[hh][:], in0=decT[hh][:], in1=caus[:], op=ALU.mult)
                io_i = S.sb("io_i", [128, 128], I32)
                io_f = S.sb("io_f", [128, 128], F32)
                S.pool("iota", io_i[:], pattern=[[1, 128]], base=1, channel_multiplier=0)
                S.pool("tensor_copy", out=io_f[:], in_=io_i[:])
                S.dve("tensor_scalar", out=io_f[:], in0=io_f[:], scalar1=cst[:, 4:5], scalar2=None, op0=ALU.mult)
                S.act("activation", out=qdec[:], in_=io_f[:], func=AF.Exp)
                for hh in range(2):
                    S.dve("tensor_scalar", out=kdecT[:, hh:hh + 1], in0=rel0[:, 127:128], scalar1=cst[:, 5 + hh:6 + hh], scalar2=None, op0=ALU.mult)
                S.act("activation", out=kdecT[:, 0:2], in_=kdecT[:, 0:2], func=AF.Exp)
                S.pool("memset", c128[:, 0:1], 128.0)
                S.dve("tensor_scalar", out=c128[:, 0:1], in0=c128[:, 0:1], scalar1=cst[:, 4:5], scalar2=None, op0=ALU.mult)
                S.act("activation", out=c128[:, 1:2], in_=c128[:, 0:1], func=AF.Exp)
            with S.scope():
                T = alloc_rope()
                for g in range(NQG):
                    tsl = slice(g * NG, (g + 1) * NG)
                    rope_tables(T, g, 0, 1)
                    for which in range(2):
                        pa = C.bank()
                        proj_fm(wr, which * 256, 128, tsl, pa[:])
                        pb = C.bank()
                        proj_fm(wr, which * 256 + 128, 128, tsl, pb[:])
                        t1 = C.tf()
                        S.dve("tensor_tensor", out=t1[:], in0=pa[:], in1=T["tcos"][:], op=ALU.mult)
                        t2 = C.tf()
                        S.dve("tensor_tensor", out=t2[:], in0=pb[:], in1=T["tsin"][:], op=ALU.mult)
                        S.dve("tensor_tensor", out=t1[:], in0=t1[:], in1=t2[:], op=ALU.add)
                        if which == 0:
                            S.act("copy", out=rQT[:, tsl], in_=t1[:])
                        else:
                            S.act("mul", out=rKT[:, tsl], in_=t1[:], mul=0.125)
                            for bi in range(4):
                                blk = g * 4 + bi
                                S.pe("transpose", tpb[:, bi * 128:(bi + 1) * 128], rKT[:, blk * 128:(blk + 1) * 128], ident_b[:])
                                for hh in range(2):
                                    S.dve("tensor_scalar", out=rKd[:, blk, hh * 64:(hh + 1) * 64], in0=tpb[:, bi * 128 + hh * 64:bi * 128 + (hh + 1) * 64],
                                          scalar1=kdecT[:, hh:hh + 1], scalar2=None, op0=ALU.mult)
                    for bi in range(4):
                        blk = g * 4 + bi
                        ps = C.bank()
                        proj_tm(wrv, 0, 512, blk, ps[:])
                        S.act("copy", out=rV[:, blk, :], in_=ps[:, 0:256])
                        S.act("activation", out=rG[:, blk, :], in_=ps[:, 256:512], func=AF.Silu)
            Sst = S.sb("Sst", [128, 128], F32)
            Sbf = [S.sb(f"Sbf{i}", [128, 128], BF16) for i in range(2)]
            S.pool("memset", Sst[:], 0.0)
            rys0 = S.sb("rys0", [128, 4, 256], F32)
            rys = [rys0, rys0]
            rsc = S.sb("rsc", [128, 64], F32)
            amb = [[S.sb(f"ram{i}{hh}", [128, 128], BF16) for hh in range(2)] for i in range(2)]
            qdb = [S.sb(f"rqd{i}", [128, 128], BF16) for i in range(2)]
            junk = S.sb("rjunk", [128, 2, 128], F32)

            def ret_stage_a(n):
                csl = slice(n * 128, (n + 1) * 128)
                if n > 0:
                    S.dve("tensor_tensor", out=qdb[n % 2][:], in0=rQT[:, csl], in1=qdec[:], op=ALU.mult)
                for hh in range(2):
                    hs = slice(hh * 64, hh * 64 + 64)
                    pa = C.bank()
                    S.pe("matmul", pa[:, 0:128], lhsT=rKT[hs, csl], rhs=rQT[hs, csl], start=True, stop=True)
                    S.dve("tensor_tensor", out=amb[n % 2][hh][:], in0=pa[:, 0:128], in1=decT[hh][:], op=ALU.mult)

            def ret_stage_b(n):
                g = n // 4
                if n < NBLK - 1:
                    pk = C.bank()
                    for hh in range(2):
                        S.pe("matmul", pk[hh * 64:(hh + 1) * 64, 0:128], lhsT=rKd[:, n, hh * 64:(hh + 1) * 64], rhs=rV[:, n, hh * 128:(hh + 1) * 128],
                             start=True, stop=True)
                    S.dve("scalar_tensor_tensor", out=Sst[:], in0=Sst[:], scalar=c128[:, 1:2], in1=pk[:, 0:128], op0=ALU.mult, op1=ALU.add)
                    S.act("copy", out=Sbf[(n + 1) % 2][:], in_=Sst[:])
                for hh in range(2):
                    hs = slice(hh * 64, hh * 64 + 64)
                    po = C.bank()
                    S.pe("matmul", po[:, 0:128], lhsT=amb[n % 2][hh][:], rhs=rV[:, n, hh * 128:(hh + 1) * 128], start=True, stop=(n == 0))
                    if n > 0:
                        S.pe("matmul", po[:, 0:128], lhsT=qdb[n % 2][hs, :], rhs=Sbf[n % 2][hs, :], start=False, stop=True)
                    c0 = ((2 * n + hh) % 8) * 8
                    S.pool("memset", rsc[:, c0:c0 + 2], 0.0)
                    S.act("activation", out=junk[:, 0, :], in_=po[:, 0:128], func=AF.Identity, accum_out=rsc[:, c0:c0 + 1])
                    S.act("activation", out=junk[:, 1, :], in_=po[:, 0:128], func=AF.Square, accum_out=rsc[:, c0 + 1:c0 + 2])
                    S.dve("tensor_scalar", out=rsc[:, c0 + 2:c0 + 3], in0=rsc[:, c0:c0 + 1], scalar1=1.0 / 128.0, scalar2=None, op0=ALU.mult)
                    S.dve("tensor_tensor", out=rsc[:, c0 + 3:c0 + 4], in0=rsc[:, c0 + 2:c0 + 3], in1=rsc[:, c0 + 2:c0 + 3], op=ALU.mult)
                    S.dve("tensor_scalar", out=rsc[:, c0 + 4:c0 + 5], in0=rsc[:, c0 + 1:c0 + 2], scalar1=1.0 / 128.0, scalar2=rsc[:, c0 + 3:c0 + 4],
                          op0=ALU.mult, op1=ALU.subtract)
                    S.act("activation", out=rsc[:, c0 + 5:c0 + 6], in_=rsc[:, c0 + 4:c0 + 5], func=AF.Sqrt, bias=EPS)
                    S.dve("reciprocal", out=rsc[:, c0 + 6:c0 + 7], in_=rsc[:, c0 + 5:c0 + 6])
                    x = C.tf()[:, 0:128]
                    S.dve("tensor_scalar", out=x, in0=po[:, 0:128], scalar1=rsc[:, c0 + 2:c0 + 3], scalar2=rsc[:, c0 + 6:c0 + 7],
                          op0=ALU.subtract, op1=ALU.mult)
                    S.dve("tensor_tensor", out=rys[g % 2][:, n % 4, hh * 128:(hh + 1) * 128], in0=x, in1=rG[:, n, hh * 128:(hh + 1) * 128], op=ALU.mult)
                if n % 4 == 3:
                    emit_y(0, y_ret, g, rys[g % 2])

            for n in range(NBLK + 1):
                if n < NBLK:
                    ret_stage_a(n)
                if n >= 1:
                    ret_stage_b(n - 1)

        if "mla" in phases:
          with S.scope():
            T = alloc_rope()
            tcos, tsin = T["tcos"], T["tsin"]
            gm = S.sb("gm", [128, 4], F32)
            S.dma_group("sp", [(gm[:], gmla)], C.ctr)
            load_late_u()
            wm = C.load_w(wmla, 0, 448)
            wq_t = C.load_w(wuq, 0, 256)
            wkv_t = C.load_w(wukv, 0, 384)
            KT = [S.sb(f"mKT{h}", [96, SEQ], BF16) for h in range(2)]
            QT = [S.sb(f"mQT{h}", [96, SEQ], BF16) for h in range(2)]
            Vm = S.sb("mV", [128, NBLK, 2, 129], BF16)
            S.pool("memset", Vm[:, :, :, 128:129], 1.0)
            cq_f = S.sb("cq_f", [128, 2, NG], F32)
            cqn = S.sb("cqn", [128, 2, NG], BF16)
            ckv_f = S.sb("ckv_f", [128, NG], F32)
            ckvn = S.sb("ckvn", [128, NG], BF16)
            for g in range(NQG):
                tsl = slice(g * NG, (g + 1) * NG)
                rope_tables(T, g, 2, 3)
                for c in range(2):
                    ps = C.bank()
                    proj_fm(wm, c * 128, 128, tsl, ps[:])
                    S.act("copy", out=cq_f[:, c, :], in_=ps[:])
                rmsnorm_fm(S, C, lambda dc: cq_f[:, dc, :], 2, gm[:, 0:2], lambda dc: cqn[:, dc, :], NG, 256)
                ps = C.bank()
                proj_fm(wm, 256, 128, tsl, ps[:])
                S.act("copy", out=ckv_f[:], in_=ps[:])
                rmsnorm_fm(S, C, lambda dc: ckv_f[:], 1, gm[:, 2:3], lambda dc: ckvn[:], NG, 128)
                pa = C.bank()
                proj_fm(wm, 384, 32, tsl, pa[64:96, :])
                pb = C.bank()
                proj_fm(wm, 416, 32, tsl, pb[64:96, :])
                t1 = C.tf()
                S.dve("tensor_tensor", out=t1[64:96, :], in0=pa[64:96, :], in1=tcos[64:96, :], op=ALU.mult)
                t2 = C.tf()
                S.dve("tensor_tensor", out=t2[64:96, :], in0=pb[64:96, :], in1=tsin[64:96, :], op=ALU.mult)
                for h in range(2):
                    S.dve("tensor_tensor", out=KT[h][64:96, tsl], in0=t1[64:96, :], in1=t2[64:96, :], op=ALU.add)
                for h in range(2):
                    ps = C.bank()
                    S.pe("matmul", ps[0:64, :], lhsT=wkv_t[:, 0, h * 192:h * 192 + 64], rhs=ckvn[:], start=True, stop=True)
                    S.act("copy", out=KT[h][0:64, tsl], in_=ps[0:64, :])
                    for bi in range(4):
                        blk = g * 4 + bi
                        ps = C.bank()
                        S.pe("matmul", ps[:, 0:128], lhsT=ckvn[:, bi * 128:(bi + 1) * 128], rhs=wkv_t[:, 0, h * 192 + 64:h * 192 + 192], start=True, stop=True)
                        S.act("copy", out=Vm[:, blk, h, 0:128], in_=ps[:, 0:128])
                    ps = C.bank()
                    mm_chain(S, ps[0:96, :], [(wq_t[:, k, h * 96:(h + 1) * 96], cqn[:, k, :]) for k in range(2)])
                    ps2 = C.bank()
                    mm_chain(S, ps2[64:96, :], [(wq_t[:, k, 192 + h * 32:192 + (h + 1) * 32], cqn[:, k, :]) for k in range(2)])
                    S.act("copy", out=QT[h][0:64, tsl], in_=ps[0:64, :])
                    t1 = C.tf()
                    S.dve("tensor_tensor", out=t1[64:96, :], in0=ps[64:96, :], in1=tcos[64:96, :], op=ALU.mult)
                    t2 = C.tf()
                    S.dve("tensor_tensor", out=t2[64:96, :], in0=ps2[64:96, :], in1=tsin[64:96, :], op=ALU.mult)
                    S.dve("tensor_tensor", out=QT[h][64:96, tsl], in0=t1[64:96, :], in1=t2[64:96, :], op=ALU.add)
            ystage = [S.sb(f"mys{i}", [128, 4, 256], F32) for i in range(2)]
            rcp = S.sb("m_rcp", [128, 8], F32)

            def mla_epi(m, g, qb, acc):
                ys = ystage[g % 2]
                col = (m * 4 + qb)
                S.dve("reciprocal", out=rcp[:, col:col + 1], in_=acc[:, 128:129])
                S.dve("tensor_scalar", out=ys[:, qb, m * 128:(m + 1) * 128], in0=acc[:, 0:128], scalar1=rcp[:, col:col + 1], scalar2=None, op0=ALU.mult)
                if m == 1 and qb == 3:
                    emit_y(2, y_mla, g, ys)

            attention(2, lambda m: KT[m][:, :], lambda m: QT[m][:, :], lambda m, kb: Vm[:, kb, m, :], 96 ** -0.5,
                      lambda m, d: (maskT[:] if d == 0 else None), lambda m: None, mla_epi)

        if "diff" in phases:
          with S.scope():
            wd = C.load_w(wdiff, 0, 512)
            wdv = C.load_w(wdiff, 512, 256)
            load_late_u()
            dQp = [S.sb(f"dQp{m}", [128, SEQ], BF16) for m in range(4)]
            for m in range(4):
                S.pool("memset", dQp[m][:], 0.0)
            dKT = S.sb("dKT", [128, 2, SEQ], BF16)
            dV = S.sb("dV", [128, NBLK, 2, 129], BF16)
            S.pool("memset", dV[:, :, :, 128:129], 1.0)
            for g in range(NQG):
                tsl = slice(g * NG, (g + 1) * NG)
                for c in range(2):
                    ps = C.bank()
                    proj_fm(wd, c * 128, 128, tsl, ps[:])
                    for w in range(2):
                        S.act("copy", out=dQp[2 * c + w][w * 64:(w + 1) * 64, tsl], in_=ps[w * 64:(w + 1) * 64, :])
                for c in range(2):
                    ps = C.bank()
                    proj_fm(wd, 256 + c * 128, 128, tsl, ps[:])
                    S.act("copy", out=dKT[:, c, tsl], in_=ps[:])
                for bi in range(4):
                    blk = g * 4 + bi
                    ps = C.bank()
                    proj_tm(wdv, 0, 256, blk, ps[:, 0:256])
                    S.act("copy", out=dV[:, blk, :, 0:128], in_=ps[:, 0:256].rearrange("p (h e) -> p h e", h=2))
            tbl = S.sb("tbl", [128, 64], F32)
            lamt = S.sb("lamt", [128, 256], F32)
            dngs = S.sb("dngs", [128, 128], F32)
            S.dma_group("sp", [(tbl[:], dtbl.partition_broadcast(128)), (lamt[:], dlam.partition_broadcast(128)),
                               (dngs[:], dng.partition_broadcast(128))], C.ctr)
            dT = S.sb("dT", [128, 64], F32)
            S.dve("tensor_tensor", out=dT[:, 1:64], in0=tbl[:, 1:64], in1=tbl[:, 0:63], op=ALU.subtract)
            rel1 = S.sb("rel1", [128, 128], F32)
            S.dve("tensor_scalar", out=rel1[:], in0=rel0[:], scalar1=128.0, scalar2=None, op0=ALU.add)
            thr = t5_thresholds()
            Bt = [[S.sb(f"Bt{hh}{dl}", [128, 128], F32) for dl in range(2)] for hh in range(2)]
            for hh in range(2):
                for dl in range(2):
                    relt = rel0 if dl == 0 else rel1
                    bt = Bt[hh][dl]
                    S.dve("tensor_scalar", out=bt[:], in0=relt[:], scalar1=0.0, scalar2=tbl[:, hh * 32:hh * 32 + 1], op0=ALU.mult, op1=ALU.add)
                    for m in range(1, 32):
                        tmp = C.tf()
                        S.dve("tensor_scalar", out=tmp[:, 0:128], in0=relt[:], scalar1=float(thr[m - 1]), scalar2=dT[:, hh * 32 + m:hh * 32 + m + 1],
                              op0=ALU.is_ge, op1=ALU.mult)
                        S.dve("tensor_tensor", out=bt[:], in0=bt[:], in1=tmp[:, 0:128], op=ALU.add)
                    S.dve("tensor_scalar", out=bt[:], in0=bt[:], scalar1=tbl[:, hh * 32 + 31:hh * 32 + 32], scalar2=8.0, op0=ALU.subtract, op1=ALU.mult)
                    if dl == 0:
                        S.dve("tensor_tensor", out=bt[:], in0=bt[:], in1=maskT[:], op=ALU.add)
            lsc = S.sb("lsc", [128, 8], F32)
            for i in range(2):
                tmp = C.tf()
                S.dve("tensor_tensor", out=tmp[:, 0:64], in0=lamt[:, i * 128:i * 128 + 64], in1=lamt[:, i * 128 + 64:i * 128 + 128], op=ALU.mult)
                S.dve("reduce_sum", out=lsc[:, i:i + 1], in_=tmp[:, 0:64], axis=AX.X)
                S.act("activation", out=lsc[:, 2 + i:3 + i], in_=lsc[:, i:i + 1], func=AF.Exp)
            S.dve("tensor_scalar", out=lsc[:, 4:5], in0=lsc[:, 3:4], scalar1=lsc[:, 2:3], scalar2=-float(lambda_init), op0=ALU.subtract, op1=ALU.add)
            S.dve("tensor_scalar", out=dngs[:], in0=dngs[:], scalar1=1.0 - float(lambda_init), scalar2=None, op0=ALU.mult)
            o1buf = S.sb("o1buf", [128, 2, 4, 128], F32)
            dys = [S.sb(f"dys{i}", [128, 4, 256], F32) for i in range(2)]
            dsc = S.sb("dsc", [128, 64], F32)
            dci = [0]

            def diff_epi(m, g, qb, acc):
                hh, which = m // 2, m % 2
                c0 = (dci[0] % 8) * 8
                dci[0] += 1
                S.dve("reciprocal", out=dsc[:, c0:c0 + 1], in_=acc[:, 128:129])
                if which == 0:
                    S.dve("tensor_scalar", out=o1buf[:, hh, qb, :], in0=acc[:, 0:128], scalar1=dsc[:, c0:c0 + 1], scalar2=None, op0=ALU.mult)
                    return
                ys = dys[g % 2]
                o2 = C.tf()
                S.dve("tensor_scalar", out=o2[:, 0:128], in0=acc[:, 0:128], scalar1=dsc[:, c0:c0 + 1], scalar2=None, op0=ALU.mult)
                S.dve("scalar_tensor_tensor", out=o2[:, 128:256], in0=o2[:, 0:128], scalar=lsc[:, 4:5], in1=o1buf[:, hh, qb, :], op0=ALU.mult, op1=ALU.add)
                S.dve("tensor_tensor", out=o2[:, 256:384], in0=o2[:, 128:256], in1=o2[:, 128:256], op=ALU.mult)
                S.dve("reduce_sum", out=dsc[:, c0 + 1:c0 + 2], in_=o2[:, 256:384], axis=AX.X)
                S.act("activation", out=dsc[:, c0 + 2:c0 + 3], in_=dsc[:, c0 + 1:c0 + 2], func=AF.Ln, scale=1.0 / 128.0, bias=EPS)
                S.act("activation", out=dsc[:, c0 + 3:c0 + 4], in_=dsc[:, c0 + 2:c0 + 3], func=AF.Exp, scale=-0.5)
                S.dve("scalar_tensor_tensor", out=ys[:, qb, hh * 128:(hh + 1) * 128], in0=o2[:, 128:256], scalar=dsc[:, c0 + 3:c0 + 4], in1=dngs[:],
                      op0=ALU.mult, op1=ALU.mult)
                if m == 3 and qb == 3:
                    emit_y(3, y_diff, g, ys)

            attention(4, lambda m: dKT[:, m // 2, :], lambda m: dQp[m][:, :],
                      lambda m, kb: dV[:, kb, m // 2, :], 0.125,
                      lambda m, d: Bt[m // 2][d][:], lambda m: tbl[:, (m // 2) * 32 + 31:(m // 2) * 32 + 32], diff_epi)

        if not fused:
            S.finish()
        print("B built: ins", S.n_ins, "waits", S.n_wait, {k: e.tr.val for k, e in S.engs.items()})
    return nc

def swap_halves(w, nheads, dh):
    K = w.shape[0]
    w4 = w.reshape(K, nheads, 2, dh // 2)
    return np.ascontiguousarray(w4[:, :, ::-1, :]).reshape(K, nheads * dh)


def b_consts(j):
    c = np.zeros((128, 16), np.float32)
    p = np.arange(128)
    c[:, 0] = np.exp(-math.log(10000.0) * ((p % 64) % 32).astype(np.float32) / 32).astype(np.float32)
    c[:, 1] = np.where((p % 64) < 32, -1.0, 1.0)
    c[:, 2] = np.exp(-math.log(10000.0) * (((p - 64) % 32) % 16).astype(np.float32) / 16).astype(np.float32)
    c[:, 3] = np.where(((p - 64) % 32) < 16, -1.0, 1.0)
    lg = np.log1p(-np.exp2(-5.0 - np.arange(4, dtype=np.float32))).astype(np.float32)
    c[:, 4] = lg[2 * j + p // 64]
    c[:, 5] = lg[2 * j]
    c[:, 6] = lg[2 * j + 1]
    return c


def b_inputs(inp, l, b, j, uT_bf16, phases=("ret", "ssd", "mla", "diff")):
    W = inp["w_in"][l]
    o = IN_OFF
    d = {"pos": np.ascontiguousarray(inp["positions"][b:b + 1, :]).astype(np.int32), "cst": b_consts(j)}
    if uT_bf16 is not None:
        d["uT"] = uT_bf16
    if "mla" in phases:
        kr = W[:, o[9]:o[10]]
        d["wmla"] = np.ascontiguousarray(np.concatenate([W[:, o[7]:o[8]], W[:, o[8]:o[9]], kr, swap_halves(kr, 1, 32)], axis=1))
        uq = inp["mla_w_uq"][l].reshape(256, 4, 96)[:, 2 * j:2 * j + 2, :]
        uq_rope_sw = swap_halves(np.ascontiguousarray(uq[:, :, 64:96]).reshape(256, 64), 2, 32)
        d["wuq"] = np.ascontiguousarray(np.concatenate([uq.reshape(256, 192), uq_rope_sw], axis=1))
        d["wukv"] = np.ascontiguousarray(inp["mla_w_ukv"][l].reshape(128, 4, 192)[:, 2 * j:2 * j + 2, :].reshape(128, 384))
        g = np.zeros((128, 4), np.float32)
        g[:, 0:2] = inp["mla_q_norm"][l].reshape(2, 128).T
        g[:, 2] = inp["mla_kv_norm"][l]
        d["gmla"] = g
    if "diff" in phases:
        dq = W[:, o[10]:o[11]][:, j * 256:(j + 1) * 256]
        dk = W[:, o[11]:o[12]][:, j * 256:(j + 1) * 256]
        dv = W[:, o[12]:o[13]][:, j * 256:(j + 1) * 256]
        d["wdiff"] = np.ascontiguousarray(np.concatenate([dq, dk, dv], axis=1))
        d["dlam"] = np.ascontiguousarray(inp["diff_lambda"][l].reshape(1, 256))
        d["dng"] = np.ascontiguousarray(inp["diff_norm"][l].reshape(1, 128))
        d["dtbl"] = np.ascontiguousarray(inp["rel_bias"][:, 2 * j:2 * j + 2].T.reshape(1, 64))
    if "ret" in phases:
        rq = W[:, o[0]:o[1]][:, j * 128:(j + 1) * 128]
        rk = W[:, o[1]:o[2]][:, j * 128:(j + 1) * 128]
        rv = W[:, o[2]:o[3]][:, j * 256:(j + 1) * 256]
        rg = W[:, o[3]:o[4]][:, j * 256:(j + 1) * 256]
        d["wret"] = np.ascontiguousarray(np.concatenate([rq, swap_halves(rq, 2, 64), rk, swap_halves(rk, 2, 64), rv, rg], axis=1))
    if "ssd" in phases:
        sz = W[:, o[4]:o[5]][:, j * 256:(j + 1) * 256]
        xbc = W[:, o[5]:o[6]]
        sx = xbc[:, j * 256:(j + 1) * 256]
        sB = xbc[:, 512 + j * 128:512 + (j + 1) * 128]
        sC = xbc[:, 768 + j * 128:768 + (j + 1) * 128]
        sdt = W[:, o[6]:o[7]][:, j * 4:(j + 1) * 4]
        d["wssd"] = np.ascontiguousarray(np.concatenate([sx, sB, sC, sz, sdt, sdt], axis=1))
        cw = inp["ssm_conv_w"][l]
        cb = inp["ssm_conv_b"][l]
        chans = [slice(j * 256, j * 256 + 128), slice(j * 256 + 128, j * 256 + 256), slice(512 + j * 128, 512 + (j + 1) * 128), slice(768 + j * 128, 768 + (j + 1) * 128)]
        cs = np.zeros((128, 24), np.float32)
        for ci, sl in enumerate(chans):
            cs[:, ci * 4:(ci + 1) * 4] = cw[:, sl].T
            cs[:, 16 + ci] = cb[sl]
        d["cssd"] = cs
        d["rssd"] = np.ascontiguousarray(np.concatenate([inp["ssm_dt_bias"][l][4 * j:4 * j + 4], inp["ssm_a_log"][l][4 * j:4 * j + 4], inp["ssm_d"][l][4 * j:4 * j + 4]]).reshape(1, 12).astype(np.float32))
    return d

from concourse.bass_utils import run_bass_kernel_spmd

DEPTH = 2
_DBG = {}


def pack_gains(d):
    g = np.zeros((128, 64), np.float32)
    for k, off in (("f1", 0), ("mix", 8), ("mixp", 16), ("xa", 24), ("mem", 32), ("f2", 40), ("fin", 48), ("ssm", 56)):
        if k in d:
            v = np.asarray(d[k], np.float32)
            n = v.shape[0] // 128
            g[:, off:off + n] = v.reshape(n, 128).T
    return g


def _c(a):
    return np.ascontiguousarray(a)


def build_fused():
    nc = bass.Bass("TRN2", target_bir_lowering=False)
    es = ExitStack()
    S = Sched(nc, es)
    C = Ctx(S, nbanks=7, nwslots=0)
    tpb = S.ps("tpb", [128, 1024], BF16)
    a_tr = [S.dma_tracker(f"a{i}") for i in range(8)]
    b_tr = [S.dma_tracker(f"b{i}") for i in range(4)]
    cc_tr = [S.dma_tracker(f"cc{i}") for i in range(12)]
    u_tr = [S.dma_tracker(f"ul{i}") for i in range(4)]

    def internal(name, shape, dt):
        return nc.dram_tensor(name, list(shape), dt, kind="Internal").ap()

    h_scr = [internal(f"h_scr{l}", [D, NTA], F32) for l in range(DEPTH)]
    u_src = [[internal(f"u_src{l}_{t}", [D, TB], BF16) for t in range(NTA // TB)] for l in range(DEPTH)]
    u_all = [[internal(f"u_all{l}_{t}", [2 * D, TB], BF16) for t in range(NTA // TB)] for l in range(DEPTH)]
    y_src = [[internal(f"y_src{l}_{i}", [256, SEQ], BF16) for i in range(4)] for l in range(DEPTH)]
    y_all = [[internal(f"y_all{l}_{i}", [512, SEQ], BF16) for i in range(4)] for l in range(DEPTH)]
    base = dict(nc=nc, S=S, C=C, tpb=tpb, a_tr=a_tr, b_tr=b_tr, cc_tr=cc_tr, u_tr=u_tr)
    build_A(False, True, False, fused=dict(base, pfx="A0_", h_src=None, h_dst=h_scr[0], u_src=u_src[0], u_all=u_all[0]))
    for l in range(DEPTH):
        lam0 = 0.8 - 0.6 * math.exp(-0.3 * l)
        build_B(lambda_init=lam0, fused=dict(base, pfx=f"B{l}_", u_all=u_all[l], y_src=y_src[l], y_all=y_all[l]))
        last = (l == DEPTH - 1)
        build_A(True, not last, last, fused=dict(base, pfx=f"A{l + 1}_", h_src=h_scr[l], h_dst=(None if last else h_scr[l + 1]),
                                                 y_all=y_all[l], u_src=(None if last else u_src[l + 1]), u_all=(None if last else u_all[l + 1])))
    S.finish()
    print("FUSED built: ins", S.n_ins, "waits", S.n_wait, {k: e.tr.val for k, e in S.engs.items()})
    es.close()
    return nc


def kernel(**inputs):
    inp = {k: np.asarray(v) for k, v in inputs.items()}
    cores = list(range(8))
    x = inp["x"]
    nc = build_fused()
    in_maps = []
    for c in cores:
        b, j = c // 2, c % 2
        d = {}
        d["A0_hT"] = _c(x[b, j * NTA:(j + 1) * NTA, :].T)
        d["A0_gains"] = pack_gains({"f1": inp["ffn1_norm"][0], "mix": inp["mix_norm"][0]})
        d["A0_f1g"] = inp["ffn1_w_gate"][0]; d["A0_f1u"] = inp["ffn1_w_up"][0]; d["A0_f1d"] = inp["ffn1_w_down"][0]
        for l in range(DEPTH):
            bi = b_inputs(inp, l, b, j, None)
            for k, v in bi.items():
                if k != "uT":
                    d[f"B{l}_{k}"] = v
            last = (l == DEPTH - 1)
            p = f"A{l + 1}_"
            g = {"mixp": inp["mix_norm"][l], "ssm": inp["ssm_norm"][l], "xa": inp["xa_norm"][l], "mem": inp["mem_norm"][l], "f2": inp["ffn2_norm"][l]}
            d[p + "memT"] = _c(inp["mem"][b].T)
            d[p + "wgl"] = _c(inp["w_in"][l][:, IN_OFF[13]:]); d[p + "wbr"] = _c(inp["w_branch"][l].reshape(2048, D)); d[p + "wout"] = inp["w_out"][l]
            d[p + "wq"] = inp["xa_wq"][l]; d[p + "wk"] = inp["xa_wk"][l]; d[p + "wv"] = inp["xa_wv"][l]; d[p + "wo"] = inp["xa_wo"][l]
            d[p + "f2g"] = inp["ffn2_w_gate"][l]; d[p + "f2u"] = inp["ffn2_w_up"][l]; d[p + "f2d"] = inp["ffn2_w_down"][l]
            if not last:
                g["f1"] = inp["ffn1_norm"][l + 1]
                g["mix"] = inp["mix_norm"][l + 1]
                d[p + "f1g"] = inp["ffn1_w_gate"][l + 1]; d[p + "f1u"] = inp["ffn1_w_up"][l + 1]; d[p + "f1d"] = inp["ffn1_w_down"][l + 1]
            else:
                g["fin"] = inp["final_norm"]
            gp = pack_gains(g)
            gp[:, 60] = 1.0 - j
            gp[:, 61] = float(j)
            d[p + "gains"] = gp
        in_maps.append(d)
    res = run_bass_kernel_spmd(nc, in_maps, core_ids=cores).results
    out = np.empty((NB, SEQ, D), np.float32)
    for c in cores:
        b, j = c // 2, c % 2
        out[b, j * NTA:(j + 1) * NTA, :] = np.asarray(res[c][f"A{DEPTH}_outT"]).T
    return out
```

```python
import math
import numpy as np
import ml_dtypes
from contextlib import ExitStack, contextmanager
import concourse.bass as bass
import concourse.mybir as mybir

F32 = mybir.dt.float32
BF16 = mybir.dt.bfloat16
I32 = mybir.dt.int32
ALU = mybir.AluOpType
AF = mybir.ActivationFunctionType
AX = mybir.AxisListType
CC_INC = 1


def _is_ap(x):
    return hasattr(x, "tensor") and hasattr(x, "ap") and hasattr(x, "offset")


def _region(ap):
    t = ap.tensor
    name = t.name
    dims = ap.ap
    off = ap.offset
    sp = str(ap.space)
    if "PSUM" in sp:
        return (name, 0, 128, 0, 1 << 40)
    if "SB" in sp:
        pstep = dims[0][0]
        pcnt = dims[0][1]
        if pstep == 0:
            p0 = 0
            lo = off
            pstep = 1 << 60
        else:
            p0 = off // pstep
            lo = off % pstep
        ext = 0
        for st, cn in dims[1:]:
            ext += abs(st) * (cn - 1)
        return (name, p0, p0 + pcnt, lo, lo + ext + 1)
    else:
        ext = 0
        for st, cn in dims:
            ext += abs(st) * (cn - 1)
        return (name, 0, 1, off, off + ext + 1)


class Tracker:
    def __init__(self, sem, name):
        self.sem = sem
        self.val = 0
        self.name = name


class Eng:
    def __init__(self, name, eng, tracker):
        self.name = name
        self.eng = eng
        self.tr = tracker
        self.known = {}
        self.pending_noinc = False


class Sched:
    def __init__(self, nc, es):
        self.nc = nc
        self.es = es
        self.es0 = es
        self.recs = {}
        self.engs = {}
        for nm, e in (("pe", nc.tensor), ("act", nc.scalar), ("dve", nc.vector), ("pool", nc.gpsimd), ("sp", nc.sync)):
            tr = Tracker(es.enter_context(nc.semaphore("s_" + nm)), nm)
            self.engs[nm] = Eng(nm, e, tr)
        self.n_ins = 0
        self.n_wait = 0
        self._dma_tr = []
        self._names = {}

    def _uniq(self, name):
        k = self._names.get(name, 0)
        self._names[name] = k + 1
        return name if k == 0 else f"{name}__{k}"

    def sb(self, name, shape, dt):
        return self.es.enter_context(self.nc.sbuf_tensor(self._uniq(name), list(shape), dt))

    def ps(self, name, shape, dt):
        return self.es.enter_context(self.nc.psum_tensor(name, list(shape), dt))

    def dma_tracker(self, name):
        tr = Tracker(self.es0.enter_context(self.nc.semaphore("d_" + name)), name)
        self._dma_tr.append(tr)
        return tr

    def _deps(self, reads, writes, same_eng=None):
        deps = {}

        def add(tr, v):
            if deps.get(tr, 0) < v:
                deps[tr] = v

        for ap in reads:
            name, p0, p1, lo, hi = _region(ap)
            for r in self.recs.get(name, ()):
                if r[4] and r[0] < p1 and p0 < r[1] and r[2] < hi and lo < r[3]:
                    add(r[5], r[6])
        for ap in writes:
            name, p0, p1, lo, hi = _region(ap)
            for r in self.recs.get(name, ()):
                if r[0] < p1 and p0 < r[1] and r[2] < hi and lo < r[3]:
                    if same_eng is not None and r[5] is same_eng:
                        continue
                    add(r[5], r[6])
        return deps

    def _record(self, reads, writes, tr, val):
        for ap in writes:
            name, p0, p1, lo, hi = _region(ap)
            lst = self.recs.setdefault(name, [])
            lst[:] = [r for r in lst if not (p0 <= r[0] and r[1] <= p1 and lo <= r[2] and r[3] <= hi)]
            lst.append([p0, p1, lo, hi, True, tr, val])
        for ap in reads:
            name, p0, p1, lo, hi = _region(ap)
            lst = self.recs.setdefault(name, [])
            for r in lst:
                if (not r[4]) and r[5] is tr and r[0] == p0 and r[1] == p1 and r[2] == lo and r[3] == hi:
                    r[6] = val
                    break
            else:
                lst.append([p0, p1, lo, hi, False, tr, val])

    def _emit_waits(self, E, deps):
        for tr, v in deps.items():
            if E.known.get(tr, 0) >= v:
                continue
            if tr is E.tr and E.name == "pe":
                continue
            E.eng.wait_ge(tr.sem, v)
            E.known[tr] = v
            self.n_wait += 1

    def op(self, engname, method, *args, inc=True, extra_reads=(), extra_writes=(), **kwargs):
        E = self.engs[engname]
        writes, reads = [], []
        if "out" in kwargs:
            writes.append(kwargs["out"])
            pos_reads = args
        else:
            writes.append(args[0])
            pos_reads = args[1:]
        for a in pos_reads:
            if _is_ap(a):
                reads.append(a)
        for k, a in kwargs.items():
            if k == "out":
                continue
            if k == "accum_out":
                if a is not None:
                    writes.append(a)
                continue
            if _is_ap(a):
                reads.append(a)
        reads.extend(extra_reads)
        writes.extend(extra_writes)
        deps = self._deps(reads, writes, same_eng=E.tr)
        self._emit_waits(E, deps)
        ins = getattr(E.eng, method)(*args, **kwargs)
        val = E.tr.val + 1
        if inc:
            ins.then_inc(E.tr.sem, 1)
            E.tr.val = val
        self._record(reads, writes, E.tr, val)
        self.n_ins += 1
        return ins

    def dma(self, queue, out, in_, tracker, **kw):
        E = self.engs[queue]
        deps = self._deps([in_], [out])
        self._emit_waits(E, deps)
        ins = E.eng.dma_start(out=out, in_=in_, **kw)
        ins.then_inc(tracker.sem, 16)
        tracker.val += 16
        self._record([in_], [out], tracker, tracker.val)
        self.n_ins += 1
        return ins

    def finish(self, queue="sp"):
        E = self.engs[queue]
        for tr in self._dma_tr:
            if tr.val > 0:
                E.eng.wait_ge(tr.sem, tr.val)
        for e in self.engs.values():
            if e.tr.val > 0 and e is not E:
                E.eng.wait_ge(e.tr.sem, e.tr.val)

    def pe(self, m, *a, **k):
        return self.op("pe", m, *a, **k)

    def act(self, m, *a, **k):
        return self.op("act", m, *a, **k)

    def dve(self, m, *a, **k):
        return self.op("dve", m, *a, **k)

    def pool(self, m, *a, **k):
        return self.op("pool", m, *a, **k)

    def dma_group(self, queue, pairs, tracker, **kw):
        E = self.engs[queue]
        for out, in_ in pairs:
            deps = self._deps([in_], [out])
            self._emit_waits(E, deps)
            E.eng.dma_start(out=out, in_=in_, **kw).then_inc(tracker.sem, 16)
            tracker.val += 16
            self.n_ins += 1
        for out, in_ in pairs:
            self._record([in_], [out], tracker, tracker.val)

    def barrier(self):
        trs = [e.tr for e in self.engs.values()] + list(self._dma_tr)
        for E in self.engs.values():
            for tr in trs:
                if tr.val > 0 and E.known.get(tr, 0) < tr.val:
                    E.eng.wait_ge(tr.sem, tr.val)
                    E.known[tr] = tr.val
                    self.n_wait += 1

    @contextmanager
    def scope(self):
        old = self.es
        with ExitStack() as es2:
            self.es = es2
            try:
                yield
            finally:
                self.barrier()
                self.es = old

    def collective(self, kind, in_ap, out_ap, groups, tracker):
        E = self.engs["pool"]
        deps = self._deps([in_ap], [out_ap])
        self._emit_waits(E, deps)
        ins = E.eng.collective_compute(kind, mybir.AluOpType.bypass, replica_groups=groups, ins=[in_ap.opt()], outs=[out_ap.opt()])
        ins.then_inc(tracker.sem, CC_INC)
        tracker.val += CC_INC
        self._record([in_ap], [out_ap], tracker, tracker.val)
        self.n_ins += 1
        return ins

D = 1024
DFF = 2816
SEQ = 4096
NB = 4
MEM = 256
EPS = 1e-6
IN_SIZES = (256, 256, 512, 512, 512, 1024, 8, 256, 128, 32, 512, 512, 512, 4096)
IN_OFF = [0]
for _s in IN_SIZES:
    IN_OFF.append(IN_OFF[-1] + _s)
NTA = 2048
TB = 1024
NG = 512
WSLOT = 5632


class Ctx:
    def __init__(self, S, nbanks=8, nwslots=4, ntmp=4):
        self.S = S
        self.banks = [S.ps(f"bank{i}", [128, 512], F32) for i in range(nbanks)]
        self._bi = 0
        self.wtr = [S.dma_tracker(f"w{i}") for i in range(6)]
        self._wgen = 0
        self.set_wslots(nwslots)
        self.tmpf = [S.sb(f"tmpf{i}", [128, 512], F32) for i in range(ntmp)]
        self._ti = 0
        self.tmpb = [S.sb(f"tmpb{i}", [128, 512], BF16) for i in range(ntmp)]
        self._tbi = 0
        self.rstd = S.sb("rstd", [128, 512], F32)
        self.ones_f = S.sb("ones_f", [128, 128], F32)
        self.ones_b = S.sb("ones_b", [128, 128], BF16)
        S.pool("memset", self.ones_f[:], 1.0)
        S.pool("memset", self.ones_b[:], 1.0)
        self.ctr = S.dma_tracker("const")

    def set_wslots(self, n):
        self._wgen += 1
        self.wslots = [self.S.sb(f"wslot{self._wgen}_{i}", [128, WSLOT], BF16) for i in range(n)]
        self._wi = 0

    def bank(self):
        b = self.banks[self._bi % len(self.banks)]
        self._bi += 1
        return b

    def tf(self):
        t = self.tmpf[self._ti % len(self.tmpf)]
        self._ti += 1
        return t

    def tb(self):
        t = self.tmpb[self._tbi % len(self.tmpb)]
        self._tbi += 1
        return t

    def load_w(self, w_dram, c0, cw, queue="pool"):
        S = self.S
        K = w_dram.shape[0]
        nk = K // 128
        assert nk * cw <= WSLOT, (nk, cw)
        i = self._wi % len(self.wslots)
        self._wi += 1
        view = self.wslots[i][:, 0:nk * cw].rearrange("p (k c) -> p k c", c=cw)
        src = w_dram.rearrange("(k p) n -> p k n", p=128)[:, :, c0:c0 + cw]
        S.dma(queue, view, src, self.wtr[i])
        return view


def mm_chain(S, out, pairs):
    n = len(pairs)
    for i, (l, r) in enumerate(pairs):
        S.pe("matmul", out, lhsT=l, rhs=r, start=(i == 0), stop=(i == n - 1), inc=(i == n - 1))


def rmsnorm_fm(S, C, src, nch, gain, dst, ntok, dtot):
    for sg in range(ntok // NG):
        sl = slice(sg * NG, (sg + 1) * NG)
        ps = C.bank()
        for dc in range(nch):
            sq = C.tb()
            S.act("activation", out=sq[:], in_=src(dc)[:, sl], func=AF.Square)
            S.pe("matmul", ps[:], lhsT=C.ones_b[:], rhs=sq[:], start=(dc == 0), stop=(dc == nch - 1))
        S.act("activation", out=C.rstd[:], in_=ps[:], func=AF.Sqrt, bias=EPS, scale=1.0 / dtot)
        S.dve("reciprocal", out=C.rstd[:], in_=C.rstd[:])
        for dc in range(nch):
            S.dve("scalar_tensor_tensor", out=dst(dc)[:, sl], in0=src(dc)[:, sl], scalar=gain[:, dc:dc + 1],
                  in1=C.rstd[:], op0=ALU.mult, op1=ALU.mult)


def linear_fm(S, C, w_dram, xin, ntok, consume, ctile=512):
    K, N = w_dram.shape
    nk = K // 128
    ctile = min(ctile, (WSLOT // nk) // 128 * 128)
    for c0 in range(0, N, ctile):
        cw = min(ctile, N - c0)
        wt = C.load_w(w_dram, c0, cw)
        for sg in range(ntok // NG):
            sl = slice(sg * NG, (sg + 1) * NG)
            for oc in range(cw // 128):
                ps = C.bank()
                mm_chain(S, ps[:], [(wt[:, k, oc * 128:(oc + 1) * 128], xin(k)[:, sl]) for k in range(nk)])
                consume(c0 // 128 + oc, ps, sl)


def ffn_fm(S, C, wg, wu, wd, xn, h, act, ntok):
    for c0 in range(0, DFF, 512):
        cw = min(512, DFF - c0)
        wgt = C.load_w(wg, c0, cw)
        wut = C.load_w(wu, c0, cw)
        for sg in range(ntok // NG):
            sl = slice(sg * NG, (sg + 1) * NG)
            for oc in range(cw // 128):
                fc = c0 // 128 + oc
                pg = C.bank()
                mm_chain(S, pg[:], [(wgt[:, k, oc * 128:(oc + 1) * 128], xn[:, k, sl]) for k in range(8)])
                pu = C.bank()
                mm_chain(S, pu[:], [(wut[:, k, oc * 128:(oc + 1) * 128], xn[:, k, sl]) for k in range(8)])
                t = C.tf()
                S.act("activation", out=t[:], in_=pg[:], func=AF.Silu)
                S.dve("tensor_tensor", out=act[:, fc, sl], in0=t[:], in1=pu[:], op=ALU.mult)

    def upd(dc, ps, sl):
        S.dve("scalar_tensor_tensor", out=h[:, dc, sl], in0=ps[:], scalar=0.5, in1=h[:, dc, sl], op0=ALU.mult, op1=ALU.add)

    linear_fm(S, C, wd, lambda k: act[:, k, :], ntok, upd, ctile=256)


PAIRS = [[0, 1], [2, 3], [4, 5], [6, 7]]


def build_A(has_post, has_pre, final, fused=None):
    nc = fused["nc"] if fused else bass.Bass("TRN2", target_bir_lowering=False)
    pfx = fused["pfx"] if fused else ""

    def din(name, shape, dt=F32):
        return nc.dram_tensor(pfx + name, list(shape), dt, kind="ExternalInput").ap()

    def dout(name, shape, dt=F32):
        return nc.dram_tensor(pfx + name, list(shape), dt, kind="ExternalOutput").ap()

    if fused and fused.get("h_src") is not None:
        hT = fused["h_src"]
    else:
        hT = din("hT", [D, NTA])
    gains_d = din("gains", [128, 64])
    if has_post:
        if not fused:
            yT = din("yT", [2048, NTA])
        memT = din("memT", [D, MEM])
        wgl = din("wgl", [D, 4096]); wbr = din("wbr", [2048, D]); wout = din("wout", [D, D])
        wq = din("wq", [D, D]); wk = din("wk", [D, D]); wv = din("wv", [D, D]); wo = din("wo", [D, D])
        f2g = din("f2g", [D, DFF]); f2u = din("f2u", [D, DFF]); f2d = din("f2d", [DFF, D])
    if has_pre:
        f1g = din("f1g", [D, DFF]); f1u = din("f1u", [D, DFF]); f1d = din("f1d", [DFF, D])
        if fused:
            h1T = fused["h_dst"]
        else:
            h1T = dout("h1T", [D, NTA])
            uT = dout("uT", [D, NTA], BF16)
    if final:
        outT = dout("outT", [D, NTA])

    with ExitStack() as es:
        if fused:
            S, C = fused["S"], fused["C"]
            es.enter_context(S.scope())
            C.set_wslots(4 if has_post else 6)
        else:
            S = Sched(nc, es)
            C = Ctx(S)
        gains = S.sb("gains_sb", [128, 64], F32)
        S.dma_group("sp", [(gains[:], gains_d)], C.ctr)
        G_F1, G_MIX, G_MIXP, G_XA, G_MEM, G_F2, G_FIN, G_SSM = 0, 8, 16, 24, 32, 40, 48, 56
        h = S.sb("h", [128, 8, TB], F32)
        xn = S.sb("xn", [128, 8, TB], BF16)
        scr = S.sb("scr", [128, 24, TB], BF16)
        if fused:
            htr, otr, otr2, ytr, ystr, mtr, ytr2, ystr2 = fused["a_tr"]
        else:
            htr = S.dma_tracker("h")
            otr = S.dma_tracker("o")
            otr2 = S.dma_tracker("o2")
            ytr = S.dma_tracker("y")
            ystr = S.dma_tracker("ys")
            mtr = S.dma_tracker("mem")
        if has_post:
            ssm_f = S.sb("ssm_f", [128, 4, TB], F32)
            macc = S.sb("macc", [128, 4, 512], F32)
            pT = S.sb("pT", [128, 2, NG], BF16)
            rl = S.sb("rl", [128, NG], F32)
            mnT = S.sb("mnT", [128, 8, MEM], BF16)
            kT = S.sb("kT", [128, 8, MEM], BF16)
            vtm = S.sb("vtm", [128, 2, D], BF16)
            with S.scope():
                mem_f = S.sb("mem_f", [128, 8, MEM], F32)
                S.dma("sp", mem_f[:], memT.rearrange("(c p) t -> p c t", p=128), mtr)
                psm = C.bank()
                for dc in range(8):
                    sq = C.tf()
                    S.dve("tensor_tensor", out=sq[:, 0:MEM], in0=mem_f[:, dc, :], in1=mem_f[:, dc, :], op=ALU.mult)
                    S.pe("matmul", psm[:, 0:MEM], lhsT=C.ones_f[:], rhs=sq[:, 0:MEM], start=(dc == 0), stop=(dc == 7))
                S.act("activation", out=C.rstd[:, 0:MEM], in_=psm[:, 0:MEM], func=AF.Sqrt, bias=EPS, scale=1.0 / D)
                S.dve("reciprocal", out=C.rstd[:, 0:MEM], in_=C.rstd[:, 0:MEM])
                for dc in range(8):
                    S.dve("scalar_tensor_tensor", out=mnT[:, dc, :], in0=mem_f[:, dc, :], scalar=gains[:, G_MEM + dc:G_MEM + dc + 1],
                          in1=C.rstd[:, 0:MEM], op0=ALU.mult, op1=ALU.mult)
            for c0 in range(0, D, 512):
                wt = C.load_w(wk, c0, 512)
                for oc in range(4):
                    ps = C.bank()
                    mm_chain(S, ps[:, 0:MEM], [(wt[:, k, oc * 128:(oc + 1) * 128], mnT[:, k, :]) for k in range(8)])
                    S.act("copy", out=kT[:, c0 // 128 + oc, :], in_=ps[:, 0:MEM])
            for c0 in range(0, D, 512):
                wt = C.load_w(wv, c0, 512)
                for mc in range(2):
                    ps = C.bank()
                    mm_chain(S, ps[:], [(mnT[:, k, mc * 128:(mc + 1) * 128], wt[:, k, :]) for k in range(8)])
                    S.act("copy", out=vtm[:, mc, c0:c0 + 512], in_=ps[:])

        for tb in range(NTA // TB):
            t0 = tb * TB
            S.dma("sp", h[:], hT.rearrange("(c p) t -> p c t", p=128)[:, :, t0:t0 + TB], htr)
            if has_post:
                ybuf = scr[:, 0:16, :]
                merged = scr[:, 16:24, :]
                rmsnorm_fm(S, C, lambda dc: h[:, dc, :], 8, gains[:, G_MIXP:G_MIXP + 8], lambda dc: xn[:, dc, :], TB, D)
                if not fused:
                    yv = yT.rearrange("(c p) t -> p c t", p=128)
                    S.dma("pool", scr[:, 0:4, :], yv[:, 0:4, t0:t0 + TB], ytr)
                    S.dma("pool", scr[:, 8:16, :], yv[:, 8:16, t0:t0 + TB], ytr)
                    S.dma("sp", ssm_f[:], yv[:, 4:8, t0:t0 + TB], ystr)
                else:
                    stg = [(scr[:, 16 + 4 * q:20 + 4 * q, 0:NG], scr[:, 16 + 4 * q:20 + 4 * q, NG:2 * NG]) for q in range(2)]
                    stg_tr = [(ytr, ystr), (ytr2, ystr2)]
                    rnd = 0
                    for i in range(4):
                        yv = fused["y_all"][i].rearrange("(c p) t -> p c t", p=128)
                        for sg in range(TB // NG):
                            c_lo = t0 + sg * NG
                            ylo, yhi = stg[rnd % 2]
                            tl, th = stg_tr[rnd % 2]
                            rnd += 1
                            S.dma("sp", ylo, yv[:, :, c_lo:c_lo + NG], tl)
                            S.dma("sp", yhi, yv[:, :, NTA + c_lo:NTA + c_lo + NG], th)
                            dst = ssm_f[:, :, sg * NG:(sg + 1) * NG] if i == 1 else scr[:, 4 * i:4 * i + 4, sg * NG:(sg + 1) * NG]
                            S.dve("tensor_scalar", out=dst, in0=ylo, scalar1=gains[:, 60:61], scalar2=None, op0=ALU.mult)
                            S.dve("scalar_tensor_tensor", out=dst, in0=yhi, scalar=gains[:, 61:62], in1=dst, op0=ALU.mult, op1=ALU.add)
                rmsnorm_fm(S, C, lambda dc: ssm_f[:, dc, :], 4, gains[:, G_SSM:G_SSM + 4], lambda dc: scr[:, 4 + dc, :], TB, 512)
                for c0 in range(0, D, 256):
                    for i in range(4):
                        wg_t = C.load_w(wgl, i * 1024 + c0, 256)
                        wb_t = C.load_w(wbr[i * 512:(i + 1) * 512, :], c0, 256)
                        for sg in range(TB // NG):
                            sl = slice(sg * NG, (sg + 1) * NG)
                            for oc in range(2):
                                dc = c0 // 128 + oc
                                ma = macc[:, sg * 2 + oc, :]
                                pg = C.bank()
                                mm_chain(S, pg[:], [(wg_t[:, k, oc * 128:(oc + 1) * 128], xn[:, k, sl]) for k in range(8)])
                                pz = C.bank()
                                mm_chain(S, pz[:], [(wb_t[:, k, oc * 128:(oc + 1) * 128], scr[:, 4 * i + k, sl]) for k in range(4)])
                                sg_t = C.tf()
                                S.act("activation", out=sg_t[:], in_=pg[:], func=AF.Sigmoid)
                                if i == 0:
                                    S.dve("tensor_tensor", out=ma, in0=sg_t[:], in1=pz[:], op=ALU.mult)
                                else:
                                    S.dve("tensor_tensor", out=sg_t[:], in0=sg_t[:], in1=pz[:], op=ALU.mult)
                                    if i < 3:
                                        S.dve("tensor_tensor", out=ma, in0=ma, in1=sg_t[:], op=ALU.add)
                                    else:
                                        S.dve("tensor_tensor", out=merged[:, dc, sl], in0=ma, in1=sg_t[:], op=ALU.add)

                def add_h(dc, ps, sl):
                    S.dve("tensor_tensor", out=h[:, dc, sl], in0=ps[:], in1=h[:, dc, sl], op=ALU.add)

                linear_fm(S, C, wout, lambda k: merged[:, k, :], TB, add_h)
                rmsnorm_fm(S, C, lambda dc: h[:, dc, :], 8, gains[:, G_XA:G_XA + 8], lambda dc: xn[:, dc, :], TB, D)
                qT = scr[:, 0:8, :]
                oT = scr[:, 8:16, :]

                def put_q(oc, ps, sl):
                    S.act("copy", out=qT[:, oc, sl], in_=ps[:])

                linear_fm(S, C, wq, lambda k: xn[:, k, :], TB, put_q)
                for hd in range(4):
                    for sg in range(TB // NG):
                        sl = slice(sg * NG, (sg + 1) * NG)
                        for mc in range(2):
                            ps = C.bank()
                            mm_chain(S, ps[:], [(kT[:, 2 * hd + dk, mc * 128:(mc + 1) * 128], qT[:, 2 * hd + dk, sl]) for dk in range(2)])
                            S.act("activation", out=pT[:, mc, :], in_=ps[:], func=AF.Exp, scale=1.0 / 16.0)
                        pl = C.bank()
                        mm_chain(S, pl[:], [(C.ones_b[:], pT[:, mc, :]) for mc in range(2)])
                        S.dve("reciprocal", out=rl[:], in_=pl[:])
                        for dvc in range(2):
                            po = C.bank()
                            mm_chain(S, po[:], [(vtm[:, mc, hd * 256 + dvc * 128: hd * 256 + (dvc + 1) * 128], pT[:, mc, :]) for mc in range(2)])
                            S.dve("tensor_tensor", out=oT[:, 2 * hd + dvc, sl], in0=po[:], in1=rl[:], op=ALU.mult)
                linear_fm(S, C, wo, lambda k: oT[:, k, :], TB, add_h)
                rmsnorm_fm(S, C, lambda dc: h[:, dc, :], 8, gains[:, G_F2:G_F2 + 8], lambda dc: xn[:, dc, :], TB, D)
                ffn_fm(S, C, f2g, f2u, f2d, xn, h, scr[:, 0:22, :], TB)
            if has_pre:
                rmsnorm_fm(S, C, lambda dc: h[:, dc, :], 8, gains[:, G_F1:G_F1 + 8], lambda dc: xn[:, dc, :], TB, D)
                ffn_fm(S, C, f1g, f1u, f1d, xn, h, scr[:, 0:22, :], TB)
                S.dma("sp", h1T.rearrange("(c p) t -> p c t", p=128)[:, :, t0:t0 + TB], h[:], otr)
                rmsnorm_fm(S, C, lambda dc: h[:, dc, :], 8, gains[:, G_MIX:G_MIX + 8], lambda dc: xn[:, dc, :], TB, D)
                if fused:
                    S.dma("sp", fused["u_src"][tb].rearrange("(c p) t -> p c t", p=128), xn[:], otr2)
                    S.collective("AllGather", fused["u_src"][tb], fused["u_all"][tb], PAIRS, fused["cc_tr"].pop())
                else:
                    S.dma("sp", uT.rearrange("(c p) t -> p c t", p=128)[:, :, t0:t0 + TB], xn[:], otr2)
            if final:
                rmsnorm_fm(S, C, lambda dc: h[:, dc, :], 8, gains[:, G_FIN:G_FIN + 8], lambda dc: h[:, dc, :], TB, D)
                S.dma("sp", outT.rearrange("(c p) t -> p c t", p=128)[:, :, t0:t0 + TB], h[:], otr)
        if not fused:
            S.finish()
        print("A built: ins", S.n_ins, "waits", S.n_wait, {k: e.tr.val for k, e in S.engs.items()})
    return nc

NQG = SEQ // NG
NBLK = SEQ // 128
TWO_PI = 2.0 * math.pi
MAGIC = 12582912.0
SKEW = 3


def t5_thresholds():
    n = np.arange(0, 512, dtype=np.int64)
    nf = np.maximum(n, 1).astype(np.float32)
    large = 16 + (np.log(nf / np.float32(16)) / np.float32(math.log(128 / 16)) * np.float32(16)).astype(np.int32)
    bucket = np.where(n < 16, n, np.minimum(large, 31))
    return [int(np.argmax(bucket >= m)) for m in range(1, 32)]


def build_B(phases=("ret", "ssd", "mla", "diff"), lambda_init=0.2, fused=None):
    nc = fused["nc"] if fused else bass.Bass("TRN2", target_bir_lowering=False)
    pfx = fused["pfx"] if fused else ""

    def din(name, shape, dt=F32):
        return nc.dram_tensor(pfx + name, list(shape), dt, kind="ExternalInput").ap()

    def dout(name, shape, dt=F32):
        if fused:
            return None
        return nc.dram_tensor(pfx + name, list(shape), dt, kind="ExternalOutput").ap()

    if not fused:
        uT_d = din("uT", [D, SEQ], BF16)
    pos_d = din("pos", [1, SEQ], mybir.dt.int32)
    cst_d = din("cst", [128, 16])
    if "mla" in phases:
        wmla = din("wmla", [D, 448])
        wuq = din("wuq", [256, 256])
        wukv = din("wukv", [128, 384])
        gmla = din("gmla", [128, 4])
        y_mla = dout("y_mla", [SEQ, 256])
    if "diff" in phases:
        wdiff = din("wdiff", [D, 768])
        dlam = din("dlam", [1, 256])
        dng = din("dng", [1, 128])
        dtbl = din("dtbl", [1, 64])
        y_diff = dout("y_diff", [SEQ, 256])
    if "ret" in phases:
        wret = din("wret", [D, 1024])
        y_ret = dout("y_ret", [SEQ, 256])
    if "ssd" in phases:
        wssd = din("wssd", [D, 776])
        cssd = din("cssd", [128, 24])
        rssd = din("rssd", [1, 12])
        y_ssd = dout("y_ssd", [SEQ, 256])

    with ExitStack() as es:
        if fused:
            S, C = fused["S"], fused["C"]
            es.enter_context(S.scope())
            C.set_wslots(3)
            tpb = fused["tpb"]
            utr, ptr_, otr0, otr1 = fused["b_tr"]
            otr = [otr0, otr1]
        else:
            S = Sched(nc, es)
            C = Ctx(S, nbanks=7, nwslots=3)
            tpb = S.ps("tpb", [128, 1024], BF16)
            utr = S.dma_tracker("u")
            ptr_ = S.dma_tracker("pos")
            otr = [S.dma_tracker(f"o{i}") for i in range(2)]
        cst = S.sb("cst_sb", [128, 16], F32)
        S.dma_group("sp", [(cst[:], cst_d)], C.ctr)
        uT = S.sb("uT_sb", [128, 8, SEQ], BF16)
        if fused:
            def load_u(tb):
                for r in range(2):
                    S.dma("sp", uT[:, :, r * NTA + tb * TB:r * NTA + (tb + 1) * TB],
                          fused["u_all"][tb][r * D:(r + 1) * D, :].rearrange("(c p) t -> p c t", p=128), fused["u_tr"][r * 2 + tb])

            load_u(0)
            load_u(1)
            late_u = [False]

            def load_late_u():
                if late_u[0]:
                    late_u[0] = False
                    load_u(1)
        else:
            S.dma("sp", uT[:], uT_d.rearrange("(c p) t -> p c t", p=128), utr)

            def load_late_u():
                pass
        yts = [S.sb(f"yts{i}", [128, 2, NG], BF16) for i in range(2)]
        yti = [0]

        def emit_y(branch, y_dram, g, ys):
            if not fused:
                S.dma("sp", y_dram[g * NG:(g + 1) * NG, :].rearrange("(q p) c -> p q c", p=128), ys[:], otr[g % 2])
                return
            yt = yts[yti[0] % 2]
            tr_ = otr[yti[0] % 2]
            yti[0] += 1
            for fc in range(2):
                ps = C.bank()
                for qb in range(4):
                    S.pe("transpose", ps[:, qb * 128:(qb + 1) * 128], ys[:, qb, fc * 128:(fc + 1) * 128], ident_f[:])
                S.act("copy", out=yt[:, fc, :], in_=ps[:])
            S.dma("sp", fused["y_src"][branch].rearrange("(c p) t -> p c t", p=128)[:, :, g * NG:(g + 1) * NG], yt[:], tr_)
            if g == NQG - 1:
                S.collective("AllGather", fused["y_src"][branch], fused["y_all"][branch], PAIRS, fused["cc_tr"].pop())
        ident_b = S.sb("ident_b", [128, 128], BF16)
        S.pool("memset", ident_b[:], 0.0)
        S.pool("affine_select", out=ident_b[:], in_=ident_b[:], pattern=[[-1, 128]], compare_op=ALU.not_equal, fill=1.0, base=0, channel_multiplier=1)
        ident_f = S.sb("ident_f", [128, 128], F32)
        S.pool("memset", ident_f[:], 0.0)
        S.pool("affine_select", out=ident_f[:], in_=ident_f[:], pattern=[[-1, 128]], compare_op=ALU.not_equal, fill=1.0, base=0, channel_multiplier=1)
        rel_i = S.sb("rel_i", [128, 128], I32)
        rel0 = S.sb("rel0", [128, 128], F32)
        S.pool("iota", rel_i[:], pattern=[[1, 128]], base=0, channel_multiplier=-1)
        S.pool("tensor_copy", out=rel0[:], in_=rel_i[:])
        ptr = ptr_

        def alloc_rope():
            return dict(pos_i=S.sb("pos_i", [128, NG], I32), pos_f=S.sb("pos_f", [128, NG], F32), tang=S.sb("tang", [128, NG], F32),
                        tcos=S.sb("tcos", [128, NG], F32), tsin=S.sb("tsin", [128, NG], F32))

        def rope_tables(T, g, inv_col, sign_col):
            pos_i, pos_f, tang, tcos, tsin = T["pos_i"], T["pos_f"], T["tang"], T["tcos"], T["tsin"]
            S.dma("sp", pos_i[:], pos_d[:, g * NG:(g + 1) * NG].partition_broadcast(128), ptr)
            S.dve("tensor_copy", out=pos_f[:], in_=pos_i[:])
            for (dst, shift) in ((tsin, 0.0), (tcos, math.pi / 2)):
                S.dve("tensor_scalar", out=tang[:], in0=pos_f[:], scalar1=cst[:, inv_col:inv_col + 1], scalar2=shift, op0=ALU.mult, op1=ALU.add)
                S.dve("tensor_scalar", out=dst[:], in0=tang[:], scalar1=1.0 / TWO_PI, scalar2=MAGIC, op0=ALU.mult, op1=ALU.add)
                S.dve("tensor_scalar", out=dst[:], in0=dst[:], scalar1=MAGIC, scalar2=-TWO_PI, op0=ALU.subtract, op1=ALU.mult)
                S.dve("tensor_tensor", out=tang[:], in0=tang[:], in1=dst[:], op=ALU.add)
                S.dve("tensor_scalar", out=tang[:], in0=tang[:], scalar1=math.pi, scalar2=-math.pi, op0=ALU.min, op1=ALU.max)
                S.act("activation", out=dst[:], in_=tang[:], func=AF.Sin)
            S.dve("tensor_scalar", out=tsin[:], in0=tsin[:], scalar1=cst[:, sign_col:sign_col + 1], scalar2=None, op0=ALU.mult)

        def proj_fm(wt, col0, ncols, tsl, ps_ap):
            mm_chain(S, ps_ap, [(wt[:, k, col0:col0 + ncols], uT[:, k, tsl]) for k in range(8)])

        def proj_tm(wt, c0, c1, blk, ps_ap):
            mm_chain(S, ps_ap, [(uT[:, k, blk * 128:(blk + 1) * 128], wt[:, k, c0:c1]) for k in range(8)])

        maskT = S.sb("maskT", [128, 128], F32)
        S.dve("tensor_scalar", out=maskT[:], in0=rel0[:], scalar1=0.0, scalar2=-30000.0, op0=ALU.is_lt, op1=ALU.mult)

        pti = [0]
        att_id = [0]

        def attention(nmaps, KT_of, QT_of, V_of, scale, prebias, exp_bias, epilogue):
            att_id[0] += 1
            PT = [S.sb(f"PT{att_id[0]}_{i}", [128, NG], BF16) for i in range(4)]

            def next_PT():
                t = PT[pti[0] % 4]
                pti[0] += 1
                return t

            accb = [C.banks[3], C.banks[4], C.banks[5], C.banks[6]]
            stb = [C.banks[0], C.banks[1], C.banks[2]]
            acc = [accb[qb][:, 0:129] for qb in range(4)]
            units = [(g, m, kb) for g in range(NQG) for m in range(nmaps) for kb in range(4 * g + 4)]

            def score_part(i):
                g, m, kb = units[i]
                r = max(0, kb - 4 * g)
                qlo = r * 128
                st = stb[i % 3]
                S.pe("matmul", st[:, qlo:NG], lhsT=KT_of(m)[:, kb * 128:(kb + 1) * 128], rhs=QT_of(m)[:, g * NG + qlo:(g + 1) * NG],
                     start=True, stop=True)
                for qb in range(r, 4):
                    delta = 4 * g + qb - kb
                    if delta <= 1:
                        pb = prebias(m, delta)
                        if pb is not None:
                            S.dve("tensor_tensor", out=st[:, qb * 128:(qb + 1) * 128], in0=st[:, qb * 128:(qb + 1) * 128], in1=pb, op=ALU.add)
                pt = next_PT()
                eb = exp_bias(m)
                if eb is None:
                    S.act("activation", out=pt[:, qlo:NG], in_=st[:, qlo:NG], func=AF.Exp, scale=scale)
                else:
                    S.act("activation", out=pt[:, qlo:NG], in_=st[:, qlo:NG], func=AF.Exp, scale=scale, bias=eb)
                return pt

            def pv_part(i, pt):
                g, m, kb = units[i]
                r = max(0, kb - 4 * g)
                for qb in range(r, 4):
                    S.pe("matmul", acc[qb], lhsT=pt[:, qb * 128:(qb + 1) * 128], rhs=V_of(m, kb),
                         start=(kb == 0), stop=(kb == 4 * g + qb))
                if kb == 4 * g + 3:
                    for qb in range(4):
                        epilogue(m, g, qb, acc[qb])

            pend = []
            for i in range(len(units)):
                pend.append((i, score_part(i)))
                if len(pend) > SKEW:
                    pv_part(*pend.pop(0))
            while pend:
                pv_part(*pend.pop(0))

        if "ssd" in phases:
          with S.scope():
            ws = C.load_w(wssd, 0, 512)
            wz = C.load_w(wssd, 512, 264)
            cs_t = S.sb("cssd_sb", [128, 24], F32)
            rs_t = S.sb("rssd_sb", [128, 12], F32)
            S.dma_group("sp", [(cs_t[:], cssd), (rs_t[:], rssd.partition_broadcast(128))], C.ctr)
            load_late_u()
            Aneg = S.sb("Aneg", [128, 4], F32)
            S.act("activation", out=Aneg[:], in_=rs_t[:, 4:8], func=AF.Exp)
            S.dve("tensor_scalar", out=Aneg[:], in0=Aneg[:], scalar1=-1.0, scalar2=None, op0=ALU.mult)
            causT = S.sb("causT", [128, 128], F32)
            S.dve("tensor_scalar", out=causT[:], in0=rel0[:], scalar1=0.0, scalar2=None, op0=ALU.is_ge)
            BT = S.sb("sBT", [128, SEQ], BF16)
            CT = S.sb("sCT", [128, SEQ], BF16)
            hst = S.sb("hst", [128, 256], F32)
            hbf = S.sb("hbf", [128, 256], BF16)
            S.pool("memset", hst[:], 0.0)
            pre = S.sb("spre", [128, 4, 3 + NG], F32)
            S.pool("memset", pre[:, :, 0:3], 0.0)
            xf = S.sb("sxf", [128, 2, NG], F32)
            zs2 = [S.sb(f"szs{i}", [128, 4, 256], F32) for i in range(2)]
            dtt2 = [S.sb(f"sdtt{i}", [128, 4, 4], F32) for i in range(2)]
            dA2 = [S.sb(f"sdA{i}", [128, 4, 4], F32) for i in range(2)]
            xtm2 = [S.sb(f"sxtm{i}", [128, 4, 256], F32) for i in range(2)]
            Btm2 = [S.sb(f"sBtm{i}", [128, 4, 128], BF16) for i in range(2)]
            dec = S.sb("sdec", [128, 4, 128], F32)
            MT = S.sb("sMT", [128, 4, 128], BF16)
            xdt = S.sb("sxdt", [128, 256], BF16)
            xd2b = [S.sb(f"sxd2{i}", [128, 256], BF16) for i in range(2)]
            ysbb = [S.sb(f"sysb{i}", [128, 256], F32) for i in range(2)]
            hbfb = [hbf, S.sb("hbf1", [128, 256], BF16)]
            cssb = [S.sb(f"scs{i}", [128, 16], F32) for i in range(2)]
            scs = [S.sb(f"ssc{i}", [128, 16], F32) for i in range(2)]
            sys_ = [S.sb(f"sys{i}", [128, 4, 256], F32) for i in range(2)]

            def ssd_project(g):
                tsl = slice(g * NG, (g + 1) * NG)
                zs, dtt, dA, xtm, Btm = zs2[g % 2], dtt2[g % 2], dA2[g % 2], xtm2[g % 2], Btm2[g % 2]
                for ci in range(4):
                    ps = C.bank()
                    proj_fm(ws, ci * 128, 128, tsl, ps[:])
                    S.act("copy", out=pre[:, ci, 3:3 + NG], in_=ps[:])
                for ci in range(4):
                    acc = C.tf()
                    S.dve("tensor_scalar", out=acc[:], in0=pre[:, ci, 0:NG], scalar1=cs_t[:, ci * 4:ci * 4 + 1], scalar2=None, op0=ALU.mult)
                    for k in range(1, 4):
                        S.dve("scalar_tensor_tensor", out=acc[:], in0=pre[:, ci, k:k + NG], scalar=cs_t[:, ci * 4 + k:ci * 4 + k + 1], in1=acc[:],
                              op0=ALU.mult, op1=ALU.add)
                    dst = xf[:, ci, :] if ci < 2 else (BT[:, tsl] if ci == 2 else CT[:, tsl])
                    S.act("activation", out=dst, in_=acc[:], func=AF.Silu, bias=cs_t[:, 16 + ci:17 + ci])
                S.dve("tensor_copy", out=pre[:, :, 0:3], in_=pre[:, :, NG:NG + 3])
                for bi in range(4):
                    blk = g * 4 + bi
                    ps = C.bank()
                    proj_tm(wz, 0, 264, blk, ps[:, 0:264])
                    S.act("activation", out=zs[:, bi, :], in_=ps[:, 0:256], func=AF.Silu)
                    S.dve("tensor_tensor", out=dtt[:, bi, :], in0=ps[:, 256:260], in1=rs_t[:, 0:4], op=ALU.add)
                S.act("activation", out=dtt[:], in_=dtt[:], func=AF.Exp)
                S.act("activation", out=dtt[:], in_=dtt[:], func=AF.Ln, bias=1.0)
                for bi in range(4):
                    S.dve("tensor_tensor", out=dA[:, bi, :], in0=dtt[:, bi, :], in1=Aneg[:], op=ALU.mult)
                for bi in range(4):
                    blk = g * 4 + bi
                    ps = C.bank()
                    for ci in range(2):
                        S.pe("transpose", ps[:, ci * 128:(ci + 1) * 128], xf[:, ci, bi * 128:(bi + 1) * 128], ident_f[:])
                    S.act("copy", out=xtm[:, bi, :], in_=ps[:, 0:256])
                    S.pe("transpose", tpb[:, bi * 128:(bi + 1) * 128], BT[:, blk * 128:(blk + 1) * 128], ident_b[:])
                    S.act("copy", out=Btm[:, bi, :], in_=tpb[:, bi * 128:(bi + 1) * 128])

            def ssd_stage_a(n):
                g, bi = n // 4, n % 4
                dtt, dA, xtm = dtt2[g % 2], dA2[g % 2], xtm2[g % 2]
                csl = slice(n * 128, (n + 1) * 128)
                dAc = dA[:, bi, :]
                cb = cssb[n % 2]
                sc = scs[n % 2]
                pc = C.bank()
                S.pe("matmul", pc[:, 0:4], lhsT=causT[:], rhs=dAc, start=True, stop=True)
                S.pe("matmul", pc[:, 8:12], lhsT=C.ones_f[:], rhs=dAc, start=True, stop=True)
                S.act("copy", out=cb[:, 0:12], in_=pc[:, 0:12])
                S.act("activation", out=sc[:, 0:4], in_=cb[:, 0:4], func=AF.Exp)
                S.dve("tensor_tensor", out=sc[:, 4:8], in0=cb[:, 8:12], in1=cb[:, 0:4], op=ALU.subtract)
                S.act("activation", out=sc[:, 4:8], in_=sc[:, 4:8], func=AF.Exp)
                S.act("activation", out=sc[:, 8:12], in_=cb[:, 8:12], func=AF.Exp)
                S.dve("tensor_tensor", out=sc[:, 12:16], in0=sc[:, 4:8], in1=dtt[:, bi, :], op=ALU.mult)
                pz = C.bank()
                dab = C.tf()
                dab4 = dab[:].rearrange("p (k l) -> p k l", k=4)
                S.dve("tensor_copy", out=dab4, in_=dAc.unsqueeze(2).to_broadcast([128, 4, 128]))
                for k in range(4):
                    S.pe("matmul", pz[:, k * 128:(k + 1) * 128], lhsT=dab[:, k * 128:(k + 1) * 128], rhs=causT[:], start=True, stop=True)
                for k in range(4):
                    S.dve("tensor_scalar", out=dec[:, k, :], in0=pz[:, k * 128:(k + 1) * 128], scalar1=cb[:, k:k + 1], scalar2=0.0,
                          op0=ALU.subtract, op1=ALU.min)
                S.act("activation", out=dec[:], in_=dec[:], func=AF.Exp)
                pcb = C.bank()
                S.pe("matmul", pcb[:, 0:128], lhsT=BT[:, csl], rhs=CT[:, csl], start=True, stop=True)
                cbm = C.tf()
                S.dve("tensor_tensor", out=cbm[:, 0:128], in0=pcb[:, 0:128], in1=causT[:], op=ALU.mult)
                S.dve("tensor_tensor", out=MT[:], in0=dec[:], in1=cbm[:, 0:128].unsqueeze(1).to_broadcast([128, 4, 128]), op=ALU.mult)
                xd2 = xd2b[n % 2]
                x3 = xtm[:, bi, :].rearrange("p (k e) -> p k e", k=4)
                S.dve("tensor_tensor", out=xdt[:].rearrange("p (k e) -> p k e", k=4), in0=x3,
                      in1=dtt[:, bi, :].unsqueeze(2).to_broadcast([128, 4, 64]), op=ALU.mult)
                S.dve("tensor_tensor", out=xd2[:].rearrange("p (k e) -> p k e", k=4), in0=x3,
                      in1=sc[:, 12:16].unsqueeze(2).to_broadcast([128, 4, 64]), op=ALU.mult)
                py = C.bank()
                for k in range(4):
                    ks = slice(k * 64, (k + 1) * 64)
                    S.pe("matmul", py[:, ks], lhsT=MT[:, k, :], rhs=xdt[:, ks], start=True, stop=True)
                S.act("copy", out=ysbb[n % 2][:], in_=py[:, 0:256])

            def ssd_stage_b(n):
                g, bi = n // 4, n % 4
                zs, xtm, Btm = zs2[g % 2], xtm2[g % 2], Btm2[g % 2]
                csl = slice(n * 128, (n + 1) * 128)
                sc = scs[n % 2]
                ysb = ysbb[n % 2]
                hb = hbfb[n % 2]
                if n > 0:
                    po = C.bank()
                    S.pe("matmul", po[:, 0:256], lhsT=CT[:, csl], rhs=hb[:], start=True, stop=True)
                if n < NBLK - 1:
                    pn = C.bank()
                    S.pe("matmul", pn[:, 0:256], lhsT=Btm[:, bi, :], rhs=xd2b[n % 2][:], start=True, stop=True)
                    h3 = hst[:].rearrange("p (k e) -> p k e", k=4)
                    S.dve("tensor_tensor", out=h3, in0=h3, in1=sc[:, 8:12].unsqueeze(2).to_broadcast([128, 4, 64]), op=ALU.mult)
                    S.dve("tensor_tensor", out=hst[:], in0=hst[:], in1=pn[:, 0:256], op=ALU.add)
                    S.act("copy", out=hbfb[(n + 1) % 2][:], in_=hst[:])
                if n > 0:
                    yo = C.tf()
                    S.dve("tensor_tensor", out=yo[:, 0:256].rearrange("p (k e) -> p k e", k=4), in0=po[:, 0:256].rearrange("p (k e) -> p k e", k=4),
                          in1=sc[:, 0:4].unsqueeze(2).to_broadcast([128, 4, 64]), op=ALU.mult)
                    S.dve("tensor_tensor", out=ysb[:], in0=ysb[:], in1=yo[:, 0:256], op=ALU.add)
                xd = C.tf()
                S.dve("tensor_tensor", out=xd[:, 0:256].rearrange("p (k e) -> p k e", k=4), in0=xtm[:, bi, :].rearrange("p (k e) -> p k e", k=4),
                      in1=rs_t[:, 8:12].unsqueeze(2).to_broadcast([128, 4, 64]), op=ALU.mult)
                S.dve("tensor_tensor", out=ysb[:], in0=ysb[:], in1=xd[:, 0:256], op=ALU.add)
                S.dve("tensor_tensor", out=sys_[g % 2][:, bi, :], in0=ysb[:], in1=zs[:, bi, :], op=ALU.mult)
                if bi == 3:
                    emit_y(1, y_ssd, g, sys_[g % 2])

            for g in range(NQG):
                ssd_project(g)
                for bi in range(4):
                    n = g * 4 + bi
                    ssd_stage_a(n)
                    if n >= 1:
                        ssd_stage_b(n - 1)
            ssd_stage_b(NBLK - 1)

        if "ret" in phases:
          with S.scope():
            wr = C.load_w(wret, 0, 512)
            wrv = C.load_w(wret, 512, 512)
            load_late_u()
            rQT = S.sb("rQT", [128, SEQ], BF16)
            rKT = S.sb("rKT", [128, SEQ], BF16)
            rKd = S.sb("rKd", [128, NBLK, 128], BF16)
            rV = S.sb("rV", [128, NBLK, 256], BF16)
            rG = S.sb("rG", [128, NBLK, 256], F32)
            decT = [S.sb(f"decT{hh}", [128, 128], F32) for hh in range(2)]
            qdec = S.sb("qdec", [128, 128], F32)
            kdecT = S.sb("kdecT", [128, 128], F32)
            c128 = S.sb("c128", [128, 2], F32)
            with S.scope():
                relp = S.sb("relp", [128, 128], F32)
                caus = S.sb("caus", [128, 128], F32)
                S.dve("tensor_scalar", out=relp[:], in0=rel0[:], scalar1=0.0, scalar2=None, op0=ALU.max)
                S.dve("tensor_scalar", out=caus[:], in0=rel0[:], scalar1=0.0, scalar2=None, op0=ALU.is_ge)
                for hh in range(2):
                    S.dve("tensor_scalar", out=decT[hh][:], in0=relp[:], scalar1=cst[:, 5 + hh:6 + hh], scalar2=None, op0=ALU.mult)
                    S.act("activation", out=decT[hh][:], in_=decT[hh][:], func=AF.Exp)
                    S.dve("tensor_tensor", out=decT[hh][:], in0=decT[hh][:], in1=caus[:], op=ALU.mult)
                io_i = S.sb("io_i", [128, 128], I32)
                io_f = S.sb("io_f", [128, 128], F32)
                S.pool("iota", io_i[:], pattern=[[1, 128]], base=1, channel_multiplier=0)
                S.pool("tensor_copy", out=io_f[:], in_=io_i[:])
                S.dve("tensor_scalar", out=io_f[:], in0=io_f[:], scalar1=cst[:, 4:5], scalar2=None, op0=ALU.mult)
                S.act("activation", out=qdec[:], in_=io_f[:], func=AF.Exp)
                for hh in range(2):
                    S.dve("tensor_scalar", out=kdecT[:, hh:hh + 1], in0=rel0[:, 127:128], scalar1=cst[:, 5 + hh:6 + hh], scalar2=None, op0=ALU.mult)
                S.act("activation", out=kdecT[:, 0:2], in_=kdecT[:, 0:2], func=AF.Exp)
                S.pool("memset", c128[:, 0:1], 128.0)
                S.dve("tensor_scalar", out=c128[:, 0:1], in0=c128[:, 0:1], scalar1=cst[:, 4:5], scalar2=None, op0=ALU.mult)
                S.act("activation", out=c128[:, 1:2], in_=c128[:, 0:1], func=AF.Exp)
            with S.scope():
                T = alloc_rope()
                for g in range(NQG):
                    tsl = slice(g * NG, (g + 1) * NG)
                    rope_tables(T, g, 0, 1)
                    for which in range(2):
                        pa = C.bank()
                        proj_fm(wr, which * 256, 128, tsl, pa[:])
                        pb = C.bank()
                        proj_fm(wr, which * 256 + 128, 128, tsl, pb[:])
                        t1 = C.tf()
                        S.dve("tensor_tensor", out=t1[:], in0=pa[:], in1=T["tcos"][:], op=ALU.mult)
                        t2 = C.tf()
                        S.dve("tensor_tensor", out=t2[:], in0=pb[:], in1=T["tsin"][:], op=ALU.mult)
                        S.dve("tensor_tensor", out=t1[:], in0=t1[:], in1=t2[:], op=ALU.add)
                        if which == 0:
                            S.act("copy", out=rQT[:, tsl], in_=t1[:])
                        else:
                            S.act("mul", out=rKT[:, tsl], in_=t1[:], mul=0.125)
                            for bi in range(4):
                                blk = g * 4 + bi
                                S.pe("transpose", tpb[:, bi * 128:(bi + 1) * 128], rKT[:, blk * 128:(blk + 1) * 128], ident_b[:])
                                for hh in range(2):
                                    S.dve("tensor_scalar", out=rKd[:, blk, hh * 64:(hh + 1) * 64], in0=tpb[:, bi * 128 + hh * 64:bi * 128 + (hh + 1) * 64],
                                          scalar1=kdecT[:, hh:hh + 1], scalar2=None, op0=ALU.mult)
                    for bi in range(4):
                        blk = g * 4 + bi
                        ps = C.bank()
                        proj_tm(wrv, 0, 512, blk, ps[:])
                        S.act("copy", out=rV[:, blk, :], in_=ps[:, 0:256])
                        S.act("activation", out=rG[:, blk, :], in_=ps[:, 256:512], func=AF.Silu)
            Sst = S.sb("Sst", [128, 128], F32)
            Sbf = [S.sb(f"Sbf{i}", [128, 128], BF16) for i in range(2)]
            S.pool("memset", Sst[:], 0.0)
            rys0 = S.sb("rys0", [128, 4, 256], F32)
            rys = [rys0, rys0]
            rsc = S.sb("rsc", [128, 64], F32)
            amb = [[S.sb(f"ram{i}{hh}", [128, 128], BF16) for hh in range(2)] for i in range(2)]
            qdb = [S.sb(f"rqd{i}", [128, 128], BF16) for i in range(2)]
            junk = S.sb("rjunk", [128, 2, 128], F32)

            def ret_stage_a(n):
                csl = slice(n * 128, (n + 1) * 128)
                if n > 0:
                    S.dve("tensor_tensor", out=qdb[n % 2][:], in0=rQT[:, csl], in1=qdec[:], op=ALU.mult)
                for hh in range(2):
                    hs = slice(hh * 64, hh * 64 + 64)
                    pa = C.bank()
                    S.pe("matmul", pa[:, 0:128], lhsT=rKT[hs, csl], rhs=rQT[hs, csl], start=True, stop=True)
                    S.dve("tensor_tensor", out=amb[n % 2][hh][:], in0=pa[:, 0:128], in1=decT[hh][:], op=ALU.mult)

            def ret_stage_b(n):
                g = n // 4
                if n < NBLK - 1:
                    pk = C.bank()
                    for hh in range(2):
                        S.pe("matmul", pk[hh * 64:(hh + 1) * 64, 0:128], lhsT=rKd[:, n, hh * 64:(hh + 1) * 64], rhs=rV[:, n, hh * 128:(hh + 1) * 128],
                             start=True, stop=True)
                    S.dve("scalar_tensor_tensor", out=Sst[:], in0=Sst[:], scalar=c128[:, 1:2], in1=pk[:, 0:128], op0=ALU.mult, op1=ALU.add)
                    S.act("copy", out=Sbf[(n + 1) % 2][:], in_=Sst[:])
                for hh in range(2):
                    hs = slice(hh * 64, hh * 64 + 64)
                    po = C.bank()
                    S.pe("matmul", po[:, 0:128], lhsT=amb[n % 2][hh][:], rhs=rV[:, n, hh * 128:(hh + 1) * 128], start=True, stop=(n == 0))
                    if n > 0:
                        S.pe("matmul", po[:, 0:128], lhsT=qdb[n % 2][hs, :], rhs=Sbf[n % 2][hs, :], start=False, stop=True)
                    c0 = ((2 * n + hh) % 8) * 8
                    S.pool("memset", rsc[:, c0:c0 + 2], 0.0)
                    S.act("activation", out=junk[:, 0, :], in_=po[:, 0:128], func=AF.Identity, accum_out=rsc[:, c0:c0 + 1])
                    S.act("activation", out=junk[:, 1, :], in_=po[:, 0:128], func=AF.Square, accum_out=rsc[:, c0 + 1:c0 + 2])
                    S.dve("tensor_scalar", out=rsc[:, c0 + 2:c0 + 3], in0=rsc[:, c0:c0 + 1], scalar1=1.0 / 128.0, scalar2=None, op0=ALU.mult)
                    S.dve("tensor_tensor", out=rsc[:, c0 + 3:c0 + 4], in0=rsc[:, c0 + 2:c0 + 3], in1=rsc[:, c0 + 2:c0 + 3], op=ALU.mult)
                    S.dve("tensor_scalar", out=rsc[:, c0 + 4:c0 + 5], in0=rsc[:, c0 + 1:c0 + 2], scalar1=1.0 / 128.0, scalar2=rsc[:, c0 + 3:c0 + 4],
                          op0=ALU.mult, op1=ALU.subtract)
                    S.act("activation", out=rsc[:, c0 + 5:c0 + 6], in_=rsc[:, c0 + 4:c0 + 5], func=AF.Sqrt, bias=EPS)
                    S.dve("reciprocal", out=rsc[:, c0 + 6:c0 + 7], in_=rsc[:, c0 + 5:c0 + 6])
                    x = C.tf()[:, 0:128]
                    S.dve("tensor_scalar", out=x, in0=po[:, 0:128], scalar1=rsc[:, c0 + 2:c0 + 3], scalar2=rsc[:, c0 + 6:c0 + 7],
                          op0=ALU.subtract, op1=ALU.mult)
                    S.dve("tensor_tensor", out=rys[g % 2][:, n % 4, hh * 128:(hh + 1) * 128], in0=x, in1=rG[:, n, hh * 128:(hh + 1) * 128], op=ALU.mult)
                if n % 4 == 3:
                    emit_y(0, y_ret, g, rys[g % 2])

            for n in range(NBLK + 1):
                if n < NBLK:
                    ret_stage_a(n)
                if n >= 1:
                    ret_stage_b(n - 1)

        if "mla" in phases:
          with S.scope():
            T = alloc_rope()
            tcos, tsin = T["tcos"], T["tsin"]
            gm = S.sb("gm", [128, 4], F32)
            S.dma_group("sp", [(gm[:], gmla)], C.ctr)
            load_late_u()
            wm = C.load_w(wmla, 0, 448)
            wq_t = C.load_w(wuq, 0, 256)
            wkv_t = C.load_w(wukv, 0, 384)
            KT = [S.sb(f"mKT{h}", [96, SEQ], BF16) for h in range(2)]
            QT = [S.sb(f"mQT{h}", [96, SEQ], BF16) for h in range(2)]
            Vm = S.sb("mV", [128, NBLK, 2, 129], BF16)
            S.pool("memset", Vm[:, :, :, 128:129], 1.0)
            cq_f = S.sb("cq_f", [128, 2, NG], F32)
            cqn = S.sb("cqn", [128, 2, NG], BF16)
            ckv_f = S.sb("ckv_f", [128, NG], F32)
            ckvn = S.sb("ckvn", [128, NG], BF16)
            for g in range(NQG):
                tsl = slice(g * NG, (g + 1) * NG)
                rope_tables(T, g, 2, 3)
                for c in range(2):
                    ps = C.bank()
                    proj_fm(wm, c * 128, 128, tsl, ps[:])
                    S.act("copy", out=cq_f[:, c, :], in_=ps[:])
                rmsnorm_fm(S, C, lambda dc: cq_f[:, dc, :], 2, gm[:, 0:2], lambda dc: cqn[:, dc, :], NG, 256)
                ps = C.bank()
                proj_fm(wm, 256, 128, tsl, ps[:])
                S.act("copy", out=ckv_f[:], in_=ps[:])
                rmsnorm_fm(S, C, lambda dc: ckv_f[:], 1, gm[:, 2:3], lambda dc: ckvn[:], NG, 128)
                pa = C.bank()
                proj_fm(wm, 384, 32, tsl, pa[64:96, :])
                pb = C.bank()
                proj_fm(wm, 416, 32, tsl, pb[64:96, :])
                t1 = C.tf()
                S.dve("tensor_tensor", out=t1[64:96, :], in0=pa[64:96, :], in1=tcos[64:96, :], op=ALU.mult)
                t2 = C.tf()
                S.dve("tensor_tensor", out=t2[64:96, :], in0=pb[64:96, :], in1=tsin[64:96, :], op=ALU.mult)
                for h in range(2):
                    S.dve("tensor_tensor", out=KT[h][64:96, tsl], in0=t1[64:96, :], in1=t2[64:96, :], op=ALU.add)
                for h in range(2):
                    ps = C.bank()
                    S.pe("matmul", ps[0:64, :], lhsT=wkv_t[:, 0, h * 192:h * 192 + 64], rhs=ckvn[:], start=True, stop=True)
                    S.act("copy", out=KT[h][0:64, tsl], in_=ps[0:64, :])
                    for bi in range(4):
                        blk = g * 4 + bi
                        ps = C.bank()
                        S.pe("matmul", ps[:, 0:128], lhsT=ckvn[:, bi * 128:(bi + 1) * 128], rhs=wkv_t[:, 0, h * 192 + 64:h * 192 + 192], start=True, stop=True)
                        S.act("copy", out=Vm[:, blk, h, 0:128], in_=ps[:, 0:128])
                    ps = C.bank()
                    mm_chain(S, ps[0:96, :], [(wq_t[:, k, h * 96:(h + 1) * 96], cqn[:, k, :]) for k in range(2)])
                    ps2 = C.bank()
                    mm_chain(S, ps2[64:96, :], [(wq_t[:, k, 192 + h * 32:192 + (h + 1) * 32], cqn[:, k, :]) for k in range(2)])
                    S.act("copy", out=QT[h][0:64, tsl], in_=ps[0:64, :])
                    t1 = C.tf()
                    S.dve("tensor_tensor", out=t1[64:96, :], in0=ps[64:96, :], in1=tcos[64:96, :], op=ALU.mult)
                    t2 = C.tf()
                    S.dve("tensor_tensor", out=t2[64:96, :], in0=ps2[64:96, :], in1=tsin[64:96, :], op=ALU.mult)
                    S.dve("tensor_tensor", out=QT[h][64:96, tsl], in0=t1[64:96, :], in1=t2[64:96, :], op=ALU.add)
            ystage = [S.sb(f"mys{i}", [128, 4, 256], F32) for i in range(2)]
            rcp = S.sb("m_rcp", [128, 8], F32)

            def mla_epi(m, g, qb, acc):
                ys = ystage[g % 2]
                col = (m * 4 + qb)
                S.dve("reciprocal", out=rcp[:, col:col + 1], in_=acc[:, 128:129])
                S.dve("tensor_scalar", out=ys[:, qb, m * 128:(m + 1) * 128], in0=acc[:, 0:128], scalar1=rcp[:, col:col + 1], scalar2=None, op0=ALU.mult)
                if m == 1 and qb == 3:
                    emit_y(2, y_mla, g, ys)

            attention(2, lambda m: KT[m][:, :], lambda m: QT[m][:, :], lambda m, kb: Vm[:, kb, m, :], 96 ** -0.5,
                      lambda m, d: (maskT[:] if d == 0 else None), lambda m: None, mla_epi)

        if "diff" in phases:
          with S.scope():
            wd = C.load_w(wdiff, 0, 512)
            wdv = C.load_w(wdiff, 512, 256)
            load_late_u()
            dQp = [S.sb(f"dQp{m}", [128, SEQ], BF16) for m in range(4)]
            for m in range(4):
                S.pool("memset", dQp[m][:], 0.0)
            dKT = S.sb("dKT", [128, 2, SEQ], BF16)
            dV = S.sb("dV", [128, NBLK, 2, 129], BF16)
            S.pool("memset", dV[:, :, :, 128:129], 1.0)
            for g in range(NQG):
                tsl = slice(g * NG, (g + 1) * NG)
                for c in range(2):
                    ps = C.bank()
                    proj_fm(wd, c * 128, 128, tsl, ps[:])
                    for w in range(2):
                        S.act("copy", out=dQp[2 * c + w][w * 64:(w + 1) * 64, tsl], in_=ps[w * 64:(w + 1) * 64, :])
                for c in range(2):
                    ps = C.bank()
                    proj_fm(wd, 256 + c * 128, 128, tsl, ps[:])
                    S.act("copy", out=dKT[:, c, tsl], in_=ps[:])
                for bi in range(4):
                    blk = g * 4 + bi
                    ps = C.bank()
                    proj_tm(wdv, 0, 256, blk, ps[:, 0:256])
                    S.act("copy", out=dV[:, blk, :, 0:128], in_=ps[:, 0:256].rearrange("p (h e) -> p h e", h=2))
            tbl = S.sb("tbl", [128, 64], F32)
            lamt = S.sb("lamt", [128, 256], F32)
            dngs = S.sb("dngs", [128, 128], F32)
            S.dma_group("sp", [(tbl[:], dtbl.partition_broadcast(128)), (lamt[:], dlam.partition_broadcast(128)),
                               (dngs[:], dng.partition_broadcast(128))], C.ctr)
            dT = S.sb("dT", [128, 64], F32)
            S.dve("tensor_tensor", out=dT[:, 1:64], in0=tbl[:, 1:64], in1=tbl[:, 0:63], op=ALU.subtract)
            rel1 = S.sb("rel1", [128, 128], F32)
            S.dve("tensor_scalar", out=rel1[:], in0=rel0[:], scalar1=128.0, scalar2=None, op0=ALU.add)
            thr = t5_thresholds()
            Bt = [[S.sb(f"Bt{hh}{dl}", [128, 128], F32) for dl in range(2)] for hh in range(2)]
            for hh in range(2):
                for dl in range(2):
                    relt = rel0 if dl == 0 else rel1
                    bt = Bt[hh][dl]
                    S.dve("tensor_scalar", out=bt[:], in0=relt[:], scalar1=0.0, scalar2=tbl[:, hh * 32:hh * 32 + 1], op0=ALU.mult, op1=ALU.add)
                    for m in range(1, 32):
                        tmp = C.tf()
                        S.dve("tensor_scalar", out=tmp[:, 0:128], in0=relt[:], scalar1=float(thr[m - 1]), scalar2=dT[:, hh * 32 + m:hh * 32 + m + 1],
                              op0=ALU.is_ge, op1=ALU.mult)
                        S.dve("tensor_tensor", out=bt[:], in0=bt[:], in1=tmp[:, 0:128], op=ALU.add)
                    S.dve("tensor_scalar", out=bt[:], in0=bt[:], scalar1=tbl[:, hh * 32 + 31:hh * 32 + 32], scalar2=8.0, op0=ALU.subtract, op1=ALU.mult)
                    if dl == 0:
                        S.dve("tensor_tensor", out=bt[:], in0=bt[:], in1=maskT[:], op=ALU.add)
            lsc = S.sb("lsc", [128, 8], F32)
            for i in range(2):
                tmp = C.tf()
                S.dve("tensor_tensor", out=tmp[:, 0:64], in0=lamt[:, i * 128:i * 128 + 64], in1=lamt[:, i * 128 + 64:i * 128 + 128], op=ALU.mult)
                S.dve("reduce_sum", out=lsc[:, i:i + 1], in_=tmp[:, 0:64], axis=AX.X)
                S.act("activation", out=lsc[:, 2 + i:3 + i], in_=lsc[:, i:i + 1], func=AF.Exp)
            S.dve("tensor_scalar", out=lsc[:, 4:5], in0=lsc[:, 3:4], scalar1=lsc[:, 2:3], scalar2=-float(lambda_init), op0=ALU.subtract, op1=ALU.add)
            S.dve("tensor_scalar", out=dngs[:], in0=dngs[:], scalar1=1.0 - float(lambda_init), scalar2=None, op0=ALU.mult)
            o1buf = S.sb("o1buf", [128, 2, 4, 128], F32)
            dys = [S.sb(f"dys{i}", [128, 4, 256], F32) for i in range(2)]
            dsc = S.sb("dsc", [128, 64], F32)
            dci = [0]

            def diff_epi(m, g, qb, acc):
                hh, which = m // 2, m % 2
                c0 = (dci[0] % 8) * 8
                dci[0] += 1
                S.dve("reciprocal", out=dsc[:, c0:c0 + 1], in_=acc[:, 128:129])
                if which == 0:
                    S.dve("tensor_scalar", out=o1buf[:, hh, qb, :], in0=acc[:, 0:128], scalar1=dsc[:, c0:c0 + 1], scalar2=None, op0=ALU.mult)
                    return
                ys = dys[g % 2]
                o2 = C.tf()
                S.dve("tensor_scalar", out=o2[:, 0:128], in0=acc[:, 0:128], scalar1=dsc[:, c0:c0 + 1], scalar2=None, op0=ALU.mult)
                S.dve("scalar_tensor_tensor", out=o2[:, 128:256], in0=o2[:, 0:128], scalar=lsc[:, 4:5], in1=o1buf[:, hh, qb, :], op0=ALU.mult, op1=ALU.add)
                S.dve("tensor_tensor", out=o2[:, 256:384], in0=o2[:, 128:256], in1=o2[:, 128:256], op=ALU.mult)
                S.dve("reduce_sum", out=dsc[:, c0 + 1:c0 + 2], in_=o2[:, 256:384], axis=AX.X)
                S.act("activation", out=dsc[:, c0 + 2:c0 + 3], in_=dsc[:, c0 + 1:c0 + 2], func=AF.Ln, scale=1.0 / 128.0, bias=EPS)
                S.act("activation", out=dsc[:, c0 + 3:c0 + 4], in_=dsc[:, c0 + 2:c0 + 3], func=AF.Exp, scale=-0.5)
                S.dve("scalar_tensor_tensor", out=ys[:, qb, hh * 128:(hh + 1) * 128], in0=o2[:, 128:256], scalar=dsc[:, c0 + 3:c0 + 4], in1=dngs[:],
                      op0=ALU.mult, op1=ALU.mult)
                if m == 3 and qb == 3:
                    emit_y(3, y_diff, g, ys)

            attention(4, lambda m: dKT[:, m // 2, :], lambda m: dQp[m][:, :],
                      lambda m, kb: dV[:, kb, m // 2, :], 0.125,
                      lambda m, d: Bt[m // 2][d][:], lambda m: tbl[:, (m // 2) * 32 + 31:(m // 2) * 32 + 32], diff_epi)

        if not fused:
            S.finish()
        print("B built: ins", S.n_ins, "waits", S.n_wait, {k: e.tr.val for k, e in S.engs.items()})
    return nc

def swap_halves(w, nheads, dh):
    K = w.shape[0]
    w4 = w.reshape(K, nheads, 2, dh // 2)
    return np.ascontiguousarray(w4[:, :, ::-1, :]).reshape(K, nheads * dh)


def b_consts(j):
    c = np.zeros((128, 16), np.float32)
    p = np.arange(128)
    c[:, 0] = np.exp(-math.log(10000.0) * ((p % 64) % 32).astype(np.float32) / 32).astype(np.float32)
    c[:, 1] = np.where((p % 64) < 32, -1.0, 1.0)
    c[:, 2] = np.exp(-math.log(10000.0) * (((p - 64) % 32) % 16).astype(np.float32) / 16).astype(np.float32)
    c[:, 3] = np.where(((p - 64) % 32) < 16, -1.0, 1.0)
    lg = np.log1p(-np.exp2(-5.0 - np.arange(4, dtype=np.float32))).astype(np.float32)
    c[:, 4] = lg[2 * j + p // 64]
    c[:, 5] = lg[2 * j]
    c[:, 6] = lg[2 * j + 1]
    return c


def b_inputs(inp, l, b, j, uT_bf16, phases=("ret", "ssd", "mla", "diff")):
    W = inp["w_in"][l]
    o = IN_OFF
    d = {"pos": np.ascontiguousarray(inp["positions"][b:b + 1, :]).astype(np.int32), "cst": b_consts(j)}
    if uT_bf16 is not None:
        d["uT"] = uT_bf16
    if "mla" in phases:
        kr = W[:, o[9]:o[10]]
        d["wmla"] = np.ascontiguousarray(np.concatenate([W[:, o[7]:o[8]], W[:, o[8]:o[9]], kr, swap_halves(kr, 1, 32)], axis=1))
        uq = inp["mla_w_uq"][l].reshape(256, 4, 96)[:, 2 * j:2 * j + 2, :]
        uq_rope_sw = swap_halves(np.ascontiguousarray(uq[:, :, 64:96]).reshape(256, 64), 2, 32)
        d["wuq"] = np.ascontiguousarray(np.concatenate([uq.reshape(256, 192), uq_rope_sw], axis=1))
        d["wukv"] = np.ascontiguousarray(inp["mla_w_ukv"][l].reshape(128, 4, 192)[:, 2 * j:2 * j + 2, :].reshape(128, 384))
        g = np.zeros((128, 4), np.float32)
        g[:, 0:2] = inp["mla_q_norm"][l].reshape(2, 128).T
        g[:, 2] = inp["mla_kv_norm"][l]
        d["gmla"] = g
    if "diff" in phases:
        dq = W[:, o[10]:o[11]][:, j * 256:(j + 1) * 256]
        dk = W[:, o[11]:o[12]][:, j * 256:(j + 1) * 256]
        dv = W[:, o[12]:o[13]][:, j * 256:(j + 1) * 256]
        d["wdiff"] = np.ascontiguousarray(np.concatenate([dq, dk, dv], axis=1))
        d["dlam"] = np.ascontiguousarray(inp["diff_lambda"][l].reshape(1, 256))
        d["dng"] = np.ascontiguousarray(inp["diff_norm"][l].reshape(1, 128))
        d["dtbl"] = np.ascontiguousarray(inp["rel_bias"][:, 2 * j:2 * j + 2].T.reshape(1, 64))
    if "ret" in phases:
        rq = W[:, o[0]:o[1]][:, j * 128:(j + 1) * 128]
        rk = W[:, o[1]:o[2]][:, j * 128:(j + 1) * 128]
        rv = W[:, o[2]:o[3]][:, j * 256:(j + 1) * 256]
        rg = W[:, o[3]:o[4]][:, j * 256:(j + 1) * 256]
        d["wret"] = np.ascontiguousarray(np.concatenate([rq, swap_halves(rq, 2, 64), rk, swap_halves(rk, 2, 64), rv, rg], axis=1))
    if "ssd" in phases:
        sz = W[:, o[4]:o[5]][:, j * 256:(j + 1) * 256]
        xbc = W[:, o[5]:o[6]]
        sx = xbc[:, j * 256:(j + 1) * 256]
        sB = xbc[:, 512 + j * 128:512 + (j + 1) * 128]
        sC = xbc[:, 768 + j * 128:768 + (j + 1) * 128]
        sdt = W[:, o[6]:o[7]][:, j * 4:(j + 1) * 4]
        d["wssd"] = np.ascontiguousarray(np.concatenate([sx, sB, sC, sz, sdt, sdt], axis=1))
        cw = inp["ssm_conv_w"][l]
        cb = inp["ssm_conv_b"][l]
        chans = [slice(j * 256, j * 256 + 128), slice(j * 256 + 128, j * 256 + 256), slice(512 + j * 128, 512 + (j + 1) * 128), slice(768 + j * 128, 768 + (j + 1) * 128)]
        cs = np.zeros((128, 24), np.float32)
        for ci, sl in enumerate(chans):
            cs[:, ci * 4:(ci + 1) * 4] = cw[:, sl].T
            cs[:, 16 + ci] = cb[sl]
        d["cssd"] = cs
        d["rssd"] = np.ascontiguousarray(np.concatenate([inp["ssm_dt_bias"][l][4 * j:4 * j + 4], inp["ssm_a_log"][l][4 * j:4 * j + 4], inp["ssm_d"][l][4 * j:4 * j + 4]]).reshape(1, 12).astype(np.float32))
    return d

from concourse.bass_utils import run_bass_kernel_spmd

DEPTH = 2
_DBG = {}


def pack_gains(d):
    g = np.zeros((128, 64), np.float32)
    for k, off in (("f1", 0), ("mix", 8), ("mixp", 16), ("xa", 24), ("mem", 32), ("f2", 40), ("fin", 48), ("ssm", 56)):
        if k in d:
            v = np.asarray(d[k], np.float32)
            n = v.shape[0] // 128
            g[:, off:off + n] = v.reshape(n, 128).T
    return g


def _c(a):
    return np.ascontiguousarray(a)


def build_fused():
    nc = bass.Bass("TRN2", target_bir_lowering=False)
    es = ExitStack()
    S = Sched(nc, es)
    C = Ctx(S, nbanks=7, nwslots=0)
    tpb = S.ps("tpb", [128, 1024], BF16)
    a_tr = [S.dma_tracker(f"a{i}") for i in range(8)]
    b_tr = [S.dma_tracker(f"b{i}") for i in range(4)]
    cc_tr = [S.dma_tracker(f"cc{i}") for i in range(12)]
    u_tr = [S.dma_tracker(f"ul{i}") for i in range(4)]

    def internal(name, shape, dt):
        return nc.dram_tensor(name, list(shape), dt, kind="Internal").ap()

    h_scr = [internal(f"h_scr{l}", [D, NTA], F32) for l in range(DEPTH)]
    u_src = [[internal(f"u_src{l}_{t}", [D, TB], BF16) for t in range(NTA // TB)] for l in range(DEPTH)]
    u_all = [[internal(f"u_all{l}_{t}", [2 * D, TB], BF16) for t in range(NTA // TB)] for l in range(DEPTH)]
    y_src = [[internal(f"y_src{l}_{i}", [256, SEQ], BF16) for i in range(4)] for l in range(DEPTH)]
    y_all = [[internal(f"y_all{l}_{i}", [512, SEQ], BF16) for i in range(4)] for l in range(DEPTH)]
    base = dict(nc=nc, S=S, C=C, tpb=tpb, a_tr=a_tr, b_tr=b_tr, cc_tr=cc_tr, u_tr=u_tr)
    build_A(False, True, False, fused=dict(base, pfx="A0_", h_src=None, h_dst=h_scr[0], u_src=u_src[0], u_all=u_all[0]))
    for l in range(DEPTH):
        lam0 = 0.8 - 0.6 * math.exp(-0.3 * l)
        build_B(lambda_init=lam0, fused=dict(base, pfx=f"B{l}_", u_all=u_all[l], y_src=y_src[l], y_all=y_all[l]))
        last = (l == DEPTH - 1)
        build_A(True, not last, last, fused=dict(base, pfx=f"A{l + 1}_", h_src=h_scr[l], h_dst=(None if last else h_scr[l + 1]),
                                                 y_all=y_all[l], u_src=(None if last else u_src[l + 1]), u_all=(None if last else u_all[l + 1])))
    S.finish()
    print("FUSED built: ins", S.n_ins, "waits", S.n_wait, {k: e.tr.val for k, e in S.engs.items()})
    es.close()
    return nc


def kernel(**inputs):
    inp = {k: np.asarray(v) for k, v in inputs.items()}
    cores = list(range(8))
    x = inp["x"]
    nc = build_fused()
    in_maps = []
    for c in cores:
        b, j = c // 2, c % 2
        d = {}
        d["A0_hT"] = _c(x[b, j * NTA:(j + 1) * NTA, :].T)
        d["A0_gains"] = pack_gains({"f1": inp["ffn1_norm"][0], "mix": inp["mix_norm"][0]})
        d["A0_f1g"] = inp["ffn1_w_gate"][0]; d["A0_f1u"] = inp["ffn1_w_up"][0]; d["A0_f1d"] = inp["ffn1_w_down"][0]
        for l in range(DEPTH):
            bi = b_inputs(inp, l, b, j, None)
            for k, v in bi.items():
                if k != "uT":
                    d[f"B{l}_{k}"] = v
            last = (l == DEPTH - 1)
            p = f"A{l + 1}_"
            g = {"mixp": inp["mix_norm"][l], "ssm": inp["ssm_norm"][l], "xa": inp["xa_norm"][l], "mem": inp["mem_norm"][l], "f2": inp["ffn2_norm"][l]}
            d[p + "memT"] = _c(inp["mem"][b].T)
            d[p + "wgl"] = _c(inp["w_in"][l][:, IN_OFF[13]:]); d[p + "wbr"] = _c(inp["w_branch"][l].reshape(2048, D)); d[p + "wout"] = inp["w_out"][l]
            d[p + "wq"] = inp["xa_wq"][l]; d[p + "wk"] = inp["xa_wk"][l]; d[p + "wv"] = inp["xa_wv"][l]; d[p + "wo"] = inp["xa_wo"][l]
            d[p + "f2g"] = inp["ffn2_w_gate"][l]; d[p + "f2u"] = inp["ffn2_w_up"][l]; d[p + "f2d"] = inp["ffn2_w_down"][l]
            if not last:
                g["f1"] = inp["ffn1_norm"][l + 1]
                g["mix"] = inp["mix_norm"][l + 1]
                d[p + "f1g"] = inp["ffn1_w_gate"][l + 1]; d[p + "f1u"] = inp["ffn1_w_up"][l + 1]; d[p + "f1d"] = inp["ffn1_w_down"][l + 1]
            else:
                g["fin"] = inp["final_norm"]
            gp = pack_gains(g)
            gp[:, 60] = 1.0 - j
            gp[:, 61] = float(j)
            d[p + "gains"] = gp
        in_maps.append(d)
    res = run_bass_kernel_spmd(nc, in_maps, core_ids=cores).results
    out = np.empty((NB, SEQ, D), np.float32)
    for c in cores:
        b, j = c // 2, c % 2
        out[b, j * NTA:(j + 1) * NTA, :] = np.asarray(res[c][f"A{DEPTH}_outT"]).T
    return out
```
